# Optimizing a Trainium2 kernel written in Bass

```python
import jax, jax.numpy as jnp
from jax import lax
import numpy as np

D_MODEL = 1024
BATCH = 8
SEQ = 8192
DEPTH = 1

GRID_W = 64
CTX_LEN = 256
N_HEADS = 8
QK_NOPE = 64
QK_ROPE = 32
ROPE_AXIS = QK_ROPE // 2
V_DIM = 64
Q_LORA = 384
KV_LORA = 256
MLA_WIDTH = N_HEADS * V_DIM
POOL_WINDOWS = (2, 4, 8, 16)
POOL_GROUPS = len(POOL_WINDOWS)
POOL_WIDTH = 512
POOL_GC = POOL_WIDTH // POOL_GROUPS
D_FF = 4 * D_MODEL
IN_SPLITS = (Q_LORA,
             Q_LORA + KV_LORA,
             Q_LORA + KV_LORA + QK_ROPE,
             Q_LORA + KV_LORA + QK_ROPE + POOL_WIDTH,
             Q_LORA + KV_LORA + QK_ROPE + POOL_WIDTH + D_MODEL)
IN_WIDTH = Q_LORA + KV_LORA + QK_ROPE + POOL_WIDTH + 2 * D_MODEL
Q_BLOCK = 128
ROPE_THETA = 10000.0
NORM_EPS = 1e-6
ATTN_SCALE = (QK_NOPE + QK_ROPE) ** -0.5

kernel_name = 'hybrid_mla_pool_dit_block'


def rmsnorm(x, g):
    xf = x.astype(jnp.float32)
    y = xf * lax.rsqrt(jnp.mean(xf * xf, axis=-1, keepdims=True) + NORM_EPS)
    return (y * g.astype(jnp.float32)).astype(x.dtype)


def modulate(x, g, shift, scale):
    return rmsnorm(x, g) * (1 + scale) + shift


def adaln(cond, w_ada, b_ada):
    return jnp.split(jax.nn.silu(cond) @ w_ada + b_ada, 6, axis=-1)


def rotate(x, ang):
    half = x.shape[-1] // 2
    cos = jnp.cos(ang).astype(x.dtype)
    sin = jnp.sin(ang).astype(x.dtype)
    x1, x2 = x[..., :half], x[..., half:]
    return jnp.concatenate([x1 * cos - x2 * sin, x1 * sin + x2 * cos], axis=-1)


def axial_rope(x, ang_row, ang_col):
    return jnp.concatenate([rotate(x[..., :ROPE_AXIS], ang_row),
                            rotate(x[..., ROPE_AXIS:], ang_col)], axis=-1)


def mixer_inputs(h, w_in, q_norm_g, kv_norm_g, w_uq, w_ukv):
    b, l, _ = h.shape
    c_q, c_kv, k_rope, pool_in, g_mla, g_pool = jnp.split(h @ w_in, IN_SPLITS, axis=-1)
    q = (rmsnorm(c_q, q_norm_g) @ w_uq).reshape(b, l, N_HEADS, QK_NOPE + QK_ROPE)
    kv = (rmsnorm(c_kv, kv_norm_g) @ w_ukv).reshape(b, l, N_HEADS, QK_NOPE + V_DIM)
    q_nope, q_rope = q[..., :QK_NOPE], q[..., QK_NOPE:]
    k_nope, v = kv[..., :QK_NOPE], kv[..., QK_NOPE:]
    return q_nope, q_rope, k_nope, k_rope, v, pool_in, g_mla, g_pool


def attend(q_nope, q_rope, k_nope, k_rope, v):
    b, lq, h, _ = q_nope.shape
    nb = lq // Q_BLOCK
    qn = q_nope.reshape(b, nb, Q_BLOCK, h, QK_NOPE).swapaxes(0, 1)
    qr = q_rope.reshape(b, nb, Q_BLOCK, h, QK_ROPE).swapaxes(0, 1)

    def block(args):
        qn_b, qr_b = args
        s = (jnp.einsum('bqhd,bkhd->bhqk', qn_b, k_nope)
             + jnp.einsum('bqhr,bkr->bhqk', qr_b, k_rope))
        p = jax.nn.softmax(s.astype(jnp.float32) * ATTN_SCALE, axis=-1).astype(v.dtype)
        return jnp.einsum('bhqk,bkhd->bqhd', p, v)

    o = lax.map(block, (qn, qr))
    return o.swapaxes(0, 1).reshape(b, lq, h * V_DIM)


def multiscale_pool(u, pool_w, pool_scale):
    b, l, _ = u.shape
    uf = u.astype(jnp.float32)
    cs = jnp.concatenate([jnp.zeros((b, 1, POOL_WIDTH), jnp.float32),
                          jnp.cumsum(uf, axis=1)], axis=1)
    t = jnp.arange(l)
    outs = []
    for g, w in enumerate(POOL_WINDOWS):
        lo = jnp.clip(t - w // 2, 0, l)
        hi = jnp.clip(t + w // 2, 0, l)
        csg = cs[..., g * POOL_GC:(g + 1) * POOL_GC]
        cnt = (hi - lo).astype(jnp.float32)[None, :, None]
        outs.append((csg[:, hi] - csg[:, lo]) / cnt - uf[..., g * POOL_GC:(g + 1) * POOL_GC])
    d = jnp.stack(outs, axis=2).astype(u.dtype)
    y = jnp.einsum('blgc,gcd->blgd', d, pool_w).reshape(b, l, POOL_WIDTH)
    return y * pool_scale


def merge_branches(attn, pool_in, g_mla, g_pool, pool_w, pool_scale, w_br_mla, w_br_pool, w_out):
    pooled = multiscale_pool(pool_in, pool_w, pool_scale)
    merged = (jax.nn.sigmoid(g_mla) * (attn @ w_br_mla)
              + jax.nn.sigmoid(g_pool) * (pooled @ w_br_pool))
    return merged @ w_out


def channel_mlp(h, w1, w2):
    return jnp.square(jax.nn.relu(h @ w1)) @ w2


def setup_inputs(seed: int = 0) -> dict:
    key = jax.random.key(seed)
    ks = jax.random.split(key, 24)

    def nrm(k, shape, fan_in, s=1.0):
        return jax.random.normal(k, shape, jnp.float32) * (s * fan_in ** -0.5)

    def gain(k, shape):
        return 1.0 + 0.05 * jax.random.normal(k, shape, jnp.float32)

    L = DEPTH
    return {
        'x': jax.random.normal(ks[0], (BATCH, SEQ, D_MODEL), jnp.float32),
        'c': jax.random.normal(ks[1], (BATCH, D_MODEL), jnp.float32),
        'ctx': jax.random.normal(ks[2], (BATCH, CTX_LEN, D_MODEL), jnp.float32),
        'c_ctx': jax.random.normal(ks[3], (D_MODEL,), jnp.float32),
        'w_ada': nrm(ks[4], (L, D_MODEL, 6 * D_MODEL), D_MODEL, 0.5),
        'b_ada': 0.02 * jax.random.normal(ks[5], (L, 6 * D_MODEL), jnp.float32),
        'norm1_g': gain(ks[6], (L, D_MODEL)),
        'w_in': nrm(ks[7], (L, D_MODEL, IN_WIDTH), D_MODEL),
        'q_norm_g': gain(ks[8], (L, Q_LORA)),
        'kv_norm_g': gain(ks[9], (L, KV_LORA)),
        'w_uq': nrm(ks[10], (L, Q_LORA, N_HEADS * (QK_NOPE + QK_ROPE)), Q_LORA),
        'w_ukv': nrm(ks[11], (L, KV_LORA, N_HEADS * (QK_NOPE + V_DIM)), KV_LORA),
        'w_br_mla': nrm(ks[12], (L, MLA_WIDTH, D_MODEL), MLA_WIDTH),
        'pool_w': nrm(ks[13], (L, POOL_GROUPS, POOL_GC, POOL_GC), POOL_GC),
        'pool_scale': gain(ks[14], (L, POOL_WIDTH)),
        'w_br_pool': nrm(ks[15], (L, POOL_WIDTH, D_MODEL), POOL_WIDTH),
        'w_out': nrm(ks[16], (L, D_MODEL, D_MODEL), D_MODEL),
        'norm2_g': gain(ks[17], (L, D_MODEL)),
        'w_mlp1': nrm(ks[18], (L, D_MODEL, D_FF), D_MODEL),
        'w_mlp2': nrm(ks[19], (L, D_FF, D_MODEL), D_FF),
        'final_g': gain(ks[20], (D_MODEL,)),
    }


def reference(x, c, ctx, c_ctx, w_ada, b_ada, norm1_g, w_in, q_norm_g, kv_norm_g, w_uq, w_ukv,
              w_br_mla, pool_w, pool_scale, w_br_pool, w_out, norm2_g, w_mlp1, w_mlp2, final_g):
    seq_len = x.shape[1]
    n_rows = seq_len // GRID_W
    row = jnp.repeat(jnp.arange(n_rows, dtype=jnp.float32), GRID_W)
    col = jnp.tile(jnp.arange(GRID_W, dtype=jnp.float32), n_rows)
    inv_freq = ROPE_THETA ** (-jnp.arange(0, ROPE_AXIS, 2, dtype=jnp.float32) / ROPE_AXIS)
    ang_row = row[:, None] * inv_freq
    ang_col = col[:, None] * inv_freq

    for i in range(DEPTH):
        sh1, sc1, gt1, sh2, sc2, gt2 = adaln(c, w_ada[i], b_ada[i])
        csh1, csc1, cgt1, csh2, csc2, cgt2 = adaln(c_ctx, w_ada[i], b_ada[i])

        h_ctx = modulate(ctx, norm1_g[i], csh1, csc1)
        qn_c, qr_c, kn_c, kr_c, v_c, pool_c, gm_c, gp_c = mixer_inputs(
            h_ctx, w_in[i], q_norm_g[i], kv_norm_g[i], w_uq[i], w_ukv[i])

        h = modulate(x, norm1_g[i], sh1[:, None], sc1[:, None])
        qn, qr, kn, kr, v, pool_x, gm, gp = mixer_inputs(
            h, w_in[i], q_norm_g[i], kv_norm_g[i], w_uq[i], w_ukv[i])
        qr = axial_rope(qr, ang_row[None, :, None], ang_col[None, :, None])
        kr = axial_rope(kr, ang_row[None], ang_col[None])

        attn = attend(qn, qr,
                      jnp.concatenate([kn_c, kn], axis=1),
                      jnp.concatenate([kr_c, kr], axis=1),
                      jnp.concatenate([v_c, v], axis=1))
        x = x + gt1[:, None] * merge_branches(attn, pool_x, gm, gp, pool_w[i], pool_scale[i],
                                              w_br_mla[i], w_br_pool[i], w_out[i])
        h2 = modulate(x, norm2_g[i], sh2[:, None], sc2[:, None])
        x = x + gt2[:, None] * channel_mlp(h2, w_mlp1[i], w_mlp2[i])

        if i < DEPTH - 1:
            attn_c = attend(qn_c, qr_c, kn_c, kr_c, v_c)
            ctx = ctx + cgt1 * merge_branches(attn_c, pool_c, gm_c, gp_c, pool_w[i], pool_scale[i],
                                              w_br_mla[i], w_br_pool[i], w_out[i])
            h2c = modulate(ctx, norm2_g[i], csh2, csc2)
            ctx = ctx + cgt2 * channel_mlp(h2c, w_mlp1[i], w_mlp2[i])

    return rmsnorm(x, final_g)
```

```python
import numpy as np
import ml_dtypes
import concourse.bass as bass
import concourse.mybir as mybir
from concourse.bass_utils import run_bass_kernel_spmd

F32 = mybir.dt.float32
BF16 = mybir.dt.bfloat16
AF = mybir.ActivationFunctionType
ALU = mybir.AluOpType

D = 1024
L = 8192
CTX = 256
NKT = (L + CTX) // 128
H = 8
EPS = 1e-6
SCALE = 96.0 ** -0.5
ENGS = ['pe', 'act', 'dve', 'pool', 'sp']


class Rec:
    def __init__(self, nc):
        self.nc = nc
        self.ops = []
        self.lastw = {}
        self.rd_eng = {}
        self.rd_dma = {}
        self.bank = {}
        self.bank_last = {}

    def op(self, eng, fn, reads=(), writes=(), dma=None):
        j = len(self.ops)
        deps = {}
        bset = set()
        for k in list(reads) + list(writes):
            if k in self.bank:
                bv = self.bank[k]
                bset.update(bv if isinstance(bv, tuple) else (bv,))
        for bnk in bset:
            la = self.bank_last.setdefault(bnk, {})
            for e2, i in la.items():
                if e2 != eng:
                    deps.setdefault(i, False)
            la[eng] = j
        for k in reads:
            i = self.lastw.get(k)
            if i is not None:
                deps[i] = True
        for k in writes:
            for i in self.rd_eng.get(k, {}).values():
                deps.setdefault(i, False)
            for i in self.rd_dma.get(k, ()):
                deps.setdefault(i, False)
            i = self.lastw.get(k)
            if i is not None:
                deps.setdefault(i, False)
        for k in reads:
            if dma is None:
                self.rd_eng.setdefault(k, {})[eng] = j
            else:
                self.rd_dma.setdefault(k, []).append(j)
        for k in writes:
            self.lastw[k] = j
            self.rd_eng[k] = {}
            self.rd_dma[k] = []
        keep = []
        for i, raw in deps.items():
            oi = self.ops[i]
            if oi['dma'] is None and dma is None and oi['eng'] == eng:
                if not raw or eng == 'pe':
                    continue
            keep.append(i)
        self.ops.append(dict(eng=eng, fn=fn, dma=dma, deps=keep, sig=False))
        return j

    def barrier(self):
        last = {}
        for idx, o in enumerate(self.ops):
            if o['fn'] is None:
                continue
            if o['dma'] is not None:
                last[('d', o['dma'])] = idx
            else:
                last[('e', o['eng'])] = idx
        deps = list(last.values())
        for e in ENGS:
            self.ops.append(dict(eng=e, fn=None, dma=None, deps=list(deps), sig=False))
        self.lastw.clear()
        self.rd_eng.clear()
        self.rd_dma.clear()
        self.bank_last.clear()

    def finalize(self):
        nc = self.nc
        ops = self.ops
        for o in ops:
            for i in o['deps']:
                ops[i]['sig'] = True
        esem = {e: nc.alloc_semaphore("s_" + e) for e in ENGS}
        dsem = {}
        ecnt = {e: 0 for e in ENGS}
        dcnt = {}
        for o in ops:
            if o['fn'] is None:
                continue
            if o['dma'] is not None:
                k = o['dma']
                if k not in dsem:
                    dsem[k] = nc.alloc_semaphore("d_%d" % len(dsem))
                    dcnt[k] = 0
                dcnt[k] += 16
                o['sem'] = dsem[k]
                o['semid'] = ('d', k)
                o['val'] = dcnt[k]
            elif o['sig']:
                ecnt[o['eng']] += 1
                o['sem'] = esem[o['eng']]
                o['semid'] = ('e', o['eng'])
                o['val'] = ecnt[o['eng']]
        streams = {e: [] for e in ENGS}
        for idx, o in enumerate(ops):
            streams[o['eng']].append(idx)

        def run(eng_name, engine):
            seen = {}
            for idx in streams[eng_name]:
                o = ops[idx]
                need = {}
                for i in o['deps']:
                    d = ops[i]
                    sid, v = d['semid'], d['val']
                    if seen.get(sid, 0) >= v:
                        continue
                    if need.get(sid, (None, 0))[1] < v:
                        need[sid] = (d['sem'], v)
                for sid, (s, v) in need.items():
                    seen[sid] = v
                    engine.wait_ge(s, v)
                if o['fn'] is None:
                    continue
                ins = o['fn'](engine)
                if o['dma'] is not None:
                    ins.then_inc(o['sem'], 16)
                elif o['sig']:
                    ins.then_inc(o['sem'], 1)

        with nc.Block() as block:
            @block.tensor
            def _(e):
                run('pe', e)

            @block.scalar
            def _(e):
                run('act', e)

            @block.vector
            def _(e):
                run('dve', e)

            @block.gpsimd
            def _(e):
                run('pool', e)

            @block.sync
            def _(e):
                run('sp', e)


class Arena:
    def __init__(self, nc, nbytes):
        self.t = nc.alloc_sbuf_tensor("arena", [128, nbytes // 2], BF16)
        self.cap = nbytes
        self.off = 0

    def mark(self):
        return self.off

    def release(self, m):
        self.off = m

    def alloc(self, shape, dtype, parts=128):
        esz = 4 if dtype == F32 else 2
        n = 1
        for s in shape:
            n *= s
        nb = n * esz
        off = (self.off + 63) // 64 * 64
        assert off + nb <= self.cap, ("arena overflow", off, nb, self.cap)
        self.off = off + nb
        ap = self.t[0:parts, off // 2:(off + nb) // 2]
        if dtype == F32:
            ap = ap.bitcast(F32)
        if len(shape) > 1:
            names = ["a%d" % i for i in range(len(shape))]
            pat = "p (" + " ".join(names) + ") -> p " + " ".join(names)
            ap = ap.rearrange(pat, **{nm: s for nm, s in zip(names[1:], shape[1:])})
        return ap


def f_dma(out, in_):
    return lambda e: e.dma_start(out=out, in_=in_)


def f_mm(out, lhsT, rhs, start, stop):
    return lambda e: e.matmul(out, lhsT=lhsT, rhs=rhs, start=start, stop=stop)


def f_tr(out, in_, ident):
    return lambda e: e.transpose(out=out, in_=in_, identity=ident)


def f_act(out, in_, func, **kw):
    return lambda e: e.activation(out=out, in_=in_, func=func, **kw)


def f_tt(out, in0, in1, op):
    return lambda e: e.tensor_tensor(out=out, in0=in0, in1=in1, op=op)


def f_ts(out, in0, s1, op0, s2=None, op1=None):
    if op1 is None:
        return lambda e: e.tensor_scalar(out=out, in0=in0, scalar1=s1, scalar2=None, op0=op0)
    return lambda e: e.tensor_scalar(out=out, in0=in0, scalar1=s1, scalar2=s2, op0=op0, op1=op1)


def f_stt(out, in0, scalar, in1, op0, op1):
    return lambda e: e.scalar_tensor_tensor(out=out, in0=in0, scalar=scalar, in1=in1, op0=op0, op1=op1)


def f_copy(out, in_):
    return lambda e: e.tensor_copy(out=out, in_=in_)


def f_recip(out, in_):
    return lambda e: e.reciprocal(out=out, in_=in_)


def f_memset(out, v):
    return lambda e: e.memset(out, v)


def cast_scaled(R, k, out, in_, scal, reads, writes):
    eng = ('dve', 'act', 'pool')[k % 3]
    if scal is None:
        if eng == 'act':
            R.op('act', f_act(out, in_, AF.Copy), reads, writes)
        else:
            R.op(eng, f_copy(out, in_), reads, writes)
    elif eng == 'act':
        R.op('act', f_act(out, in_, AF.Copy, scale=scal), reads, writes)
    elif eng == 'dve':
        R.op('dve', f_ts(out, in_, scal, ALU.mult), reads, writes)
    else:
        R.op('pool', f_ts(out, in_, scal, ALU.mult, 1.0, ALU.mult), reads, writes)


def build_program(stop=9, dbg=False):
    nc = bass.Bass("TRN2", target_bir_lowering=False)
    skind = "ExternalOutput" if dbg else "Internal"

    def din(name, shape, dt=F32):
        return nc.dram_tensor(name, shape, dt, kind="ExternalInput").ap()

    x = din("x", [L, D])
    ctx = din("ctx", [CTX, D])
    smallf = din("smallf", [128, 89])
    bigbc_d = din("bigbc", [128, 3072])
    w_ada = din("w_ada", [D, 6 * D])
    w_in = din("w_in", [D, 3232])
    w_uq = din("w_uq", [384, 768])
    w_ukv = din("w_ukv", [256, 1024])
    w_br_mla = din("w_br_mla", [512, 1024])
    pool_w = din("pool_w", [4, 128, 128])
    w_br_pool = din("w_br_pool", [512, 1024])
    w_out = din("w_out", [D, D])
    w_mlp1 = din("w_mlp1", [D, 4096])
    w_mlp2 = din("w_mlp2", [4096, D])
    ident_d = din("ident", [128, 128], BF16)
    cosF_d = din("cosF", [128, 64 * 32])
    sinS_d = din("sinS", [128, 64 * 32])
    band_d = din("band", [128, 20 * 128], BF16)
    out = nc.dram_tensor("out", [L, D], F32, kind="ExternalOutput").ap()

    QT = nc.dram_tensor("QT", [H, 96, L], BF16, kind=skind).ap()
    KT = nc.dram_tensor("KT", [H, 96, NKT * 128], BF16, kind=skind).ap()
    Vs = nc.dram_tensor("Vs", [2, 128, NKT, 4, 66], BF16, kind=skind).ap()
    OT = nc.dram_tensor("OT", [512, L], BF16, kind=skind).ap()
    U = nc.dram_tensor("U", [L, 512], BF16, kind=skind).ap()
    X1 = nc.dram_tensor("X1", [L, D], F32, kind=skind).ap()

    R = Rec(nc)

    def gt(name, shape, dt=F32):
        return nc.alloc_sbuf_tensor(name, shape, dt)

    sm = gt("sm", [128, 89])
    ident = gt("ident_s", [128, 128], BF16)
    ones_f = gt("ones_f", [128, 128])
    epsb = gt("epsb", [128, 1])
    sil = gt("sil", [128, 8, 2])
    vcol = gt("vcol", [128, 4, 8, 2])
    gtbc = gt("gtbc", [128, 2, 1024])
    A1 = gt("A1", [128, 8])
    cA1 = gt("cA1", [128, 8])
    A2 = gt("A2", [128, 8])
    rstd1 = gt("rstd1", [128, 64])
    stat = gt("stat", [128, 64])
    AR = Arena(nc, 196 * 1024)

    PSall = nc.alloc_psum_tensor("psall", [128, 8, 512], F32)

    def psb(i, dt=F32):
        ap = PSall[:, i, :]
        if dt == BF16:
            ap = ap.bitcast(BF16)
        return ap

    R.bank = {('pV', 0): 0, ('pV', 1): 1, ('pR', 0): 2, ('pR', 1): 3, 'pBias0': 4, 'pBias1': 5}
    R.op('sp', f_dma(sm[:], smallf), writes=['sm'], dma='sm')
    R.op('sp', f_dma(ident[:], ident_d), writes=['ident'], dma='ident')
    R.op('dve', f_memset(ones_f[:], 1.0), writes=['ones'])
    R.op('dve', f_memset(epsb[:], EPS), writes=['epsb'])
    R.op('act', f_act(sil[:].rearrange("p c v -> p (c v)"), sm[:, 0:16], AF.Silu), reads=['sm'], writes=['sil'])

    w1x = AR.alloc([8, 1184], BF16)
    w1c = AR.alloc([8, 288], BF16)
    wuq = AR.alloc([3, 768], BF16)
    wukv = AR.alloc([2, 1024], BF16)
    bias1x = AR.alloc([1184], F32)
    bias1c = AR.alloc([288], F32)
    cosF = AR.alloc([64, 32], F32)
    sinS = AR.alloc([64, 32], F32)
    m_p1 = AR.mark()
    stgh = {'b': [AR.alloc([8192], F32) for _ in range(2)]}
    bigbc = AR.alloc([2048], F32)
    R.op('sp', f_dma(bigbc, bigbc_d[:, 0:2048]), writes=['bigbc'], dma='bigbc')
    sil_rep = AR.alloc([8, 128], F32)
    sh1_rep = AR.alloc([8, 128], F32)
    csh1_rep = AR.alloc([8, 128], F32)

    R.op('sp', f_dma(cosF.rearrange("p a b -> p (a b)"), cosF_d), writes=['cosF'], dma='cosF')
    R.op('sp', f_dma(sinS.rearrange("p a b -> p (a b)"), sinS_d), writes=['sinS'], dma='sinS')

    for c in range(8):
        R.op('dve', f_ts(sil_rep[:, c, :], ones_f[:], sil[:, c, 0:1], ALU.mult),
             reads=['ones', 'sil'], writes=[('silrep', c)])

    stg_n = [0]

    def stage_load(src_ap, kc, ncols):
        i = stg_n[0] % 2
        stg_n[0] += 1
        view = stgh['b'][i][:, 0:kc * ncols].rearrange("p (c n) -> p c n", c=kc)
        R.op('sp', f_dma(view, src_ap), writes=[('stg', i)], dma=('stg', i))
        return view, ('stg', i)

    pV = [psb(0)[:, 0:16].rearrange("p (m v) -> p m v", v=2), psb(1)[:, 0:16].rearrange("p (m v) -> p m v", v=2)]
    pR = [psb(2), psb(3)]
    vmap = {0: 0, 1: 1, 3: 2, 4: 3}
    for v in range(6):
        view, skey = stage_load(w_ada[:, v * 1024:(v + 1) * 1024].rearrange("(c p) n -> p c n", p=128), 8, 1024)
        if v in vmap:
            vi = vmap[v]
            pk = ('pV', vi % 2)
            for m in range(8):
                for c in range(8):
                    R.op('pe', f_mm(pV[vi % 2][:, m, :], view[:, c, m * 128:(m + 1) * 128], sil[:, c, :], c == 0, c == 7),
                         reads=[skey, 'sil'], writes=[pk])
            R.op('dve', f_tt(vcol[:, vi], pV[vi % 2],
                             sm[:, 16 + v * 8:16 + (v + 1) * 8].unsqueeze(2).broadcast_to([128, 8, 2]), ALU.add),
                 reads=[pk, 'sm'], writes=[('vcol', vi)])
        else:
            gi = 0 if v == 2 else 1
            for half in range(2):
                pk = ('pR', half)
                for c in range(8):
                    R.op('pe', f_mm(pR[half], sil_rep[:, c, :], view[:, c, half * 512:(half + 1) * 512], c == 0, c == 7),
                         reads=[skey, ('silrep', c)], writes=[pk])
                R.op('dve', f_tt(gtbc[:, gi, half * 512:(half + 1) * 512], pR[half],
                                 bigbc[:, gi * 1024 + half * 512:gi * 1024 + (half + 1) * 512], ALU.add),
                     reads=[pk, 'bigbc'], writes=[('gtbc', gi)])

    R.op('dve', f_stt(A1[:], vcol[:, 1, :, 0], 1.0, sm[:, 64:72], ALU.add, ALU.mult), reads=[('vcol', 1), 'sm'], writes=['A1'])
    R.op('dve', f_stt(cA1[:], vcol[:, 1, :, 1], 1.0, sm[:, 64:72], ALU.add, ALU.mult), reads=[('vcol', 1), 'sm'], writes=['cA1'])
    R.op('dve', f_stt(A2[:], vcol[:, 3, :, 0], 1.0, sm[:, 72:80], ALU.add, ALU.mult), reads=[('vcol', 3), 'sm'], writes=['A2'])
    for c in range(8):
        R.op('dve', f_ts(sh1_rep[:, c, :], ones_f[:], vcol[:, 0, c, 0:1], ALU.mult), reads=['ones', ('vcol', 0)], writes=[('sh1rep', c)])
        R.op('dve', f_ts(csh1_rep[:, c, :], ones_f[:], vcol[:, 0, c, 1:2], ALU.mult), reads=['ones', ('vcol', 0)], writes=[('csh1rep', c)])

    kk = 0
    pBias = [psb(4), psb(5)]
    for pi, (n0, n1) in enumerate([(0, 384), (384, 672), (672, 1184)]):
        n = n1 - n0
        view, skey = stage_load(w_in[:, n0:n1].rearrange("(c p) n -> p c n", p=128), 8, n)
        for c in range(8):
            cast_scaled(R, kk, w1x[:, c, n0:n1], view[:, c, :], A1[:, c:c + 1], [skey, 'A1'], [('w1x', pi, c)])
            kk += 1
        for c in range(8):
            R.op('pe', f_mm(pBias[0][:, 0:n], sh1_rep[:, c, :], view[:, c, :], c == 0, c == 7),
                 reads=[skey, ('sh1rep', c)], writes=['pBias0'])
        R.op('dve', f_copy(bias1x[:, n0:n1], pBias[0][:, 0:n]), reads=['pBias0'], writes=[('bias1x', pi)])
        if pi == 1:
            for c in range(8):
                cast_scaled(R, kk, w1c[:, c, :], view[:, c, :], cA1[:, c:c + 1], [skey, 'cA1'], [('w1c', c)])
                kk += 1
            for c in range(8):
                R.op('pe', f_mm(pBias[1][:, 0:n], csh1_rep[:, c, :], view[:, c, :], c == 0, c == 7),
                     reads=[skey, ('csh1rep', c)], writes=['pBias1'])
            R.op('dve', f_copy(bias1c[:], pBias[1][:, 0:n]), reads=['pBias1'], writes=['bias1c'])
    view, skey = stage_load(w_uq.rearrange("(c p) n -> p c n", p=128), 3, 768)
    for c in range(3):
        cast_scaled(R, kk, wuq[:, c, :], view[:, c, :], sm[:, 80 + c:81 + c], [skey, 'sm'], [('wuq', c)])
        kk += 1
    view, skey = stage_load(w_ukv.rearrange("(c p) n -> p c n", p=128), 2, 1024)
    for c in range(2):
        cast_scaled(R, kk, wukv[:, c, :], view[:, c, :], sm[:, 83 + c:84 + c], [skey, 'sm'], [('wukv', c)])
        kk += 1

    if dbg:
        dbg0 = nc.dram_tensor("dbg0", [128, 4 * 16 + 2048 + 24 + 1184 + 288], F32, kind="ExternalOutput").ap()
        R.op('sp', f_dma(dbg0[:, 0:64], vcol[:].rearrange("p a b c -> p (a b c)")), reads=[('vcol', i) for i in range(4)], writes=['dbg0a'], dma='dbg0a')
        R.op('sp', f_dma(dbg0[:, 64:2112], gtbc[:].rearrange("p a b -> p (a b)")), reads=[('gtbc', 0), ('gtbc', 1)], writes=['dbg0b'], dma='dbg0b')
        R.op('sp', f_dma(dbg0[:, 2112:2120], A1[:]), reads=['A1'], writes=['dbg0c'], dma='dbg0c')
        R.op('sp', f_dma(dbg0[:, 2120:2128], cA1[:]), reads=['cA1'], writes=['dbg0d'], dma='dbg0d')
        R.op('sp', f_dma(dbg0[:, 2128:2136], A2[:]), reads=['A2'], writes=['dbg0e'], dma='dbg0e')
        R.op('sp', f_dma(dbg0[:, 2136:2136 + 1184], bias1x), reads=[('bias1x', i) for i in range(3)], writes=['dbg0f'], dma='dbg0f')
        R.op('sp', f_dma(dbg0[:, 2136 + 1184:2136 + 1184 + 288], bias1c), reads=['bias1c'], writes=['dbg0g'], dma='dbg0g')
        dbg1 = nc.dram_tensor("dbg1", [128, 8 * 1184], BF16, kind="ExternalOutput").ap()
        R.op('sp', f_dma(dbg1, w1x.rearrange("p a b -> p (a b)")), reads=[('w1x', pi, c) for pi in range(3) for c in range(8)], writes=['dbg1'], dma='dbg1')
    R.barrier()
    if stop == 0:
        R.finalize()
        return nc
    AR.release(m_p1)

    R.bank = {'pT': 0, 'psA': 1, 'psB': 2, 'psC': 3, 'pT2q': 4, 'pT2k': 4, 'psWk': 5, 'psWq': 6, 'pXTk': 7, 'pXTq': 7}
    NX, NB = 4, 3
    xs = [AR.alloc([1024], F32) for _ in range(NX)]
    junk = [AR.alloc([1024], BF16) for _ in range(3)]
    xn = [AR.alloc([1024], BF16) for _ in range(NB)]
    xT = [AR.alloc([8, 128], BF16) for _ in range(NB)]
    u_sb = [AR.alloc([512], BF16) for _ in range(NB)]
    cqn = [AR.alloc([384], BF16) for _ in range(NB)]
    cqT = [AR.alloc([3, 128], BF16) for _ in range(NB)]
    ckvn = [AR.alloc([256], BF16) for _ in range(NB)]
    ckvT = [AR.alloc([2, 128], BF16) for _ in range(NB)]
    krs_sb = [AR.alloc([32], F32) for _ in range(NB)]
    t1 = [AR.alloc([8, 32], F32) for _ in range(NB)]
    t2 = [AR.alloc([8, 32], F32) for _ in range(NB)]
    qf = [AR.alloc([8, 96], BF16) for _ in range(NB)]
    kf = [AR.alloc([8, 96], BF16) for _ in range(NB)]
    vaug = [AR.alloc([8, 66], BF16) for _ in range(NB)]
    qT_sb = [AR.alloc([8, 128], BF16) for _ in range(NB)]
    kT_sb = [AR.alloc([8, 128], BF16) for _ in range(NB)]
    t1k = [AR.alloc([32], F32) for _ in range(NB)]
    t2k = [AR.alloc([32], F32) for _ in range(NB)]
    kr = [AR.alloc([32], F32) for _ in range(NB)]
    bhi = AR.alloc([1184], BF16)
    blo = AR.alloc([1184], BF16)
    btmp = AR.alloc([1184], F32)
    bchi = AR.alloc([288], BF16)
    bclo = AR.alloc([288], BF16)
    ones_b = AR.alloc([128], BF16)

    R.op('dve', f_memset(ones_b[0:1], 1.0), writes=['ones_b'])
    for (hi_, lo_, src_, n_, kn) in [(bhi, blo, bias1x, 1184, 'x'), (bchi, bclo, bias1c, 288, 'c')]:
        R.op('dve', f_copy(hi_[0:1], src_[0:1]), writes=[('bhi', kn)])
        R.op('dve', f_tt(btmp[0:1, 0:n_], src_[0:1], hi_[0:1], ALU.subtract), reads=[('bhi', kn)], writes=[('btmp', kn)])
        R.op('dve', f_copy(lo_[0:1], btmp[0:1, 0:n_]), reads=[('btmp', kn)], writes=[('blo', kn)])
    for s in range(NB):
        R.op('dve', f_memset(vaug[s], 1.0), writes=[('vaug', s)])

    pT = psb(0, BF16).rearrange("p (c n) -> p c n", c=8)
    psA, psB, psC = psb(1), psb(2), psb(3)
    pT2 = psb(4, BF16).rearrange("p (c n) -> p c n", c=8)
    psWk = psb(5).rearrange("p (h d) -> p h d", d=128)
    psWq = psb(6)[:, 0:384].rearrange("p (h d) -> p h d", d=96)
    pXT = psb(7, BF16).rearrange("p (c n) -> p c n", c=8)
    QTv = QT.rearrange("h d t -> d h t")
    KTv = KT.rearrange("h d t -> d h t")

    def st(col, s):
        return stat[:, col * 4 + s:col * 4 + s + 1]

    class Stage:
        def __init__(self):
            self.l = []

        def op(self, eng, fn, reads=(), writes=(), dma=None):
            self.l.append((eng, fn, reads, writes, dma))

    def emit_merged(stages):
        items = []
        for si, sg in enumerate(stages):
            n = len(sg.l)
            for k, o in enumerate(sg.l):
                items.append(((k + 0.5) / n, si, k, o))
        items.sort(key=lambda z: (z[0], z[1]))
        for _, _, _, o in items:
            R.op(o[0], o[1], reads=o[2], writes=o[3], dma=o[4])

    def tile_src(t):
        return ctx[t * 128:(t + 1) * 128, :] if t < 2 else x[(t - 2) * 128:(t - 1) * 128, :]

    def SL(G, t):
        G.op('sp', f_dma(xs[t % NX], tile_src(t)), writes=[('xs', t % NX)], dma=('xs', t % NX))

    def S1(G, t):
        is_ctx = t < 2
        xi = t - 2
        sx, s, s4_ = t % NX, t % NB, t % 4
        G.op('act', f_act(junk[0], xs[sx], AF.Square, accum_out=st(0, s4_)), reads=[('xs', sx)], writes=[('ss', s4_)])
        G.op('act', f_act(st(1, s4_), st(0, s4_), AF.Sqrt, scale=1.0 / D, bias=epsb[:]), reads=[('ss', s4_)], writes=[('sd', s4_)])
        if is_ctx:
            rst, rkey = st(2, s4_), ('rstc', s4_)
        else:
            rst, rkey = rstd1[:, xi:xi + 1], ('rstd1', xi)
        G.op('dve', f_recip(rst, st(1, s4_)), reads=[('sd', s4_)], writes=[rkey])
        G.op('dve', f_ts(xn[s], xs[sx], rst, ALU.mult), reads=[('xs', sx), rkey], writes=[('xn', s)])
        for c in range(8):
            G.op('pe', f_tr(pT[:, c, :], xn[s][:, c * 128:(c + 1) * 128], ident[:]), reads=[('xn', s)], writes=['pT'])
        G.op('act', f_copy_act(xT[s].rearrange("p c n -> p (c n)"), psb(0, BF16)), reads=['pT'], writes=[('xT', s)])
        if not is_ctx:
            groups = [(psA, 'psA', 0, 384), (psB, 'psB', 384, 672), (psC, 'psC', 672, 1184)]
            for (ps, key, n0, n1) in groups:
                for c in range(8):
                    G.op('pe', f_mm(ps[:, 0:n1 - n0], xT[s][:, c, :], w1x[:, c, n0:n1], c == 0, False), reads=[('xT', s)], writes=[key])
                G.op('pe', f_mm(ps[:, 0:n1 - n0], ones_b[0:1, :], bhi[0:1, n0:n1], False, False), writes=[key])
                G.op('pe', f_mm(ps[:, 0:n1 - n0], ones_b[0:1, :], blo[0:1, n0:n1], False, True), writes=[key])
        else:
            for c in range(8):
                G.op('pe', f_mm(psB[:, 0:288], xT[s][:, c, :], w1c[:, c, :], c == 0, False), reads=[('xT', s)], writes=['psB'])
            G.op('pe', f_mm(psB[:, 0:288], ones_b[0:1, :], bchi[0:1, :], False, False), writes=['psB'])
            G.op('pe', f_mm(psB[:, 0:288], ones_b[0:1, :], bclo[0:1, :], False, True), writes=['psB'])
        G.op('act', f_act(junk[0][:, 0:256], psB[:, 0:256], AF.Square, accum_out=st(3, s4_)), reads=['psB'], writes=[('sskv', s4_)])
        G.op('act', f_act(st(4, s4_), st(3, s4_), AF.Sqrt, scale=1.0 / 256, bias=epsb[:]), reads=[('sskv', s4_)], writes=[('sdkv', s4_)])
        G.op('dve', f_recip(st(5, s4_), st(4, s4_)), reads=[('sdkv', s4_)], writes=[('rkv', s4_)])
        G.op('dve', f_ts(ckvn[s], psB[:, 0:256], st(5, s4_), ALU.mult), reads=['psB', ('rkv', s4_)], writes=[('ckvn', s)])
        G.op('dve', f_copy(krs_sb[s], psB[:, 256:288]), reads=['psB'], writes=[('krs', s)])
        if not is_ctx:
            G.op('act', f_act(junk[0][:, 0:384], psA[:, 0:384], AF.Square, accum_out=st(6, s4_)), reads=['psA'], writes=[('ssq', s4_)])
            G.op('act', f_act(st(7, s4_), st(6, s4_), AF.Sqrt, scale=1.0 / 384, bias=epsb[:]), reads=[('ssq', s4_)], writes=[('sdq', s4_)])
            G.op('dve', f_recip(st(8, s4_), st(7, s4_)), reads=[('sdq', s4_)], writes=[('rq', s4_)])
            G.op('dve', f_ts(cqn[s], psA[:, 0:384], st(8, s4_), ALU.mult), reads=['psA', ('rq', s4_)], writes=[('cqn', s)])
            G.op('act', f_copy_act(u_sb[s], psC[:, 0:512]), reads=['psC'], writes=[('u', s)])
            G.op('pool', f_dma(U[xi * 128:(xi + 1) * 128, :], u_sb[s]), reads=[('u', s)], writes=[('U', xi)], dma=('ust', s))

    def S2(G, t):
        is_ctx = t < 2
        xi = t - 2
        s = t % NB
        for c in range(2):
            G.op('pe', f_tr(pT2[:, 3 + c, :], ckvn[s][:, c * 128:(c + 1) * 128], ident[:]), reads=[('ckvn', s)], writes=['pT2k'])
        G.op('dve', f_copy(ckvT[s], pT2[:, 3:5, :]), reads=['pT2k'], writes=[('ckvT', s)])
        if is_ctx:
            krsrc, krkey = krs_sb[s], ('krs', s)
        else:
            cos_t = cosF[:, xi, :]
            sin_t = sinS[:, xi, :]
            G.op('pool', f_tt(t1k[s], krs_sb[s], cos_t, ALU.mult), reads=[('krs', s)], writes=[('t1k', s)])
            kv4 = krs_sb[s].rearrange("p (a f i) -> p a f i", a=2, f=2)
            o4 = t2k[s].rearrange("p (a f i) -> p a f i", a=2, f=2)
            s4 = sin_t.rearrange("p (a f i) -> p a f i", a=2, f=2)
            for f in range(2):
                G.op('pool', f_tt(o4[:, :, f, :], kv4[:, :, 1 - f, :], s4[:, :, f, :], ALU.mult), reads=[('krs', s)], writes=[('t2k', s, f)])
            G.op('pool', f_tt(kr[s], t1k[s], t2k[s], ALU.add), reads=[('t1k', s), ('t2k', s, 0), ('t2k', s, 1)], writes=[('kr', s)])
            krsrc, krkey = kr[s], ('kr', s)
        G.op('pool', f_copy(kf[s][:, :, 64:96], krsrc.unsqueeze(1).broadcast_to([128, 8, 32])), reads=[krkey], writes=[('kfr', s)])
        for hb in range(2):
            hs = slice(hb * 4, (hb + 1) * 4)
            for c in range(2):
                G.op('pe', f_mm(psb(5), ckvT[s][:, c, :], wukv[:, c, hb * 512:(hb + 1) * 512], c == 0, c == 1), reads=[('ckvT', s)], writes=['psWk'])
            G.op('act', f_copy_act(kf[s][:, hs, 0:64], psWk[:, :, 0:64]), reads=['psWk'], writes=[('kfn', s, hb)])
            G.op('dve', f_copy(vaug[s][:, hs, 0:64], psWk[:, :, 64:128]), reads=['psWk'], writes=[('vaug', s, hb)])
        for hb in range(2):
            for h4 in range(4):
                h = hb * 4 + h4
                G.op('pe', f_tr(pXT[0:96, h4, :], kf[s][:, h, :], ident[:]), reads=[('kfr', s), ('kfn', s, hb)], writes=['pXTk'])
            G.op('dve' if hb == 0 else 'act', (f_copy if hb == 0 else f_copy_act)(kT_sb[s][0:96, hb * 4:(hb + 1) * 4, :], pXT[0:96, 0:4, :]),
                 reads=['pXTk'], writes=[('kT', s, hb)])
        G.op('sp', f_dma(KTv[:, :, t * 128:(t + 1) * 128], kT_sb[s][0:96]), reads=[('kT', s, 0), ('kT', s, 1)], writes=[('KT', t)], dma=('ktst', s))
        G.op('pool', f_dma(Vs[:, :, t].rearrange("g p hh e -> p g hh e"), vaug[s].rearrange("p (g hh) e -> p g hh e", g=2)),
             reads=[('vaug', s, 0), ('vaug', s, 1)], writes=[('Vs', t)], dma=('vst', s))

    def S3(G, t):
        xi = t - 2
        s = t % NB
        cos_t = cosF[:, xi, :]
        sin_t = sinS[:, xi, :]
        s4 = sin_t.rearrange("p (a f i) -> p a f i", a=2, f=2)
        for c in range(3):
            G.op('pe', f_tr(pT2[:, c, :], cqn[s][:, c * 128:(c + 1) * 128], ident[:]), reads=[('cqn', s)], writes=['pT2q'])
        G.op('dve', f_copy(cqT[s], pT2[:, 0:3, :]), reads=['pT2q'], writes=[('cqT', s)])
        for hb in range(2):
            hs = slice(hb * 4, (hb + 1) * 4)
            for c in range(3):
                G.op('pe', f_mm(psb(6)[:, 0:384], cqT[s][:, c, :], wuq[:, c, hb * 384:(hb + 1) * 384], c == 0, c == 2),
                     reads=[('cqT', s)], writes=['psWq'])
            qr = psWq[:, :, 64:96]
            G.op('dve', f_tt(t1[s][:, hs, :], qr, cos_t.unsqueeze(1).broadcast_to([128, 4, 32]), ALU.mult), reads=['psWq'], writes=[('t1', s, hb)])
            q5 = qr.rearrange("p h (a f i) -> p h a f i", a=2, f=2)
            o5 = t2[s][:, hs, :].rearrange("p h (a f i) -> p h a f i", a=2, f=2)
            for f in range(2):
                G.op('dve', f_tt(o5[:, :, :, f, :], q5[:, :, :, 1 - f, :], s4[:, :, f, :].unsqueeze(1).broadcast_to([128, 4, 2, 8]), ALU.mult),
                     reads=['psWq'], writes=[('t2', s, hb, f)])
            G.op('act', f_copy_act(qf[s][:, hs, 0:64], psWq[:, :, 0:64]), reads=['psWq'], writes=[('qfn', s, hb)])
        G.op('pool', f_tt(qf[s][:, :, 64:96], t1[s], t2[s], ALU.add),
             reads=[('t1', s, 0), ('t1', s, 1)] + [('t2', s, hb, f) for hb in range(2) for f in range(2)], writes=[('qfr', s)])
        for hb in range(2):
            for h4 in range(4):
                h = hb * 4 + h4
                G.op('pe', f_tr(pXT[0:96, 4 + h4, :], qf[s][:, h, :], ident[:]), reads=[('qfr', s), ('qfn', s, hb)], writes=['pXTq'])
            G.op('dve', f_copy(qT_sb[s][0:96, hb * 4:(hb + 1) * 4, :], pXT[0:96, 4:8, :]), reads=['pXTq'], writes=[('qT', s, hb)])
        G.op('sp', f_dma(QTv[:, :, xi * 128:(xi + 1) * 128], qT_sb[s][0:96]), reads=[('qT', s, 0), ('qT', s, 1)], writes=[('QT', xi)], dma=('qtst', s))

    import os
    _nt = int(os.environ.get('P1_TILES', NKT))
    G0 = Stage()
    SL(G0, 0)
    if _nt > 1:
        SL(G0, 1)
    emit_merged([G0])
    for i in range(_nt + 2):
        stages = []
        if i + 2 < _nt:
            g = Stage(); SL(g, i + 2); stages.append(g)
        if i < _nt:
            g = Stage(); S1(g, i); stages.append(g)
        if 0 <= i - 1 < _nt:
            g = Stage(); S2(g, i - 1); stages.append(g)
        if 2 <= i - 2 < _nt:
            g = Stage(); S3(g, i - 2); stages.append(g)
        emit_merged(stages)

    R.barrier()
    if stop == 1:
        R.finalize()
        return nc
    AR.release(0)

    R.bank = {('pS', 0): (0, 1), ('pS', 1): (2, 3), ('pO', 0): 4, ('pO', 1): 5, 'pB': 6}
    kt_sb = AR.alloc([4, NKT * 128], BF16)
    v_sb = AR.alloc([NKT, 4, 66], BF16)
    qt_sb = [AR.alloc([4, 512], BF16) for _ in range(2)]
    NP, LOOK = 3, 1
    PT = [AR.alloc([2, 512], BF16) for _ in range(NP)]
    rden = [AR.alloc([512], F32) for _ in range(2)]
    bcs = [AR.alloc([512], F32) for _ in range(2)]
    OTn = [AR.alloc([512], BF16) for _ in range(2)]
    pS2 = [PSall[:, 0:2, :], PSall[:, 2:4, :]]
    pO = [psb(4), psb(5)]
    pB = psb(6)
    NKP = NKT // 2

    for g in range(2):
        for hh in range(4):
            R.op('sp', f_dma(kt_sb[0:96, hh, :], KT[g * 4 + hh]), writes=[('kt', hh)], dma=('kt', hh))
        R.op('sp', f_dma(v_sb.rearrange("p a b c -> p (a b c)"), Vs[g].rearrange("p a b c -> p (a b c)")), writes=['v'], dma='v')
        its = [(qb, hh, kp) for qb in range(16) for hh in range(4) for kp in range(NKP)]
        n = len(its)
        pend = []

        def emit_S(i):
            qb, hh, kp = its[i]
            if hh == 0 and kp == 0:
                R.op('sp', f_dma(qt_sb[qb % 2][0:96], QTv[:, g * 4:(g + 1) * 4, qb * 512:(qb + 1) * 512]),
                     writes=[('qt', qb % 2)], dma=('qt', qb % 2))
            for e in range(2):
                kt = 2 * kp + e
                R.op('pe', f_mm(pS2[i % 2][:, e, :], kt_sb[0:96, hh, kt * 128:(kt + 1) * 128], qt_sb[qb % 2][0:96, hh, :], True, True),
                     reads=[('kt', hh), ('qt', qb % 2)], writes=[('pS', i % 2)])
            R.op('act', f_act(PT[i % NP], pS2[i % 2], AF.Exp, scale=SCALE), reads=[('pS', i % 2)], writes=[('PT', i % NP)])

        def emit_PV(i):
            qb, hh, kp = its[i]
            hidx = qb * 4 + hh
            o = hidx % 2
            for e in range(2):
                kt = 2 * kp + e
                R.op('pe', f_mm(pO[o][0:65, :], v_sb[:, kt, hh, 0:65], PT[i % NP][:, e, :], kt == 0, kt == NKT - 1),
                     reads=['v', ('PT', i % NP)], writes=[('pO', o)])
            if kp == NKP - 1:
                R.op('dve', f_recip(rden[o][64:65, :], pO[o][64:65, :]), reads=[('pO', o)], writes=[('rden', o)])
                pend.append((i + 3, hidx, qb, hh))

        def emit_epi(hidx, qb, hh):
            o = hidx % 2
            R.op('pe', f_mm(pB[0:64, :], ones_f[64:65, 0:64], rden[o][64:65, :], True, True), reads=[('rden', o), 'ones'], writes=['pB'])
            R.op('dve', f_copy(bcs[o][0:64], pB[0:64, :]), reads=['pB'], writes=[('bcs', o)])
            R.op('dve', f_tt(OTn[o][0:64], pO[o][0:64, :], bcs[o][0:64], ALU.mult), reads=[('pO', o), ('bcs', o)], writes=[('OTn', o)])
            r0 = (g * 4 + hh) * 64
            R.op('pool', f_dma(OT[r0:r0 + 64, qb * 512:(qb + 1) * 512], OTn[o][0:64]), reads=[('OTn', o)], writes=[('OT', g, hidx)], dma=('otst', o))

        for i in range(n + LOOK):
            if i < n:
                emit_S(i)
            if i >= LOOK:
                emit_PV(i - LOOK)
            while pend and (pend[0][0] <= i - LOOK or i == n + LOOK - 1):
                _, hidx, qb, hh = pend.pop(0)
                emit_epi(hidx, qb, hh)

    R.barrier()
    if stop == 2:
        R.finalize()
        return nc
    AR.release(0)

    R.bank = {'pBg': 0, 'pT': 0, ('pG', 0): 1, ('pG', 1): 2, 'pD': 3, 'pY': 4, 'pM': 5, 'pP': 6, 'pW': 7}
    wg = AR.alloc([8, 2048], BF16)
    bias_g = AR.alloc([16], F32)
    wbm = AR.alloc([4, 1024], BF16)
    wbp = AR.alloc([4, 1024], BF16)
    poolw = AR.alloc([4, 128], BF16)
    wo = AR.alloc([8, 1024], BF16)
    band = AR.alloc([20, 128], BF16)
    m_p3 = AR.mark()
    stgh['b'] = [AR.alloc([8192], F32) for _ in range(2)]
    stg_n[0] = 0

    R.op('sp', f_dma(band.rearrange("p a b -> p (a b)"), band_d), writes=['band'], dma='band')
    pBg = psb(0)[:, 0:32].rearrange("p (m v) -> p m v", v=2)
    kk = 0
    for pi in range(4):
        n0 = 1184 + pi * 512
        view, skey = stage_load(w_in[:, n0:n0 + 512].rearrange("(c p) n -> p c n", p=128), 8, 512)
        for c in range(8):
            cast_scaled(R, kk, wg[:, c, pi * 512:(pi + 1) * 512], view[:, c, :], A1[:, c:c + 1], [skey], [('wg', pi, c)])
            kk += 1
        for m in range(4):
            for c in range(8):
                R.op('pe', f_mm(pBg[:, pi * 4 + m, :], view[:, c, m * 128:(m + 1) * 128], vcol[:, 0, c, :], c == 0, c == 7),
                     reads=[skey], writes=['pBg'])
    R.op('dve', f_copy(bias_g, pBg[:, :, 0]), reads=['pBg'], writes=['bias_g'])
    view, skey = stage_load(w_br_mla.rearrange("(c p) n -> p c n", p=128), 4, 1024)
    for c in range(4):
        cast_scaled(R, kk, wbm[:, c, :], view[:, c, :], None, [skey], [('wbm', c)])
        kk += 1
    view, skey = stage_load(w_br_pool.rearrange("(c p) n -> p c n", p=128), 4, 1024)
    for c in range(4):
        cast_scaled(R, kk, wbp[:, c, :], view[:, c, :], sm[:, 85 + c:86 + c], [skey], [('wbp', c)])
        kk += 1
    view, skey = stage_load(pool_w.rearrange("g c d -> c g d"), 4, 128)
    R.op('dve', f_copy(poolw, view), reads=[skey], writes=['poolw'])
    for pi in range(2):
        view, skey = stage_load(w_out[:, pi * 512:(pi + 1) * 512].rearrange("(c p) n -> p c n", p=128), 8, 512)
        for c in range(8):
            eng = 'dve' if c % 2 == 0 else 'pool'
            R.op(eng, f_tt(wo[:, c, pi * 512:(pi + 1) * 512], view[:, c, :], gtbc[:, 0, pi * 512:(pi + 1) * 512], ALU.mult),
                 reads=[skey], writes=[('wo', pi, c)])
    R.barrier()
    AR.release(m_p3)

    xs3 = [AR.alloc([4, 1024], F32) for _ in range(2)]
    xn3 = [AR.alloc([1024], BF16) for _ in range(2)]
    xT3 = AR.alloc([8, 512], BF16)
    sig = AR.alloc([16, 512], BF16)
    us = [AR.alloc([6, 512], BF16) for _ in range(2)]
    dT = [AR.alloc([512], BF16) for _ in range(2)]
    yT = AR.alloc([4, 512], BF16)
    ot_sb = [AR.alloc([4, 512], BF16) for _ in range(2)]
    tA = [AR.alloc([512], F32) for _ in range(2)]
    tB = [AR.alloc([512], F32) for _ in range(2)]
    mT = AR.alloc([8, 512], BF16)

    pT = psb(0, BF16).rearrange("p (c n) -> p c n", c=8)
    pG = [psb(1), psb(2)]
    pD, pY, pM, pP, pW = psb(3), psb(4), psb(5), psb(6), psb(7)

    for b in range(16):
        s = b % 2
        R.op('sp', f_dma(xs3[s], x[b * 512:(b + 1) * 512, :].rearrange("(j p) f -> p j f", p=128)),
             writes=[('xs3', s, j) for j in range(4)], dma=('xs3', s))
        R.op('sp', f_dma(ot_sb[s], OT[:, b * 512:(b + 1) * 512].rearrange("(c p) t -> p c t", p=128)), writes=[('ot', s)], dma=('ot', s))
        lo = max(4 * b - 1, 0)
        hi = min(4 * b + 5, 64)
        d0 = lo - (4 * b - 1)
        R.op('sp', f_dma(us[s][:, d0:d0 + hi - lo, :], U[lo * 128:hi * 128, :].rearrange("(j p) f -> p j f", p=128)),
             writes=[('us', s)], dma=('us', s))
        for j in range(4):
            T = 4 * b + j
            R.op('act', f_act(xn3[j % 2], xs3[s][:, j, :], AF.Copy, scale=rstd1[:, T:T + 1]), reads=[('xs3', s, j)], writes=[('xn3', j % 2)])
            for c in range(8):
                R.op('pe', f_tr(pT[:, c, :], xn3[j % 2][:, c * 128:(c + 1) * 128], ident[:]), reads=[('xn3', j % 2)], writes=['pT'])
            R.op('dve', f_copy(xT3[:, :, j * 128:(j + 1) * 128], pT), reads=['pT'], writes=[('xT3', j)])
        xT3keys = [('xT3', j) for j in range(4)]
        for m in range(16):
            for c in range(8):
                R.op('pe', f_mm(pG[m % 2], wg[:, c, m * 128:(m + 1) * 128], xT3[:, c, :], c == 0, c == 7), reads=xT3keys, writes=[('pG', m % 2)])
            R.op('act', f_act(sig[:, m, :], pG[m % 2], AF.Sigmoid, bias=bias_g[:, m:m + 1]), reads=[('pG', m % 2)], writes=[('sig', m)])
        for g in range(4):
            for j in range(4):
                T = 4 * b + j
                parts = []
                if T > 0:
                    parts.append((j, g * 5 + 0))
                parts.append((j + 1, g * 5 + (3 if T == 0 else (4 if T == 63 else 1))))
                if T < 63:
                    parts.append((j + 2, g * 5 + 2))
                for k, (slot, bi) in enumerate(parts):
                    R.op('pe', f_mm(pD[:, j * 128:(j + 1) * 128], us[s][:, slot, g * 128:(g + 1) * 128], band[:, bi, :], k == 0, k == len(parts) - 1),
                         reads=[('us', s), 'band'], writes=['pD'])
            R.op('dve', f_copy(dT[g % 2], pD), reads=['pD'], writes=[('dT', g % 2)])
            R.op('pe', f_mm(pY, poolw[:, g, :], dT[g % 2], True, True), reads=[('dT', g % 2)], writes=['pY'])
            R.op('act', f_copy_act(yT[:, g, :], pY), reads=['pY'], writes=[('yT', g)])
        yTkeys = [('yT', g) for g in range(4)]
        for m in range(8):
            for c in range(4):
                R.op('pe', f_mm(pM, wbm[:, c, m * 128:(m + 1) * 128], ot_sb[s][:, c, :], c == 0, c == 3), reads=[('ot', s)], writes=['pM'])
            for g in range(4):
                R.op('pe', f_mm(pP, wbp[:, g, m * 128:(m + 1) * 128], yT[:, g, :], g == 0, g == 3), reads=yTkeys, writes=['pP'])
            R.op('dve', f_tt(tA[m % 2], pM, sig[:, m, :], ALU.mult), reads=['pM', ('sig', m)], writes=[('tA', m % 2)])
            R.op('dve', f_tt(tB[m % 2], pP, sig[:, 8 + m, :], ALU.mult), reads=['pP', ('sig', 8 + m)], writes=[('tB', m % 2)])
            R.op('pool', f_tt(mT[:, m, :], tA[m % 2], tB[m % 2], ALU.add), reads=[('tA', m % 2), ('tB', m % 2)], writes=[('mT', m)])
        mTkeys = [('mT', m) for m in range(8)]
        for j in range(4):
            for half in range(2):
                for c in range(8):
                    R.op('pe', f_mm(pW, mT[:, c, j * 128:(j + 1) * 128], wo[:, c, half * 512:(half + 1) * 512], c == 0, c == 7), reads=mTkeys, writes=['pW'])
                R.op('dve', f_tt(xs3[s][:, j, half * 512:(half + 1) * 512], pW, xs3[s][:, j, half * 512:(half + 1) * 512], ALU.add),
                     reads=['pW', ('xs3', s, j)], writes=[('xs3', s, j)])
        R.op('pool', f_dma(X1[b * 512:(b + 1) * 512, :].rearrange("(j p) f -> p j f", p=128), xs3[s]),
             reads=[('xs3', s, j) for j in range(4)], writes=[('X1', b)], dma=('x1st', s))

    R.barrier()
    if stop == 3:
        R.finalize()
        return nc
    AR.release(0)

    R.bank = {'pB1': 0, 'pT': 0, ('pH', 0): 1, ('pH', 1): 2, ('pH', 2): 3, ('pY2', 0): 4, ('pY2', 1): 5}
    w1 = AR.alloc([8, 4096], BF16)
    b1 = AR.alloc([32], F32)
    w2 = AR.alloc([32, 1024], BF16)
    fg = AR.alloc([1024], F32)
    m_p4 = AR.mark()
    stgh['b'] = [AR.alloc([4096], F32) for _ in range(2)]
    stg_n[0] = 0
    R.op('sp', f_dma(fg, bigbc_d[:, 2048:3072]), writes=['fg'], dma='fg')
    pB1 = psb(0)[:, 0:64].rearrange("p (m v) -> p m v", v=2)
    kk = 0
    for pi in range(16):
        view, skey = stage_load(w_mlp1[:, pi * 256:(pi + 1) * 256].rearrange("(c p) n -> p c n", p=128), 8, 256)
        for c in range(8):
            cast_scaled(R, kk, w1[:, c, pi * 256:(pi + 1) * 256], view[:, c, :], A2[:, c:c + 1], [skey], [('w1', pi, c)])
            kk += 1
        for m in range(2):
            for c in range(8):
                R.op('pe', f_mm(pB1[:, pi * 2 + m, :], view[:, c, m * 128:(m + 1) * 128], vcol[:, 2, c, :], c == 0, c == 7),
                     reads=[skey], writes=['pB1'])
    R.op('dve', f_copy(b1, pB1[:, :, 0]), reads=['pB1'], writes=['b1'])
    for pi in range(16):
        view, skey = stage_load(w_mlp2[pi * 256:(pi + 1) * 256, :].rearrange("(c p) n -> p c n", p=128), 2, 1024)
        for c in range(2):
            eng = 'dve' if c % 2 == 0 else 'pool'
            R.op(eng, f_tt(w2[:, pi * 2 + c, :], view[:, c, :], gtbc[:, 1, :], ALU.mult), reads=[skey], writes=[('w2', pi, c)])
    R.barrier()
    AR.release(m_p4)

    x1s = [AR.alloc([2, 1024], F32) for _ in range(2)]
    xn4 = [AR.alloc([1024], BF16) for _ in range(2)]
    h2T = AR.alloc([8, 256], BF16)
    rr = [AR.alloc([256], F32) for _ in range(2)]
    hidT = AR.alloc([32, 256], BF16)
    x2 = [AR.alloc([1024], F32) for _ in range(2)]
    outs = x2
    junk4 = AR.alloc([1024], BF16)

    pT = psb(0, BF16).rearrange("p (c n) -> p c n", c=8)
    pH = [psb(1), psb(2), psb(3)]
    pY2 = [psb(4), psb(5)]
    outkeys = []
    for b in range(32):
        s = b % 2
        R.op('sp', f_dma(x1s[s], X1[b * 256:(b + 1) * 256, :].rearrange("(j p) f -> p j f", p=128)),
             writes=[('x1s', s, 0), ('x1s', s, 1)], dma=('x1s', s))
        for j in range(2):
            R.op('act', f_act(junk4, x1s[s][:, j, :], AF.Square, accum_out=st(9, j)), reads=[('x1s', s, j)], writes=['junk4', ('ss2', j)])
            R.op('act', f_act(st(10, j), st(9, j), AF.Sqrt, scale=1.0 / D, bias=epsb[:]), reads=[('ss2', j)], writes=[('sd2', j)])
            R.op('dve', f_recip(st(11, j), st(10, j)), reads=[('sd2', j)], writes=[('r2', j)])
            R.op('dve', f_ts(xn4[j], x1s[s][:, j, :], st(11, j), ALU.mult), reads=[('x1s', s, j), ('r2', j)], writes=[('xn4', j)])
            for c in range(8):
                R.op('pe', f_tr(pT[:, c, :], xn4[j][:, c * 128:(c + 1) * 128], ident[:]), reads=[('xn4', j)], writes=['pT'])
            R.op('act', f_copy_act(h2T[:, :, j * 128:(j + 1) * 128], pT), reads=['pT'], writes=[('h2T', j)])
        for m in range(32):
            for c in range(8):
                R.op('pe', f_mm(pH[m % 3][:, 0:256], w1[:, c, m * 128:(m + 1) * 128], h2T[:, c, :], c == 0, c == 7),
                     reads=[('h2T', 0), ('h2T', 1)], writes=[('pH', m % 3)])
            R.op('act', f_act(rr[m % 2], pH[m % 3][:, 0:256], AF.Relu, bias=b1[:, m:m + 1]), reads=[('pH', m % 3)], writes=[('rr', m % 2)])
            R.op('dve' if m % 2 == 0 else 'pool', f_tt(hidT[:, m, :], rr[m % 2], rr[m % 2], ALU.mult), reads=[('rr', m % 2)], writes=[('hidT', m)])
        hkeys = [('hidT', m) for m in range(32)]
        for j in range(2):
            for half in range(2):
                pk = (j * 2 + half) % 2
                for m in range(32):
                    R.op('pe', f_mm(pY2[pk], hidT[:, m, j * 128:(j + 1) * 128], w2[:, m, half * 512:(half + 1) * 512], m == 0, m == 31),
                         reads=hkeys, writes=[('pY2', pk)])
                R.op('dve', f_tt(x2[j][:, half * 512:(half + 1) * 512], pY2[pk], x1s[s][:, j, half * 512:(half + 1) * 512], ALU.add),
                     reads=[('pY2', pk), ('x1s', s, j)], writes=[('x2', j, half)])
            R.op('act', f_act(junk4, x2[j], AF.Square, accum_out=st(12, j)), reads=[('x2', j, 0), ('x2', j, 1)], writes=['junk4', ('ss3', j)])
            R.op('act', f_act(st(13, j), st(12, j), AF.Sqrt, scale=1.0 / D, bias=epsb[:]), reads=[('ss3', j)], writes=[('sd3', j)])
            R.op('dve', f_recip(st(14, j), st(13, j)), reads=[('sd3', j)], writes=[('r3', j)])
            R.op('dve', f_stt(outs[j], x2[j], st(14, j), fg, ALU.mult, ALU.mult), reads=[('x2', j, 0), ('x2', j, 1), ('r3', j), 'fg'],
                 writes=[('x2', j, 0), ('x2', j, 1)])
            row = (b * 2 + j) * 128
            R.op('pool', f_dma(out[row:row + 128, :], outs[j]), reads=[('x2', j, 0), ('x2', j, 1)], writes=[('out', b, j)], dma=('outst', j))
            outkeys.append(('out', b, j))
    R.op('sp', None, reads=outkeys)
    R.finalize()
    return nc


def f_copy_act(out, in_):
    return lambda e: e.activation(out=out, in_=in_, func=AF.Copy)


def _host_consts():
    bf = ml_dtypes.bfloat16
    ident = np.eye(128, dtype=np.float32).astype(bf)
    t = np.arange(L)
    row = (t // 64).astype(np.float32)
    col = (t % 64).astype(np.float32)
    inv_freq = (np.float32(10000.0) ** (-np.arange(0, 16, 2, dtype=np.float32) / np.float32(16))).astype(np.float32)
    ar = (row[:, None] * inv_freq).astype(np.float32)
    ac = (col[:, None] * inv_freq).astype(np.float32)
    cr, sr, cc, sc = np.cos(ar), np.sin(ar), np.cos(ac), np.sin(ac)
    cosF = np.concatenate([cr, cr, cc, cc], axis=1).astype(np.float32)
    sinS = np.concatenate([-sr, sr, -sc, sc], axis=1).astype(np.float32)
    cosF = np.ascontiguousarray(cosF.reshape(64, 128, 32).transpose(1, 0, 2).reshape(128, 64 * 32))
    sinS = np.ascontiguousarray(sinS.reshape(64, 128, 32).transpose(1, 0, 2).reshape(128, 64 * 32))
    band = np.zeros((128, 20, 128), np.float32)
    for g, w in enumerate((2, 4, 8, 16)):
        hw = w // 2
        for kind in range(5):
            T = {0: 5, 1: 5, 2: 5, 3: 0, 4: 63}[kind]
            dT = {0: -1, 1: 0, 2: 1, 3: 0, 4: 0}[kind]
            m = np.zeros((128, 128), np.float32)
            for tl in range(128):
                tg = T * 128 + tl
                lo = max(tg - hw, 0)
                hi = min(tg + hw, L)
                cnt = hi - lo
                for tp in range(lo, hi):
                    p = tp - (T + dT) * 128
                    if 0 <= p < 128:
                        m[p, tl] += 1.0 / cnt
                p = tg - (T + dT) * 128
                if 0 <= p < 128:
                    m[p, tl] -= 1.0
            band[:, g * 5 + kind, :] = m
    band = np.ascontiguousarray(band.reshape(128, 20 * 128)).astype(bf)
    return ident, cosF, sinS, band


_CACHE = {}


def prep_inputs(x, c, ctx, c_ctx, w_ada, b_ada, norm1_g, w_in, q_norm_g, kv_norm_g, w_uq, w_ukv,
                w_br_mla, pool_w, pool_scale, w_br_pool, w_out, norm2_g, w_mlp1, w_mlp2, final_g):
    f = lambda a: np.ascontiguousarray(np.asarray(a, dtype=np.float32))
    x, c, ctx, c_ctx = f(x), f(c), f(ctx), f(c_ctx)
    w_ada, b_ada, w_in, w_uq, w_ukv = f(w_ada)[0], f(b_ada)[0], f(w_in)[0], f(w_uq)[0], f(w_ukv)[0]
    w_br_mla, pool_w, w_br_pool, w_out = f(w_br_mla)[0], f(pool_w)[0], f(w_br_pool)[0], f(w_out)[0]
    w_mlp1, w_mlp2 = f(w_mlp1)[0], f(w_mlp2)[0]
    norm1_g, norm2_g, q_norm_g, kv_norm_g, pool_scale, final_g = (f(norm1_g)[0], f(norm2_g)[0], f(q_norm_g)[0],
                                                                   f(kv_norm_g)[0], f(pool_scale)[0], f(final_g))
    if 'consts' not in _CACHE:
        _CACHE['consts'] = _host_consts()
    ident, cosF, sinS, band = _CACHE['consts']
    col = lambda v: v.reshape(-1, 128).T
    bigbc = np.ascontiguousarray(np.broadcast_to(
        np.concatenate([b_ada[2 * D:3 * D], b_ada[5 * D:6 * D], final_g])[None, :], (128, 3 * D)))
    shared = dict(w_ada=w_ada, w_in=w_in, w_uq=w_uq, w_ukv=w_ukv, w_br_mla=w_br_mla, pool_w=pool_w,
                  w_br_pool=w_br_pool, w_out=w_out, w_mlp1=w_mlp1, w_mlp2=w_mlp2, ident=ident,
                  cosF=cosF, sinS=sinS, band=band, bigbc=bigbc)
    in_maps = []
    for b in range(x.shape[0]):
        ccol = np.stack([col(c[b]), col(c_ctx)], axis=2).reshape(128, 16)
        smallf = np.ascontiguousarray(np.concatenate(
            [ccol, col(b_ada), col(norm1_g), col(norm2_g), col(q_norm_g), col(kv_norm_g), col(pool_scale)], axis=1).astype(np.float32))
        assert smallf.shape == (128, 89)
        m = dict(shared)
        m.update(x=x[b], ctx=ctx[b], smallf=smallf)
        in_maps.append(m)
    return in_maps


def kernel(**inputs):
    in_maps = prep_inputs(**inputs)
    if 'nc' not in _CACHE:
        _CACHE['nc'] = build_program()
    nc = _CACHE['nc']
    res = run_bass_kernel_spmd(nc, in_maps, core_ids=list(range(8)))
    return np.stack([np.asarray(r["out"], dtype=np.float32) for r in res.results], axis=0)
```

```python
import numpy as np
import ml_dtypes
import concourse.bass as bass
import concourse.mybir as mybir
from concourse.bass_utils import run_bass_kernel_spmd

F32 = mybir.dt.float32
BF16 = mybir.dt.bfloat16
AF = mybir.ActivationFunctionType
ALU = mybir.AluOpType

D = 1024
L = 8192
CTX = 256
NKT = (L + CTX) // 128
H = 8
EPS = 1e-6
SCALE = 96.0 ** -0.5
ENGS = ['pe', 'act', 'dve', 'pool', 'sp']


class Rec:
    def __init__(self, nc):
        self.nc = nc
        self.ops = []
        self.lastw = {}
        self.rd_eng = {}
        self.rd_dma = {}
        self.bank = {}
        self.bank_last = {}

    def op(self, eng, fn, reads=(), writes=(), dma=None):
        j = len(self.ops)
        deps = {}
        bset = set()
        for k in list(reads) + list(writes):
            if k in self.bank:
                bv = self.bank[k]
                bset.update(bv if isinstance(bv, tuple) else (bv,))
        for bnk in bset:
            la = self.bank_last.setdefault(bnk, {})
            for e2, i in la.items():
                if e2 != eng:
                    deps.setdefault(i, False)
            la[eng] = j
        for k in reads:
            i = self.lastw.get(k)
            if i is not None:
                deps[i] = True
        for k in writes:
            for i in self.rd_eng.get(k, {}).values():
                deps.setdefault(i, False)
            for i in self.rd_dma.get(k, ()):
                deps.setdefault(i, False)
            i = self.lastw.get(k)
            if i is not None:
                deps.setdefault(i, False)
        for k in reads:
            if dma is None:
                self.rd_eng.setdefault(k, {})[eng] = j
            else:
                self.rd_dma.setdefault(k, []).append(j)
        for k in writes:
            self.lastw[k] = j
            self.rd_eng[k] = {}
            self.rd_dma[k] = []
        keep = []
        for i, raw in deps.items():
            oi = self.ops[i]
            if oi['dma'] is None and dma is None and oi['eng'] == eng:
                if not raw or eng == 'pe':
                    continue
            keep.append(i)
        self.ops.append(dict(eng=eng, fn=fn, dma=dma, deps=keep, sig=False))
        return j

    def barrier(self):
        last = {}
        for idx, o in enumerate(self.ops):
            if o['fn'] is None:
                continue
            if o['dma'] is not None:
                last[('d', o['dma'])] = idx
            else:
                last[('e', o['eng'])] = idx
        deps = list(last.values())
        for e in ENGS:
            self.ops.append(dict(eng=e, fn=None, dma=None, deps=list(deps), sig=False))
        self.lastw.clear()
        self.rd_eng.clear()
        self.rd_dma.clear()
        self.bank_last.clear()

    def finalize(self):
        nc = self.nc
        ops = self.ops
        for o in ops:
            for i in o['deps']:
                ops[i]['sig'] = True
        esem = {e: nc.alloc_semaphore("s_" + e) for e in ENGS}
        dsem = {}
        ecnt = {e: 0 for e in ENGS}
        dcnt = {}
        for o in ops:
            if o['fn'] is None:
                continue
            if o['dma'] is not None:
                k = o['dma']
                if k not in dsem:
                    dsem[k] = nc.alloc_semaphore("d_%d" % len(dsem))
                    dcnt[k] = 0
                dcnt[k] += 16
                o['sem'] = dsem[k]
                o['semid'] = ('d', k)
                o['val'] = dcnt[k]
            elif o['sig']:
                ecnt[o['eng']] += 1
                o['sem'] = esem[o['eng']]
                o['semid'] = ('e', o['eng'])
                o['val'] = ecnt[o['eng']]
        streams = {e: [] for e in ENGS}
        for idx, o in enumerate(ops):
            streams[o['eng']].append(idx)

        def run(eng_name, engine):
            seen = {}
            for idx in streams[eng_name]:
                o = ops[idx]
                need = {}
                for i in o['deps']:
                    d = ops[i]
                    sid, v = d['semid'], d['val']
                    if seen.get(sid, 0) >= v:
                        continue
                    if need.get(sid, (None, 0))[1] < v:
                        need[sid] = (d['sem'], v)
                for sid, (s, v) in need.items():
                    seen[sid] = v
                    engine.wait_ge(s, v)
                if o['fn'] is None:
                    continue
                ins = o['fn'](engine)
                if o['dma'] is not None:
                    ins.then_inc(o['sem'], 16)
                elif o['sig']:
                    ins.then_inc(o['sem'], 1)

        with nc.Block() as block:
            @block.tensor
            def _(e):
                run('pe', e)

            @block.scalar
            def _(e):
                run('act', e)

            @block.vector
            def _(e):
                run('dve', e)

            @block.gpsimd
            def _(e):
                run('pool', e)

            @block.sync
            def _(e):
                run('sp', e)


class Arena:
    def __init__(self, nc, nbytes):
        self.t = nc.alloc_sbuf_tensor("arena", [128, nbytes // 2], BF16)
        self.cap = nbytes
        self.off = 0

    def mark(self):
        return self.off

    def release(self, m):
        self.off = m

    def alloc(self, shape, dtype, parts=128):
        esz = 4 if dtype == F32 else 2
        n = 1
        for s in shape:
            n *= s
        nb = n * esz
        off = (self.off + 63) // 64 * 64
        assert off + nb <= self.cap, ("arena overflow", off, nb, self.cap)
        self.off = off + nb
        ap = self.t[0:parts, off // 2:(off + nb) // 2]
        if dtype == F32:
            ap = ap.bitcast(F32)
        if len(shape) > 1:
            names = ["a%d" % i for i in range(len(shape))]
            pat = "p (" + " ".join(names) + ") -> p " + " ".join(names)
            ap = ap.rearrange(pat, **{nm: s for nm, s in zip(names[1:], shape[1:])})
        return ap


def f_dma(out, in_):
    return lambda e: e.dma_start(out=out, in_=in_)


def f_mm(out, lhsT, rhs, start, stop):
    return lambda e: e.matmul(out, lhsT=lhsT, rhs=rhs, start=start, stop=stop)


def f_tr(out, in_, ident):
    return lambda e: e.transpose(out=out, in_=in_, identity=ident)


def f_act(out, in_, func, **kw):
    return lambda e: e.activation(out=out, in_=in_, func=func, **kw)


def f_tt(out, in0, in1, op):
    return lambda e: e.tensor_tensor(out=out, in0=in0, in1=in1, op=op)


def f_ts(out, in0, s1, op0, s2=None, op1=None):
    if op1 is None:
        return lambda e: e.tensor_scalar(out=out, in0=in0, scalar1=s1, scalar2=None, op0=op0)
    return lambda e: e.tensor_scalar(out=out, in0=in0, scalar1=s1, scalar2=s2, op0=op0, op1=op1)


def f_stt(out, in0, scalar, in1, op0, op1):
    return lambda e: e.scalar_tensor_tensor(out=out, in0=in0, scalar=scalar, in1=in1, op0=op0, op1=op1)


def f_copy(out, in_):
    return lambda e: e.tensor_copy(out=out, in_=in_)


def f_recip(out, in_):
    return lambda e: e.reciprocal(out=out, in_=in_)


def f_memset(out, v):
    return lambda e: e.memset(out, v)


def cast_scaled(R, k, out, in_, scal, reads, writes):
    eng = ('dve', 'act', 'pool')[k % 3]
    if scal is None:
        if eng == 'act':
            R.op('act', f_act(out, in_, AF.Copy), reads, writes)
        else:
            R.op(eng, f_copy(out, in_), reads, writes)
    elif eng == 'act':
        R.op('act', f_act(out, in_, AF.Copy, scale=scal), reads, writes)
    elif eng == 'dve':
        R.op('dve', f_ts(out, in_, scal, ALU.mult), reads, writes)
    else:
        R.op('pool', f_ts(out, in_, scal, ALU.mult, 1.0, ALU.mult), reads, writes)


def build_program(stop=9, dbg=False):
    nc = bass.Bass("TRN2", target_bir_lowering=False)
    skind = "ExternalOutput" if dbg else "Internal"

    def din(name, shape, dt=F32):
        return nc.dram_tensor(name, shape, dt, kind="ExternalInput").ap()

    x = din("x", [L, D])
    ctx = din("ctx", [CTX, D])
    smallf = din("smallf", [128, 89])
    bigbc_d = din("bigbc", [128, 3072])
    w_ada = din("w_ada", [D, 6 * D])
    w_in = din("w_in", [D, 3232])
    w_uq = din("w_uq", [384, 768])
    w_ukv = din("w_ukv", [256, 1024])
    w_br_mla = din("w_br_mla", [512, 1024])
    pool_w = din("pool_w", [4, 128, 128])
    w_br_pool = din("w_br_pool", [512, 1024])
    w_out = din("w_out", [D, D])
    w_mlp1 = din("w_mlp1", [D, 4096])
    w_mlp2 = din("w_mlp2", [4096, D])
    ident_d = din("ident", [128, 128], BF16)
    cosF_d = din("cosF", [128, 64 * 32])
    sinS_d = din("sinS", [128, 64 * 32])
    band_d = din("band", [128, 20 * 128], BF16)
    out = nc.dram_tensor("out", [L, D], F32, kind="ExternalOutput").ap()

    QT = nc.dram_tensor("QT", [H, 96, L], BF16, kind=skind).ap()
    KT = nc.dram_tensor("KT", [H, 96, NKT * 128], BF16, kind=skind).ap()
    Vs = nc.dram_tensor("Vs", [2, 128, NKT, 4, 66], BF16, kind=skind).ap()
    OT = nc.dram_tensor("OT", [512, L], BF16, kind=skind).ap()
    U = nc.dram_tensor("U", [L, 512], BF16, kind=skind).ap()
    X1 = nc.dram_tensor("X1", [L, D], F32, kind=skind).ap()
    WB_wg = nc.dram_tensor("WB_wg", [128, 8 * 2048], BF16, kind="Internal").ap()
    WB_wbm = nc.dram_tensor("WB_wbm", [128, 4 * 1024], BF16, kind="Internal").ap()
    WB_wbp = nc.dram_tensor("WB_wbp", [128, 4 * 1024], BF16, kind="Internal").ap()
    WB_pw = nc.dram_tensor("WB_pw", [128, 4 * 128], BF16, kind="Internal").ap()
    WB_wo = nc.dram_tensor("WB_wo", [128, 8 * 1024], BF16, kind="Internal").ap()
    WB_w1 = nc.dram_tensor("WB_w1", [128, 8 * 4096], BF16, kind="Internal").ap()
    WB_w2 = nc.dram_tensor("WB_w2", [128, 32 * 1024], BF16, kind="Internal").ap()

    R = Rec(nc)

    def gt(name, shape, dt=F32):
        return nc.alloc_sbuf_tensor(name, shape, dt)

    sm = gt("sm", [128, 89])
    ident = gt("ident_s", [128, 128], BF16)
    ones_f = gt("ones_f", [128, 128])
    epsb = gt("epsb", [128, 1])
    sil = gt("sil", [128, 8, 2])
    vcol = gt("vcol", [128, 4, 8, 2])
    gtbc = gt("gtbc", [128, 2, 1024])
    A1 = gt("A1", [128, 8])
    cA1 = gt("cA1", [128, 8])
    A2 = gt("A2", [128, 8])
    rstd1 = gt("rstd1", [128, 64])
    stat = gt("stat", [128, 64])
    bias_g = gt("bias_g", [128, 16])
    b1p = gt("b1p", [128, 32])
    AR = Arena(nc, 196 * 1024)

    PSall = nc.alloc_psum_tensor("psall", [128, 8, 512], F32)

    def psb(i, dt=F32):
        ap = PSall[:, i, :]
        if dt == BF16:
            ap = ap.bitcast(BF16)
        return ap

    R.bank = {('pV', 0): 0, ('pV', 1): 1, ('pR', 0): 2, ('pR', 1): 3, 'pBias0': 4, 'pBias1': 5}
    R.op('sp', f_dma(sm[:], smallf), writes=['sm'], dma='sm')
    R.op('sp', f_dma(ident[:], ident_d), writes=['ident'], dma='ident')
    R.op('dve', f_memset(ones_f[:], 1.0), writes=['ones'])
    R.op('dve', f_memset(epsb[:], EPS), writes=['epsb'])
    R.op('act', f_act(sil[:].rearrange("p c v -> p (c v)"), sm[:, 0:16], AF.Silu), reads=['sm'], writes=['sil'])

    w1x = AR.alloc([8, 1184], BF16)
    w1c = AR.alloc([8, 288], BF16)
    wuq = AR.alloc([3, 768], BF16)
    wukv = AR.alloc([2, 1024], BF16)
    bias1x = AR.alloc([1184], F32)
    bias1c = AR.alloc([288], F32)
    cosF = AR.alloc([64, 32], F32)
    sinS = AR.alloc([64, 32], F32)
    m_p1 = AR.mark()
    stgh = {'b': [AR.alloc([8192], F32) for _ in range(2)]}
    bigbc = AR.alloc([2048], F32)
    R.op('sp', f_dma(bigbc, bigbc_d[:, 0:2048]), writes=['bigbc'], dma='bigbc')
    sil_rep = AR.alloc([8, 128], F32)
    sh1_rep = AR.alloc([8, 128], F32)
    csh1_rep = AR.alloc([8, 128], F32)

    R.op('sp', f_dma(cosF.rearrange("p a b -> p (a b)"), cosF_d), writes=['cosF'], dma='cosF')
    R.op('sp', f_dma(sinS.rearrange("p a b -> p (a b)"), sinS_d), writes=['sinS'], dma='sinS')

    for c in range(8):
        R.op('dve', f_ts(sil_rep[:, c, :], ones_f[:], sil[:, c, 0:1], ALU.mult),
             reads=['ones', 'sil'], writes=[('silrep', c)])

    stg_n = [0]

    def stage_load(src_ap, kc, ncols):
        i = stg_n[0] % 2
        stg_n[0] += 1
        view = stgh['b'][i][:, 0:kc * ncols].rearrange("p (c n) -> p c n", c=kc)
        R.op('sp', f_dma(view, src_ap), writes=[('stg', i)], dma=('stg', i))
        return view, ('stg', i)

    pV = [psb(0)[:, 0:16].rearrange("p (m v) -> p m v", v=2), psb(1)[:, 0:16].rearrange("p (m v) -> p m v", v=2)]
    pR = [psb(2), psb(3)]
    vmap = {0: 0, 1: 1, 3: 2, 4: 3}
    for v in range(6):
        view, skey = stage_load(w_ada[:, v * 1024:(v + 1) * 1024].rearrange("(c p) n -> p c n", p=128), 8, 1024)
        if v in vmap:
            vi = vmap[v]
            pk = ('pV', vi % 2)
            for m in range(8):
                for c in range(8):
                    R.op('pe', f_mm(pV[vi % 2][:, m, :], view[:, c, m * 128:(m + 1) * 128], sil[:, c, :], c == 0, c == 7),
                         reads=[skey, 'sil'], writes=[pk])
            R.op('dve', f_tt(vcol[:, vi], pV[vi % 2],
                             sm[:, 16 + v * 8:16 + (v + 1) * 8].unsqueeze(2).broadcast_to([128, 8, 2]), ALU.add),
                 reads=[pk, 'sm'], writes=[('vcol', vi)])
        else:
            gi = 0 if v == 2 else 1
            for half in range(2):
                pk = ('pR', half)
                for c in range(8):
                    R.op('pe', f_mm(pR[half], sil_rep[:, c, :], view[:, c, half * 512:(half + 1) * 512], c == 0, c == 7),
                         reads=[skey, ('silrep', c)], writes=[pk])
                R.op('dve', f_tt(gtbc[:, gi, half * 512:(half + 1) * 512], pR[half],
                                 bigbc[:, gi * 1024 + half * 512:gi * 1024 + (half + 1) * 512], ALU.add),
                     reads=[pk, 'bigbc'], writes=[('gtbc', gi)])

    R.op('dve', f_stt(A1[:], vcol[:, 1, :, 0], 1.0, sm[:, 64:72], ALU.add, ALU.mult), reads=[('vcol', 1), 'sm'], writes=['A1'])
    R.op('dve', f_stt(cA1[:], vcol[:, 1, :, 1], 1.0, sm[:, 64:72], ALU.add, ALU.mult), reads=[('vcol', 1), 'sm'], writes=['cA1'])
    R.op('dve', f_stt(A2[:], vcol[:, 3, :, 0], 1.0, sm[:, 72:80], ALU.add, ALU.mult), reads=[('vcol', 3), 'sm'], writes=['A2'])
    for c in range(8):
        R.op('dve', f_ts(sh1_rep[:, c, :], ones_f[:], vcol[:, 0, c, 0:1], ALU.mult), reads=['ones', ('vcol', 0)], writes=[('sh1rep', c)])
        R.op('dve', f_ts(csh1_rep[:, c, :], ones_f[:], vcol[:, 0, c, 1:2], ALU.mult), reads=['ones', ('vcol', 0)], writes=[('csh1rep', c)])

    kk = 0
    pBias = [psb(4), psb(5)]
    for pi, (n0, n1) in enumerate([(0, 384), (384, 672), (672, 1184)]):
        n = n1 - n0
        view, skey = stage_load(w_in[:, n0:n1].rearrange("(c p) n -> p c n", p=128), 8, n)
        for c in range(8):
            cast_scaled(R, kk, w1x[:, c, n0:n1], view[:, c, :], A1[:, c:c + 1], [skey, 'A1'], [('w1x', pi, c)])
            kk += 1
        for c in range(8):
            R.op('pe', f_mm(pBias[0][:, 0:n], sh1_rep[:, c, :], view[:, c, :], c == 0, c == 7),
                 reads=[skey, ('sh1rep', c)], writes=['pBias0'])
        R.op('dve', f_copy(bias1x[:, n0:n1], pBias[0][:, 0:n]), reads=['pBias0'], writes=[('bias1x', pi)])
        if pi == 1:
            for c in range(8):
                cast_scaled(R, kk, w1c[:, c, :], view[:, c, :], cA1[:, c:c + 1], [skey, 'cA1'], [('w1c', c)])
                kk += 1
            for c in range(8):
                R.op('pe', f_mm(pBias[1][:, 0:n], csh1_rep[:, c, :], view[:, c, :], c == 0, c == 7),
                     reads=[skey, ('csh1rep', c)], writes=['pBias1'])
            R.op('dve', f_copy(bias1c[:], pBias[1][:, 0:n]), reads=['pBias1'], writes=['bias1c'])
    view, skey = stage_load(w_uq.rearrange("(c p) n -> p c n", p=128), 3, 768)
    for c in range(3):
        cast_scaled(R, kk, wuq[:, c, :], view[:, c, :], sm[:, 80 + c:81 + c], [skey, 'sm'], [('wuq', c)])
        kk += 1
    view, skey = stage_load(w_ukv.rearrange("(c p) n -> p c n", p=128), 2, 1024)
    for c in range(2):
        cast_scaled(R, kk, wukv[:, c, :], view[:, c, :], sm[:, 83 + c:84 + c], [skey, 'sm'], [('wukv', c)])
        kk += 1

    if dbg:
        dbg0 = nc.dram_tensor("dbg0", [128, 4 * 16 + 2048 + 24 + 1184 + 288], F32, kind="ExternalOutput").ap()
        R.op('sp', f_dma(dbg0[:, 0:64], vcol[:].rearrange("p a b c -> p (a b c)")), reads=[('vcol', i) for i in range(4)], writes=['dbg0a'], dma='dbg0a')
        R.op('sp', f_dma(dbg0[:, 64:2112], gtbc[:].rearrange("p a b -> p (a b)")), reads=[('gtbc', 0), ('gtbc', 1)], writes=['dbg0b'], dma='dbg0b')
        R.op('sp', f_dma(dbg0[:, 2112:2120], A1[:]), reads=['A1'], writes=['dbg0c'], dma='dbg0c')
        R.op('sp', f_dma(dbg0[:, 2120:2128], cA1[:]), reads=['cA1'], writes=['dbg0d'], dma='dbg0d')
        R.op('sp', f_dma(dbg0[:, 2128:2136], A2[:]), reads=['A2'], writes=['dbg0e'], dma='dbg0e')
        R.op('sp', f_dma(dbg0[:, 2136:2136 + 1184], bias1x), reads=[('bias1x', i) for i in range(3)], writes=['dbg0f'], dma='dbg0f')
        R.op('sp', f_dma(dbg0[:, 2136 + 1184:2136 + 1184 + 288], bias1c), reads=['bias1c'], writes=['dbg0g'], dma='dbg0g')
        dbg1 = nc.dram_tensor("dbg1", [128, 8 * 1184], BF16, kind="ExternalOutput").ap()
        R.op('sp', f_dma(dbg1, w1x.rearrange("p a b -> p (a b)")), reads=[('w1x', pi, c) for pi in range(3) for c in range(8)], writes=['dbg1'], dma='dbg1')
    R.barrier()
    if stop == 0:
        R.finalize()
        return nc
    AR.release(m_p1)

    R.bank = {'pT': 0, 'psA': 1, 'psB': 2, 'psC': 3, 'pT2q': 4, 'pT2k': 4, 'psWk': 5, 'psWq': 6, 'pXTk': 7, 'pXTq': 7}
    NX, NB = 4, 3
    xs = [AR.alloc([1024], F32) for _ in range(NX)]
    junk = [AR.alloc([1024], BF16) for _ in range(3)]
    xn = [AR.alloc([1024], BF16) for _ in range(NB)]
    xT = [AR.alloc([8, 128], BF16) for _ in range(NB)]
    u_sb = [AR.alloc([512], BF16) for _ in range(NB)]
    cqn = [AR.alloc([384], BF16) for _ in range(NB)]
    cqT = [AR.alloc([3, 128], BF16) for _ in range(NB)]
    ckvn = [AR.alloc([256], BF16) for _ in range(NB)]
    ckvT = [AR.alloc([2, 128], BF16) for _ in range(NB)]
    krs_sb = [AR.alloc([32], F32) for _ in range(NB)]
    t1 = [AR.alloc([8, 32], F32) for _ in range(NB)]
    t2 = [AR.alloc([8, 32], F32) for _ in range(NB)]
    qf = [AR.alloc([8, 96], BF16) for _ in range(NB)]
    kf = [AR.alloc([8, 96], BF16) for _ in range(NB)]
    vaug = [AR.alloc([8, 66], BF16) for _ in range(NB)]
    qT_sb = [AR.alloc([8, 128], BF16) for _ in range(NB)]
    kT_sb = [AR.alloc([8, 128], BF16) for _ in range(NB)]
    t1k = [AR.alloc([32], F32) for _ in range(NB)]
    t2k = [AR.alloc([32], F32) for _ in range(NB)]
    kr = [AR.alloc([32], F32) for _ in range(NB)]
    bhi = AR.alloc([1184], BF16)
    blo = AR.alloc([1184], BF16)
    btmp = AR.alloc([1184], F32)
    bchi = AR.alloc([288], BF16)
    bclo = AR.alloc([288], BF16)
    ones_b = AR.alloc([128], BF16)

    R.op('dve', f_memset(ones_b[0:1], 1.0), writes=['ones_b'])
    for (hi_, lo_, src_, n_, kn) in [(bhi, blo, bias1x, 1184, 'x'), (bchi, bclo, bias1c, 288, 'c')]:
        R.op('dve', f_copy(hi_[0:1], src_[0:1]), writes=[('bhi', kn)])
        R.op('dve', f_tt(btmp[0:1, 0:n_], src_[0:1], hi_[0:1], ALU.subtract), reads=[('bhi', kn)], writes=[('btmp', kn)])
        R.op('dve', f_copy(lo_[0:1], btmp[0:1, 0:n_]), reads=[('btmp', kn)], writes=[('blo', kn)])
    for s in range(NB):
        R.op('dve', f_memset(vaug[s], 1.0), writes=[('vaug', s)])

    pT = psb(0, BF16).rearrange("p (c n) -> p c n", c=8)
    psA, psB, psC = psb(1), psb(2), psb(3)
    pT2 = psb(4, BF16).rearrange("p (c n) -> p c n", c=8)
    psWk = psb(5).rearrange("p (h d) -> p h d", d=128)
    psWq = psb(6)[:, 0:384].rearrange("p (h d) -> p h d", d=96)
    pXT = psb(7, BF16).rearrange("p (c n) -> p c n", c=8)
    QTv = QT.rearrange("h d t -> d h t")
    KTv = KT.rearrange("h d t -> d h t")

    def st(col, s):
        return stat[:, col * 4 + s:col * 4 + s + 1]

    class Stage:
        def __init__(self):
            self.l = []

        def op(self, eng, fn, reads=(), writes=(), dma=None):
            self.l.append((eng, fn, reads, writes, dma))

    def emit_merged(stages):
        items = []
        for si, sg in enumerate(stages):
            n = len(sg.l)
            for k, o in enumerate(sg.l):
                items.append(((k + 0.5) / n, si, k, o))
        items.sort(key=lambda z: (z[0], z[1]))
        for _, _, _, o in items:
            R.op(o[0], o[1], reads=o[2], writes=o[3], dma=o[4])

    def tile_src(t):
        return ctx[t * 128:(t + 1) * 128, :] if t < 2 else x[(t - 2) * 128:(t - 1) * 128, :]

    def SL(G, t):
        G.op('sp', f_dma(xs[t % NX], tile_src(t)), writes=[('xs', t % NX)], dma=('xs', t % NX))

    def S1(G, t):
        is_ctx = t < 2
        xi = t - 2
        sx, s, s4_ = t % NX, t % NB, t % 4
        G.op('act', f_act(junk[0], xs[sx], AF.Square, accum_out=st(0, s4_)), reads=[('xs', sx)], writes=[('ss', s4_)])
        G.op('act', f_act(st(1, s4_), st(0, s4_), AF.Sqrt, scale=1.0 / D, bias=epsb[:]), reads=[('ss', s4_)], writes=[('sd', s4_)])
        if is_ctx:
            rst, rkey = st(2, s4_), ('rstc', s4_)
        else:
            rst, rkey = rstd1[:, xi:xi + 1], ('rstd1', xi)
        G.op('dve', f_recip(rst, st(1, s4_)), reads=[('sd', s4_)], writes=[rkey])
        G.op('dve', f_ts(xn[s], xs[sx], rst, ALU.mult), reads=[('xs', sx), rkey], writes=[('xn', s)])
        for c in range(8):
            G.op('pe', f_tr(pT[:, c, :], xn[s][:, c * 128:(c + 1) * 128], ident[:]), reads=[('xn', s)], writes=['pT'])
        G.op('act', f_copy_act(xT[s].rearrange("p c n -> p (c n)"), psb(0, BF16)), reads=['pT'], writes=[('xT', s)])
        if not is_ctx:
            groups = [(psA, 'psA', 0, 384), (psB, 'psB', 384, 672), (psC, 'psC', 672, 1184)]
            for (ps, key, n0, n1) in groups:
                for c in range(8):
                    G.op('pe', f_mm(ps[:, 0:n1 - n0], xT[s][:, c, :], w1x[:, c, n0:n1], c == 0, False), reads=[('xT', s)], writes=[key])
                G.op('pe', f_mm(ps[:, 0:n1 - n0], ones_b[0:1, :], bhi[0:1, n0:n1], False, False), writes=[key])
                G.op('pe', f_mm(ps[:, 0:n1 - n0], ones_b[0:1, :], blo[0:1, n0:n1], False, True), writes=[key])
        else:
            for c in range(8):
                G.op('pe', f_mm(psB[:, 0:288], xT[s][:, c, :], w1c[:, c, :], c == 0, False), reads=[('xT', s)], writes=['psB'])
            G.op('pe', f_mm(psB[:, 0:288], ones_b[0:1, :], bchi[0:1, :], False, False), writes=['psB'])
            G.op('pe', f_mm(psB[:, 0:288], ones_b[0:1, :], bclo[0:1, :], False, True), writes=['psB'])
        G.op('act', f_act(junk[0][:, 0:256], psB[:, 0:256], AF.Square, accum_out=st(3, s4_)), reads=['psB'], writes=[('sskv', s4_)])
        G.op('act', f_act(st(4, s4_), st(3, s4_), AF.Sqrt, scale=1.0 / 256, bias=epsb[:]), reads=[('sskv', s4_)], writes=[('sdkv', s4_)])
        G.op('dve', f_recip(st(5, s4_), st(4, s4_)), reads=[('sdkv', s4_)], writes=[('rkv', s4_)])
        G.op('dve', f_ts(ckvn[s], psB[:, 0:256], st(5, s4_), ALU.mult), reads=['psB', ('rkv', s4_)], writes=[('ckvn', s)])
        G.op('dve', f_copy(krs_sb[s], psB[:, 256:288]), reads=['psB'], writes=[('krs', s)])
        if not is_ctx:
            G.op('act', f_act(junk[0][:, 0:384], psA[:, 0:384], AF.Square, accum_out=st(6, s4_)), reads=['psA'], writes=[('ssq', s4_)])
            G.op('act', f_act(st(7, s4_), st(6, s4_), AF.Sqrt, scale=1.0 / 384, bias=epsb[:]), reads=[('ssq', s4_)], writes=[('sdq', s4_)])
            G.op('dve', f_recip(st(8, s4_), st(7, s4_)), reads=[('sdq', s4_)], writes=[('rq', s4_)])
            G.op('dve', f_ts(cqn[s], psA[:, 0:384], st(8, s4_), ALU.mult), reads=['psA', ('rq', s4_)], writes=[('cqn', s)])
            G.op('act', f_copy_act(u_sb[s], psC[:, 0:512]), reads=['psC'], writes=[('u', s)])
            G.op('pool', f_dma(U[xi * 128:(xi + 1) * 128, :], u_sb[s]), reads=[('u', s)], writes=[('U', xi)], dma=('ust', s))

    def S2(G, t):
        is_ctx = t < 2
        xi = t - 2
        s = t % NB
        for c in range(2):
            G.op('pe', f_tr(pT2[:, 3 + c, :], ckvn[s][:, c * 128:(c + 1) * 128], ident[:]), reads=[('ckvn', s)], writes=['pT2k'])
        G.op('dve', f_copy(ckvT[s], pT2[:, 3:5, :]), reads=['pT2k'], writes=[('ckvT', s)])
        if is_ctx:
            krsrc, krkey = krs_sb[s], ('krs', s)
        else:
            cos_t = cosF[:, xi, :]
            sin_t = sinS[:, xi, :]
            G.op('pool', f_tt(t1k[s], krs_sb[s], cos_t, ALU.mult), reads=[('krs', s)], writes=[('t1k', s)])
            kv4 = krs_sb[s].rearrange("p (a f i) -> p a f i", a=2, f=2)
            o4 = t2k[s].rearrange("p (a f i) -> p a f i", a=2, f=2)
            s4 = sin_t.rearrange("p (a f i) -> p a f i", a=2, f=2)
            for f in range(2):
                G.op('pool', f_tt(o4[:, :, f, :], kv4[:, :, 1 - f, :], s4[:, :, f, :], ALU.mult), reads=[('krs', s)], writes=[('t2k', s, f)])
            G.op('pool', f_tt(kr[s], t1k[s], t2k[s], ALU.add), reads=[('t1k', s), ('t2k', s, 0), ('t2k', s, 1)], writes=[('kr', s)])
            krsrc, krkey = kr[s], ('kr', s)
        G.op('pool', f_copy(kf[s][:, :, 64:96], krsrc.unsqueeze(1).broadcast_to([128, 8, 32])), reads=[krkey], writes=[('kfr', s)])
        for hb in range(2):
            hs = slice(hb * 4, (hb + 1) * 4)
            for c in range(2):
                G.op('pe', f_mm(psb(5), ckvT[s][:, c, :], wukv[:, c, hb * 512:(hb + 1) * 512], c == 0, c == 1), reads=[('ckvT', s)], writes=['psWk'])
            G.op('act', f_copy_act(kf[s][:, hs, 0:64], psWk[:, :, 0:64]), reads=['psWk'], writes=[('kfn', s, hb)])
            G.op('dve', f_copy(vaug[s][:, hs, 0:64], psWk[:, :, 64:128]), reads=['psWk'], writes=[('vaug', s, hb)])
        for hb in range(2):
            for h4 in range(4):
                h = hb * 4 + h4
                G.op('pe', f_tr(pXT[0:96, h4, :], kf[s][:, h, :], ident[:]), reads=[('kfr', s), ('kfn', s, hb)], writes=['pXTk'])
            G.op('dve' if hb == 0 else 'act', (f_copy if hb == 0 else f_copy_act)(kT_sb[s][0:96, hb * 4:(hb + 1) * 4, :], pXT[0:96, 0:4, :]),
                 reads=['pXTk'], writes=[('kT', s, hb)])
        G.op('sp', f_dma(KTv[:, :, t * 128:(t + 1) * 128], kT_sb[s][0:96]), reads=[('kT', s, 0), ('kT', s, 1)], writes=[('KT', t)], dma=('ktst', s))
        G.op('pool', f_dma(Vs[:, :, t].rearrange("g p hh e -> p g hh e"), vaug[s].rearrange("p (g hh) e -> p g hh e", g=2)),
             reads=[('vaug', s, 0), ('vaug', s, 1)], writes=[('Vs', t)], dma=('vst', s))

    def S3(G, t):
        xi = t - 2
        s = t % NB
        cos_t = cosF[:, xi, :]
        sin_t = sinS[:, xi, :]
        s4 = sin_t.rearrange("p (a f i) -> p a f i", a=2, f=2)
        for c in range(3):
            G.op('pe', f_tr(pT2[:, c, :], cqn[s][:, c * 128:(c + 1) * 128], ident[:]), reads=[('cqn', s)], writes=['pT2q'])
        G.op('dve', f_copy(cqT[s], pT2[:, 0:3, :]), reads=['pT2q'], writes=[('cqT', s)])
        for hb in range(2):
            hs = slice(hb * 4, (hb + 1) * 4)
            for c in range(3):
                G.op('pe', f_mm(psb(6)[:, 0:384], cqT[s][:, c, :], wuq[:, c, hb * 384:(hb + 1) * 384], c == 0, c == 2),
                     reads=[('cqT', s)], writes=['psWq'])
            qr = psWq[:, :, 64:96]
            G.op('dve', f_tt(t1[s][:, hs, :], qr, cos_t.unsqueeze(1).broadcast_to([128, 4, 32]), ALU.mult), reads=['psWq'], writes=[('t1', s, hb)])
            q5 = qr.rearrange("p h (a f i) -> p h a f i", a=2, f=2)
            o5 = t2[s][:, hs, :].rearrange("p h (a f i) -> p h a f i", a=2, f=2)
            for f in range(2):
                G.op('dve', f_tt(o5[:, :, :, f, :], q5[:, :, :, 1 - f, :], s4[:, :, f, :].unsqueeze(1).broadcast_to([128, 4, 2, 8]), ALU.mult),
                     reads=['psWq'], writes=[('t2', s, hb, f)])
            G.op('act', f_copy_act(qf[s][:, hs, 0:64], psWq[:, :, 0:64]), reads=['psWq'], writes=[('qfn', s, hb)])
        G.op('pool', f_tt(qf[s][:, :, 64:96], t1[s], t2[s], ALU.add),
             reads=[('t1', s, 0), ('t1', s, 1)] + [('t2', s, hb, f) for hb in range(2) for f in range(2)], writes=[('qfr', s)])
        for hb in range(2):
            for h4 in range(4):
                h = hb * 4 + h4
                G.op('pe', f_tr(pXT[0:96, 4 + h4, :], qf[s][:, h, :], ident[:]), reads=[('qfr', s), ('qfn', s, hb)], writes=['pXTq'])
            G.op('dve', f_copy(qT_sb[s][0:96, hb * 4:(hb + 1) * 4, :], pXT[0:96, 4:8, :]), reads=['pXTq'], writes=[('qT', s, hb)])
        G.op('sp', f_dma(QTv[:, :, xi * 128:(xi + 1) * 128], qT_sb[s][0:96]), reads=[('qT', s, 0), ('qT', s, 1)], writes=[('QT', xi)], dma=('qtst', s))

    import os
    _nt = int(os.environ.get('P1_TILES', NKT))
    G0 = Stage()
    SL(G0, 0)
    if _nt > 1:
        SL(G0, 1)
    emit_merged([G0])
    for i in range(_nt + 2):
        stages = []
        if i + 2 < _nt:
            g = Stage(); SL(g, i + 2); stages.append(g)
        if i < _nt:
            g = Stage(); S1(g, i); stages.append(g)
        if 0 <= i - 1 < _nt:
            g = Stage(); S2(g, i - 1); stages.append(g)
        if 2 <= i - 2 < _nt:
            g = Stage(); S3(g, i - 2); stages.append(g)
        emit_merged(stages)

    R.barrier()
    if stop == 1:
        R.finalize()
        return nc
    AR.release(0)

    R.bank = {('pS', 0): 0, ('pS', 1): 1, ('pS', 2): 2, ('pS', 3): 3, ('pO', 0): 4, ('pO', 1): 5, 'pB': 6, 'pBg': 7, 'pB1': 7}
    kt_sb = AR.alloc([4, NKT * 128], BF16)
    v_sb = AR.alloc([NKT, 4, 66], BF16)
    qt_sb = [AR.alloc([4, 512], BF16) for _ in range(2)]
    NS, NP, LOOK = 4, 4, 2
    PT = [AR.alloc([512], BF16) for _ in range(NP)]
    rden = [AR.alloc([512], F32) for _ in range(2)]
    bcs = [AR.alloc([512], F32) for _ in range(2)]
    OTn = [AR.alloc([512], BF16) for _ in range(2)]
    pstg = [AR.alloc([4096], F32) for _ in range(2)]
    pcast = [AR.alloc([4096], BF16) for _ in range(2)]
    pS = [psb(i) for i in range(NS)]
    pO = [psb(4), psb(5)]
    pB = psb(6)
    pBg = psb(7)[:, 0:32].rearrange("p (m v) -> p m v", v=2)
    pB1 = psb(7)[:, 32:96].rearrange("p (m v) -> p m v", v=2)

    PREP = Stage()
    pcnt = [0]

    def prep_piece(src_ap, kc, ncols, dst_ap, row_scale=None, col_scale=None, bias=None):
        i = pcnt[0] % 2
        pcnt[0] += 1
        view = pstg[i][:, 0:kc * ncols].rearrange("p (c n) -> p c n", c=kc)
        cview = pcast[i][:, 0:kc * ncols].rearrange("p (c n) -> p c n", c=kc)
        PREP.op('sp', f_dma(view, src_ap), writes=[('pstg', i)], dma=('pstg', i))
        for c in range(kc):
            eng = 'dve' if c % 2 == 0 else 'pool'
            if col_scale is not None:
                fn = f_tt(cview[:, c, :], view[:, c, :], col_scale, ALU.mult)
            elif row_scale is not None:
                fn = f_ts(cview[:, c, :], view[:, c, :], row_scale(c), ALU.mult, 1.0, ALU.mult)
            else:
                fn = f_copy(cview[:, c, :], view[:, c, :])
            PREP.op(eng, fn, reads=[('pstg', i)], writes=[('pcast', i, c)])
        if bias is not None:
            ptile, m0, rhs_of, key = bias
            for m in range(ncols // 128):
                for c in range(kc):
                    PREP.op('pe', f_mm(ptile[:, m0 + m, :], view[:, c, m * 128:(m + 1) * 128], rhs_of(c), c == 0, c == kc - 1),
                            reads=[('pstg', i)], writes=[key])
        PREP.op('pool', f_dma(dst_ap, cview), reads=[('pcast', i, c) for c in range(kc)], writes=[('WB', pcnt[0])], dma=('pcst', i))

    WBg3 = WB_wg.rearrange("p (c n) -> p c n", c=8)
    for pi in range(4):
        n0 = 1184 + pi * 512
        prep_piece(w_in[:, n0:n0 + 512].rearrange("(c p) n -> p c n", p=128), 8, 512, WBg3[:, :, pi * 512:(pi + 1) * 512],
                   row_scale=lambda c: A1[:, c:c + 1], bias=(pBg, pi * 4, lambda c: vcol[:, 0, c, :], 'pBg'))
    PREP.op('dve', f_copy(bias_g[:], pBg[:, :, 0]), reads=['pBg'], writes=['bias_g'])
    prep_piece(w_br_mla.rearrange("(c p) n -> p c n", p=128), 4, 1024, WB_wbm.rearrange("p (c n) -> p c n", c=4))
    prep_piece(w_br_pool.rearrange("(c p) n -> p c n", p=128), 4, 1024, WB_wbp.rearrange("p (c n) -> p c n", c=4),
               row_scale=lambda c: sm[:, 85 + c:86 + c])
    prep_piece(pool_w.rearrange("g c d -> c g d"), 4, 128, WB_pw.rearrange("p (c n) -> p c n", c=4))
    WBo3 = WB_wo.rearrange("p (c n) -> p c n", c=8)
    for pi in range(2):
        prep_piece(w_out[:, pi * 512:(pi + 1) * 512].rearrange("(c p) n -> p c n", p=128), 8, 512, WBo3[:, :, pi * 512:(pi + 1) * 512],
                   col_scale=gtbc[:, 0, pi * 512:(pi + 1) * 512])
    WB13 = WB_w1.rearrange("p (c n) -> p c n", c=8)
    for pi in range(8):
        prep_piece(w_mlp1[:, pi * 512:(pi + 1) * 512].rearrange("(c p) n -> p c n", p=128), 8, 512, WB13[:, :, pi * 512:(pi + 1) * 512],
                   row_scale=lambda c: A2[:, c:c + 1], bias=(pB1, pi * 4, lambda c: vcol[:, 2, c, :], 'pB1'))
    PREP.op('dve', f_copy(b1p[:], pB1[:, :, 0]), reads=['pB1'], writes=['b1p'])
    WB23 = WB_w2.rearrange("p (c n) -> p c n", c=32)
    for pi in range(8):
        prep_piece(w_mlp2[pi * 512:(pi + 1) * 512, :].rearrange("(c p) n -> p c n", p=128), 4, 1024, WB23[:, pi * 4:(pi + 1) * 4, :],
                   col_scale=gtbc[:, 1, :])
    prepq = list(PREP.l)
    PSTEP = 6

    def emit_prep(k=1):
        for _ in range(k):
            if prepq:
                o = prepq.pop(0)
                R.op(o[0], o[1], reads=o[2], writes=o[3], dma=o[4])

    for g in range(2):
        for hh in range(4):
            R.op('sp', f_dma(kt_sb[0:96, hh, :], KT[g * 4 + hh]), writes=[('kt', hh)], dma=('kt', hh))
        R.op('sp', f_dma(v_sb.rearrange("p a b c -> p (a b c)"), Vs[g].rearrange("p a b c -> p (a b c)")), writes=['v'], dma='v')
        its = [(qb, hh, kt) for qb in range(16) for hh in range(4) for kt in range(NKT)]
        n = len(its)
        pend = []

        def emit_S(i):
            qb, hh, kt = its[i]
            if hh == 0 and kt == 0:
                R.op('sp', f_dma(qt_sb[qb % 2][0:96], QTv[:, g * 4:(g + 1) * 4, qb * 512:(qb + 1) * 512]),
                     writes=[('qt', qb % 2)], dma=('qt', qb % 2))
            R.op('pe', f_mm(pS[i % NS], kt_sb[0:96, hh, kt * 128:(kt + 1) * 128], qt_sb[qb % 2][0:96, hh, :], True, True),
                 reads=[('kt', hh), ('qt', qb % 2)], writes=[('pS', i % NS)])
            R.op('act', f_act(PT[i % NP], pS[i % NS], AF.Exp, scale=SCALE), reads=[('pS', i % NS)], writes=[('PT', i % NP)])

        def emit_PV(i):
            qb, hh, kt = its[i]
            hidx = qb * 4 + hh
            o = hidx % 2
            R.op('pe', f_mm(pO[o][0:65, :], v_sb[:, kt, hh, 0:65], PT[i % NP], kt == 0, kt == NKT - 1),
                 reads=['v', ('PT', i % NP)], writes=[('pO', o)])
            if kt == NKT - 1:
                R.op('dve', f_recip(rden[o][64:65, :], pO[o][64:65, :]), reads=[('pO', o)], writes=[('rden', o)])
                pend.append((i + 4, hidx, qb, hh))

        def emit_epi(hidx, qb, hh):
            o = hidx % 2
            R.op('pe', f_mm(pB[0:64, :], ones_f[64:65, 0:64], rden[o][64:65, :], True, True), reads=[('rden', o), 'ones'], writes=['pB'])
            R.op('dve', f_copy(bcs[o][0:64], pB[0:64, :]), reads=['pB'], writes=[('bcs', o)])
            R.op('dve', f_tt(OTn[o][0:64], pO[o][0:64, :], bcs[o][0:64], ALU.mult), reads=[('pO', o), ('bcs', o)], writes=[('OTn', o)])
            r0 = (g * 4 + hh) * 64
            R.op('pool', f_dma(OT[r0:r0 + 64, qb * 512:(qb + 1) * 512], OTn[o][0:64]), reads=[('OTn', o)], writes=[('OT', g, hidx)], dma=('otst', o))

        for i in range(n + LOOK):
            if i < n:
                emit_S(i)
            if i >= LOOK:
                emit_PV(i - LOOK)
            while pend and (pend[0][0] <= i - LOOK or i == n + LOOK - 1):
                _, hidx, qb, hh = pend.pop(0)
                emit_epi(hidx, qb, hh)
            if i % PSTEP == 0:
                emit_prep()
    emit_prep(len(prepq))

    R.barrier()
    if stop == 2:
        R.finalize()
        return nc
    AR.release(0)

    R.bank = {'pBg': 0, 'pT': 0, ('pG', 0): 1, ('pG', 1): 2, 'pD': 3, 'pY': 4, 'pM': 5, 'pP': 6, 'pW': 7}
    wg = AR.alloc([8, 2048], BF16)
    wbm = AR.alloc([4, 1024], BF16)
    wbp = AR.alloc([4, 1024], BF16)
    poolw = AR.alloc([4, 128], BF16)
    wo = AR.alloc([8, 1024], BF16)
    band = AR.alloc([20, 128], BF16)
    R.op('sp', f_dma(band.rearrange("p a b -> p (a b)"), band_d), writes=['band'], dma='band')
    for nm, dst, src in [('wg', wg, WB_wg), ('wbm', wbm, WB_wbm), ('wbp', wbp, WB_wbp), ('poolw', poolw, WB_pw), ('wo', wo, WB_wo)]:
        R.op('sp', f_dma(dst.rearrange("p a b -> p (a b)"), src), writes=[nm], dma=nm)
    R.barrier()

    xs3 = [AR.alloc([4, 1024], F32) for _ in range(2)]
    xn3 = [AR.alloc([1024], BF16) for _ in range(2)]
    xT3 = AR.alloc([8, 512], BF16)
    sig = AR.alloc([16, 512], BF16)
    us = [AR.alloc([6, 512], BF16) for _ in range(2)]
    dT = [AR.alloc([512], BF16) for _ in range(2)]
    yT = AR.alloc([4, 512], BF16)
    ot_sb = [AR.alloc([4, 512], BF16) for _ in range(2)]
    tA = [AR.alloc([512], F32) for _ in range(2)]
    tB = [AR.alloc([512], F32) for _ in range(2)]
    mT = AR.alloc([8, 512], BF16)

    pT = psb(0, BF16).rearrange("p (c n) -> p c n", c=8)
    pG = [psb(1), psb(2)]
    pD, pY, pM, pP, pW = psb(3), psb(4), psb(5), psb(6), psb(7)

    for b in range(16):
        s = b % 2
        R.op('sp', f_dma(xs3[s], x[b * 512:(b + 1) * 512, :].rearrange("(j p) f -> p j f", p=128)),
             writes=[('xs3', s, j) for j in range(4)], dma=('xs3', s))
        R.op('sp', f_dma(ot_sb[s], OT[:, b * 512:(b + 1) * 512].rearrange("(c p) t -> p c t", p=128)), writes=[('ot', s)], dma=('ot', s))
        lo = max(4 * b - 1, 0)
        hi = min(4 * b + 5, 64)
        d0 = lo - (4 * b - 1)
        R.op('sp', f_dma(us[s][:, d0:d0 + hi - lo, :], U[lo * 128:hi * 128, :].rearrange("(j p) f -> p j f", p=128)),
             writes=[('us', s)], dma=('us', s))
        for j in range(4):
            T = 4 * b + j
            R.op('act', f_act(xn3[j % 2], xs3[s][:, j, :], AF.Copy, scale=rstd1[:, T:T + 1]), reads=[('xs3', s, j)], writes=[('xn3', j % 2)])
            for c in range(8):
                R.op('pe', f_tr(pT[:, c, :], xn3[j % 2][:, c * 128:(c + 1) * 128], ident[:]), reads=[('xn3', j % 2)], writes=['pT'])
            R.op('dve', f_copy(xT3[:, :, j * 128:(j + 1) * 128], pT), reads=['pT'], writes=[('xT3', j)])
        xT3keys = [('xT3', j) for j in range(4)]
        for m in range(16):
            for c in range(8):
                R.op('pe', f_mm(pG[m % 2], wg[:, c, m * 128:(m + 1) * 128], xT3[:, c, :], c == 0, c == 7), reads=xT3keys, writes=[('pG', m % 2)])
            R.op('act', f_act(sig[:, m, :], pG[m % 2], AF.Sigmoid, bias=bias_g[:, m:m + 1]), reads=[('pG', m % 2)], writes=[('sig', m)])
        for g in range(4):
            for j in range(4):
                T = 4 * b + j
                parts = []
                if T > 0:
                    parts.append((j, g * 5 + 0))
                parts.append((j + 1, g * 5 + (3 if T == 0 else (4 if T == 63 else 1))))
                if T < 63:
                    parts.append((j + 2, g * 5 + 2))
                for k, (slot, bi) in enumerate(parts):
                    R.op('pe', f_mm(pD[:, j * 128:(j + 1) * 128], us[s][:, slot, g * 128:(g + 1) * 128], band[:, bi, :], k == 0, k == len(parts) - 1),
                         reads=[('us', s), 'band'], writes=['pD'])
            R.op('dve', f_copy(dT[g % 2], pD), reads=['pD'], writes=[('dT', g % 2)])
            R.op('pe', f_mm(pY, poolw[:, g, :], dT[g % 2], True, True), reads=[('dT', g % 2)], writes=['pY'])
            R.op('act', f_copy_act(yT[:, g, :], pY), reads=['pY'], writes=[('yT', g)])
        yTkeys = [('yT', g) for g in range(4)]
        for m in range(8):
            for c in range(4):
                R.op('pe', f_mm(pM, wbm[:, c, m * 128:(m + 1) * 128], ot_sb[s][:, c, :], c == 0, c == 3), reads=[('ot', s)], writes=['pM'])
            for g in range(4):
                R.op('pe', f_mm(pP, wbp[:, g, m * 128:(m + 1) * 128], yT[:, g, :], g == 0, g == 3), reads=yTkeys, writes=['pP'])
            R.op('dve', f_tt(tA[m % 2], pM, sig[:, m, :], ALU.mult), reads=['pM', ('sig', m)], writes=[('tA', m % 2)])
            R.op('dve', f_tt(tB[m % 2], pP, sig[:, 8 + m, :], ALU.mult), reads=['pP', ('sig', 8 + m)], writes=[('tB', m % 2)])
            R.op('pool', f_tt(mT[:, m, :], tA[m % 2], tB[m % 2], ALU.add), reads=[('tA', m % 2), ('tB', m % 2)], writes=[('mT', m)])
        mTkeys = [('mT', m) for m in range(8)]
        for j in range(4):
            for half in range(2):
                for c in range(8):
                    R.op('pe', f_mm(pW, mT[:, c, j * 128:(j + 1) * 128], wo[:, c, half * 512:(half + 1) * 512], c == 0, c == 7), reads=mTkeys, writes=['pW'])
                R.op('dve', f_tt(xs3[s][:, j, half * 512:(half + 1) * 512], pW, xs3[s][:, j, half * 512:(half + 1) * 512], ALU.add),
                     reads=['pW', ('xs3', s, j)], writes=[('xs3', s, j)])
        R.op('pool', f_dma(X1[b * 512:(b + 1) * 512, :].rearrange("(j p) f -> p j f", p=128), xs3[s]),
             reads=[('xs3', s, j) for j in range(4)], writes=[('X1', b)], dma=('x1st', s))

    R.barrier()
    if stop == 3:
        R.finalize()
        return nc
    AR.release(0)

    R.bank = {'pB1': 0, 'pT': 0, ('pH', 0): 1, ('pH', 1): 2, ('pH', 2): 3, ('pY2', 0): 4, ('pY2', 1): 5}
    w1 = AR.alloc([8, 4096], BF16)
    w2 = AR.alloc([32, 1024], BF16)
    fg = AR.alloc([1024], F32)
    b1 = b1p
    R.op('sp', f_dma(fg, bigbc_d[:, 2048:3072]), writes=['fg'], dma='fg')
    R.op('sp', f_dma(w1.rearrange("p a b -> p (a b)"), WB_w1), writes=['w1'], dma='w1')
    R.op('sp', f_dma(w2.rearrange("p a b -> p (a b)"), WB_w2), writes=['w2'], dma='w2')
    R.barrier()

    x1s = [AR.alloc([2, 1024], F32) for _ in range(2)]
    xn4 = [AR.alloc([1024], BF16) for _ in range(2)]
    h2T = AR.alloc([8, 256], BF16)
    rr = [AR.alloc([256], F32) for _ in range(2)]
    hidT = AR.alloc([32, 256], BF16)
    x2 = [AR.alloc([1024], F32) for _ in range(2)]
    outs = x2
    junk4 = AR.alloc([1024], BF16)

    pT = psb(0, BF16).rearrange("p (c n) -> p c n", c=8)
    pH = [psb(1), psb(2), psb(3)]
    pY2 = [psb(4), psb(5)]
    outkeys = []
    for b in range(32):
        s = b % 2
        R.op('sp', f_dma(x1s[s], X1[b * 256:(b + 1) * 256, :].rearrange("(j p) f -> p j f", p=128)),
             writes=[('x1s', s, 0), ('x1s', s, 1)], dma=('x1s', s))
        for j in range(2):
            R.op('act', f_act(junk4, x1s[s][:, j, :], AF.Square, accum_out=st(9, j)), reads=[('x1s', s, j)], writes=['junk4', ('ss2', j)])
            R.op('act', f_act(st(10, j), st(9, j), AF.Sqrt, scale=1.0 / D, bias=epsb[:]), reads=[('ss2', j)], writes=[('sd2', j)])
            R.op('dve', f_recip(st(11, j), st(10, j)), reads=[('sd2', j)], writes=[('r2', j)])
            R.op('dve', f_ts(xn4[j], x1s[s][:, j, :], st(11, j), ALU.mult), reads=[('x1s', s, j), ('r2', j)], writes=[('xn4', j)])
            for c in range(8):
                R.op('pe', f_tr(pT[:, c, :], xn4[j][:, c * 128:(c + 1) * 128], ident[:]), reads=[('xn4', j)], writes=['pT'])
            R.op('act', f_copy_act(h2T[:, :, j * 128:(j + 1) * 128], pT), reads=['pT'], writes=[('h2T', j)])
        for m in range(32):
            for c in range(8):
                R.op('pe', f_mm(pH[m % 3][:, 0:256], w1[:, c, m * 128:(m + 1) * 128], h2T[:, c, :], c == 0, c == 7),
                     reads=[('h2T', 0), ('h2T', 1)], writes=[('pH', m % 3)])
            R.op('act', f_act(rr[m % 2], pH[m % 3][:, 0:256], AF.Relu, bias=b1[:, m:m + 1]), reads=[('pH', m % 3)], writes=[('rr', m % 2)])
            R.op('dve' if m % 2 == 0 else 'pool', f_tt(hidT[:, m, :], rr[m % 2], rr[m % 2], ALU.mult), reads=[('rr', m % 2)], writes=[('hidT', m)])
        hkeys = [('hidT', m) for m in range(32)]
        for j in range(2):
            for half in range(2):
                pk = (j * 2 + half) % 2
                for m in range(32):
                    R.op('pe', f_mm(pY2[pk], hidT[:, m, j * 128:(j + 1) * 128], w2[:, m, half * 512:(half + 1) * 512], m == 0, m == 31),
                         reads=hkeys, writes=[('pY2', pk)])
                R.op('dve', f_tt(x2[j][:, half * 512:(half + 1) * 512], pY2[pk], x1s[s][:, j, half * 512:(half + 1) * 512], ALU.add),
                     reads=[('pY2', pk), ('x1s', s, j)], writes=[('x2', j, half)])
            R.op('act', f_act(junk4, x2[j], AF.Square, accum_out=st(12, j)), reads=[('x2', j, 0), ('x2', j, 1)], writes=['junk4', ('ss3', j)])
            R.op('act', f_act(st(13, j), st(12, j), AF.Sqrt, scale=1.0 / D, bias=epsb[:]), reads=[('ss3', j)], writes=[('sd3', j)])
            R.op('dve', f_recip(st(14, j), st(13, j)), reads=[('sd3', j)], writes=[('r3', j)])
            R.op('dve', f_stt(outs[j], x2[j], st(14, j), fg, ALU.mult, ALU.mult), reads=[('x2', j, 0), ('x2', j, 1), ('r3', j), 'fg'],
                 writes=[('x2', j, 0), ('x2', j, 1)])
            row = (b * 2 + j) * 128
            R.op('pool', f_dma(out[row:row + 128, :], outs[j]), reads=[('x2', j, 0), ('x2', j, 1)], writes=[('out', b, j)], dma=('outst', j))
            outkeys.append(('out', b, j))
    R.op('sp', None, reads=outkeys)
    R.finalize()
    return nc


def f_copy_act(out, in_):
    return lambda e: e.activation(out=out, in_=in_, func=AF.Copy)


def _host_consts():
    bf = ml_dtypes.bfloat16
    ident = np.eye(128, dtype=np.float32).astype(bf)
    t = np.arange(L)
    row = (t // 64).astype(np.float32)
    col = (t % 64).astype(np.float32)
    inv_freq = (np.float32(10000.0) ** (-np.arange(0, 16, 2, dtype=np.float32) / np.float32(16))).astype(np.float32)
    ar = (row[:, None] * inv_freq).astype(np.float32)
    ac = (col[:, None] * inv_freq).astype(np.float32)
    cr, sr, cc, sc = np.cos(ar), np.sin(ar), np.cos(ac), np.sin(ac)
    cosF = np.concatenate([cr, cr, cc, cc], axis=1).astype(np.float32)
    sinS = np.concatenate([-sr, sr, -sc, sc], axis=1).astype(np.float32)
    cosF = np.ascontiguousarray(cosF.reshape(64, 128, 32).transpose(1, 0, 2).reshape(128, 64 * 32))
    sinS = np.ascontiguousarray(sinS.reshape(64, 128, 32).transpose(1, 0, 2).reshape(128, 64 * 32))
    band = np.zeros((128, 20, 128), np.float32)
    for g, w in enumerate((2, 4, 8, 16)):
        hw = w // 2
        for kind in range(5):
            T = {0: 5, 1: 5, 2: 5, 3: 0, 4: 63}[kind]
            dT = {0: -1, 1: 0, 2: 1, 3: 0, 4: 0}[kind]
            m = np.zeros((128, 128), np.float32)
            for tl in range(128):
                tg = T * 128 + tl
                lo = max(tg - hw, 0)
                hi = min(tg + hw, L)
                cnt = hi - lo
                for tp in range(lo, hi):
                    p = tp - (T + dT) * 128
                    if 0 <= p < 128:
                        m[p, tl] += 1.0 / cnt
                p = tg - (T + dT) * 128
                if 0 <= p < 128:
                    m[p, tl] -= 1.0
            band[:, g * 5 + kind, :] = m
    band = np.ascontiguousarray(band.reshape(128, 20 * 128)).astype(bf)
    return ident, cosF, sinS, band


_CACHE = {}


def prep_inputs(x, c, ctx, c_ctx, w_ada, b_ada, norm1_g, w_in, q_norm_g, kv_norm_g, w_uq, w_ukv,
                w_br_mla, pool_w, pool_scale, w_br_pool, w_out, norm2_g, w_mlp1, w_mlp2, final_g):
    f = lambda a: np.ascontiguousarray(np.asarray(a, dtype=np.float32))
    x, c, ctx, c_ctx = f(x), f(c), f(ctx), f(c_ctx)
    w_ada, b_ada, w_in, w_uq, w_ukv = f(w_ada)[0], f(b_ada)[0], f(w_in)[0], f(w_uq)[0], f(w_ukv)[0]
    w_br_mla, pool_w, w_br_pool, w_out = f(w_br_mla)[0], f(pool_w)[0], f(w_br_pool)[0], f(w_out)[0]
    w_mlp1, w_mlp2 = f(w_mlp1)[0], f(w_mlp2)[0]
    norm1_g, norm2_g, q_norm_g, kv_norm_g, pool_scale, final_g = (f(norm1_g)[0], f(norm2_g)[0], f(q_norm_g)[0],
                                                                   f(kv_norm_g)[0], f(pool_scale)[0], f(final_g))
    if 'consts' not in _CACHE:
        _CACHE['consts'] = _host_consts()
    ident, cosF, sinS, band = _CACHE['consts']
    col = lambda v: v.reshape(-1, 128).T
    bigbc = np.ascontiguousarray(np.broadcast_to(
        np.concatenate([b_ada[2 * D:3 * D], b_ada[5 * D:6 * D], final_g])[None, :], (128, 3 * D)))
    shared = dict(w_ada=w_ada, w_in=w_in, w_uq=w_uq, w_ukv=w_ukv, w_br_mla=w_br_mla, pool_w=pool_w,
                  w_br_pool=w_br_pool, w_out=w_out, w_mlp1=w_mlp1, w_mlp2=w_mlp2, ident=ident,
                  cosF=cosF, sinS=sinS, band=band, bigbc=bigbc)
    in_maps = []
    for b in range(x.shape[0]):
        ccol = np.stack([col(c[b]), col(c_ctx)], axis=2).reshape(128, 16)
        smallf = np.ascontiguousarray(np.concatenate(
            [ccol, col(b_ada), col(norm1_g), col(norm2_g), col(q_norm_g), col(kv_norm_g), col(pool_scale)], axis=1).astype(np.float32))
        assert smallf.shape == (128, 89)
        m = dict(shared)
        m.update(x=x[b], ctx=ctx[b], smallf=smallf)
        in_maps.append(m)
    return in_maps


def kernel(**inputs):
    in_maps = prep_inputs(**inputs)
    if 'nc' not in _CACHE:
        _CACHE['nc'] = build_program()
    nc = _CACHE['nc']
    res = run_bass_kernel_spmd(nc, in_maps, core_ids=list(range(8)))
    return np.stack([np.asarray(r["out"], dtype=np.float32) for r in res.results], axis=0)
```

```python
import numpy as np
import ml_dtypes
import concourse.bass as bass
import concourse.mybir as mybir
from concourse.bass_utils import run_bass_kernel_spmd

F32 = mybir.dt.float32
BF16 = mybir.dt.bfloat16
AF = mybir.ActivationFunctionType
ALU = mybir.AluOpType

D = 1024
L = 8192
CTX = 256
NKT = (L + CTX) // 128
H = 8
EPS = 1e-6
SCALE = 96.0 ** -0.5
ENGS = ['pe', 'act', 'dve', 'pool', 'sp']


class Rec:
    def __init__(self, nc):
        self.nc = nc
        self.ops = []
        self.lastw = {}
        self.rd_eng = {}
        self.rd_dma = {}
        self.bank = {}
        self.bank_last = {}

    def op(self, eng, fn, reads=(), writes=(), dma=None):
        j = len(self.ops)
        deps = {}
        bset = set()
        for k in list(reads) + list(writes):
            if k in self.bank:
                bv = self.bank[k]
                bset.update(bv if isinstance(bv, tuple) else (bv,))
        for bnk in bset:
            la = self.bank_last.setdefault(bnk, {})
            for e2, i in la.items():
                if e2 != eng:
                    deps.setdefault(i, False)
            la[eng] = j
        for k in reads:
            i = self.lastw.get(k)
            if i is not None:
                deps[i] = True
        for k in writes:
            for i in self.rd_eng.get(k, {}).values():
                deps.setdefault(i, False)
            for i in self.rd_dma.get(k, ()):
                deps.setdefault(i, False)
            i = self.lastw.get(k)
            if i is not None:
                deps.setdefault(i, False)
        for k in reads:
            if dma is None:
                self.rd_eng.setdefault(k, {})[eng] = j
            else:
                self.rd_dma.setdefault(k, []).append(j)
        for k in writes:
            self.lastw[k] = j
            self.rd_eng[k] = {}
            self.rd_dma[k] = []
        keep = []
        for i, raw in deps.items():
            oi = self.ops[i]
            if oi['dma'] is None and dma is None and oi['eng'] == eng:
                if not raw or eng == 'pe':
                    continue
            keep.append(i)
        self.ops.append(dict(eng=eng, fn=fn, dma=dma, deps=keep, sig=False))
        return j

    def barrier(self):
        last = {}
        for idx, o in enumerate(self.ops):
            if o['fn'] is None:
                continue
            if o['dma'] is not None:
                last[('d', o['dma'])] = idx
            else:
                last[('e', o['eng'])] = idx
        deps = list(last.values())
        for e in ENGS:
            self.ops.append(dict(eng=e, fn=None, dma=None, deps=list(deps), sig=False))
        self.lastw.clear()
        self.rd_eng.clear()
        self.rd_dma.clear()
        self.bank_last.clear()

    def finalize(self):
        nc = self.nc
        ops = self.ops
        for o in ops:
            for i in o['deps']:
                ops[i]['sig'] = True
        esem = {e: nc.alloc_semaphore("s_" + e) for e in ENGS}
        dsem = {}
        ecnt = {e: 0 for e in ENGS}
        dcnt = {}
        for o in ops:
            if o['fn'] is None:
                continue
            if o['dma'] is not None:
                k = o['dma']
                if k not in dsem:
                    dsem[k] = nc.alloc_semaphore("d_%d" % len(dsem))
                    dcnt[k] = 0
                dcnt[k] += 16
                o['sem'] = dsem[k]
                o['semid'] = ('d', k)
                o['val'] = dcnt[k]
            elif o['sig']:
                ecnt[o['eng']] += 1
                o['sem'] = esem[o['eng']]
                o['semid'] = ('e', o['eng'])
                o['val'] = ecnt[o['eng']]
        streams = {e: [] for e in ENGS}
        for idx, o in enumerate(ops):
            streams[o['eng']].append(idx)

        def run(eng_name, engine):
            seen = {}
            for idx in streams[eng_name]:
                o = ops[idx]
                need = {}
                for i in o['deps']:
                    d = ops[i]
                    sid, v = d['semid'], d['val']
                    if seen.get(sid, 0) >= v:
                        continue
                    if need.get(sid, (None, 0))[1] < v:
                        need[sid] = (d['sem'], v)
                for sid, (s, v) in need.items():
                    seen[sid] = v
                    engine.wait_ge(s, v)
                if o['fn'] is None:
                    continue
                ins = o['fn'](engine)
                if o['dma'] is not None:
                    ins.then_inc(o['sem'], 16)
                elif o['sig']:
                    ins.then_inc(o['sem'], 1)

        with nc.Block() as block:
            @block.tensor
            def _(e):
                run('pe', e)

            @block.scalar
            def _(e):
                run('act', e)

            @block.vector
            def _(e):
                run('dve', e)

            @block.gpsimd
            def _(e):
                run('pool', e)

            @block.sync
            def _(e):
                run('sp', e)


class Arena:
    def __init__(self, nc, nbytes):
        self.t = nc.alloc_sbuf_tensor("arena", [128, nbytes // 2], BF16)
        self.cap = nbytes
        self.off = 0

    def mark(self):
        return self.off

    def release(self, m):
        self.off = m

    def alloc(self, shape, dtype, parts=128):
        esz = 4 if dtype == F32 else 2
        n = 1
        for s in shape:
            n *= s
        nb = n * esz
        off = (self.off + 63) // 64 * 64
        assert off + nb <= self.cap, ("arena overflow", off, nb, self.cap)
        self.off = off + nb
        ap = self.t[0:parts, off // 2:(off + nb) // 2]
        if dtype == F32:
            ap = ap.bitcast(F32)
        if len(shape) > 1:
            names = ["a%d" % i for i in range(len(shape))]
            pat = "p (" + " ".join(names) + ") -> p " + " ".join(names)
            ap = ap.rearrange(pat, **{nm: s for nm, s in zip(names[1:], shape[1:])})
        return ap


def f_dma(out, in_):
    return lambda e: e.dma_start(out=out, in_=in_)


def f_mm(out, lhsT, rhs, start, stop):
    return lambda e: e.matmul(out, lhsT=lhsT, rhs=rhs, start=start, stop=stop)


def f_tr(out, in_, ident):
    return lambda e: e.transpose(out=out, in_=in_, identity=ident)


def f_act(out, in_, func, **kw):
    return lambda e: e.activation(out=out, in_=in_, func=func, **kw)


def f_tt(out, in0, in1, op):
    return lambda e: e.tensor_tensor(out=out, in0=in0, in1=in1, op=op)


def f_ts(out, in0, s1, op0, s2=None, op1=None):
    if op1 is None:
        return lambda e: e.tensor_scalar(out=out, in0=in0, scalar1=s1, scalar2=None, op0=op0)
    return lambda e: e.tensor_scalar(out=out, in0=in0, scalar1=s1, scalar2=s2, op0=op0, op1=op1)


def f_stt(out, in0, scalar, in1, op0, op1):
    return lambda e: e.scalar_tensor_tensor(out=out, in0=in0, scalar=scalar, in1=in1, op0=op0, op1=op1)


def f_copy(out, in_):
    return lambda e: e.tensor_copy(out=out, in_=in_)


def f_recip(out, in_):
    return lambda e: e.reciprocal(out=out, in_=in_)


def f_memset(out, v):
    return lambda e: e.memset(out, v)


def cast_scaled(R, k, out, in_, scal, reads, writes):
    eng = ('dve', 'act', 'pool')[k % 3]
    if scal is None:
        if eng == 'act':
            R.op('act', f_act(out, in_, AF.Copy), reads, writes)
        else:
            R.op(eng, f_copy(out, in_), reads, writes)
    elif eng == 'act':
        R.op('act', f_act(out, in_, AF.Copy, scale=scal), reads, writes)
    elif eng == 'dve':
        R.op('dve', f_ts(out, in_, scal, ALU.mult), reads, writes)
    else:
        R.op('pool', f_ts(out, in_, scal, ALU.mult, 1.0, ALU.mult), reads, writes)


def build_program(stop=9, dbg=False):
    nc = bass.Bass("TRN2", target_bir_lowering=False)
    skind = "ExternalOutput" if dbg else "Internal"

    def din(name, shape, dt=F32):
        return nc.dram_tensor(name, shape, dt, kind="ExternalInput").ap()

    x = din("x", [L, D])
    ctx = din("ctx", [CTX, D])
    smallf = din("smallf", [128, 89])
    bigbc_d = din("bigbc", [128, 3072])
    w_ada = din("w_ada", [D, 6 * D])
    w_in = din("w_in", [D, 3232])
    w_uq = din("w_uq", [384, 768])
    w_ukv = din("w_ukv", [256, 1024])
    w_br_mla = din("w_br_mla", [512, 1024])
    pool_w = din("pool_w", [4, 128, 128])
    w_br_pool = din("w_br_pool", [512, 1024])
    w_out = din("w_out", [D, D])
    w_mlp1 = din("w_mlp1", [D, 4096])
    w_mlp2 = din("w_mlp2", [4096, D])
    ident_d = din("ident", [128, 128], BF16)
    cosF_d = din("cosF", [128, 64 * 32])
    sinS_d = din("sinS", [128, 64 * 32])
    band_d = din("band", [128, 20 * 128], BF16)
    out = nc.dram_tensor("out", [L, D], F32, kind="ExternalOutput").ap()

    QT = nc.dram_tensor("QT", [H, 96, L], BF16, kind=skind).ap()
    KT = nc.dram_tensor("KT", [H, 96, NKT * 128], BF16, kind=skind).ap()
    Vs = nc.dram_tensor("Vs", [2, 128, NKT, 4, 66], BF16, kind=skind).ap()
    OT = nc.dram_tensor("OT", [512, L], BF16, kind=skind).ap()
    U = nc.dram_tensor("U", [L, 512], BF16, kind=skind).ap()
    X1 = nc.dram_tensor("X1", [L, D], F32, kind=skind).ap()
    WB_wg = nc.dram_tensor("WB_wg", [128, 8 * 2048], BF16, kind="Internal").ap()
    WB_wbm = nc.dram_tensor("WB_wbm", [128, 4 * 1024], BF16, kind="Internal").ap()
    WB_wbp = nc.dram_tensor("WB_wbp", [128, 4 * 1024], BF16, kind="Internal").ap()
    WB_pw = nc.dram_tensor("WB_pw", [128, 4 * 128], BF16, kind="Internal").ap()
    WB_wo = nc.dram_tensor("WB_wo", [128, 8 * 1024], BF16, kind="Internal").ap()
    WB_w1 = nc.dram_tensor("WB_w1", [128, 8 * 4096], BF16, kind="Internal").ap()
    WB_w2 = nc.dram_tensor("WB_w2", [128, 32 * 1024], BF16, kind="Internal").ap()

    R = Rec(nc)

    def gt(name, shape, dt=F32):
        return nc.alloc_sbuf_tensor(name, shape, dt)

    sm = gt("sm", [128, 89])
    ident = gt("ident_s", [128, 128], BF16)
    ones_f = gt("ones_f", [128, 128])
    epsb = gt("epsb", [128, 1])
    sil = gt("sil", [128, 8, 2])
    vcol = gt("vcol", [128, 4, 8, 2])
    gtbc = gt("gtbc", [128, 2, 1024])
    A1 = gt("A1", [128, 8])
    cA1 = gt("cA1", [128, 8])
    A2 = gt("A2", [128, 8])
    rstd1 = gt("rstd1", [128, 64])
    stat = gt("stat", [128, 64])
    bias_g = gt("bias_g", [128, 16])
    b1p = gt("b1p", [128, 32])
    AR = Arena(nc, 196 * 1024)

    PSall = nc.alloc_psum_tensor("psall", [128, 8, 512], F32)

    def psb(i, dt=F32):
        ap = PSall[:, i, :]
        if dt == BF16:
            ap = ap.bitcast(BF16)
        return ap

    R.bank = {('pV', 0): 0, ('pV', 1): 1, ('pR', 0): 2, ('pR', 1): 3, 'pBias0': 4, 'pBias1': 5}
    R.op('sp', f_dma(sm[:], smallf), writes=['sm'], dma='sm')
    R.op('sp', f_dma(ident[:], ident_d), writes=['ident'], dma='ident')
    R.op('dve', f_memset(ones_f[:], 1.0), writes=['ones'])
    R.op('dve', f_memset(epsb[:], EPS), writes=['epsb'])
    R.op('act', f_act(sil[:].rearrange("p c v -> p (c v)"), sm[:, 0:16], AF.Silu), reads=['sm'], writes=['sil'])

    w1x = AR.alloc([8, 1184], BF16)
    w1c = AR.alloc([8, 288], BF16)
    wuq = AR.alloc([3, 768], BF16)
    wukv = AR.alloc([2, 1024], BF16)
    bias1x = AR.alloc([1184], F32)
    bias1c = AR.alloc([288], F32)
    cosF = AR.alloc([64, 32], F32)
    sinS = AR.alloc([64, 32], F32)
    m_p1 = AR.mark()
    stgh = {'b': [AR.alloc([8192], F32) for _ in range(2)]}
    bigbc = AR.alloc([2048], F32)
    R.op('sp', f_dma(bigbc, bigbc_d[:, 0:2048]), writes=['bigbc'], dma='bigbc')
    sil_rep = AR.alloc([8, 128], F32)
    sh1_rep = AR.alloc([8, 128], F32)
    csh1_rep = AR.alloc([8, 128], F32)

    R.op('sp', f_dma(cosF.rearrange("p a b -> p (a b)"), cosF_d), writes=['cosF'], dma='cosF')
    R.op('sp', f_dma(sinS.rearrange("p a b -> p (a b)"), sinS_d), writes=['sinS'], dma='sinS')

    for c in range(8):
        R.op('dve', f_ts(sil_rep[:, c, :], ones_f[:], sil[:, c, 0:1], ALU.mult),
             reads=['ones', 'sil'], writes=[('silrep', c)])

    stg_n = [0]

    def stage_load(src_ap, kc, ncols):
        i = stg_n[0] % 2
        stg_n[0] += 1
        view = stgh['b'][i][:, 0:kc * ncols].rearrange("p (c n) -> p c n", c=kc)
        R.op('sp', f_dma(view, src_ap), writes=[('stg', i)], dma=('stg', i))
        return view, ('stg', i)

    pV = [psb(0)[:, 0:16].rearrange("p (m v) -> p m v", v=2), psb(1)[:, 0:16].rearrange("p (m v) -> p m v", v=2)]
    pR = [psb(2), psb(3)]
    vmap = {0: 0, 1: 1, 3: 2, 4: 3}
    for v in range(6):
        view, skey = stage_load(w_ada[:, v * 1024:(v + 1) * 1024].rearrange("(c p) n -> p c n", p=128), 8, 1024)
        if v in vmap:
            vi = vmap[v]
            pk = ('pV', vi % 2)
            for m in range(8):
                for c in range(8):
                    R.op('pe', f_mm(pV[vi % 2][:, m, :], view[:, c, m * 128:(m + 1) * 128], sil[:, c, :], c == 0, c == 7),
                         reads=[skey, 'sil'], writes=[pk])
            R.op('dve', f_tt(vcol[:, vi], pV[vi % 2],
                             sm[:, 16 + v * 8:16 + (v + 1) * 8].unsqueeze(2).broadcast_to([128, 8, 2]), ALU.add),
                 reads=[pk, 'sm'], writes=[('vcol', vi)])
        else:
            gi = 0 if v == 2 else 1
            for half in range(2):
                pk = ('pR', half)
                for c in range(8):
                    R.op('pe', f_mm(pR[half], sil_rep[:, c, :], view[:, c, half * 512:(half + 1) * 512], c == 0, c == 7),
                         reads=[skey, ('silrep', c)], writes=[pk])
                R.op('dve', f_tt(gtbc[:, gi, half * 512:(half + 1) * 512], pR[half],
                                 bigbc[:, gi * 1024 + half * 512:gi * 1024 + (half + 1) * 512], ALU.add),
                     reads=[pk, 'bigbc'], writes=[('gtbc', gi)])

    R.op('dve', f_stt(A1[:], vcol[:, 1, :, 0], 1.0, sm[:, 64:72], ALU.add, ALU.mult), reads=[('vcol', 1), 'sm'], writes=['A1'])
    R.op('dve', f_stt(cA1[:], vcol[:, 1, :, 1], 1.0, sm[:, 64:72], ALU.add, ALU.mult), reads=[('vcol', 1), 'sm'], writes=['cA1'])
    R.op('dve', f_stt(A2[:], vcol[:, 3, :, 0], 1.0, sm[:, 72:80], ALU.add, ALU.mult), reads=[('vcol', 3), 'sm'], writes=['A2'])
    for c in range(8):
        R.op('dve', f_ts(sh1_rep[:, c, :], ones_f[:], vcol[:, 0, c, 0:1], ALU.mult), reads=['ones', ('vcol', 0)], writes=[('sh1rep', c)])
        R.op('dve', f_ts(csh1_rep[:, c, :], ones_f[:], vcol[:, 0, c, 1:2], ALU.mult), reads=['ones', ('vcol', 0)], writes=[('csh1rep', c)])

    kk = 0
    pBias = [psb(4), psb(5)]
    for pi, (n0, n1) in enumerate([(0, 384), (384, 672), (672, 1184)]):
        n = n1 - n0
        view, skey = stage_load(w_in[:, n0:n1].rearrange("(c p) n -> p c n", p=128), 8, n)
        for c in range(8):
            cast_scaled(R, kk, w1x[:, c, n0:n1], view[:, c, :], A1[:, c:c + 1], [skey, 'A1'], [('w1x', pi, c)])
            kk += 1
        for c in range(8):
            R.op('pe', f_mm(pBias[0][:, 0:n], sh1_rep[:, c, :], view[:, c, :], c == 0, c == 7),
                 reads=[skey, ('sh1rep', c)], writes=['pBias0'])
        R.op('dve', f_copy(bias1x[:, n0:n1], pBias[0][:, 0:n]), reads=['pBias0'], writes=[('bias1x', pi)])
        if pi == 1:
            for c in range(8):
                cast_scaled(R, kk, w1c[:, c, :], view[:, c, :], cA1[:, c:c + 1], [skey, 'cA1'], [('w1c', c)])
                kk += 1
            for c in range(8):
                R.op('pe', f_mm(pBias[1][:, 0:n], csh1_rep[:, c, :], view[:, c, :], c == 0, c == 7),
                     reads=[skey, ('csh1rep', c)], writes=['pBias1'])
            R.op('dve', f_copy(bias1c[:], pBias[1][:, 0:n]), reads=['pBias1'], writes=['bias1c'])
    view, skey = stage_load(w_uq.rearrange("(c p) n -> p c n", p=128), 3, 768)
    for c in range(3):
        cast_scaled(R, kk, wuq[:, c, :], view[:, c, :], sm[:, 80 + c:81 + c], [skey, 'sm'], [('wuq', c)])
        kk += 1
    view, skey = stage_load(w_ukv.rearrange("(c p) n -> p c n", p=128), 2, 1024)
    for c in range(2):
        cast_scaled(R, kk, wukv[:, c, :], view[:, c, :], sm[:, 83 + c:84 + c], [skey, 'sm'], [('wukv', c)])
        kk += 1

    if dbg:
        dbg0 = nc.dram_tensor("dbg0", [128, 4 * 16 + 2048 + 24 + 1184 + 288], F32, kind="ExternalOutput").ap()
        R.op('sp', f_dma(dbg0[:, 0:64], vcol[:].rearrange("p a b c -> p (a b c)")), reads=[('vcol', i) for i in range(4)], writes=['dbg0a'], dma='dbg0a')
        R.op('sp', f_dma(dbg0[:, 64:2112], gtbc[:].rearrange("p a b -> p (a b)")), reads=[('gtbc', 0), ('gtbc', 1)], writes=['dbg0b'], dma='dbg0b')
        R.op('sp', f_dma(dbg0[:, 2112:2120], A1[:]), reads=['A1'], writes=['dbg0c'], dma='dbg0c')
        R.op('sp', f_dma(dbg0[:, 2120:2128], cA1[:]), reads=['cA1'], writes=['dbg0d'], dma='dbg0d')
        R.op('sp', f_dma(dbg0[:, 2128:2136], A2[:]), reads=['A2'], writes=['dbg0e'], dma='dbg0e')
        R.op('sp', f_dma(dbg0[:, 2136:2136 + 1184], bias1x), reads=[('bias1x', i) for i in range(3)], writes=['dbg0f'], dma='dbg0f')
        R.op('sp', f_dma(dbg0[:, 2136 + 1184:2136 + 1184 + 288], bias1c), reads=['bias1c'], writes=['dbg0g'], dma='dbg0g')
        dbg1 = nc.dram_tensor("dbg1", [128, 8 * 1184], BF16, kind="ExternalOutput").ap()
        R.op('sp', f_dma(dbg1, w1x.rearrange("p a b -> p (a b)")), reads=[('w1x', pi, c) for pi in range(3) for c in range(8)], writes=['dbg1'], dma='dbg1')
    R.barrier()
    if stop == 0:
        R.finalize()
        return nc
    AR.release(m_p1)

    R.bank = {'pT': 0, 'psA': 1, 'psB': 2, 'psC': 3, 'pT2q': 4, 'pT2k': 4, 'psWk': 5, 'psWq': 6, 'pXTk': 7, 'pXTq': 7}
    NX, NB = 4, 3
    xs = [AR.alloc([1024], F32) for _ in range(NX)]
    junk = [AR.alloc([1024], BF16) for _ in range(3)]
    xn = [AR.alloc([1024], BF16) for _ in range(NB)]
    xT = [AR.alloc([8, 128], BF16) for _ in range(NB)]
    u_sb = [AR.alloc([512], BF16) for _ in range(NB)]
    cqn = [AR.alloc([384], BF16) for _ in range(NB)]
    cqT = [AR.alloc([3, 128], BF16) for _ in range(NB)]
    ckvn = [AR.alloc([256], BF16) for _ in range(NB)]
    ckvT = [AR.alloc([2, 128], BF16) for _ in range(NB)]
    krs_sb = [AR.alloc([32], F32) for _ in range(NB)]
    t1 = [AR.alloc([8, 32], F32) for _ in range(NB)]
    t2 = [AR.alloc([8, 32], F32) for _ in range(NB)]
    qf = [AR.alloc([8, 96], BF16) for _ in range(NB)]
    kf = [AR.alloc([8, 96], BF16) for _ in range(NB)]
    vaug = [AR.alloc([8, 66], BF16) for _ in range(NB)]
    qT_sb = [AR.alloc([8, 128], BF16) for _ in range(NB)]
    kT_sb = [AR.alloc([8, 128], BF16) for _ in range(NB)]
    t1k = [AR.alloc([32], F32) for _ in range(NB)]
    t2k = [AR.alloc([32], F32) for _ in range(NB)]
    kr = [AR.alloc([32], F32) for _ in range(NB)]
    bhi = AR.alloc([1184], BF16)
    blo = AR.alloc([1184], BF16)
    btmp = AR.alloc([1184], F32)
    bchi = AR.alloc([288], BF16)
    bclo = AR.alloc([288], BF16)
    ones_b = AR.alloc([128], BF16)

    R.op('dve', f_memset(ones_b[0:1], 1.0), writes=['ones_b'])
    for (hi_, lo_, src_, n_, kn) in [(bhi, blo, bias1x, 1184, 'x'), (bchi, bclo, bias1c, 288, 'c')]:
        R.op('dve', f_copy(hi_[0:1], src_[0:1]), writes=[('bhi', kn)])
        R.op('dve', f_tt(btmp[0:1, 0:n_], src_[0:1], hi_[0:1], ALU.subtract), reads=[('bhi', kn)], writes=[('btmp', kn)])
        R.op('dve', f_copy(lo_[0:1], btmp[0:1, 0:n_]), reads=[('btmp', kn)], writes=[('blo', kn)])
    for s in range(NB):
        R.op('dve', f_memset(vaug[s], 1.0), writes=[('vaug', s)])

    pT = psb(0, BF16).rearrange("p (c n) -> p c n", c=8)
    psA, psB, psC = psb(1), psb(2), psb(3)
    pT2 = psb(4, BF16).rearrange("p (c n) -> p c n", c=8)
    psWk = psb(5).rearrange("p (h d) -> p h d", d=128)
    psWq = psb(6)[:, 0:384].rearrange("p (h d) -> p h d", d=96)
    pXT = psb(7, BF16).rearrange("p (c n) -> p c n", c=8)
    QTv = QT.rearrange("h d t -> d h t")
    KTv = KT.rearrange("h d t -> d h t")

    def st(col, s):
        return stat[:, col * 4 + s:col * 4 + s + 1]

    class Stage:
        def __init__(self):
            self.l = []

        def op(self, eng, fn, reads=(), writes=(), dma=None):
            self.l.append((eng, fn, reads, writes, dma))

    def emit_merged(stages, group_pe=False):
        items = []
        for si, sg in enumerate(stages):
            n = len(sg.l)
            k = 0
            while k < n:
                k1 = k + 1
                if group_pe and sg.l[k][0] == 'pe':
                    while k1 < n and sg.l[k1][0] == 'pe':
                        k1 += 1
                items.append(((k + 0.5) / n, si, k, sg.l[k:k1]))
                k = k1
        items.sort(key=lambda z: (z[0], z[1]))
        for _, _, _, grp in items:
            for o in grp:
                R.op(o[0], o[1], reads=o[2], writes=o[3], dma=o[4])

    def tile_src(t):
        return ctx[t * 128:(t + 1) * 128, :] if t < 2 else x[(t - 2) * 128:(t - 1) * 128, :]

    def SL(G, t):
        G.op('sp', f_dma(xs[t % NX], tile_src(t)), writes=[('xs', t % NX)], dma=('xs', t % NX))

    def S1(G, t):
        is_ctx = t < 2
        xi = t - 2
        sx, s, s4_ = t % NX, t % NB, t % 4
        G.op('act', f_act(junk[0], xs[sx], AF.Square, accum_out=st(0, s4_)), reads=[('xs', sx)], writes=[('ss', s4_)])
        G.op('act', f_act(st(1, s4_), st(0, s4_), AF.Sqrt, scale=1.0 / D, bias=epsb[:]), reads=[('ss', s4_)], writes=[('sd', s4_)])
        if is_ctx:
            rst, rkey = st(2, s4_), ('rstc', s4_)
        else:
            rst, rkey = rstd1[:, xi:xi + 1], ('rstd1', xi)
        G.op('dve', f_recip(rst, st(1, s4_)), reads=[('sd', s4_)], writes=[rkey])
        G.op('dve', f_ts(xn[s], xs[sx], rst, ALU.mult), reads=[('xs', sx), rkey], writes=[('xn', s)])
        for c in range(8):
            G.op('pe', f_tr(pT[:, c, :], xn[s][:, c * 128:(c + 1) * 128], ident[:]), reads=[('xn', s)], writes=['pT'])
        G.op('act', f_copy_act(xT[s].rearrange("p c n -> p (c n)"), psb(0, BF16)), reads=['pT'], writes=[('xT', s)])
        if not is_ctx:
            groups = [(psA, 'psA', 0, 384), (psB, 'psB', 384, 672), (psC, 'psC', 672, 1184)]
            for (ps, key, n0, n1) in groups:
                for c in range(8):
                    G.op('pe', f_mm(ps[:, 0:n1 - n0], xT[s][:, c, :], w1x[:, c, n0:n1], c == 0, False), reads=[('xT', s)], writes=[key])
                G.op('pe', f_mm(ps[:, 0:n1 - n0], ones_b[0:1, :], bhi[0:1, n0:n1], False, False), writes=[key])
                G.op('pe', f_mm(ps[:, 0:n1 - n0], ones_b[0:1, :], blo[0:1, n0:n1], False, True), writes=[key])
        else:
            for c in range(8):
                G.op('pe', f_mm(psB[:, 0:288], xT[s][:, c, :], w1c[:, c, :], c == 0, False), reads=[('xT', s)], writes=['psB'])
            G.op('pe', f_mm(psB[:, 0:288], ones_b[0:1, :], bchi[0:1, :], False, False), writes=['psB'])
            G.op('pe', f_mm(psB[:, 0:288], ones_b[0:1, :], bclo[0:1, :], False, True), writes=['psB'])
        G.op('act', f_act(junk[0][:, 0:256], psB[:, 0:256], AF.Square, accum_out=st(3, s4_)), reads=['psB'], writes=[('sskv', s4_)])
        G.op('act', f_act(st(4, s4_), st(3, s4_), AF.Sqrt, scale=1.0 / 256, bias=epsb[:]), reads=[('sskv', s4_)], writes=[('sdkv', s4_)])
        G.op('dve', f_recip(st(5, s4_), st(4, s4_)), reads=[('sdkv', s4_)], writes=[('rkv', s4_)])
        G.op('dve', f_ts(ckvn[s], psB[:, 0:256], st(5, s4_), ALU.mult), reads=['psB', ('rkv', s4_)], writes=[('ckvn', s)])
        G.op('dve', f_copy(krs_sb[s], psB[:, 256:288]), reads=['psB'], writes=[('krs', s)])
        if not is_ctx:
            G.op('act', f_act(junk[0][:, 0:384], psA[:, 0:384], AF.Square, accum_out=st(6, s4_)), reads=['psA'], writes=[('ssq', s4_)])
            G.op('act', f_act(st(7, s4_), st(6, s4_), AF.Sqrt, scale=1.0 / 384, bias=epsb[:]), reads=[('ssq', s4_)], writes=[('sdq', s4_)])
            G.op('dve', f_recip(st(8, s4_), st(7, s4_)), reads=[('sdq', s4_)], writes=[('rq', s4_)])
            G.op('dve', f_ts(cqn[s], psA[:, 0:384], st(8, s4_), ALU.mult), reads=['psA', ('rq', s4_)], writes=[('cqn', s)])
            G.op('act', f_copy_act(u_sb[s], psC[:, 0:512]), reads=['psC'], writes=[('u', s)])
            G.op('pool', f_dma(U[xi * 128:(xi + 1) * 128, :], u_sb[s]), reads=[('u', s)], writes=[('U', xi)], dma=('ust', s))

    def S2(G, t):
        is_ctx = t < 2
        xi = t - 2
        s = t % NB
        for c in range(2):
            G.op('pe', f_tr(pT2[:, 3 + c, :], ckvn[s][:, c * 128:(c + 1) * 128], ident[:]), reads=[('ckvn', s)], writes=['pT2k'])
        G.op('dve', f_copy(ckvT[s], pT2[:, 3:5, :]), reads=['pT2k'], writes=[('ckvT', s)])
        if is_ctx:
            krsrc, krkey = krs_sb[s], ('krs', s)
        else:
            cos_t = cosF[:, xi, :]
            sin_t = sinS[:, xi, :]
            G.op('pool', f_tt(t1k[s], krs_sb[s], cos_t, ALU.mult), reads=[('krs', s)], writes=[('t1k', s)])
            kv4 = krs_sb[s].rearrange("p (a f i) -> p a f i", a=2, f=2)
            o4 = t2k[s].rearrange("p (a f i) -> p a f i", a=2, f=2)
            s4 = sin_t.rearrange("p (a f i) -> p a f i", a=2, f=2)
            for f in range(2):
                G.op('pool', f_tt(o4[:, :, f, :], kv4[:, :, 1 - f, :], s4[:, :, f, :], ALU.mult), reads=[('krs', s)], writes=[('t2k', s, f)])
            G.op('pool', f_tt(kr[s], t1k[s], t2k[s], ALU.add), reads=[('t1k', s), ('t2k', s, 0), ('t2k', s, 1)], writes=[('kr', s)])
            krsrc, krkey = kr[s], ('kr', s)
        G.op('pool', f_copy(kf[s][:, :, 64:96], krsrc.unsqueeze(1).broadcast_to([128, 8, 32])), reads=[krkey], writes=[('kfr', s)])
        for hb in range(2):
            hs = slice(hb * 4, (hb + 1) * 4)
            for c in range(2):
                G.op('pe', f_mm(psb(5), ckvT[s][:, c, :], wukv[:, c, hb * 512:(hb + 1) * 512], c == 0, c == 1), reads=[('ckvT', s)], writes=['psWk'])
            G.op('act', f_copy_act(kf[s][:, hs, 0:64], psWk[:, :, 0:64]), reads=['psWk'], writes=[('kfn', s, hb)])
            G.op('dve', f_copy(vaug[s][:, hs, 0:64], psWk[:, :, 64:128]), reads=['psWk'], writes=[('vaug', s, hb)])
        for hb in range(2):
            for h4 in range(4):
                h = hb * 4 + h4
                G.op('pe', f_tr(pXT[0:96, h4, :], kf[s][:, h, :], ident[:]), reads=[('kfr', s), ('kfn', s, hb)], writes=['pXTk'])
            G.op('dve' if hb == 0 else 'act', (f_copy if hb == 0 else f_copy_act)(kT_sb[s][0:96, hb * 4:(hb + 1) * 4, :], pXT[0:96, 0:4, :]),
                 reads=['pXTk'], writes=[('kT', s, hb)])
        G.op('sp', f_dma(KTv[:, :, t * 128:(t + 1) * 128], kT_sb[s][0:96]), reads=[('kT', s, 0), ('kT', s, 1)], writes=[('KT', t)], dma=('ktst', s))
        G.op('pool', f_dma(Vs[:, :, t].rearrange("g p hh e -> p g hh e"), vaug[s].rearrange("p (g hh) e -> p g hh e", g=2)),
             reads=[('vaug', s, 0), ('vaug', s, 1)], writes=[('Vs', t)], dma=('vst', s))

    def S3(G, t):
        xi = t - 2
        s = t % NB
        cos_t = cosF[:, xi, :]
        sin_t = sinS[:, xi, :]
        s4 = sin_t.rearrange("p (a f i) -> p a f i", a=2, f=2)
        for c in range(3):
            G.op('pe', f_tr(pT2[:, c, :], cqn[s][:, c * 128:(c + 1) * 128], ident[:]), reads=[('cqn', s)], writes=['pT2q'])
        G.op('dve', f_copy(cqT[s], pT2[:, 0:3, :]), reads=['pT2q'], writes=[('cqT', s)])
        for hb in range(2):
            hs = slice(hb * 4, (hb + 1) * 4)
            for c in range(3):
                G.op('pe', f_mm(psb(6)[:, 0:384], cqT[s][:, c, :], wuq[:, c, hb * 384:(hb + 1) * 384], c == 0, c == 2),
                     reads=[('cqT', s)], writes=['psWq'])
            qr = psWq[:, :, 64:96]
            G.op('dve', f_tt(t1[s][:, hs, :], qr, cos_t.unsqueeze(1).broadcast_to([128, 4, 32]), ALU.mult), reads=['psWq'], writes=[('t1', s, hb)])
            q5 = qr.rearrange("p h (a f i) -> p h a f i", a=2, f=2)
            o5 = t2[s][:, hs, :].rearrange("p h (a f i) -> p h a f i", a=2, f=2)
            for f in range(2):
                G.op('dve', f_tt(o5[:, :, :, f, :], q5[:, :, :, 1 - f, :], s4[:, :, f, :].unsqueeze(1).broadcast_to([128, 4, 2, 8]), ALU.mult),
                     reads=['psWq'], writes=[('t2', s, hb, f)])
            G.op('act', f_copy_act(qf[s][:, hs, 0:64], psWq[:, :, 0:64]), reads=['psWq'], writes=[('qfn', s, hb)])
        G.op('pool', f_tt(qf[s][:, :, 64:96], t1[s], t2[s], ALU.add),
             reads=[('t1', s, 0), ('t1', s, 1)] + [('t2', s, hb, f) for hb in range(2) for f in range(2)], writes=[('qfr', s)])
        for hb in range(2):
            for h4 in range(4):
                h = hb * 4 + h4
                G.op('pe', f_tr(pXT[0:96, 4 + h4, :], qf[s][:, h, :], ident[:]), reads=[('qfr', s), ('qfn', s, hb)], writes=['pXTq'])
            G.op('dve', f_copy(qT_sb[s][0:96, hb * 4:(hb + 1) * 4, :], pXT[0:96, 4:8, :]), reads=['pXTq'], writes=[('qT', s, hb)])
        G.op('sp', f_dma(QTv[:, :, xi * 128:(xi + 1) * 128], qT_sb[s][0:96]), reads=[('qT', s, 0), ('qT', s, 1)], writes=[('QT', xi)], dma=('qtst', s))

    import os
    _nt = int(os.environ.get('P1_TILES', NKT))
    G0 = Stage()
    SL(G0, 0)
    if _nt > 1:
        SL(G0, 1)
    emit_merged([G0])
    for i in range(_nt + 2):
        stages = []
        if i + 2 < _nt:
            g = Stage(); SL(g, i + 2); stages.append(g)
        if i < _nt:
            g = Stage(); S1(g, i); stages.append(g)
        if 0 <= i - 1 < _nt:
            g = Stage(); S2(g, i - 1); stages.append(g)
        if 2 <= i - 2 < _nt:
            g = Stage(); S3(g, i - 2); stages.append(g)
        emit_merged(stages, group_pe=True)

    R.barrier()
    if stop == 1:
        R.finalize()
        return nc
    AR.release(0)

    R.bank = {('pS', 0): 0, ('pS', 1): 1, ('pS', 2): 2, ('pS', 3): 3, ('pO', 0): 4, ('pO', 1): 5, 'pB': 6, 'pBg': 7, 'pB1': 7}
    kt_sb = AR.alloc([4, NKT * 128], BF16)
    v_sb = AR.alloc([NKT, 4, 66], BF16)
    qt_sb = [AR.alloc([4, 512], BF16) for _ in range(2)]
    NS, NP, LOOK = 4, 4, 2
    PT = [AR.alloc([512], BF16) for _ in range(NP)]
    rden = [AR.alloc([512], F32) for _ in range(2)]
    bcs = [AR.alloc([512], F32) for _ in range(2)]
    OTn = [AR.alloc([512], BF16) for _ in range(2)]
    pstg = [AR.alloc([4096], F32) for _ in range(2)]
    pcast = [AR.alloc([4096], BF16) for _ in range(2)]
    pS = [psb(i) for i in range(NS)]
    pO = [psb(4), psb(5)]
    pB = psb(6)
    pBg = psb(7)[:, 0:32].rearrange("p (m v) -> p m v", v=2)
    pB1 = psb(7)[:, 32:96].rearrange("p (m v) -> p m v", v=2)

    PREP = Stage()
    pcnt = [0]

    def prep_piece(src_ap, kc, ncols, dst_ap, row_scale=None, col_scale=None, bias=None):
        i = pcnt[0] % 2
        pcnt[0] += 1
        view = pstg[i][:, 0:kc * ncols].rearrange("p (c n) -> p c n", c=kc)
        cview = pcast[i][:, 0:kc * ncols].rearrange("p (c n) -> p c n", c=kc)
        PREP.op('sp', f_dma(view, src_ap), writes=[('pstg', i)], dma=('pstg', i))
        for c in range(kc):
            eng = 'dve' if c % 2 == 0 else 'pool'
            if col_scale is not None:
                fn = f_tt(cview[:, c, :], view[:, c, :], col_scale, ALU.mult)
            elif row_scale is not None:
                fn = f_ts(cview[:, c, :], view[:, c, :], row_scale(c), ALU.mult, 1.0, ALU.mult)
            else:
                fn = f_copy(cview[:, c, :], view[:, c, :])
            PREP.op(eng, fn, reads=[('pstg', i)], writes=[('pcast', i, c)])
        if bias is not None:
            ptile, m0, rhs_of, key = bias
            for m in range(ncols // 128):
                for c in range(kc):
                    PREP.op('pe', f_mm(ptile[:, m0 + m, :], view[:, c, m * 128:(m + 1) * 128], rhs_of(c), c == 0, c == kc - 1),
                            reads=[('pstg', i)], writes=[key])
        PREP.op('pool', f_dma(dst_ap, cview), reads=[('pcast', i, c) for c in range(kc)], writes=[('WB', pcnt[0])], dma=('pcst', i))

    WBg3 = WB_wg.rearrange("p (c n) -> p c n", c=8)
    for pi in range(4):
        n0 = 1184 + pi * 512
        prep_piece(w_in[:, n0:n0 + 512].rearrange("(c p) n -> p c n", p=128), 8, 512, WBg3[:, :, pi * 512:(pi + 1) * 512],
                   row_scale=lambda c: A1[:, c:c + 1], bias=(pBg, pi * 4, lambda c: vcol[:, 0, c, :], 'pBg'))
    PREP.op('dve', f_copy(bias_g[:], pBg[:, :, 0]), reads=['pBg'], writes=['bias_g'])
    prep_piece(w_br_mla.rearrange("(c p) n -> p c n", p=128), 4, 1024, WB_wbm.rearrange("p (c n) -> p c n", c=4))
    prep_piece(w_br_pool.rearrange("(c p) n -> p c n", p=128), 4, 1024, WB_wbp.rearrange("p (c n) -> p c n", c=4),
               row_scale=lambda c: sm[:, 85 + c:86 + c])
    prep_piece(pool_w.rearrange("g c d -> c g d"), 4, 128, WB_pw.rearrange("p (c n) -> p c n", c=4))
    WBo3 = WB_wo.rearrange("p (c n) -> p c n", c=8)
    for pi in range(2):
        prep_piece(w_out[:, pi * 512:(pi + 1) * 512].rearrange("(c p) n -> p c n", p=128), 8, 512, WBo3[:, :, pi * 512:(pi + 1) * 512],
                   col_scale=gtbc[:, 0, pi * 512:(pi + 1) * 512])
    WB13 = WB_w1.rearrange("p (c n) -> p c n", c=8)
    for pi in range(8):
        prep_piece(w_mlp1[:, pi * 512:(pi + 1) * 512].rearrange("(c p) n -> p c n", p=128), 8, 512, WB13[:, :, pi * 512:(pi + 1) * 512],
                   row_scale=lambda c: A2[:, c:c + 1], bias=(pB1, pi * 4, lambda c: vcol[:, 2, c, :], 'pB1'))
    PREP.op('dve', f_copy(b1p[:], pB1[:, :, 0]), reads=['pB1'], writes=['b1p'])
    WB23 = WB_w2.rearrange("p (c n) -> p c n", c=32)
    for pi in range(8):
        prep_piece(w_mlp2[pi * 512:(pi + 1) * 512, :].rearrange("(c p) n -> p c n", p=128), 4, 1024, WB23[:, pi * 4:(pi + 1) * 4, :],
                   col_scale=gtbc[:, 1, :])
    prepq = list(PREP.l)
    PSTEP = 6

    def emit_prep(k=1):
        for _ in range(k):
            if prepq:
                o = prepq.pop(0)
                R.op(o[0], o[1], reads=o[2], writes=o[3], dma=o[4])

    for g in range(2):
        for hh in range(4):
            R.op('sp', f_dma(kt_sb[0:96, hh, :], KT[g * 4 + hh]), writes=[('kt', hh)], dma=('kt', hh))
        R.op('sp', f_dma(v_sb.rearrange("p a b c -> p (a b c)"), Vs[g].rearrange("p a b c -> p (a b c)")), writes=['v'], dma='v')
        its = [(qb, hh, kt) for qb in range(16) for hh in range(4) for kt in range(NKT)]
        n = len(its)
        pend = []

        def emit_S(i):
            qb, hh, kt = its[i]
            if hh == 0 and kt == 0:
                R.op('sp', f_dma(qt_sb[qb % 2][0:96], QTv[:, g * 4:(g + 1) * 4, qb * 512:(qb + 1) * 512]),
                     writes=[('qt', qb % 2)], dma=('qt', qb % 2))
            R.op('pe', f_mm(pS[i % NS], kt_sb[0:96, hh, kt * 128:(kt + 1) * 128], qt_sb[qb % 2][0:96, hh, :], True, True),
                 reads=[('kt', hh), ('qt', qb % 2)], writes=[('pS', i % NS)])
            R.op('act', f_act(PT[i % NP], pS[i % NS], AF.Exp, scale=SCALE), reads=[('pS', i % NS)], writes=[('PT', i % NP)])

        def emit_PV(i):
            qb, hh, kt = its[i]
            hidx = qb * 4 + hh
            o = hidx % 2
            R.op('pe', f_mm(pO[o][0:65, :], v_sb[:, kt, hh, 0:65], PT[i % NP], kt == 0, kt == NKT - 1),
                 reads=['v', ('PT', i % NP)], writes=[('pO', o)])
            if kt == NKT - 1:
                R.op('dve', f_recip(rden[o][64:65, :], pO[o][64:65, :]), reads=[('pO', o)], writes=[('rden', o)])
                pend.append((i + 4, hidx, qb, hh))

        def emit_epi(hidx, qb, hh):
            o = hidx % 2
            R.op('pe', f_mm(pB[0:64, :], ones_f[64:65, 0:64], rden[o][64:65, :], True, True), reads=[('rden', o), 'ones'], writes=['pB'])
            R.op('dve', f_copy(bcs[o][0:64], pB[0:64, :]), reads=['pB'], writes=[('bcs', o)])
            R.op('dve', f_tt(OTn[o][0:64], pO[o][0:64, :], bcs[o][0:64], ALU.mult), reads=[('pO', o), ('bcs', o)], writes=[('OTn', o)])
            r0 = (g * 4 + hh) * 64
            R.op('pool', f_dma(OT[r0:r0 + 64, qb * 512:(qb + 1) * 512], OTn[o][0:64]), reads=[('OTn', o)], writes=[('OT', g, hidx)], dma=('otst', o))

        for i in range(n + LOOK):
            if i < n:
                emit_S(i)
            if i >= LOOK:
                emit_PV(i - LOOK)
            while pend and (pend[0][0] <= i - LOOK or i == n + LOOK - 1):
                _, hidx, qb, hh = pend.pop(0)
                emit_epi(hidx, qb, hh)
            if i % PSTEP == 0:
                emit_prep()
    emit_prep(len(prepq))

    R.barrier()
    if stop == 2:
        R.finalize()
        return nc
    AR.release(0)

    R.bank = {'pT': 0, ('pG', 0): 1, ('pG', 1): 2, ('pD', 0): 3, ('pD', 1): 5, ('pY', 0): 4, ('pY', 1): 6, ('pM', 0): 5, ('pM', 1): 1, ('pP', 0): 6, ('pP', 1): 2, ('pW', 0): 7, ('pW', 1): 3}
    wg = AR.alloc([8, 2048], BF16)
    wbm = AR.alloc([4, 1024], BF16)
    wbp = AR.alloc([4, 1024], BF16)
    poolw = AR.alloc([4, 128], BF16)
    wo = AR.alloc([8, 1024], BF16)
    band = AR.alloc([20, 128], BF16)
    R.op('sp', f_dma(band.rearrange("p a b -> p (a b)"), band_d), writes=['band'], dma='band')
    for nm, dst, src in [('wg', wg, WB_wg), ('wbm', wbm, WB_wbm), ('wbp', wbp, WB_wbp), ('poolw', poolw, WB_pw), ('wo', wo, WB_wo)]:
        R.op('sp', f_dma(dst.rearrange("p a b -> p (a b)"), src), writes=[nm], dma=nm)
    R.barrier()

    xs3 = [AR.alloc([4, 1024], F32) for _ in range(2)]
    xn3 = [AR.alloc([1024], BF16) for _ in range(2)]
    xT3 = [AR.alloc([8, 512], BF16) for _ in range(2)]
    sig = AR.alloc([16, 512], BF16)
    us = [AR.alloc([6, 512], BF16) for _ in range(2)]
    dT = [AR.alloc([512], BF16) for _ in range(2)]
    yT = AR.alloc([4, 512], BF16)
    ot_sb = [AR.alloc([4, 512], BF16) for _ in range(2)]
    tA = [AR.alloc([512], F32) for _ in range(2)]
    tB = [AR.alloc([512], F32) for _ in range(2)]
    mT = AR.alloc([8, 512], BF16)

    pT = psb(0, BF16).rearrange("p (c n) -> p c n", c=8)
    pG = [psb(1), psb(2)]
    pDs = [psb(3), psb(5)]
    pYs = [psb(4), psb(6)]
    pMs = [psb(5), psb(1)]
    pPs = [psb(6), psb(2)]
    pWs = [psb(7), psb(3)]

    def F3(G, b):
        s = b % 2
        G.op('sp', f_dma(xs3[s], x[b * 512:(b + 1) * 512, :].rearrange("(j p) f -> p j f", p=128)),
             writes=[('xs3', s, j) for j in range(4)], dma=('xs3', s))
        G.op('sp', f_dma(ot_sb[s], OT[:, b * 512:(b + 1) * 512].rearrange("(c p) t -> p c t", p=128)), writes=[('ot', s)], dma=('ot', s))
        lo = max(4 * b - 1, 0)
        hi = min(4 * b + 5, 64)
        d0 = lo - (4 * b - 1)
        G.op('sp', f_dma(us[s][:, d0:d0 + hi - lo, :], U[lo * 128:hi * 128, :].rearrange("(j p) f -> p j f", p=128)),
             writes=[('us', s)], dma=('us', s))
        for j in range(4):
            T = 4 * b + j
            G.op('act', f_act(xn3[j % 2], xs3[s][:, j, :], AF.Copy, scale=rstd1[:, T:T + 1]), reads=[('xs3', s, j)], writes=[('xn3', j % 2)])
            for c in range(8):
                G.op('pe', f_tr(pT[:, c, :], xn3[j % 2][:, c * 128:(c + 1) * 128], ident[:]), reads=[('xn3', j % 2)], writes=['pT'])
            G.op('dve', f_copy(xT3[s][:, :, j * 128:(j + 1) * 128], pT), reads=['pT'], writes=[('xT3', s, j)])

    def G3(G, b):
        s = b % 2
        xT3keys = [('xT3', s, j) for j in range(4)]
        for m in range(16):
            for c in range(8):
                G.op('pe', f_mm(pG[m % 2], wg[:, c, m * 128:(m + 1) * 128], xT3[s][:, c, :], c == 0, c == 7), reads=xT3keys, writes=[('pG', m % 2)])
            G.op('act', f_act(sig[:, m, :], pG[m % 2], AF.Sigmoid, bias=bias_g[:, m:m + 1]), reads=[('pG', m % 2)], writes=[('sig', m)])
        for g in range(4):
            for j in range(4):
                T = 4 * b + j
                parts = []
                if T > 0:
                    parts.append((j, g * 5 + 0))
                parts.append((j + 1, g * 5 + (3 if T == 0 else (4 if T == 63 else 1))))
                if T < 63:
                    parts.append((j + 2, g * 5 + 2))
                for k, (slot, bi) in enumerate(parts):
                    G.op('pe', f_mm(pDs[g % 2][:, j * 128:(j + 1) * 128], us[s][:, slot, g * 128:(g + 1) * 128], band[:, bi, :], k == 0, k == len(parts) - 1),
                         reads=[('us', s), 'band'], writes=[('pD', g % 2)])
            G.op('dve', f_copy(dT[g % 2], pDs[g % 2]), reads=[('pD', g % 2)], writes=[('dT', g % 2)])
            G.op('pe', f_mm(pYs[g % 2], poolw[:, g, :], dT[g % 2], True, True), reads=[('dT', g % 2)], writes=[('pY', g % 2)])
            G.op('act', f_copy_act(yT[:, g, :], pYs[g % 2]), reads=[('pY', g % 2)], writes=[('yT', g)])
        yTkeys = [('yT', g) for g in range(4)]
        for m in range(8):
            k2 = m % 2
            for c in range(4):
                G.op('pe', f_mm(pMs[k2], wbm[:, c, m * 128:(m + 1) * 128], ot_sb[s][:, c, :], c == 0, c == 3), reads=[('ot', s)], writes=[('pM', k2)])
            for g in range(4):
                G.op('pe', f_mm(pPs[k2], wbp[:, g, m * 128:(m + 1) * 128], yT[:, g, :], g == 0, g == 3), reads=yTkeys, writes=[('pP', k2)])
            G.op('dve', f_tt(tA[k2], pMs[k2], sig[:, m, :], ALU.mult), reads=[('pM', k2), ('sig', m)], writes=[('tA', k2)])
            G.op('dve', f_tt(tB[k2], pPs[k2], sig[:, 8 + m, :], ALU.mult), reads=[('pP', k2), ('sig', 8 + m)], writes=[('tB', k2)])
            G.op('pool', f_tt(mT[:, m, :], tA[k2], tB[k2], ALU.add), reads=[('tA', k2), ('tB', k2)], writes=[('mT', m)])
        mTkeys = [('mT', m) for m in range(8)]
        for j in range(4):
            for half in range(2):
                k2 = half
                for c in range(8):
                    G.op('pe', f_mm(pWs[k2], mT[:, c, j * 128:(j + 1) * 128], wo[:, c, half * 512:(half + 1) * 512], c == 0, c == 7), reads=mTkeys, writes=[('pW', k2)])
                G.op('dve', f_tt(xs3[s][:, j, half * 512:(half + 1) * 512], pWs[k2], xs3[s][:, j, half * 512:(half + 1) * 512], ALU.add),
                     reads=[('pW', k2), ('xs3', s, j)], writes=[('xs3', s, j)])
        G.op('pool', f_dma(X1[b * 512:(b + 1) * 512, :].rearrange("(j p) f -> p j f", p=128), xs3[s]),
             reads=[('xs3', s, j) for j in range(4)], writes=[('X1', b)], dma=('x1st', s))

    g0 = Stage(); F3(g0, 0); emit_merged([g0])
    for b in range(16):
        stages = [Stage()]
        G3(stages[0], b)
        if b + 1 < 16:
            g1 = Stage(); F3(g1, b + 1); stages.append(g1)
        emit_merged(stages)

    R.barrier()
    if stop == 3:
        R.finalize()
        return nc
    AR.release(0)

    R.bank = {'pB1': 0, 'pT': 0, ('pH', 0): 1, ('pH', 1): 2, ('pH', 2): 3, ('pY2', 0): 4, ('pY2', 1): 5}
    w1 = AR.alloc([8, 4096], BF16)
    w2 = AR.alloc([32, 1024], BF16)
    fg = AR.alloc([1024], F32)
    b1 = b1p
    R.op('sp', f_dma(fg, bigbc_d[:, 2048:3072]), writes=['fg'], dma='fg')
    R.op('sp', f_dma(w1.rearrange("p a b -> p (a b)"), WB_w1), writes=['w1'], dma='w1')
    R.op('sp', f_dma(w2.rearrange("p a b -> p (a b)"), WB_w2), writes=['w2'], dma='w2')
    R.barrier()

    x1s = [AR.alloc([2, 1024], F32) for _ in range(2)]
    xn4 = [AR.alloc([1024], BF16) for _ in range(2)]
    h2T = [AR.alloc([8, 256], BF16) for _ in range(2)]
    rr = [AR.alloc([256], F32) for _ in range(2)]
    hidT = AR.alloc([32, 256], BF16)
    x2 = [AR.alloc([1024], F32) for _ in range(2)]
    outs = x2
    junk4 = AR.alloc([1024], BF16)

    pT = psb(0, BF16).rearrange("p (c n) -> p c n", c=8)
    pH = [psb(1), psb(2), psb(3)]
    pY2 = [psb(4), psb(5)]
    outkeys = []
    def F4(G, b):
        s = b % 2
        G.op('sp', f_dma(x1s[s], X1[b * 256:(b + 1) * 256, :].rearrange("(j p) f -> p j f", p=128)),
             writes=[('x1s', s, 0), ('x1s', s, 1)], dma=('x1s', s))
        for j in range(2):
            G.op('act', f_act(junk4, x1s[s][:, j, :], AF.Square, accum_out=st(9, j)), reads=[('x1s', s, j)], writes=[('ss2', j)])
            G.op('act', f_act(st(10, j), st(9, j), AF.Sqrt, scale=1.0 / D, bias=epsb[:]), reads=[('ss2', j)], writes=[('sd2', j)])
            G.op('dve', f_recip(st(11, j), st(10, j)), reads=[('sd2', j)], writes=[('r2', j)])
            G.op('dve', f_ts(xn4[j], x1s[s][:, j, :], st(11, j), ALU.mult), reads=[('x1s', s, j), ('r2', j)], writes=[('xn4', j)])
            for c in range(8):
                G.op('pe', f_tr(pT[:, c, :], xn4[j][:, c * 128:(c + 1) * 128], ident[:]), reads=[('xn4', j)], writes=['pT'])
            G.op('act', f_copy_act(h2T[s][:, :, j * 128:(j + 1) * 128], pT), reads=['pT'], writes=[('h2T', s, j)])

    def M1(G, b):
        s = b % 2
        for m in range(32):
            for c in range(8):
                G.op('pe', f_mm(pH[m % 3][:, 0:256], w1[:, c, m * 128:(m + 1) * 128], h2T[s][:, c, :], c == 0, c == 7),
                     reads=[('h2T', s, 0), ('h2T', s, 1)], writes=[('pH', m % 3)])
            G.op('act', f_act(rr[m % 2], pH[m % 3][:, 0:256], AF.Relu, bias=b1[:, m:m + 1]), reads=[('pH', m % 3)], writes=[('rr', m % 2)])
            G.op('dve' if m % 2 == 0 else 'pool', f_tt(hidT[:, m, :], rr[m % 2], rr[m % 2], ALU.mult), reads=[('rr', m % 2)], writes=[('hidT', m)])

    def M2(G, b):
        s = b % 2
        hkeys = [('hidT', m) for m in range(32)]
        for j in range(2):
            for half in range(2):
                pk = (j * 2 + half) % 2
                for m in range(32):
                    G.op('pe', f_mm(pY2[pk], hidT[:, m, j * 128:(j + 1) * 128], w2[:, m, half * 512:(half + 1) * 512], m == 0, m == 31),
                         reads=hkeys, writes=[('pY2', pk)])
                G.op('dve', f_tt(x2[j][:, half * 512:(half + 1) * 512], pY2[pk], x1s[s][:, j, half * 512:(half + 1) * 512], ALU.add),
                     reads=[('pY2', pk), ('x1s', s, j)], writes=[('x2', j, half)])
            G.op('act', f_act(junk4, x2[j], AF.Square, accum_out=st(12, j)), reads=[('x2', j, 0), ('x2', j, 1)], writes=[('ss3', j)])
            G.op('act', f_act(st(13, j), st(12, j), AF.Sqrt, scale=1.0 / D, bias=epsb[:]), reads=[('ss3', j)], writes=[('sd3', j)])
            G.op('dve', f_recip(st(14, j), st(13, j)), reads=[('sd3', j)], writes=[('r3', j)])
            G.op('dve', f_stt(outs[j], x2[j], st(14, j), fg, ALU.mult, ALU.mult), reads=[('x2', j, 0), ('x2', j, 1), ('r3', j), 'fg'],
                 writes=[('x2', j, 0), ('x2', j, 1)])
            row = (b * 2 + j) * 128
            G.op('pool', f_dma(out[row:row + 128, :], outs[j]), reads=[('x2', j, 0), ('x2', j, 1)], writes=[('out', b, j)], dma=('outst', j))
            outkeys.append(('out', b, j))

    g0 = Stage(); F4(g0, 0); emit_merged([g0])
    for b in range(32):
        stages = [Stage()]
        M1(stages[0], b)
        if b + 1 < 32:
            g1 = Stage(); F4(g1, b + 1); stages.append(g1)
        emit_merged(stages)
        g2 = Stage(); M2(g2, b); emit_merged([g2])
    R.op('sp', None, reads=outkeys)
    R.finalize()
    return nc


def f_copy_act(out, in_):
    return lambda e: e.activation(out=out, in_=in_, func=AF.Copy)


def _host_consts():
    bf = ml_dtypes.bfloat16
    ident = np.eye(128, dtype=np.float32).astype(bf)
    t = np.arange(L)
    row = (t // 64).astype(np.float32)
    col = (t % 64).astype(np.float32)
    inv_freq = (np.float32(10000.0) ** (-np.arange(0, 16, 2, dtype=np.float32) / np.float32(16))).astype(np.float32)
    ar = (row[:, None] * inv_freq).astype(np.float32)
    ac = (col[:, None] * inv_freq).astype(np.float32)
    cr, sr, cc, sc = np.cos(ar), np.sin(ar), np.cos(ac), np.sin(ac)
    cosF = np.concatenate([cr, cr, cc, cc], axis=1).astype(np.float32)
    sinS = np.concatenate([-sr, sr, -sc, sc], axis=1).astype(np.float32)
    cosF = np.ascontiguousarray(cosF.reshape(64, 128, 32).transpose(1, 0, 2).reshape(128, 64 * 32))
    sinS = np.ascontiguousarray(sinS.reshape(64, 128, 32).transpose(1, 0, 2).reshape(128, 64 * 32))
    band = np.zeros((128, 20, 128), np.float32)
    for g, w in enumerate((2, 4, 8, 16)):
        hw = w // 2
        for kind in range(5):
            T = {0: 5, 1: 5, 2: 5, 3: 0, 4: 63}[kind]
            dT = {0: -1, 1: 0, 2: 1, 3: 0, 4: 0}[kind]
            m = np.zeros((128, 128), np.float32)
            for tl in range(128):
                tg = T * 128 + tl
                lo = max(tg - hw, 0)
                hi = min(tg + hw, L)
                cnt = hi - lo
                for tp in range(lo, hi):
                    p = tp - (T + dT) * 128
                    if 0 <= p < 128:
                        m[p, tl] += 1.0 / cnt
                p = tg - (T + dT) * 128
                if 0 <= p < 128:
                    m[p, tl] -= 1.0
            band[:, g * 5 + kind, :] = m
    band = np.ascontiguousarray(band.reshape(128, 20 * 128)).astype(bf)
    return ident, cosF, sinS, band


_CACHE = {}


def prep_inputs(x, c, ctx, c_ctx, w_ada, b_ada, norm1_g, w_in, q_norm_g, kv_norm_g, w_uq, w_ukv,
                w_br_mla, pool_w, pool_scale, w_br_pool, w_out, norm2_g, w_mlp1, w_mlp2, final_g):
    f = lambda a: np.ascontiguousarray(np.asarray(a, dtype=np.float32))
    x, c, ctx, c_ctx = f(x), f(c), f(ctx), f(c_ctx)
    w_ada, b_ada, w_in, w_uq, w_ukv = f(w_ada)[0], f(b_ada)[0], f(w_in)[0], f(w_uq)[0], f(w_ukv)[0]
    w_br_mla, pool_w, w_br_pool, w_out = f(w_br_mla)[0], f(pool_w)[0], f(w_br_pool)[0], f(w_out)[0]
    w_mlp1, w_mlp2 = f(w_mlp1)[0], f(w_mlp2)[0]
    norm1_g, norm2_g, q_norm_g, kv_norm_g, pool_scale, final_g = (f(norm1_g)[0], f(norm2_g)[0], f(q_norm_g)[0],
                                                                   f(kv_norm_g)[0], f(pool_scale)[0], f(final_g))
    if 'consts' not in _CACHE:
        _CACHE['consts'] = _host_consts()
    ident, cosF, sinS, band = _CACHE['consts']
    col = lambda v: v.reshape(-1, 128).T
    bigbc = np.ascontiguousarray(np.broadcast_to(
        np.concatenate([b_ada[2 * D:3 * D], b_ada[5 * D:6 * D], final_g])[None, :], (128, 3 * D)))
    shared = dict(w_ada=w_ada, w_in=w_in, w_uq=w_uq, w_ukv=w_ukv, w_br_mla=w_br_mla, pool_w=pool_w,
                  w_br_pool=w_br_pool, w_out=w_out, w_mlp1=w_mlp1, w_mlp2=w_mlp2, ident=ident,
                  cosF=cosF, sinS=sinS, band=band, bigbc=bigbc)
    in_maps = []
    for b in range(x.shape[0]):
        ccol = np.stack([col(c[b]), col(c_ctx)], axis=2).reshape(128, 16)
        smallf = np.ascontiguousarray(np.concatenate(
            [ccol, col(b_ada), col(norm1_g), col(norm2_g), col(q_norm_g), col(kv_norm_g), col(pool_scale)], axis=1).astype(np.float32))
        assert smallf.shape == (128, 89)
        m = dict(shared)
        m.update(x=x[b], ctx=ctx[b], smallf=smallf)
        in_maps.append(m)
    return in_maps


def kernel(**inputs):
    in_maps = prep_inputs(**inputs)
    if 'nc' not in _CACHE:
        _CACHE['nc'] = build_program()
    nc = _CACHE['nc']
    res = run_bass_kernel_spmd(nc, in_maps, core_ids=list(range(8)))
    return np.stack([np.asarray(r["out"], dtype=np.float32) for r in res.results], axis=0)
```

```python
import numpy as np
import ml_dtypes
import concourse.bass as bass
import concourse.mybir as mybir
from concourse.bass_utils import run_bass_kernel_spmd

F32 = mybir.dt.float32
BF16 = mybir.dt.bfloat16
AF = mybir.ActivationFunctionType
ALU = mybir.AluOpType

D = 1024
L = 8192
CTX = 256
NKT = (L + CTX) // 128
H = 8
EPS = 1e-6
SCALE = 96.0 ** -0.5
ENGS = ['pe', 'act', 'dve', 'pool', 'sp']


class Rec:
    def __init__(self, nc):
        self.nc = nc
        self.ops = []
        self.lastw = {}
        self.rd_eng = {}
        self.rd_dma = {}
        self.bank = {}
        self.bank_last = {}

    def op(self, eng, fn, reads=(), writes=(), dma=None):
        j = len(self.ops)
        deps = {}
        bset = set()
        for k in list(reads) + list(writes):
            if k in self.bank:
                bv = self.bank[k]
                bset.update(bv if isinstance(bv, tuple) else (bv,))
        for bnk in bset:
            la = self.bank_last.setdefault(bnk, {})
            for e2, i in la.items():
                if e2 != eng:
                    deps.setdefault(i, False)
            la[eng] = j
        for k in reads:
            i = self.lastw.get(k)
            if i is not None:
                deps[i] = True
        for k in writes:
            for i in self.rd_eng.get(k, {}).values():
                deps.setdefault(i, False)
            for i in self.rd_dma.get(k, ()):
                deps.setdefault(i, False)
            i = self.lastw.get(k)
            if i is not None:
                deps.setdefault(i, False)
        for k in reads:
            if dma is None:
                self.rd_eng.setdefault(k, {})[eng] = j
            else:
                self.rd_dma.setdefault(k, []).append(j)
        for k in writes:
            self.lastw[k] = j
            self.rd_eng[k] = {}
            self.rd_dma[k] = []
        keep = []
        for i, raw in deps.items():
            oi = self.ops[i]
            if oi['dma'] is None and dma is None and oi['eng'] == eng:
                if not raw or eng == 'pe':
                    continue
            keep.append(i)
        self.ops.append(dict(eng=eng, fn=fn, dma=dma, deps=keep, sig=False))
        return j

    def barrier(self):
        last = {}
        for idx, o in enumerate(self.ops):
            if o['fn'] is None:
                continue
            if o['dma'] is not None:
                last[('d', o['dma'])] = idx
            else:
                last[('e', o['eng'])] = idx
        deps = list(last.values())
        for e in ENGS:
            self.ops.append(dict(eng=e, fn=None, dma=None, deps=list(deps), sig=False))
        self.lastw.clear()
        self.rd_eng.clear()
        self.rd_dma.clear()
        self.bank_last.clear()

    def finalize(self):
        nc = self.nc
        ops = self.ops
        for o in ops:
            for i in o['deps']:
                ops[i]['sig'] = True
        esem = {e: nc.alloc_semaphore("s_" + e) for e in ENGS}
        dsem = {}
        ecnt = {e: 0 for e in ENGS}
        dcnt = {}
        for o in ops:
            if o['fn'] is None:
                continue
            if o['dma'] is not None:
                k = o['dma']
                if k not in dsem:
                    dsem[k] = nc.alloc_semaphore("d_%d" % len(dsem))
                    dcnt[k] = 0
                dcnt[k] += 16
                o['sem'] = dsem[k]
                o['semid'] = ('d', k)
                o['val'] = dcnt[k]
            elif o['sig']:
                ecnt[o['eng']] += 1
                o['sem'] = esem[o['eng']]
                o['semid'] = ('e', o['eng'])
                o['val'] = ecnt[o['eng']]
        streams = {e: [] for e in ENGS}
        for idx, o in enumerate(ops):
            streams[o['eng']].append(idx)

        def run(eng_name, engine):
            seen = {}
            for idx in streams[eng_name]:
                o = ops[idx]
                need = {}
                for i in o['deps']:
                    d = ops[i]
                    sid, v = d['semid'], d['val']
                    if seen.get(sid, 0) >= v:
                        continue
                    if need.get(sid, (None, 0))[1] < v:
                        need[sid] = (d['sem'], v)
                for sid, (s, v) in need.items():
                    seen[sid] = v
                    engine.wait_ge(s, v)
                if o['fn'] is None:
                    continue
                ins = o['fn'](engine)
                if o['dma'] is not None:
                    ins.then_inc(o['sem'], 16)
                elif o['sig']:
                    ins.then_inc(o['sem'], 1)

        with nc.Block() as block:
            @block.tensor
            def _(e):
                run('pe', e)

            @block.scalar
            def _(e):
                run('act', e)

            @block.vector
            def _(e):
                run('dve', e)

            @block.gpsimd
            def _(e):
                run('pool', e)

            @block.sync
            def _(e):
                run('sp', e)


class Arena:
    def __init__(self, nc, nbytes):
        self.t = nc.alloc_sbuf_tensor("arena", [128, nbytes // 2], BF16)
        self.cap = nbytes
        self.off = 0

    def mark(self):
        return self.off

    def release(self, m):
        self.off = m

    def alloc(self, shape, dtype, parts=128):
        esz = 4 if dtype == F32 else 2
        n = 1
        for s in shape:
            n *= s
        nb = n * esz
        off = (self.off + 63) // 64 * 64
        assert off + nb <= self.cap, ("arena overflow", off, nb, self.cap)
        self.off = off + nb
        ap = self.t[0:parts, off // 2:(off + nb) // 2]
        if dtype == F32:
            ap = ap.bitcast(F32)
        if len(shape) > 1:
            names = ["a%d" % i for i in range(len(shape))]
            pat = "p (" + " ".join(names) + ") -> p " + " ".join(names)
            ap = ap.rearrange(pat, **{nm: s for nm, s in zip(names[1:], shape[1:])})
        return ap


def f_dma(out, in_):
    return lambda e: e.dma_start(out=out, in_=in_)


def f_mm(out, lhsT, rhs, start, stop):
    return lambda e: e.matmul(out, lhsT=lhsT, rhs=rhs, start=start, stop=stop)


def f_tr(out, in_, ident):
    return lambda e: e.transpose(out=out, in_=in_, identity=ident)


def f_act(out, in_, func, **kw):
    return lambda e: e.activation(out=out, in_=in_, func=func, **kw)


def f_tt(out, in0, in1, op):
    return lambda e: e.tensor_tensor(out=out, in0=in0, in1=in1, op=op)


def f_ts(out, in0, s1, op0, s2=None, op1=None):
    if op1 is None:
        return lambda e: e.tensor_scalar(out=out, in0=in0, scalar1=s1, scalar2=None, op0=op0)
    return lambda e: e.tensor_scalar(out=out, in0=in0, scalar1=s1, scalar2=s2, op0=op0, op1=op1)


def f_stt(out, in0, scalar, in1, op0, op1):
    return lambda e: e.scalar_tensor_tensor(out=out, in0=in0, scalar=scalar, in1=in1, op0=op0, op1=op1)


def f_copy(out, in_):
    return lambda e: e.tensor_copy(out=out, in_=in_)


def f_recip(out, in_):
    return lambda e: e.reciprocal(out=out, in_=in_)


def f_memset(out, v):
    return lambda e: e.memset(out, v)


def cast_scaled(R, k, out, in_, scal, reads, writes):
    eng = ('dve', 'act', 'pool')[k % 3]
    if scal is None:
        if eng == 'act':
            R.op('act', f_act(out, in_, AF.Copy), reads, writes)
        else:
            R.op(eng, f_copy(out, in_), reads, writes)
    elif eng == 'act':
        R.op('act', f_act(out, in_, AF.Copy, scale=scal), reads, writes)
    elif eng == 'dve':
        R.op('dve', f_ts(out, in_, scal, ALU.mult), reads, writes)
    else:
        R.op('pool', f_ts(out, in_, scal, ALU.mult, 1.0, ALU.mult), reads, writes)


def build_program(stop=9, dbg=False):
    nc = bass.Bass("TRN2", target_bir_lowering=False)
    skind = "ExternalOutput" if dbg else "Internal"

    def din(name, shape, dt=F32):
        return nc.dram_tensor(name, shape, dt, kind="ExternalInput").ap()

    x = din("x", [L, D])
    ctx = din("ctx", [CTX, D])
    smallf = din("smallf", [128, 89])
    bigbc_d = din("bigbc", [128, 3072])
    w_ada = din("w_ada", [D, 6 * D])
    w_in = din("w_in", [D, 3232])
    w_uq = din("w_uq", [384, 768])
    w_ukv = din("w_ukv", [256, 1024])
    w_br_mla = din("w_br_mla", [512, 1024])
    pool_w = din("pool_w", [4, 128, 128])
    w_br_pool = din("w_br_pool", [512, 1024])
    w_out = din("w_out", [D, D])
    w_mlp1 = din("w_mlp1", [D, 4096])
    w_mlp2 = din("w_mlp2", [4096, D])
    ident_d = din("ident", [128, 128], BF16)
    cosF_d = din("cosF", [128, 64 * 32])
    sinS_d = din("sinS", [128, 64 * 32])
    band_d = din("band", [128, 20 * 128], BF16)
    out = nc.dram_tensor("out", [L, D], F32, kind="ExternalOutput").ap()

    QT = nc.dram_tensor("QT", [H, 96, L], BF16, kind=skind).ap()
    KT = nc.dram_tensor("KT", [H, 96, NKT * 128], BF16, kind=skind).ap()
    Vs = nc.dram_tensor("Vs", [2, 128, NKT, 4, 66], BF16, kind=skind).ap()
    OT = nc.dram_tensor("OT", [512, L], BF16, kind=skind).ap()
    U = nc.dram_tensor("U", [L, 512], BF16, kind=skind).ap()
    X1 = nc.dram_tensor("X1", [L, D], F32, kind=skind).ap()
    WB_wg = nc.dram_tensor("WB_wg", [128, 8 * 2048], BF16, kind="Internal").ap()
    WB_wbm = nc.dram_tensor("WB_wbm", [128, 4 * 1024], BF16, kind="Internal").ap()
    WB_wbp = nc.dram_tensor("WB_wbp", [128, 4 * 1024], BF16, kind="Internal").ap()
    WB_pw = nc.dram_tensor("WB_pw", [128, 4 * 128], BF16, kind="Internal").ap()
    WB_wo = nc.dram_tensor("WB_wo", [128, 8 * 1024], BF16, kind="Internal").ap()
    WB_w1 = nc.dram_tensor("WB_w1", [128, 8 * 4096], BF16, kind="Internal").ap()
    WB_w2 = nc.dram_tensor("WB_w2", [128, 32 * 1024], BF16, kind="Internal").ap()

    R = Rec(nc)

    def gt(name, shape, dt=F32):
        return nc.alloc_sbuf_tensor(name, shape, dt)

    sm = gt("sm", [128, 89])
    ident = gt("ident_s", [128, 128], BF16)
    ones_f = gt("ones_f", [128, 128])
    epsb = gt("epsb", [128, 1])
    sil = gt("sil", [128, 8, 2])
    vcol = gt("vcol", [128, 4, 8, 2])
    gtbc = gt("gtbc", [128, 2, 1024])
    A1 = gt("A1", [128, 8])
    cA1 = gt("cA1", [128, 8])
    A2 = gt("A2", [128, 8])
    rstd1 = gt("rstd1", [128, 64])
    stat = gt("stat", [128, 64])
    bias_g = gt("bias_g", [128, 16])
    b1p = gt("b1p", [128, 32])
    AR = Arena(nc, 196 * 1024)

    PSall = nc.alloc_psum_tensor("psall", [128, 8, 512], F32)

    def psb(i, dt=F32):
        ap = PSall[:, i, :]
        if dt == BF16:
            ap = ap.bitcast(BF16)
        return ap

    R.bank = {('pV', 0): 0, ('pV', 1): 1, ('pR', 0): 2, ('pR', 1): 3, 'pBias0': 4, 'pBias1': 5}
    R.op('sp', f_dma(sm[:], smallf), writes=['sm'], dma='sm')
    R.op('sp', f_dma(ident[:], ident_d), writes=['ident'], dma='ident')
    R.op('dve', f_memset(ones_f[:], 1.0), writes=['ones'])
    R.op('dve', f_memset(epsb[:], EPS), writes=['epsb'])
    R.op('act', f_act(sil[:].rearrange("p c v -> p (c v)"), sm[:, 0:16], AF.Silu), reads=['sm'], writes=['sil'])

    w1x = AR.alloc([8, 1184], BF16)
    w1c = AR.alloc([8, 288], BF16)
    wuq = AR.alloc([3, 768], BF16)
    wukv = AR.alloc([2, 1024], BF16)
    bias1x = AR.alloc([1184], F32)
    bias1c = AR.alloc([288], F32)
    cosF = AR.alloc([64, 32], F32)
    sinS = AR.alloc([64, 32], F32)
    m_p1 = AR.mark()
    stgh = {'b': [AR.alloc([8192], F32) for _ in range(2)]}
    bigbc = AR.alloc([2048], F32)
    R.op('sp', f_dma(bigbc, bigbc_d[:, 0:2048]), writes=['bigbc'], dma='bigbc')
    sil_rep = AR.alloc([8, 128], F32)
    sh1_rep = AR.alloc([8, 128], F32)
    csh1_rep = AR.alloc([8, 128], F32)

    R.op('sp', f_dma(cosF.rearrange("p a b -> p (a b)"), cosF_d), writes=['cosF'], dma='cosF')
    R.op('sp', f_dma(sinS.rearrange("p a b -> p (a b)"), sinS_d), writes=['sinS'], dma='sinS')

    for c in range(8):
        R.op('dve', f_ts(sil_rep[:, c, :], ones_f[:], sil[:, c, 0:1], ALU.mult),
             reads=['ones', 'sil'], writes=[('silrep', c)])

    stg_n = [0]

    def stage_load(src_ap, kc, ncols):
        i = stg_n[0] % 2
        stg_n[0] += 1
        view = stgh['b'][i][:, 0:kc * ncols].rearrange("p (c n) -> p c n", c=kc)
        R.op('sp', f_dma(view, src_ap), writes=[('stg', i)], dma=('stg', i))
        return view, ('stg', i)

    pV = [psb(0)[:, 0:16].rearrange("p (m v) -> p m v", v=2), psb(1)[:, 0:16].rearrange("p (m v) -> p m v", v=2)]
    pR = [psb(2), psb(3)]
    vmap = {0: 0, 1: 1, 3: 2, 4: 3}
    for v in range(6):
        view, skey = stage_load(w_ada[:, v * 1024:(v + 1) * 1024].rearrange("(c p) n -> p c n", p=128), 8, 1024)
        if v in vmap:
            vi = vmap[v]
            pk = ('pV', vi % 2)
            for m in range(8):
                for c in range(8):
                    R.op('pe', f_mm(pV[vi % 2][:, m, :], view[:, c, m * 128:(m + 1) * 128], sil[:, c, :], c == 0, c == 7),
                         reads=[skey, 'sil'], writes=[pk])
            R.op('dve', f_tt(vcol[:, vi], pV[vi % 2],
                             sm[:, 16 + v * 8:16 + (v + 1) * 8].unsqueeze(2).broadcast_to([128, 8, 2]), ALU.add),
                 reads=[pk, 'sm'], writes=[('vcol', vi)])
        else:
            gi = 0 if v == 2 else 1
            for half in range(2):
                pk = ('pR', half)
                for c in range(8):
                    R.op('pe', f_mm(pR[half], sil_rep[:, c, :], view[:, c, half * 512:(half + 1) * 512], c == 0, c == 7),
                         reads=[skey, ('silrep', c)], writes=[pk])
                R.op('dve', f_tt(gtbc[:, gi, half * 512:(half + 1) * 512], pR[half],
                                 bigbc[:, gi * 1024 + half * 512:gi * 1024 + (half + 1) * 512], ALU.add),
                     reads=[pk, 'bigbc'], writes=[('gtbc', gi)])

    R.op('dve', f_stt(A1[:], vcol[:, 1, :, 0], 1.0, sm[:, 64:72], ALU.add, ALU.mult), reads=[('vcol', 1), 'sm'], writes=['A1'])
    R.op('dve', f_stt(cA1[:], vcol[:, 1, :, 1], 1.0, sm[:, 64:72], ALU.add, ALU.mult), reads=[('vcol', 1), 'sm'], writes=['cA1'])
    R.op('dve', f_stt(A2[:], vcol[:, 3, :, 0], 1.0, sm[:, 72:80], ALU.add, ALU.mult), reads=[('vcol', 3), 'sm'], writes=['A2'])
    for c in range(8):
        R.op('dve', f_ts(sh1_rep[:, c, :], ones_f[:], vcol[:, 0, c, 0:1], ALU.mult), reads=['ones', ('vcol', 0)], writes=[('sh1rep', c)])
        R.op('dve', f_ts(csh1_rep[:, c, :], ones_f[:], vcol[:, 0, c, 1:2], ALU.mult), reads=['ones', ('vcol', 0)], writes=[('csh1rep', c)])

    kk = 0
    pBias = [psb(4), psb(5)]
    for pi, (n0, n1) in enumerate([(0, 384), (384, 672), (672, 1184)]):
        n = n1 - n0
        view, skey = stage_load(w_in[:, n0:n1].rearrange("(c p) n -> p c n", p=128), 8, n)
        for c in range(8):
            cast_scaled(R, kk, w1x[:, c, n0:n1], view[:, c, :], A1[:, c:c + 1], [skey, 'A1'], [('w1x', pi, c)])
            kk += 1
        for c in range(8):
            R.op('pe', f_mm(pBias[0][:, 0:n], sh1_rep[:, c, :], view[:, c, :], c == 0, c == 7),
                 reads=[skey, ('sh1rep', c)], writes=['pBias0'])
        R.op('dve', f_copy(bias1x[:, n0:n1], pBias[0][:, 0:n]), reads=['pBias0'], writes=[('bias1x', pi)])
        if pi == 1:
            for c in range(8):
                cast_scaled(R, kk, w1c[:, c, :], view[:, c, :], cA1[:, c:c + 1], [skey, 'cA1'], [('w1c', c)])
                kk += 1
            for c in range(8):
                R.op('pe', f_mm(pBias[1][:, 0:n], csh1_rep[:, c, :], view[:, c, :], c == 0, c == 7),
                     reads=[skey, ('csh1rep', c)], writes=['pBias1'])
            R.op('dve', f_copy(bias1c[:], pBias[1][:, 0:n]), reads=['pBias1'], writes=['bias1c'])
    view, skey = stage_load(w_uq.rearrange("(c p) n -> p c n", p=128), 3, 768)
    for c in range(3):
        cast_scaled(R, kk, wuq[:, c, :], view[:, c, :], sm[:, 80 + c:81 + c], [skey, 'sm'], [('wuq', c)])
        kk += 1
    view, skey = stage_load(w_ukv.rearrange("(c p) n -> p c n", p=128), 2, 1024)
    for c in range(2):
        cast_scaled(R, kk, wukv[:, c, :], view[:, c, :], sm[:, 83 + c:84 + c], [skey, 'sm'], [('wukv', c)])
        kk += 1

    if dbg:
        dbg0 = nc.dram_tensor("dbg0", [128, 4 * 16 + 2048 + 24 + 1184 + 288], F32, kind="ExternalOutput").ap()
        R.op('sp', f_dma(dbg0[:, 0:64], vcol[:].rearrange("p a b c -> p (a b c)")), reads=[('vcol', i) for i in range(4)], writes=['dbg0a'], dma='dbg0a')
        R.op('sp', f_dma(dbg0[:, 64:2112], gtbc[:].rearrange("p a b -> p (a b)")), reads=[('gtbc', 0), ('gtbc', 1)], writes=['dbg0b'], dma='dbg0b')
        R.op('sp', f_dma(dbg0[:, 2112:2120], A1[:]), reads=['A1'], writes=['dbg0c'], dma='dbg0c')
        R.op('sp', f_dma(dbg0[:, 2120:2128], cA1[:]), reads=['cA1'], writes=['dbg0d'], dma='dbg0d')
        R.op('sp', f_dma(dbg0[:, 2128:2136], A2[:]), reads=['A2'], writes=['dbg0e'], dma='dbg0e')
        R.op('sp', f_dma(dbg0[:, 2136:2136 + 1184], bias1x), reads=[('bias1x', i) for i in range(3)], writes=['dbg0f'], dma='dbg0f')
        R.op('sp', f_dma(dbg0[:, 2136 + 1184:2136 + 1184 + 288], bias1c), reads=['bias1c'], writes=['dbg0g'], dma='dbg0g')
        dbg1 = nc.dram_tensor("dbg1", [128, 8 * 1184], BF16, kind="ExternalOutput").ap()
        R.op('sp', f_dma(dbg1, w1x.rearrange("p a b -> p (a b)")), reads=[('w1x', pi, c) for pi in range(3) for c in range(8)], writes=['dbg1'], dma='dbg1')
    R.barrier()
    if stop == 0:
        R.finalize()
        return nc
    AR.release(m_p1)

    R.bank = {'pT': 0, 'psA': 1, 'psB': 2, 'psC': 3, 'pT2q': 4, 'pT2k': 4, 'psWk': 5, 'psWq': 6, 'pXTk': 7, 'pXTq': 7}
    NX, NB = 4, 4
    xs = [AR.alloc([1024], F32) for _ in range(NX)]
    junk = [AR.alloc([1024], BF16) for _ in range(3)]
    xn = [AR.alloc([1024], BF16) for _ in range(NB)]
    xT = [AR.alloc([8, 128], BF16) for _ in range(NB)]
    u_sb = [AR.alloc([512], BF16) for _ in range(NB)]
    cqn = [AR.alloc([384], BF16) for _ in range(NB)]
    cqT = [AR.alloc([3, 128], BF16) for _ in range(NB)]
    ckvn = [AR.alloc([256], BF16) for _ in range(NB)]
    ckvT = [AR.alloc([2, 128], BF16) for _ in range(NB)]
    krs_sb = [AR.alloc([32], F32) for _ in range(NB)]
    t1 = [AR.alloc([8, 32], F32) for _ in range(NB)]
    t2 = [AR.alloc([8, 32], F32) for _ in range(NB)]
    qf = [AR.alloc([8, 96], BF16) for _ in range(NB)]
    kf = [AR.alloc([8, 96], BF16) for _ in range(NB)]
    vaug = [AR.alloc([8, 66], BF16) for _ in range(NB)]
    qT_sb = [AR.alloc([8, 128], BF16) for _ in range(NB)]
    kT_sb = [AR.alloc([8, 128], BF16) for _ in range(NB)]
    t1k = [AR.alloc([32], F32) for _ in range(NB)]
    t2k = [AR.alloc([32], F32) for _ in range(NB)]
    kr = [AR.alloc([32], F32) for _ in range(NB)]
    bhi = AR.alloc([1184], BF16)
    blo = AR.alloc([1184], BF16)
    btmp = AR.alloc([1184], F32)
    bchi = AR.alloc([288], BF16)
    bclo = AR.alloc([288], BF16)
    ones_b = AR.alloc([128], BF16)

    R.op('dve', f_memset(ones_b[0:1], 1.0), writes=['ones_b'])
    for (hi_, lo_, src_, n_, kn) in [(bhi, blo, bias1x, 1184, 'x'), (bchi, bclo, bias1c, 288, 'c')]:
        R.op('dve', f_copy(hi_[0:1], src_[0:1]), writes=[('bhi', kn)])
        R.op('dve', f_tt(btmp[0:1, 0:n_], src_[0:1], hi_[0:1], ALU.subtract), reads=[('bhi', kn)], writes=[('btmp', kn)])
        R.op('dve', f_copy(lo_[0:1], btmp[0:1, 0:n_]), reads=[('btmp', kn)], writes=[('blo', kn)])
    for s in range(NB):
        R.op('dve', f_memset(vaug[s], 1.0), writes=[('vaug', s)])

    pT = psb(0, BF16).rearrange("p (c n) -> p c n", c=8)
    psA, psB, psC = psb(1), psb(2), psb(3)
    pT2 = psb(4, BF16).rearrange("p (c n) -> p c n", c=8)
    psWk = psb(5).rearrange("p (h d) -> p h d", d=128)
    psWq = psb(6)[:, 0:384].rearrange("p (h d) -> p h d", d=96)
    pXT = psb(7, BF16).rearrange("p (c n) -> p c n", c=8)
    QTv = QT.rearrange("h d t -> d h t")
    KTv = KT.rearrange("h d t -> d h t")

    def st(col, s):
        return stat[:, col * 4 + s:col * 4 + s + 1]

    class Stage:
        def __init__(self):
            self.l = []

        def op(self, eng, fn, reads=(), writes=(), dma=None):
            self.l.append((eng, fn, reads, writes, dma))

    def emit_merged(stages, group_pe=False):
        items = []
        for si, sg in enumerate(stages):
            n = len(sg.l)
            k = 0
            while k < n:
                k1 = k + 1
                if group_pe and sg.l[k][0] == 'pe':
                    while k1 < n and sg.l[k1][0] == 'pe':
                        k1 += 1
                items.append(((k + 0.5) / n, si, k, sg.l[k:k1]))
                k = k1
        items.sort(key=lambda z: (z[0], z[1]))
        for _, _, _, grp in items:
            for o in grp:
                R.op(o[0], o[1], reads=o[2], writes=o[3], dma=o[4])

    def tile_src(t):
        return ctx[t * 128:(t + 1) * 128, :] if t < 2 else x[(t - 2) * 128:(t - 1) * 128, :]

    def SL(G, t):
        G.op('sp', f_dma(xs[t % NX], tile_src(t)), writes=[('xs', t % NX)], dma=('xs', t % NX))

    def S1(G, t):
        is_ctx = t < 2
        xi = t - 2
        sx, s, s4_ = t % NX, t % NB, t % 4
        G.op('act', f_act(junk[0], xs[sx], AF.Square, accum_out=st(0, s4_)), reads=[('xs', sx)], writes=[('ss', s4_)])
        G.op('act', f_act(st(1, s4_), st(0, s4_), AF.Sqrt, scale=1.0 / D, bias=epsb[:]), reads=[('ss', s4_)], writes=[('sd', s4_)])
        if is_ctx:
            rst, rkey = st(2, s4_), ('rstc', s4_)
        else:
            rst, rkey = rstd1[:, xi:xi + 1], ('rstd1', xi)
        G.op('dve', f_recip(rst, st(1, s4_)), reads=[('sd', s4_)], writes=[rkey])
        G.op('dve', f_ts(xn[s], xs[sx], rst, ALU.mult), reads=[('xs', sx), rkey], writes=[('xn', s)])
        for c in range(8):
            G.op('pe', f_tr(pT[:, c, :], xn[s][:, c * 128:(c + 1) * 128], ident[:]), reads=[('xn', s)], writes=['pT'])
        G.op('act', f_copy_act(xT[s].rearrange("p c n -> p (c n)"), psb(0, BF16)), reads=['pT'], writes=[('xT', s)])

    def S1b(G, t):
        is_ctx = t < 2
        xi = t - 2
        sx, s, s4_ = t % NX, t % NB, t % 4
        if not is_ctx:
            groups = [(psA, 'psA', 0, 384), (psB, 'psB', 384, 672), (psC, 'psC', 672, 1184)]
            for (ps, key, n0, n1) in groups:
                for c in range(8):
                    G.op('pe', f_mm(ps[:, 0:n1 - n0], xT[s][:, c, :], w1x[:, c, n0:n1], c == 0, False), reads=[('xT', s)], writes=[key])
                G.op('pe', f_mm(ps[:, 0:n1 - n0], ones_b[0:1, :], bhi[0:1, n0:n1], False, False), writes=[key])
                G.op('pe', f_mm(ps[:, 0:n1 - n0], ones_b[0:1, :], blo[0:1, n0:n1], False, True), writes=[key])
        else:
            for c in range(8):
                G.op('pe', f_mm(psB[:, 0:288], xT[s][:, c, :], w1c[:, c, :], c == 0, False), reads=[('xT', s)], writes=['psB'])
            G.op('pe', f_mm(psB[:, 0:288], ones_b[0:1, :], bchi[0:1, :], False, False), writes=['psB'])
            G.op('pe', f_mm(psB[:, 0:288], ones_b[0:1, :], bclo[0:1, :], False, True), writes=['psB'])
        G.op('act', f_act(junk[0][:, 0:256], psB[:, 0:256], AF.Square, accum_out=st(3, s4_)), reads=['psB'], writes=[('sskv', s4_)])
        G.op('act', f_act(st(4, s4_), st(3, s4_), AF.Sqrt, scale=1.0 / 256, bias=epsb[:]), reads=[('sskv', s4_)], writes=[('sdkv', s4_)])
        G.op('dve', f_recip(st(5, s4_), st(4, s4_)), reads=[('sdkv', s4_)], writes=[('rkv', s4_)])
        G.op('dve', f_ts(ckvn[s], psB[:, 0:256], st(5, s4_), ALU.mult), reads=['psB', ('rkv', s4_)], writes=[('ckvn', s)])
        G.op('dve', f_copy(krs_sb[s], psB[:, 256:288]), reads=['psB'], writes=[('krs', s)])
        if not is_ctx:
            G.op('act', f_act(junk[0][:, 0:384], psA[:, 0:384], AF.Square, accum_out=st(6, s4_)), reads=['psA'], writes=[('ssq', s4_)])
            G.op('act', f_act(st(7, s4_), st(6, s4_), AF.Sqrt, scale=1.0 / 384, bias=epsb[:]), reads=[('ssq', s4_)], writes=[('sdq', s4_)])
            G.op('dve', f_recip(st(8, s4_), st(7, s4_)), reads=[('sdq', s4_)], writes=[('rq', s4_)])
            G.op('dve', f_ts(cqn[s], psA[:, 0:384], st(8, s4_), ALU.mult), reads=['psA', ('rq', s4_)], writes=[('cqn', s)])
            G.op('act', f_copy_act(u_sb[s], psC[:, 0:512]), reads=['psC'], writes=[('u', s)])
            G.op('pool', f_dma(U[xi * 128:(xi + 1) * 128, :], u_sb[s]), reads=[('u', s)], writes=[('U', xi)], dma=('ust', s))

    def S2(G, t):
        is_ctx = t < 2
        xi = t - 2
        s = t % NB
        for c in range(2):
            G.op('pe', f_tr(pT2[:, 3 + c, :], ckvn[s][:, c * 128:(c + 1) * 128], ident[:]), reads=[('ckvn', s)], writes=['pT2k'])
        G.op('dve', f_copy(ckvT[s], pT2[:, 3:5, :]), reads=['pT2k'], writes=[('ckvT', s)])
        if is_ctx:
            krsrc, krkey = krs_sb[s], ('krs', s)
        else:
            cos_t = cosF[:, xi, :]
            sin_t = sinS[:, xi, :]
            G.op('pool', f_tt(t1k[s], krs_sb[s], cos_t, ALU.mult), reads=[('krs', s)], writes=[('t1k', s)])
            kv4 = krs_sb[s].rearrange("p (a f i) -> p a f i", a=2, f=2)
            o4 = t2k[s].rearrange("p (a f i) -> p a f i", a=2, f=2)
            s4 = sin_t.rearrange("p (a f i) -> p a f i", a=2, f=2)
            for f in range(2):
                G.op('pool', f_tt(o4[:, :, f, :], kv4[:, :, 1 - f, :], s4[:, :, f, :], ALU.mult), reads=[('krs', s)], writes=[('t2k', s, f)])
            G.op('pool', f_tt(kr[s], t1k[s], t2k[s], ALU.add), reads=[('t1k', s), ('t2k', s, 0), ('t2k', s, 1)], writes=[('kr', s)])
            krsrc, krkey = kr[s], ('kr', s)
        G.op('pool', f_copy(kf[s][:, :, 64:96], krsrc.unsqueeze(1).broadcast_to([128, 8, 32])), reads=[krkey], writes=[('kfr', s)])
        for hb in range(2):
            hs = slice(hb * 4, (hb + 1) * 4)
            for c in range(2):
                G.op('pe', f_mm(psb(5), ckvT[s][:, c, :], wukv[:, c, hb * 512:(hb + 1) * 512], c == 0, c == 1), reads=[('ckvT', s)], writes=['psWk'])
            G.op('act', f_copy_act(kf[s][:, hs, 0:64], psWk[:, :, 0:64]), reads=['psWk'], writes=[('kfn', s, hb)])
            G.op('dve', f_copy(vaug[s][:, hs, 0:64], psWk[:, :, 64:128]), reads=['psWk'], writes=[('vaug', s, hb)])
        for hb in range(2):
            for h4 in range(4):
                h = hb * 4 + h4
                G.op('pe', f_tr(pXT[0:96, h4, :], kf[s][:, h, :], ident[:]), reads=[('kfr', s), ('kfn', s, hb)], writes=['pXTk'])
            G.op('dve' if hb == 0 else 'act', (f_copy if hb == 0 else f_copy_act)(kT_sb[s][0:96, hb * 4:(hb + 1) * 4, :], pXT[0:96, 0:4, :]),
                 reads=['pXTk'], writes=[('kT', s, hb)])
        G.op('sp', f_dma(KTv[:, :, t * 128:(t + 1) * 128], kT_sb[s][0:96]), reads=[('kT', s, 0), ('kT', s, 1)], writes=[('KT', t)], dma=('ktst', s))
        G.op('pool', f_dma(Vs[:, :, t].rearrange("g p hh e -> p g hh e"), vaug[s].rearrange("p (g hh) e -> p g hh e", g=2)),
             reads=[('vaug', s, 0), ('vaug', s, 1)], writes=[('Vs', t)], dma=('vst', s))

    def S3(G, t):
        xi = t - 2
        s = t % NB
        cos_t = cosF[:, xi, :]
        sin_t = sinS[:, xi, :]
        s4 = sin_t.rearrange("p (a f i) -> p a f i", a=2, f=2)
        for c in range(3):
            G.op('pe', f_tr(pT2[:, c, :], cqn[s][:, c * 128:(c + 1) * 128], ident[:]), reads=[('cqn', s)], writes=['pT2q'])
        G.op('dve', f_copy(cqT[s], pT2[:, 0:3, :]), reads=['pT2q'], writes=[('cqT', s)])
        for hb in range(2):
            hs = slice(hb * 4, (hb + 1) * 4)
            for c in range(3):
                G.op('pe', f_mm(psb(6)[:, 0:384], cqT[s][:, c, :], wuq[:, c, hb * 384:(hb + 1) * 384], c == 0, c == 2),
                     reads=[('cqT', s)], writes=['psWq'])
            qr = psWq[:, :, 64:96]
            G.op('dve', f_tt(t1[s][:, hs, :], qr, cos_t.unsqueeze(1).broadcast_to([128, 4, 32]), ALU.mult), reads=['psWq'], writes=[('t1', s, hb)])
            q5 = qr.rearrange("p h (a f i) -> p h a f i", a=2, f=2)
            o5 = t2[s][:, hs, :].rearrange("p h (a f i) -> p h a f i", a=2, f=2)
            for f in range(2):
                G.op('dve', f_tt(o5[:, :, :, f, :], q5[:, :, :, 1 - f, :], s4[:, :, f, :].unsqueeze(1).broadcast_to([128, 4, 2, 8]), ALU.mult),
                     reads=['psWq'], writes=[('t2', s, hb, f)])
            G.op('act', f_copy_act(qf[s][:, hs, 0:64], psWq[:, :, 0:64]), reads=['psWq'], writes=[('qfn', s, hb)])
        G.op('pool', f_tt(qf[s][:, :, 64:96], t1[s], t2[s], ALU.add),
             reads=[('t1', s, 0), ('t1', s, 1)] + [('t2', s, hb, f) for hb in range(2) for f in range(2)], writes=[('qfr', s)])
        for hb in range(2):
            for h4 in range(4):
                h = hb * 4 + h4
                G.op('pe', f_tr(pXT[0:96, 4 + h4, :], qf[s][:, h, :], ident[:]), reads=[('qfr', s), ('qfn', s, hb)], writes=['pXTq'])
            G.op('dve', f_copy(qT_sb[s][0:96, hb * 4:(hb + 1) * 4, :], pXT[0:96, 4:8, :]), reads=['pXTq'], writes=[('qT', s, hb)])
        G.op('sp', f_dma(QTv[:, :, xi * 128:(xi + 1) * 128], qT_sb[s][0:96]), reads=[('qT', s, 0), ('qT', s, 1)], writes=[('QT', xi)], dma=('qtst', s))

    import os
    _nt = int(os.environ.get('P1_TILES', NKT))
    G0 = Stage()
    SL(G0, 0)
    if _nt > 1:
        SL(G0, 1)
    emit_merged([G0])
    for i in range(_nt + 3):
        stages = []
        if i + 2 < _nt:
            g = Stage(); SL(g, i + 2); stages.append(g)
        if i < _nt:
            g = Stage(); S1(g, i); stages.append(g)
        if 0 <= i - 1 < _nt:
            g = Stage(); S1b(g, i - 1); stages.append(g)
        if 0 <= i - 2 < _nt:
            g = Stage(); S2(g, i - 2); stages.append(g)
        if 2 <= i - 3 < _nt:
            g = Stage(); S3(g, i - 3); stages.append(g)
        emit_merged(stages)

    R.barrier()
    if stop == 1:
        R.finalize()
        return nc
    AR.release(0)

    R.bank = {('pS', 0): 0, ('pS', 1): 1, ('pS', 2): 2, ('pS', 3): 3, ('pO', 0): 4, ('pO', 1): 5, 'pB': 6, 'pBg': 7, 'pB1': 7}
    kt_sb = AR.alloc([4, NKT * 128], BF16)
    v_sb = AR.alloc([NKT, 4, 66], BF16)
    qt_sb = [AR.alloc([4, 512], BF16) for _ in range(2)]
    NS, NP, LOOK = 4, 4, 2
    PT = [AR.alloc([512], BF16) for _ in range(NP)]
    rden = [AR.alloc([512], F32) for _ in range(2)]
    bcs = [AR.alloc([512], F32) for _ in range(2)]
    OTn = [AR.alloc([512], BF16) for _ in range(2)]
    pstg = [AR.alloc([4096], F32) for _ in range(2)]
    pcast = [AR.alloc([4096], BF16) for _ in range(2)]
    pS = [psb(i) for i in range(NS)]
    pO = [psb(4), psb(5)]
    pB = psb(6)
    pBg = psb(7)[:, 0:32].rearrange("p (m v) -> p m v", v=2)
    pB1 = psb(7)[:, 32:96].rearrange("p (m v) -> p m v", v=2)

    PREP = Stage()
    pcnt = [0]

    def prep_piece(src_ap, kc, ncols, dst_ap, row_scale=None, col_scale=None, bias=None):
        i = pcnt[0] % 2
        pcnt[0] += 1
        view = pstg[i][:, 0:kc * ncols].rearrange("p (c n) -> p c n", c=kc)
        cview = pcast[i][:, 0:kc * ncols].rearrange("p (c n) -> p c n", c=kc)
        PREP.op('sp', f_dma(view, src_ap), writes=[('pstg', i)], dma=('pstg', i))
        for c in range(kc):
            eng = 'dve' if c % 2 == 0 else 'pool'
            if col_scale is not None:
                fn = f_tt(cview[:, c, :], view[:, c, :], col_scale, ALU.mult)
            elif row_scale is not None:
                fn = f_ts(cview[:, c, :], view[:, c, :], row_scale(c), ALU.mult, 1.0, ALU.mult)
            else:
                fn = f_copy(cview[:, c, :], view[:, c, :])
            PREP.op(eng, fn, reads=[('pstg', i)], writes=[('pcast', i, c)])
        if bias is not None:
            ptile, m0, rhs_of, key = bias
            for m in range(ncols // 128):
                for c in range(kc):
                    PREP.op('pe', f_mm(ptile[:, m0 + m, :], view[:, c, m * 128:(m + 1) * 128], rhs_of(c), c == 0, c == kc - 1),
                            reads=[('pstg', i)], writes=[key])
        PREP.op('pool', f_dma(dst_ap, cview), reads=[('pcast', i, c) for c in range(kc)], writes=[('WB', pcnt[0])], dma=('pcst', i))

    WBg3 = WB_wg.rearrange("p (c n) -> p c n", c=8)
    for pi in range(4):
        n0 = 1184 + pi * 512
        prep_piece(w_in[:, n0:n0 + 512].rearrange("(c p) n -> p c n", p=128), 8, 512, WBg3[:, :, pi * 512:(pi + 1) * 512],
                   row_scale=lambda c: A1[:, c:c + 1], bias=(pBg, pi * 4, lambda c: vcol[:, 0, c, :], 'pBg'))
    PREP.op('dve', f_copy(bias_g[:], pBg[:, :, 0]), reads=['pBg'], writes=['bias_g'])
    prep_piece(w_br_mla.rearrange("(c p) n -> p c n", p=128), 4, 1024, WB_wbm.rearrange("p (c n) -> p c n", c=4))
    prep_piece(w_br_pool.rearrange("(c p) n -> p c n", p=128), 4, 1024, WB_wbp.rearrange("p (c n) -> p c n", c=4),
               row_scale=lambda c: sm[:, 85 + c:86 + c])
    prep_piece(pool_w.rearrange("g c d -> c g d"), 4, 128, WB_pw.rearrange("p (c n) -> p c n", c=4))
    WBo3 = WB_wo.rearrange("p (c n) -> p c n", c=8)
    for pi in range(2):
        prep_piece(w_out[:, pi * 512:(pi + 1) * 512].rearrange("(c p) n -> p c n", p=128), 8, 512, WBo3[:, :, pi * 512:(pi + 1) * 512],
                   col_scale=gtbc[:, 0, pi * 512:(pi + 1) * 512])
    WB13 = WB_w1.rearrange("p (c n) -> p c n", c=8)
    for pi in range(8):
        prep_piece(w_mlp1[:, pi * 512:(pi + 1) * 512].rearrange("(c p) n -> p c n", p=128), 8, 512, WB13[:, :, pi * 512:(pi + 1) * 512],
                   row_scale=lambda c: A2[:, c:c + 1], bias=(pB1, pi * 4, lambda c: vcol[:, 2, c, :], 'pB1'))
    PREP.op('dve', f_copy(b1p[:], pB1[:, :, 0]), reads=['pB1'], writes=['b1p'])
    WB23 = WB_w2.rearrange("p (c n) -> p c n", c=32)
    for pi in range(8):
        prep_piece(w_mlp2[pi * 512:(pi + 1) * 512, :].rearrange("(c p) n -> p c n", p=128), 4, 1024, WB23[:, pi * 4:(pi + 1) * 4, :],
                   col_scale=gtbc[:, 1, :])
    prepq = list(PREP.l)
    PSTEP = 6

    def emit_prep(k=1):
        for _ in range(k):
            if prepq:
                o = prepq.pop(0)
                R.op(o[0], o[1], reads=o[2], writes=o[3], dma=o[4])

    for g in range(2):
        for hh in range(4):
            R.op('sp', f_dma(kt_sb[0:96, hh, :], KT[g * 4 + hh]), writes=[('kt', hh)], dma=('kt', hh))
        R.op('sp', f_dma(v_sb.rearrange("p a b c -> p (a b c)"), Vs[g].rearrange("p a b c -> p (a b c)")), writes=['v'], dma='v')
        its = [(qb, hh, kt) for qb in range(16) for hh in range(4) for kt in range(NKT)]
        n = len(its)
        pend = []

        def emit_S(i):
            qb, hh, kt = its[i]
            if hh == 0 and kt == 0:
                R.op('sp', f_dma(qt_sb[qb % 2][0:96], QTv[:, g * 4:(g + 1) * 4, qb * 512:(qb + 1) * 512]),
                     writes=[('qt', qb % 2)], dma=('qt', qb % 2))
            R.op('pe', f_mm(pS[i % NS], kt_sb[0:96, hh, kt * 128:(kt + 1) * 128], qt_sb[qb % 2][0:96, hh, :], True, True),
                 reads=[('kt', hh), ('qt', qb % 2)], writes=[('pS', i % NS)])
            R.op('act', f_act(PT[i % NP], pS[i % NS], AF.Exp, scale=SCALE), reads=[('pS', i % NS)], writes=[('PT', i % NP)])

        def emit_PV(i):
            qb, hh, kt = its[i]
            hidx = qb * 4 + hh
            o = hidx % 2
            R.op('pe', f_mm(pO[o][0:65, :], v_sb[:, kt, hh, 0:65], PT[i % NP], kt == 0, kt == NKT - 1),
                 reads=['v', ('PT', i % NP)], writes=[('pO', o)])
            if kt == NKT - 1:
                R.op('dve', f_recip(rden[o][64:65, :], pO[o][64:65, :]), reads=[('pO', o)], writes=[('rden', o)])
                pend.append((i + 4, hidx, qb, hh))

        def emit_epi(hidx, qb, hh):
            o = hidx % 2
            R.op('pe', f_mm(pB[0:64, :], ones_f[64:65, 0:64], rden[o][64:65, :], True, True), reads=[('rden', o), 'ones'], writes=['pB'])
            R.op('dve', f_copy(bcs[o][0:64], pB[0:64, :]), reads=['pB'], writes=[('bcs', o)])
            R.op('dve', f_tt(OTn[o][0:64], pO[o][0:64, :], bcs[o][0:64], ALU.mult), reads=[('pO', o), ('bcs', o)], writes=[('OTn', o)])
            r0 = (g * 4 + hh) * 64
            R.op('pool', f_dma(OT[r0:r0 + 64, qb * 512:(qb + 1) * 512], OTn[o][0:64]), reads=[('OTn', o)], writes=[('OT', g, hidx)], dma=('otst', o))

        for i in range(n + LOOK):
            if i < n:
                emit_S(i)
            if i >= LOOK:
                emit_PV(i - LOOK)
            while pend and (pend[0][0] <= i - LOOK or i == n + LOOK - 1):
                _, hidx, qb, hh = pend.pop(0)
                emit_epi(hidx, qb, hh)
            if i % PSTEP == 0:
                emit_prep()
    emit_prep(len(prepq))

    R.barrier()
    if stop == 2:
        R.finalize()
        return nc
    AR.release(0)

    R.bank = {'pT': 0, ('pG', 0): 1, ('pG', 1): 2, ('pD', 0): 3, ('pD', 1): 5, ('pY', 0): 4, ('pY', 1): 6, ('pM', 0): 5, ('pM', 1): 1, ('pP', 0): 6, ('pP', 1): 2, ('pW', 0): 7, ('pW', 1): 3}
    wg = AR.alloc([8, 2048], BF16)
    wbm = AR.alloc([4, 1024], BF16)
    wbp = AR.alloc([4, 1024], BF16)
    poolw = AR.alloc([4, 128], BF16)
    wo = AR.alloc([8, 1024], BF16)
    band = AR.alloc([20, 128], BF16)
    R.op('sp', f_dma(band.rearrange("p a b -> p (a b)"), band_d), writes=['band'], dma='band')
    for nm, dst, src in [('wg', wg, WB_wg), ('wbm', wbm, WB_wbm), ('wbp', wbp, WB_wbp), ('poolw', poolw, WB_pw), ('wo', wo, WB_wo)]:
        R.op('sp', f_dma(dst.rearrange("p a b -> p (a b)"), src), writes=[nm], dma=nm)
    R.barrier()

    xs3 = [AR.alloc([4, 1024], F32) for _ in range(2)]
    xn3 = [AR.alloc([1024], BF16) for _ in range(2)]
    xT3 = [AR.alloc([8, 512], BF16) for _ in range(2)]
    sig = AR.alloc([16, 512], BF16)
    us = [AR.alloc([6, 512], BF16) for _ in range(2)]
    dT = [AR.alloc([512], BF16) for _ in range(2)]
    yT = AR.alloc([4, 512], BF16)
    ot_sb = [AR.alloc([4, 512], BF16) for _ in range(2)]
    tA = [AR.alloc([512], F32) for _ in range(2)]
    tB = [AR.alloc([512], F32) for _ in range(2)]
    mT = AR.alloc([8, 512], BF16)

    pT = psb(0, BF16).rearrange("p (c n) -> p c n", c=8)
    pG = [psb(1), psb(2)]
    pDs = [psb(3), psb(5)]
    pYs = [psb(4), psb(6)]
    pMs = [psb(5), psb(1)]
    pPs = [psb(6), psb(2)]
    pWs = [psb(7), psb(3)]

    def F3(G, b):
        s = b % 2
        G.op('sp', f_dma(xs3[s], x[b * 512:(b + 1) * 512, :].rearrange("(j p) f -> p j f", p=128)),
             writes=[('xs3', s, j) for j in range(4)], dma=('xs3', s))
        G.op('sp', f_dma(ot_sb[s], OT[:, b * 512:(b + 1) * 512].rearrange("(c p) t -> p c t", p=128)), writes=[('ot', s)], dma=('ot', s))
        lo = max(4 * b - 1, 0)
        hi = min(4 * b + 5, 64)
        d0 = lo - (4 * b - 1)
        G.op('sp', f_dma(us[s][:, d0:d0 + hi - lo, :], U[lo * 128:hi * 128, :].rearrange("(j p) f -> p j f", p=128)),
             writes=[('us', s)], dma=('us', s))
        for j in range(4):
            T = 4 * b + j
            G.op('act', f_act(xn3[j % 2], xs3[s][:, j, :], AF.Copy, scale=rstd1[:, T:T + 1]), reads=[('xs3', s, j)], writes=[('xn3', j % 2)])
            for c in range(8):
                G.op('pe', f_tr(pT[:, c, :], xn3[j % 2][:, c * 128:(c + 1) * 128], ident[:]), reads=[('xn3', j % 2)], writes=['pT'])
            G.op('dve', f_copy(xT3[s][:, :, j * 128:(j + 1) * 128], pT), reads=['pT'], writes=[('xT3', s, j)])

    def G3(G, b):
        s = b % 2
        xT3keys = [('xT3', s, j) for j in range(4)]
        for m in range(16):
            for c in range(8):
                G.op('pe', f_mm(pG[m % 2], wg[:, c, m * 128:(m + 1) * 128], xT3[s][:, c, :], c == 0, c == 7), reads=xT3keys, writes=[('pG', m % 2)])
            G.op('act', f_act(sig[:, m, :], pG[m % 2], AF.Sigmoid, bias=bias_g[:, m:m + 1]), reads=[('pG', m % 2)], writes=[('sig', m)])
        for g in range(4):
            for j in range(4):
                T = 4 * b + j
                parts = []
                if T > 0:
                    parts.append((j, g * 5 + 0))
                parts.append((j + 1, g * 5 + (3 if T == 0 else (4 if T == 63 else 1))))
                if T < 63:
                    parts.append((j + 2, g * 5 + 2))
                for k, (slot, bi) in enumerate(parts):
                    G.op('pe', f_mm(pDs[g % 2][:, j * 128:(j + 1) * 128], us[s][:, slot, g * 128:(g + 1) * 128], band[:, bi, :], k == 0, k == len(parts) - 1),
                         reads=[('us', s), 'band'], writes=[('pD', g % 2)])
            G.op('dve', f_copy(dT[g % 2], pDs[g % 2]), reads=[('pD', g % 2)], writes=[('dT', g % 2)])
            G.op('pe', f_mm(pYs[g % 2], poolw[:, g, :], dT[g % 2], True, True), reads=[('dT', g % 2)], writes=[('pY', g % 2)])
            G.op('act', f_copy_act(yT[:, g, :], pYs[g % 2]), reads=[('pY', g % 2)], writes=[('yT', g)])
        yTkeys = [('yT', g) for g in range(4)]
        for m in range(8):
            k2 = m % 2
            for c in range(4):
                G.op('pe', f_mm(pMs[k2], wbm[:, c, m * 128:(m + 1) * 128], ot_sb[s][:, c, :], c == 0, c == 3), reads=[('ot', s)], writes=[('pM', k2)])
            for g in range(4):
                G.op('pe', f_mm(pPs[k2], wbp[:, g, m * 128:(m + 1) * 128], yT[:, g, :], g == 0, g == 3), reads=yTkeys, writes=[('pP', k2)])
            G.op('dve', f_tt(tA[k2], pMs[k2], sig[:, m, :], ALU.mult), reads=[('pM', k2), ('sig', m)], writes=[('tA', k2)])
            G.op('dve', f_tt(tB[k2], pPs[k2], sig[:, 8 + m, :], ALU.mult), reads=[('pP', k2), ('sig', 8 + m)], writes=[('tB', k2)])
            G.op('pool', f_tt(mT[:, m, :], tA[k2], tB[k2], ALU.add), reads=[('tA', k2), ('tB', k2)], writes=[('mT', m)])
        mTkeys = [('mT', m) for m in range(8)]
        for j in range(4):
            for half in range(2):
                k2 = half
                for c in range(8):
                    G.op('pe', f_mm(pWs[k2], mT[:, c, j * 128:(j + 1) * 128], wo[:, c, half * 512:(half + 1) * 512], c == 0, c == 7), reads=mTkeys, writes=[('pW', k2)])
                G.op('dve', f_tt(xs3[s][:, j, half * 512:(half + 1) * 512], pWs[k2], xs3[s][:, j, half * 512:(half + 1) * 512], ALU.add),
                     reads=[('pW', k2), ('xs3', s, j)], writes=[('xs3', s, j)])
        G.op('pool', f_dma(X1[b * 512:(b + 1) * 512, :].rearrange("(j p) f -> p j f", p=128), xs3[s]),
             reads=[('xs3', s, j) for j in range(4)], writes=[('X1', b)], dma=('x1st', s))

    g0 = Stage(); F3(g0, 0); emit_merged([g0])
    for b in range(16):
        stages = [Stage()]
        G3(stages[0], b)
        if b + 1 < 16:
            g1 = Stage(); F3(g1, b + 1); stages.append(g1)
        emit_merged(stages)

    R.barrier()
    if stop == 3:
        R.finalize()
        return nc
    AR.release(0)

    R.bank = {'pB1': 0, 'pT': 0, ('pH', 0): 1, ('pH', 1): 2, ('pH', 2): 3, ('pY2', 0): 4, ('pY2', 1): 5}
    w1 = AR.alloc([8, 4096], BF16)
    w2 = AR.alloc([32, 1024], BF16)
    fg = AR.alloc([1024], F32)
    b1 = b1p
    R.op('sp', f_dma(fg, bigbc_d[:, 2048:3072]), writes=['fg'], dma='fg')
    R.op('sp', f_dma(w1.rearrange("p a b -> p (a b)"), WB_w1), writes=['w1'], dma='w1')
    R.op('sp', f_dma(w2.rearrange("p a b -> p (a b)"), WB_w2), writes=['w2'], dma='w2')
    R.barrier()

    x1s = [AR.alloc([2, 1024], F32) for _ in range(2)]
    xn4 = [AR.alloc([1024], BF16) for _ in range(2)]
    h2T = [AR.alloc([8, 256], BF16) for _ in range(2)]
    rr = [AR.alloc([256], F32) for _ in range(2)]
    hidT = AR.alloc([32, 256], BF16)
    x2 = [AR.alloc([1024], F32) for _ in range(2)]
    outs = x2
    junk4 = AR.alloc([1024], BF16)

    pT = psb(0, BF16).rearrange("p (c n) -> p c n", c=8)
    pH = [psb(1), psb(2), psb(3)]
    pY2 = [psb(4), psb(5)]
    outkeys = []
    def F4(G, b):
        s = b % 2
        G.op('sp', f_dma(x1s[s], X1[b * 256:(b + 1) * 256, :].rearrange("(j p) f -> p j f", p=128)),
             writes=[('x1s', s, 0), ('x1s', s, 1)], dma=('x1s', s))
        for j in range(2):
            G.op('act', f_act(junk4, x1s[s][:, j, :], AF.Square, accum_out=st(9, j)), reads=[('x1s', s, j)], writes=[('ss2', j)])
            G.op('act', f_act(st(10, j), st(9, j), AF.Sqrt, scale=1.0 / D, bias=epsb[:]), reads=[('ss2', j)], writes=[('sd2', j)])
            G.op('dve', f_recip(st(11, j), st(10, j)), reads=[('sd2', j)], writes=[('r2', j)])
            G.op('dve', f_ts(xn4[j], x1s[s][:, j, :], st(11, j), ALU.mult), reads=[('x1s', s, j), ('r2', j)], writes=[('xn4', j)])
            for c in range(8):
                G.op('pe', f_tr(pT[:, c, :], xn4[j][:, c * 128:(c + 1) * 128], ident[:]), reads=[('xn4', j)], writes=['pT'])
            G.op('act', f_copy_act(h2T[s][:, :, j * 128:(j + 1) * 128], pT), reads=['pT'], writes=[('h2T', s, j)])

    def M1(G, b):
        s = b % 2
        for m in range(32):
            for c in range(8):
                G.op('pe', f_mm(pH[m % 3][:, 0:256], w1[:, c, m * 128:(m + 1) * 128], h2T[s][:, c, :], c == 0, c == 7),
                     reads=[('h2T', s, 0), ('h2T', s, 1)], writes=[('pH', m % 3)])
            G.op('act', f_act(rr[m % 2], pH[m % 3][:, 0:256], AF.Relu, bias=b1[:, m:m + 1]), reads=[('pH', m % 3)], writes=[('rr', m % 2)])
            G.op('dve' if m % 2 == 0 else 'pool', f_tt(hidT[:, m, :], rr[m % 2], rr[m % 2], ALU.mult), reads=[('rr', m % 2)], writes=[('hidT', m)])

    def M2(G, b):
        s = b % 2
        hkeys = [('hidT', m) for m in range(32)]
        for j in range(2):
            for half in range(2):
                pk = (j * 2 + half) % 2
                for m in range(32):
                    G.op('pe', f_mm(pY2[pk], hidT[:, m, j * 128:(j + 1) * 128], w2[:, m, half * 512:(half + 1) * 512], m == 0, m == 31),
                         reads=hkeys, writes=[('pY2', pk)])
                G.op('dve', f_tt(x2[j][:, half * 512:(half + 1) * 512], pY2[pk], x1s[s][:, j, half * 512:(half + 1) * 512], ALU.add),
                     reads=[('pY2', pk), ('x1s', s, j)], writes=[('x2', j, half)])
            G.op('act', f_act(junk4, x2[j], AF.Square, accum_out=st(12, j)), reads=[('x2', j, 0), ('x2', j, 1)], writes=[('ss3', j)])
            G.op('act', f_act(st(13, j), st(12, j), AF.Sqrt, scale=1.0 / D, bias=epsb[:]), reads=[('ss3', j)], writes=[('sd3', j)])
            G.op('dve', f_recip(st(14, j), st(13, j)), reads=[('sd3', j)], writes=[('r3', j)])
            G.op('dve', f_stt(outs[j], x2[j], st(14, j), fg, ALU.mult, ALU.mult), reads=[('x2', j, 0), ('x2', j, 1), ('r3', j), 'fg'],
                 writes=[('x2', j, 0), ('x2', j, 1)])
            row = (b * 2 + j) * 128
            G.op('pool', f_dma(out[row:row + 128, :], outs[j]), reads=[('x2', j, 0), ('x2', j, 1)], writes=[('out', b, j)], dma=('outst', j))
            outkeys.append(('out', b, j))

    g0 = Stage(); F4(g0, 0); emit_merged([g0])
    for b in range(32):
        stages = [Stage()]
        M1(stages[0], b)
        if b + 1 < 32:
            g1 = Stage(); F4(g1, b + 1); stages.append(g1)
        emit_merged(stages)
        g2 = Stage(); M2(g2, b); emit_merged([g2])
    R.op('sp', None, reads=outkeys)
    R.finalize()
    return nc


def f_copy_act(out, in_):
    return lambda e: e.activation(out=out, in_=in_, func=AF.Copy)


def _host_consts():
    bf = ml_dtypes.bfloat16
    ident = np.eye(128, dtype=np.float32).astype(bf)
    t = np.arange(L)
    row = (t // 64).astype(np.float32)
    col = (t % 64).astype(np.float32)
    inv_freq = (np.float32(10000.0) ** (-np.arange(0, 16, 2, dtype=np.float32) / np.float32(16))).astype(np.float32)
    ar = (row[:, None] * inv_freq).astype(np.float32)
    ac = (col[:, None] * inv_freq).astype(np.float32)
    cr, sr, cc, sc = np.cos(ar), np.sin(ar), np.cos(ac), np.sin(ac)
    cosF = np.concatenate([cr, cr, cc, cc], axis=1).astype(np.float32)
    sinS = np.concatenate([-sr, sr, -sc, sc], axis=1).astype(np.float32)
    cosF = np.ascontiguousarray(cosF.reshape(64, 128, 32).transpose(1, 0, 2).reshape(128, 64 * 32))
    sinS = np.ascontiguousarray(sinS.reshape(64, 128, 32).transpose(1, 0, 2).reshape(128, 64 * 32))
    band = np.zeros((128, 20, 128), np.float32)
    for g, w in enumerate((2, 4, 8, 16)):
        hw = w // 2
        for kind in range(5):
            T = {0: 5, 1: 5, 2: 5, 3: 0, 4: 63}[kind]
            dT = {0: -1, 1: 0, 2: 1, 3: 0, 4: 0}[kind]
            m = np.zeros((128, 128), np.float32)
            for tl in range(128):
                tg = T * 128 + tl
                lo = max(tg - hw, 0)
                hi = min(tg + hw, L)
                cnt = hi - lo
                for tp in range(lo, hi):
                    p = tp - (T + dT) * 128
                    if 0 <= p < 128:
                        m[p, tl] += 1.0 / cnt
                p = tg - (T + dT) * 128
                if 0 <= p < 128:
                    m[p, tl] -= 1.0
            band[:, g * 5 + kind, :] = m
    band = np.ascontiguousarray(band.reshape(128, 20 * 128)).astype(bf)
    return ident, cosF, sinS, band


_CACHE = {}


def prep_inputs(x, c, ctx, c_ctx, w_ada, b_ada, norm1_g, w_in, q_norm_g, kv_norm_g, w_uq, w_ukv,
                w_br_mla, pool_w, pool_scale, w_br_pool, w_out, norm2_g, w_mlp1, w_mlp2, final_g):
    f = lambda a: np.ascontiguousarray(np.asarray(a, dtype=np.float32))
    x, c, ctx, c_ctx = f(x), f(c), f(ctx), f(c_ctx)
    w_ada, b_ada, w_in, w_uq, w_ukv = f(w_ada)[0], f(b_ada)[0], f(w_in)[0], f(w_uq)[0], f(w_ukv)[0]
    w_br_mla, pool_w, w_br_pool, w_out = f(w_br_mla)[0], f(pool_w)[0], f(w_br_pool)[0], f(w_out)[0]
    w_mlp1, w_mlp2 = f(w_mlp1)[0], f(w_mlp2)[0]
    norm1_g, norm2_g, q_norm_g, kv_norm_g, pool_scale, final_g = (f(norm1_g)[0], f(norm2_g)[0], f(q_norm_g)[0],
                                                                   f(kv_norm_g)[0], f(pool_scale)[0], f(final_g))
    if 'consts' not in _CACHE:
        _CACHE['consts'] = _host_consts()
    ident, cosF, sinS, band = _CACHE['consts']
    col = lambda v: v.reshape(-1, 128).T
    bigbc = np.ascontiguousarray(np.broadcast_to(
        np.concatenate([b_ada[2 * D:3 * D], b_ada[5 * D:6 * D], final_g])[None, :], (128, 3 * D)))
    shared = dict(w_ada=w_ada, w_in=w_in, w_uq=w_uq, w_ukv=w_ukv, w_br_mla=w_br_mla, pool_w=pool_w,
                  w_br_pool=w_br_pool, w_out=w_out, w_mlp1=w_mlp1, w_mlp2=w_mlp2, ident=ident,
                  cosF=cosF, sinS=sinS, band=band, bigbc=bigbc)
    in_maps = []
    for b in range(x.shape[0]):
        ccol = np.stack([col(c[b]), col(c_ctx)], axis=2).reshape(128, 16)
        smallf = np.ascontiguousarray(np.concatenate(
            [ccol, col(b_ada), col(norm1_g), col(norm2_g), col(q_norm_g), col(kv_norm_g), col(pool_scale)], axis=1).astype(np.float32))
        assert smallf.shape == (128, 89)
        m = dict(shared)
        m.update(x=x[b], ctx=ctx[b], smallf=smallf)
        in_maps.append(m)
    return in_maps


def kernel(**inputs):
    in_maps = prep_inputs(**inputs)
    if 'nc' not in _CACHE:
        _CACHE['nc'] = build_program()
    nc = _CACHE['nc']
    res = run_bass_kernel_spmd(nc, in_maps, core_ids=list(range(8)))
    return np.stack([np.asarray(r["out"], dtype=np.float32) for r in res.results], axis=0)
```

```python
import numpy as np
import ml_dtypes
import concourse.bass as bass
import concourse.mybir as mybir
from concourse.bass_utils import run_bass_kernel_spmd

F32 = mybir.dt.float32
BF16 = mybir.dt.bfloat16
AF = mybir.ActivationFunctionType
ALU = mybir.AluOpType

D = 1024
L = 8192
CTX = 256
NKT = (L + CTX) // 128
H = 8
EPS = 1e-6
SCALE = 96.0 ** -0.5
ENGS = ['pe', 'act', 'dve', 'pool', 'sp']


class Rec:
    def __init__(self, nc):
        self.nc = nc
        self.ops = []
        self.lastw = {}
        self.rd_eng = {}
        self.rd_dma = {}
        self.bank = {}
        self.bank_last = {}

    def op(self, eng, fn, reads=(), writes=(), dma=None):
        j = len(self.ops)
        deps = {}
        bset = set()
        for k in list(reads) + list(writes):
            if k in self.bank:
                bv = self.bank[k]
                bset.update(bv if isinstance(bv, tuple) else (bv,))
        for bnk in bset:
            la = self.bank_last.setdefault(bnk, {})
            for e2, i in la.items():
                if e2 != eng:
                    deps.setdefault(i, False)
            la[eng] = j
        for k in reads:
            i = self.lastw.get(k)
            if i is not None:
                deps[i] = True
        for k in writes:
            for i in self.rd_eng.get(k, {}).values():
                deps.setdefault(i, False)
            for i in self.rd_dma.get(k, ()):
                deps.setdefault(i, False)
            i = self.lastw.get(k)
            if i is not None:
                deps.setdefault(i, False)
        for k in reads:
            if dma is None:
                self.rd_eng.setdefault(k, {})[eng] = j
            else:
                self.rd_dma.setdefault(k, []).append(j)
        for k in writes:
            self.lastw[k] = j
            self.rd_eng[k] = {}
            self.rd_dma[k] = []
        keep = []
        for i, raw in deps.items():
            oi = self.ops[i]
            if oi['dma'] is None and dma is None and oi['eng'] == eng:
                if not raw or eng == 'pe':
                    continue
            keep.append(i)
        self.ops.append(dict(eng=eng, fn=fn, dma=dma, deps=keep, sig=False))
        return j

    def barrier(self):
        last = {}
        for idx, o in enumerate(self.ops):
            if o['fn'] is None:
                continue
            if o['dma'] is not None:
                last[('d', o['dma'])] = idx
            else:
                last[('e', o['eng'])] = idx
        deps = list(last.values())
        for e in ENGS:
            self.ops.append(dict(eng=e, fn=None, dma=None, deps=list(deps), sig=False))
        self.lastw.clear()
        self.rd_eng.clear()
        self.rd_dma.clear()
        self.bank_last.clear()

    def finalize(self):
        nc = self.nc
        ops = self.ops
        for o in ops:
            for i in o['deps']:
                ops[i]['sig'] = True
        esem = {e: nc.alloc_semaphore("s_" + e) for e in ENGS}
        dsem = {}
        ecnt = {e: 0 for e in ENGS}
        dcnt = {}
        for o in ops:
            if o['fn'] is None:
                continue
            if o['dma'] is not None:
                k = o['dma']
                if k not in dsem:
                    dsem[k] = nc.alloc_semaphore("d_%d" % len(dsem))
                    dcnt[k] = 0
                dcnt[k] += 16
                o['sem'] = dsem[k]
                o['semid'] = ('d', k)
                o['val'] = dcnt[k]
            elif o['sig']:
                ecnt[o['eng']] += 1
                o['sem'] = esem[o['eng']]
                o['semid'] = ('e', o['eng'])
                o['val'] = ecnt[o['eng']]
        streams = {e: [] for e in ENGS}
        for idx, o in enumerate(ops):
            streams[o['eng']].append(idx)

        def run(eng_name, engine):
            seen = {}
            for idx in streams[eng_name]:
                o = ops[idx]
                need = {}
                for i in o['deps']:
                    d = ops[i]
                    sid, v = d['semid'], d['val']
                    if seen.get(sid, 0) >= v:
                        continue
                    if need.get(sid, (None, 0))[1] < v:
                        need[sid] = (d['sem'], v)
                for sid, (s, v) in need.items():
                    seen[sid] = v
                    engine.wait_ge(s, v)
                if o['fn'] is None:
                    continue
                ins = o['fn'](engine)
                if o['dma'] is not None:
                    ins.then_inc(o['sem'], 16)
                elif o['sig']:
                    ins.then_inc(o['sem'], 1)

        with nc.Block() as block:
            @block.tensor
            def _(e):
                run('pe', e)

            @block.scalar
            def _(e):
                run('act', e)

            @block.vector
            def _(e):
                run('dve', e)

            @block.gpsimd
            def _(e):
                run('pool', e)

            @block.sync
            def _(e):
                run('sp', e)


class Arena:
    def __init__(self, nc, nbytes):
        self.t = nc.alloc_sbuf_tensor("arena", [128, nbytes // 2], BF16)
        self.cap = nbytes
        self.off = 0

    def mark(self):
        return self.off

    def release(self, m):
        self.off = m

    def alloc(self, shape, dtype, parts=128):
        esz = 4 if dtype == F32 else 2
        n = 1
        for s in shape:
            n *= s
        nb = n * esz
        off = (self.off + 63) // 64 * 64
        assert off + nb <= self.cap, ("arena overflow", off, nb, self.cap)
        self.off = off + nb
        ap = self.t[0:parts, off // 2:(off + nb) // 2]
        if dtype == F32:
            ap = ap.bitcast(F32)
        if len(shape) > 1:
            names = ["a%d" % i for i in range(len(shape))]
            pat = "p (" + " ".join(names) + ") -> p " + " ".join(names)
            ap = ap.rearrange(pat, **{nm: s for nm, s in zip(names[1:], shape[1:])})
        return ap


def f_dma(out, in_):
    return lambda e: e.dma_start(out=out, in_=in_)


def f_mm(out, lhsT, rhs, start, stop):
    return lambda e: e.matmul(out, lhsT=lhsT, rhs=rhs, start=start, stop=stop)


def f_tr(out, in_, ident):
    return lambda e: e.transpose(out=out, in_=in_, identity=ident)


def f_act(out, in_, func, **kw):
    return lambda e: e.activation(out=out, in_=in_, func=func, **kw)


def f_tt(out, in0, in1, op):
    return lambda e: e.tensor_tensor(out=out, in0=in0, in1=in1, op=op)


def f_ts(out, in0, s1, op0, s2=None, op1=None):
    if op1 is None:
        return lambda e: e.tensor_scalar(out=out, in0=in0, scalar1=s1, scalar2=None, op0=op0)
    return lambda e: e.tensor_scalar(out=out, in0=in0, scalar1=s1, scalar2=s2, op0=op0, op1=op1)


def f_stt(out, in0, scalar, in1, op0, op1):
    return lambda e: e.scalar_tensor_tensor(out=out, in0=in0, scalar=scalar, in1=in1, op0=op0, op1=op1)


def f_copy(out, in_):
    return lambda e: e.tensor_copy(out=out, in_=in_)


def f_recip(out, in_):
    return lambda e: e.reciprocal(out=out, in_=in_)


def f_memset(out, v):
    return lambda e: e.memset(out, v)


def cast_scaled(R, k, out, in_, scal, reads, writes):
    eng = ('dve', 'act', 'pool')[k % 3]
    if scal is None:
        if eng == 'act':
            R.op('act', f_act(out, in_, AF.Copy), reads, writes)
        else:
            R.op(eng, f_copy(out, in_), reads, writes)
    elif eng == 'act':
        R.op('act', f_act(out, in_, AF.Copy, scale=scal), reads, writes)
    elif eng == 'dve':
        R.op('dve', f_ts(out, in_, scal, ALU.mult), reads, writes)
    else:
        R.op('pool', f_ts(out, in_, scal, ALU.mult, 1.0, ALU.mult), reads, writes)


def build_program(stop=9, dbg=False):
    nc = bass.Bass("TRN2", target_bir_lowering=False)
    skind = "ExternalOutput" if dbg else "Internal"

    def din(name, shape, dt=F32):
        return nc.dram_tensor(name, shape, dt, kind="ExternalInput").ap()

    x = din("x", [L, D])
    ctx = din("ctx", [CTX, D])
    smallf = din("smallf", [128, 89])
    bigbc_d = din("bigbc", [128, 3072])
    w_ada = din("w_ada", [D, 6 * D])
    w_in = din("w_in", [D, 3232])
    w_uq = din("w_uq", [384, 768])
    w_ukv = din("w_ukv", [256, 1024])
    w_br_mla = din("w_br_mla", [512, 1024])
    pool_w = din("pool_w", [4, 128, 128])
    w_br_pool = din("w_br_pool", [512, 1024])
    w_out = din("w_out", [D, D])
    w_mlp1 = din("w_mlp1", [D, 4096])
    w_mlp2 = din("w_mlp2", [4096, D])
    ident_d = din("ident", [128, 128], BF16)
    cosF_d = din("cosF", [128, 64 * 32])
    sinS_d = din("sinS", [128, 64 * 32])
    band_d = din("band", [128, 20 * 128], BF16)
    out = nc.dram_tensor("out", [L, D], F32, kind="ExternalOutput").ap()

    QT = nc.dram_tensor("QT", [H, 96, L], BF16, kind=skind).ap()
    KT = nc.dram_tensor("KT", [H, 96, NKT * 128], BF16, kind=skind).ap()
    Vs = nc.dram_tensor("Vs", [2, 128, NKT, 4, 66], BF16, kind=skind).ap()
    OT = nc.dram_tensor("OT", [512, L], BF16, kind=skind).ap()
    U = nc.dram_tensor("U", [L, 512], BF16, kind=skind).ap()
    X1 = nc.dram_tensor("X1", [L, D], F32, kind=skind).ap()
    WB_wg = nc.dram_tensor("WB_wg", [128, 8 * 2048], BF16, kind="Internal").ap()
    WB_wbm = nc.dram_tensor("WB_wbm", [128, 4 * 1024], BF16, kind="Internal").ap()
    WB_wbp = nc.dram_tensor("WB_wbp", [128, 4 * 1024], BF16, kind="Internal").ap()
    WB_pw = nc.dram_tensor("WB_pw", [128, 4 * 128], BF16, kind="Internal").ap()
    WB_wo = nc.dram_tensor("WB_wo", [128, 8 * 1024], BF16, kind="Internal").ap()
    WB_w1 = nc.dram_tensor("WB_w1", [128, 8 * 4096], BF16, kind="Internal").ap()
    WB_w2 = nc.dram_tensor("WB_w2", [128, 32 * 1024], BF16, kind="Internal").ap()

    R = Rec(nc)

    def gt(name, shape, dt=F32):
        return nc.alloc_sbuf_tensor(name, shape, dt)

    sm = gt("sm", [128, 89])
    ident = gt("ident_s", [128, 128], BF16)
    ones_f = gt("ones_f", [128, 128])
    epsb = gt("epsb", [128, 1])
    sil = gt("sil", [128, 8, 2])
    vcol = gt("vcol", [128, 4, 8, 2])
    gtbc = gt("gtbc", [128, 2, 1024])
    A1 = gt("A1", [128, 8])
    cA1 = gt("cA1", [128, 8])
    A2 = gt("A2", [128, 8])
    rstd1 = gt("rstd1", [128, 64])
    stat = gt("stat", [128, 64])
    bias_g = gt("bias_g", [128, 16])
    b1p = gt("b1p", [128, 32])
    AR = Arena(nc, 196 * 1024)

    PSall = nc.alloc_psum_tensor("psall", [128, 8, 512], F32)

    def psb(i, dt=F32):
        ap = PSall[:, i, :]
        if dt == BF16:
            ap = ap.bitcast(BF16)
        return ap

    R.bank = {('pV', 0): 0, ('pV', 1): 1, ('pR', 0): 2, ('pR', 1): 3, 'pBias0': 4, 'pBias1': 5}
    R.op('sp', f_dma(sm[:], smallf), writes=['sm'], dma='sm')
    R.op('sp', f_dma(ident[:], ident_d), writes=['ident'], dma='ident')
    R.op('dve', f_memset(ones_f[:], 1.0), writes=['ones'])
    R.op('dve', f_memset(epsb[:], EPS), writes=['epsb'])
    R.op('act', f_act(sil[:].rearrange("p c v -> p (c v)"), sm[:, 0:16], AF.Silu), reads=['sm'], writes=['sil'])

    w1x = AR.alloc([8, 1184], BF16)
    w1c = AR.alloc([8, 288], BF16)
    wuq = AR.alloc([3, 768], BF16)
    wukv = AR.alloc([2, 1024], BF16)
    bias1x = AR.alloc([1184], F32)
    bias1c = AR.alloc([288], F32)
    cosF = AR.alloc([64, 32], F32)
    sinS = AR.alloc([64, 32], F32)
    m_p1 = AR.mark()
    stgh = {'b': [AR.alloc([8192], F32) for _ in range(2)]}
    bigbc = AR.alloc([2048], F32)
    R.op('sp', f_dma(bigbc, bigbc_d[:, 0:2048]), writes=['bigbc'], dma='bigbc')
    sil_rep = AR.alloc([8, 128], F32)
    sh1_rep = AR.alloc([8, 128], F32)
    csh1_rep = AR.alloc([8, 128], F32)

    R.op('sp', f_dma(cosF.rearrange("p a b -> p (a b)"), cosF_d), writes=['cosF'], dma='cosF')
    R.op('sp', f_dma(sinS.rearrange("p a b -> p (a b)"), sinS_d), writes=['sinS'], dma='sinS')

    for c in range(8):
        R.op('dve', f_ts(sil_rep[:, c, :], ones_f[:], sil[:, c, 0:1], ALU.mult),
             reads=['ones', 'sil'], writes=[('silrep', c)])

    stg_n = [0]

    def stage_load(src_ap, kc, ncols):
        i = stg_n[0] % 2
        stg_n[0] += 1
        view = stgh['b'][i][:, 0:kc * ncols].rearrange("p (c n) -> p c n", c=kc)
        R.op('sp', f_dma(view, src_ap), writes=[('stg', i)], dma=('stg', i))
        return view, ('stg', i)

    pV = [psb(0)[:, 0:16].rearrange("p (m v) -> p m v", v=2), psb(1)[:, 0:16].rearrange("p (m v) -> p m v", v=2)]
    pR = [psb(2), psb(3)]
    vmap = {0: 0, 1: 1, 3: 2, 4: 3}
    for v in range(6):
        view, skey = stage_load(w_ada[:, v * 1024:(v + 1) * 1024].rearrange("(c p) n -> p c n", p=128), 8, 1024)
        if v in vmap:
            vi = vmap[v]
            pk = ('pV', vi % 2)
            for m in range(8):
                for c in range(8):
                    R.op('pe', f_mm(pV[vi % 2][:, m, :], view[:, c, m * 128:(m + 1) * 128], sil[:, c, :], c == 0, c == 7),
                         reads=[skey, 'sil'], writes=[pk])
            R.op('dve', f_tt(vcol[:, vi], pV[vi % 2],
                             sm[:, 16 + v * 8:16 + (v + 1) * 8].unsqueeze(2).broadcast_to([128, 8, 2]), ALU.add),
                 reads=[pk, 'sm'], writes=[('vcol', vi)])
        else:
            gi = 0 if v == 2 else 1
            for half in range(2):
                pk = ('pR', half)
                for c in range(8):
                    R.op('pe', f_mm(pR[half], sil_rep[:, c, :], view[:, c, half * 512:(half + 1) * 512], c == 0, c == 7),
                         reads=[skey, ('silrep', c)], writes=[pk])
                R.op('dve', f_tt(gtbc[:, gi, half * 512:(half + 1) * 512], pR[half],
                                 bigbc[:, gi * 1024 + half * 512:gi * 1024 + (half + 1) * 512], ALU.add),
                     reads=[pk, 'bigbc'], writes=[('gtbc', gi)])

    R.op('dve', f_stt(A1[:], vcol[:, 1, :, 0], 1.0, sm[:, 64:72], ALU.add, ALU.mult), reads=[('vcol', 1), 'sm'], writes=['A1'])
    R.op('dve', f_stt(cA1[:], vcol[:, 1, :, 1], 1.0, sm[:, 64:72], ALU.add, ALU.mult), reads=[('vcol', 1), 'sm'], writes=['cA1'])
    R.op('dve', f_stt(A2[:], vcol[:, 3, :, 0], 1.0, sm[:, 72:80], ALU.add, ALU.mult), reads=[('vcol', 3), 'sm'], writes=['A2'])
    for c in range(8):
        R.op('dve', f_ts(sh1_rep[:, c, :], ones_f[:], vcol[:, 0, c, 0:1], ALU.mult), reads=['ones', ('vcol', 0)], writes=[('sh1rep', c)])
        R.op('dve', f_ts(csh1_rep[:, c, :], ones_f[:], vcol[:, 0, c, 1:2], ALU.mult), reads=['ones', ('vcol', 0)], writes=[('csh1rep', c)])

    kk = 0
    pBias = [psb(4), psb(5)]
    for pi, (n0, n1) in enumerate([(0, 384), (384, 672), (672, 1184)]):
        n = n1 - n0
        view, skey = stage_load(w_in[:, n0:n1].rearrange("(c p) n -> p c n", p=128), 8, n)
        for c in range(8):
            cast_scaled(R, kk, w1x[:, c, n0:n1], view[:, c, :], A1[:, c:c + 1], [skey, 'A1'], [('w1x', pi, c)])
            kk += 1
        for c in range(8):
            R.op('pe', f_mm(pBias[0][:, 0:n], sh1_rep[:, c, :], view[:, c, :], c == 0, c == 7),
                 reads=[skey, ('sh1rep', c)], writes=['pBias0'])
        R.op('dve', f_copy(bias1x[:, n0:n1], pBias[0][:, 0:n]), reads=['pBias0'], writes=[('bias1x', pi)])
        if pi == 1:
            for c in range(8):
                cast_scaled(R, kk, w1c[:, c, :], view[:, c, :], cA1[:, c:c + 1], [skey, 'cA1'], [('w1c', c)])
                kk += 1
            for c in range(8):
                R.op('pe', f_mm(pBias[1][:, 0:n], csh1_rep[:, c, :], view[:, c, :], c == 0, c == 7),
                     reads=[skey, ('csh1rep', c)], writes=['pBias1'])
            R.op('dve', f_copy(bias1c[:], pBias[1][:, 0:n]), reads=['pBias1'], writes=['bias1c'])
    view, skey = stage_load(w_uq.rearrange("(c p) n -> p c n", p=128), 3, 768)
    for c in range(3):
        cast_scaled(R, kk, wuq[:, c, :], view[:, c, :], sm[:, 80 + c:81 + c], [skey, 'sm'], [('wuq', c)])
        kk += 1
    view, skey = stage_load(w_ukv.rearrange("(c p) n -> p c n", p=128), 2, 1024)
    for c in range(2):
        cast_scaled(R, kk, wukv[:, c, :], view[:, c, :], sm[:, 83 + c:84 + c], [skey, 'sm'], [('wukv', c)])
        kk += 1

    if dbg:
        dbg0 = nc.dram_tensor("dbg0", [128, 4 * 16 + 2048 + 24 + 1184 + 288], F32, kind="ExternalOutput").ap()
        R.op('sp', f_dma(dbg0[:, 0:64], vcol[:].rearrange("p a b c -> p (a b c)")), reads=[('vcol', i) for i in range(4)], writes=['dbg0a'], dma='dbg0a')
        R.op('sp', f_dma(dbg0[:, 64:2112], gtbc[:].rearrange("p a b -> p (a b)")), reads=[('gtbc', 0), ('gtbc', 1)], writes=['dbg0b'], dma='dbg0b')
        R.op('sp', f_dma(dbg0[:, 2112:2120], A1[:]), reads=['A1'], writes=['dbg0c'], dma='dbg0c')
        R.op('sp', f_dma(dbg0[:, 2120:2128], cA1[:]), reads=['cA1'], writes=['dbg0d'], dma='dbg0d')
        R.op('sp', f_dma(dbg0[:, 2128:2136], A2[:]), reads=['A2'], writes=['dbg0e'], dma='dbg0e')
        R.op('sp', f_dma(dbg0[:, 2136:2136 + 1184], bias1x), reads=[('bias1x', i) for i in range(3)], writes=['dbg0f'], dma='dbg0f')
        R.op('sp', f_dma(dbg0[:, 2136 + 1184:2136 + 1184 + 288], bias1c), reads=['bias1c'], writes=['dbg0g'], dma='dbg0g')
        dbg1 = nc.dram_tensor("dbg1", [128, 8 * 1184], BF16, kind="ExternalOutput").ap()
        R.op('sp', f_dma(dbg1, w1x.rearrange("p a b -> p (a b)")), reads=[('w1x', pi, c) for pi in range(3) for c in range(8)], writes=['dbg1'], dma='dbg1')
    R.barrier()
    if stop == 0:
        R.finalize()
        return nc
    AR.release(m_p1)

    R.bank = {'pT': 0, 'psA': 1, 'psB': 2, 'psC': 3, 'pT2q': 4, 'pT2k': 4, 'psWk': 5, 'psWq': 6, 'pXTk': 7, 'pXTq': 7}
    NX, NB = 4, 4
    xs = [AR.alloc([1024], F32) for _ in range(NX)]
    junk = [AR.alloc([1024], BF16) for _ in range(3)]
    xn = [AR.alloc([1024], BF16) for _ in range(NB)]
    xT = [AR.alloc([8, 128], BF16) for _ in range(NB)]
    u_sb = [AR.alloc([512], BF16) for _ in range(NB)]
    cqn = [AR.alloc([384], BF16) for _ in range(NB)]
    cqT = [AR.alloc([3, 128], BF16) for _ in range(NB)]
    ckvn = [AR.alloc([256], BF16) for _ in range(NB)]
    ckvT = [AR.alloc([2, 128], BF16) for _ in range(NB)]
    krs_sb = [AR.alloc([32], F32) for _ in range(NB)]
    t1 = [AR.alloc([8, 32], F32) for _ in range(NB)]
    t2 = [AR.alloc([8, 32], F32) for _ in range(NB)]
    qf = [AR.alloc([8, 96], BF16) for _ in range(NB)]
    kf = [AR.alloc([8, 96], BF16) for _ in range(NB)]
    vaug = [AR.alloc([8, 66], BF16) for _ in range(NB)]
    qT_sb = [AR.alloc([8, 128], BF16) for _ in range(NB)]
    kT_sb = [AR.alloc([8, 128], BF16) for _ in range(NB)]
    t1k = [AR.alloc([32], F32) for _ in range(NB)]
    t2k = [AR.alloc([32], F32) for _ in range(NB)]
    kr = [AR.alloc([32], F32) for _ in range(NB)]
    ones2 = AR.alloc([128], BF16)
    bias2x = AR.alloc([1184], BF16)
    bias2c = AR.alloc([288], BF16)
    bhi32 = AR.alloc([1184], BF16)
    btmp = AR.alloc([1184], F32)
    R.op('dve', f_memset(ones2, 0.0), writes=['ones2'])
    R.op('dve', f_memset(ones2[0:1], 1.0), writes=['ones2'])
    R.op('dve', f_memset(ones2[32:33], 1.0), writes=['ones2'])
    for (b2, src_, n_, kn) in [(bias2x, bias1x, 1184, 'x'), (bias2c, bias1c, 288, 'c')]:
        R.op('dve', f_memset(b2, 0.0), writes=[('b2', kn)])
        R.op('dve', f_copy(b2[0:1], src_[0:1]), writes=[('b2', kn)])
        R.op('dve', f_copy(bhi32[32:33, 0:n_], src_[32:33]), writes=[('bhi32', kn)])
        R.op('dve', f_tt(btmp[32:33, 0:n_], src_[32:33], bhi32[32:33, 0:n_], ALU.subtract), reads=[('bhi32', kn)], writes=[('btmp', kn)])
        R.op('dve', f_copy(b2[32:33], btmp[32:33, 0:n_]), reads=[('btmp', kn)], writes=[('b2', kn)])
    for s in range(NB):
        R.op('dve', f_memset(vaug[s], 1.0), writes=[('vaug', s)])

    pT = psb(0, BF16).rearrange("p (c n) -> p c n", c=8)
    psA, psB, psC = psb(1), psb(2), psb(3)
    pT2 = psb(4, BF16).rearrange("p (c n) -> p c n", c=8)
    psWk = psb(5).rearrange("p (h d) -> p h d", d=128)
    psWq = psb(6)[:, 0:384].rearrange("p (h d) -> p h d", d=96)
    pXT = psb(7, BF16).rearrange("p (c n) -> p c n", c=8)
    QTv = QT.rearrange("h d t -> d h t")
    KTv = KT.rearrange("h d t -> d h t")

    def st(col, s):
        return stat[:, col * 4 + s:col * 4 + s + 1]

    class Stage:
        def __init__(self):
            self.l = []

        def op(self, eng, fn, reads=(), writes=(), dma=None):
            self.l.append((eng, fn, reads, writes, dma))

    def emit_merged(stages, group_pe=False):
        items = []
        for si, sg in enumerate(stages):
            n = len(sg.l)
            k = 0
            while k < n:
                k1 = k + 1
                if group_pe and sg.l[k][0] == 'pe':
                    while k1 < n and sg.l[k1][0] == 'pe':
                        k1 += 1
                items.append(((k + 0.5) / n, si, k, sg.l[k:k1]))
                k = k1
        items.sort(key=lambda z: (z[0], z[1]))
        for _, _, _, grp in items:
            for o in grp:
                R.op(o[0], o[1], reads=o[2], writes=o[3], dma=o[4])

    def tile_src(t):
        return ctx[t * 128:(t + 1) * 128, :] if t < 2 else x[(t - 2) * 128:(t - 1) * 128, :]

    def SL(G, t):
        G.op('sp', f_dma(xs[t % NX], tile_src(t)), writes=[('xs', t % NX)], dma=('xs', t % NX))

    def S1(G, t):
        is_ctx = t < 2
        xi = t - 2
        sx, s, s4_ = t % NX, t % NB, t % 4
        G.op('act', f_act(junk[0], xs[sx], AF.Square, accum_out=st(0, s4_)), reads=[('xs', sx)], writes=[('ss', s4_)])
        G.op('act', f_act(st(1, s4_), st(0, s4_), AF.Sqrt, scale=1.0 / D, bias=epsb[:]), reads=[('ss', s4_)], writes=[('sd', s4_)])
        if is_ctx:
            rst, rkey = st(2, s4_), ('rstc', s4_)
        else:
            rst, rkey = rstd1[:, xi:xi + 1], ('rstd1', xi)
        G.op('dve', f_recip(rst, st(1, s4_)), reads=[('sd', s4_)], writes=[rkey])
        G.op('dve', f_ts(xn[s], xs[sx], rst, ALU.mult), reads=[('xs', sx), rkey], writes=[('xn', s)])
        for c in range(8):
            G.op('pe', f_tr(pT[:, c, :], xn[s][:, c * 128:(c + 1) * 128], ident[:]), reads=[('xn', s)], writes=['pT'])
        G.op('act', f_copy_act(xT[s].rearrange("p c n -> p (c n)"), psb(0, BF16)), reads=['pT'], writes=[('xT', s)])

    def S1b(G, t):
        is_ctx = t < 2
        xi = t - 2
        sx, s, s4_ = t % NX, t % NB, t % 4
        if not is_ctx:
            groups = [(psA, 'psA', 0, 384), (psB, 'psB', 384, 672), (psC, 'psC', 672, 1184)]
            for (ps, key, n0, n1) in groups:
                for c in range(8):
                    G.op('pe', f_mm(ps[:, 0:n1 - n0], xT[s][:, c, :], w1x[:, c, n0:n1], c == 0, False), reads=[('xT', s)], writes=[key])
                G.op('pe', f_mm(ps[:, 0:n1 - n0], ones2, bias2x[:, n0:n1], False, True), writes=[key])
        else:
            for c in range(8):
                G.op('pe', f_mm(psB[:, 0:288], xT[s][:, c, :], w1c[:, c, :], c == 0, False), reads=[('xT', s)], writes=['psB'])
            G.op('pe', f_mm(psB[:, 0:288], ones2, bias2c, False, True), writes=['psB'])
        G.op('act', f_act(junk[0][:, 0:256], psB[:, 0:256], AF.Square, accum_out=st(3, s4_)), reads=['psB'], writes=[('sskv', s4_)])
        G.op('act', f_act(st(4, s4_), st(3, s4_), AF.Sqrt, scale=1.0 / 256, bias=epsb[:]), reads=[('sskv', s4_)], writes=[('sdkv', s4_)])
        G.op('dve', f_recip(st(5, s4_), st(4, s4_)), reads=[('sdkv', s4_)], writes=[('rkv', s4_)])
        G.op('dve', f_ts(ckvn[s], psB[:, 0:256], st(5, s4_), ALU.mult), reads=['psB', ('rkv', s4_)], writes=[('ckvn', s)])
        G.op('dve', f_copy(krs_sb[s], psB[:, 256:288]), reads=['psB'], writes=[('krs', s)])
        if not is_ctx:
            G.op('act', f_act(junk[0][:, 0:384], psA[:, 0:384], AF.Square, accum_out=st(6, s4_)), reads=['psA'], writes=[('ssq', s4_)])
            G.op('act', f_act(st(7, s4_), st(6, s4_), AF.Sqrt, scale=1.0 / 384, bias=epsb[:]), reads=[('ssq', s4_)], writes=[('sdq', s4_)])
            G.op('dve', f_recip(st(8, s4_), st(7, s4_)), reads=[('sdq', s4_)], writes=[('rq', s4_)])
            G.op('dve', f_ts(cqn[s], psA[:, 0:384], st(8, s4_), ALU.mult), reads=['psA', ('rq', s4_)], writes=[('cqn', s)])
            G.op('act', f_copy_act(u_sb[s], psC[:, 0:512]), reads=['psC'], writes=[('u', s)])
            G.op('pool', f_dma(U[xi * 128:(xi + 1) * 128, :], u_sb[s]), reads=[('u', s)], writes=[('U', xi)], dma=('ust', s))

    def S2(G, t):
        is_ctx = t < 2
        xi = t - 2
        s = t % NB
        for c in range(2):
            G.op('pe', f_tr(pT2[:, 3 + c, :], ckvn[s][:, c * 128:(c + 1) * 128], ident[:]), reads=[('ckvn', s)], writes=['pT2k'])
        G.op('dve', f_copy(ckvT[s], pT2[:, 3:5, :]), reads=['pT2k'], writes=[('ckvT', s)])
        if is_ctx:
            krsrc, krkey = krs_sb[s], ('krs', s)
        else:
            cos_t = cosF[:, xi, :]
            sin_t = sinS[:, xi, :]
            G.op('pool', f_tt(t1k[s], krs_sb[s], cos_t, ALU.mult), reads=[('krs', s)], writes=[('t1k', s)])
            kv4 = krs_sb[s].rearrange("p (a f i) -> p a f i", a=2, f=2)
            o4 = t2k[s].rearrange("p (a f i) -> p a f i", a=2, f=2)
            s4 = sin_t.rearrange("p (a f i) -> p a f i", a=2, f=2)
            for f in range(2):
                G.op('pool', f_tt(o4[:, :, f, :], kv4[:, :, 1 - f, :], s4[:, :, f, :], ALU.mult), reads=[('krs', s)], writes=[('t2k', s, f)])
            G.op('pool', f_tt(kr[s], t1k[s], t2k[s], ALU.add), reads=[('t1k', s), ('t2k', s, 0), ('t2k', s, 1)], writes=[('kr', s)])
            krsrc, krkey = kr[s], ('kr', s)
        G.op('pool', f_copy(kf[s][:, :, 64:96], krsrc.unsqueeze(1).broadcast_to([128, 8, 32])), reads=[krkey], writes=[('kfr', s)])
        for hb in range(2):
            hs = slice(hb * 4, (hb + 1) * 4)
            for c in range(2):
                G.op('pe', f_mm(psb(5), ckvT[s][:, c, :], wukv[:, c, hb * 512:(hb + 1) * 512], c == 0, c == 1), reads=[('ckvT', s)], writes=['psWk'])
            G.op('act', f_copy_act(kf[s][:, hs, 0:64], psWk[:, :, 0:64]), reads=['psWk'], writes=[('kfn', s, hb)])
            G.op('dve', f_copy(vaug[s][:, hs, 0:64], psWk[:, :, 64:128]), reads=['psWk'], writes=[('vaug', s, hb)])
        for hb in range(2):
            for h4 in range(4):
                h = hb * 4 + h4
                G.op('pe', f_tr(pXT[0:96, h4, :], kf[s][:, h, :], ident[:]), reads=[('kfr', s), ('kfn', s, hb)], writes=['pXTk'])
            G.op('dve' if hb == 0 else 'act', (f_copy if hb == 0 else f_copy_act)(kT_sb[s][0:96, hb * 4:(hb + 1) * 4, :], pXT[0:96, 0:4, :]),
                 reads=['pXTk'], writes=[('kT', s, hb)])
        G.op('sp', f_dma(KTv[:, :, t * 128:(t + 1) * 128], kT_sb[s][0:96]), reads=[('kT', s, 0), ('kT', s, 1)], writes=[('KT', t)], dma=('ktst', s))
        G.op('pool', f_dma(Vs[:, :, t].rearrange("g p hh e -> p g hh e"), vaug[s].rearrange("p (g hh) e -> p g hh e", g=2)),
             reads=[('vaug', s, 0), ('vaug', s, 1)], writes=[('Vs', t)], dma=('vst', s))

    def S3(G, t):
        xi = t - 2
        s = t % NB
        cos_t = cosF[:, xi, :]
        sin_t = sinS[:, xi, :]
        s4 = sin_t.rearrange("p (a f i) -> p a f i", a=2, f=2)
        for c in range(3):
            G.op('pe', f_tr(pT2[:, c, :], cqn[s][:, c * 128:(c + 1) * 128], ident[:]), reads=[('cqn', s)], writes=['pT2q'])
        G.op('dve', f_copy(cqT[s], pT2[:, 0:3, :]), reads=['pT2q'], writes=[('cqT', s)])
        for hb in range(2):
            hs = slice(hb * 4, (hb + 1) * 4)
            for c in range(3):
                G.op('pe', f_mm(psb(6)[:, 0:384], cqT[s][:, c, :], wuq[:, c, hb * 384:(hb + 1) * 384], c == 0, c == 2),
                     reads=[('cqT', s)], writes=['psWq'])
            qr = psWq[:, :, 64:96]
            G.op('dve', f_tt(t1[s][:, hs, :], qr, cos_t.unsqueeze(1).broadcast_to([128, 4, 32]), ALU.mult), reads=['psWq'], writes=[('t1', s, hb)])
            q5 = qr.rearrange("p h (a f i) -> p h a f i", a=2, f=2)
            o5 = t2[s][:, hs, :].rearrange("p h (a f i) -> p h a f i", a=2, f=2)
            for f in range(2):
                G.op('dve', f_tt(o5[:, :, :, f, :], q5[:, :, :, 1 - f, :], s4[:, :, f, :].unsqueeze(1).broadcast_to([128, 4, 2, 8]), ALU.mult),
                     reads=['psWq'], writes=[('t2', s, hb, f)])
            G.op('act', f_copy_act(qf[s][:, hs, 0:64], psWq[:, :, 0:64]), reads=['psWq'], writes=[('qfn', s, hb)])
        G.op('pool', f_tt(qf[s][:, :, 64:96], t1[s], t2[s], ALU.add),
             reads=[('t1', s, 0), ('t1', s, 1)] + [('t2', s, hb, f) for hb in range(2) for f in range(2)], writes=[('qfr', s)])
        for hb in range(2):
            for h4 in range(4):
                h = hb * 4 + h4
                G.op('pe', f_tr(pXT[0:96, 4 + h4, :], qf[s][:, h, :], ident[:]), reads=[('qfr', s), ('qfn', s, hb)], writes=['pXTq'])
            G.op('dve', f_copy(qT_sb[s][0:96, hb * 4:(hb + 1) * 4, :], pXT[0:96, 4:8, :]), reads=['pXTq'], writes=[('qT', s, hb)])
        G.op('sp', f_dma(QTv[:, :, xi * 128:(xi + 1) * 128], qT_sb[s][0:96]), reads=[('qT', s, 0), ('qT', s, 1)], writes=[('QT', xi)], dma=('qtst', s))

    import os
    _nt = int(os.environ.get('P1_TILES', NKT))
    G0 = Stage()
    SL(G0, 0)
    if _nt > 1:
        SL(G0, 1)
    emit_merged([G0])
    for i in range(_nt + 3):
        stages = []
        if i + 2 < _nt:
            g = Stage(); SL(g, i + 2); stages.append(g)
        if i < _nt:
            g = Stage(); S1(g, i); stages.append(g)
        if 0 <= i - 1 < _nt:
            g = Stage(); S1b(g, i - 1); stages.append(g)
        if 0 <= i - 2 < _nt:
            g = Stage(); S2(g, i - 2); stages.append(g)
        if 2 <= i - 3 < _nt:
            g = Stage(); S3(g, i - 3); stages.append(g)
        emit_merged(stages)

    R.barrier()
    if stop == 1:
        R.finalize()
        return nc
    AR.release(0)

    R.bank = {('pS', 0): 0, ('pS', 1): 1, ('pS', 2): 2, ('pS', 3): 3, ('pO', 0): 4, ('pO', 1): 5, 'pB': 6, 'pBg': 7, 'pB1': 7}
    kt_sb = AR.alloc([4, NKT * 128], BF16)
    v_sb = AR.alloc([NKT, 4, 66], BF16)
    qt_sb = [AR.alloc([4, 512], BF16) for _ in range(2)]
    NS, NP, LOOK = 4, 4, 2
    PT = [AR.alloc([512], BF16) for _ in range(NP)]
    rden = [AR.alloc([512], F32) for _ in range(2)]
    bcs = [AR.alloc([512], F32) for _ in range(2)]
    OTn = [AR.alloc([512], BF16) for _ in range(2)]
    pstg = [AR.alloc([4096], F32) for _ in range(2)]
    pcast = [AR.alloc([4096], BF16) for _ in range(2)]
    pS = [psb(i) for i in range(NS)]
    pO = [psb(4), psb(5)]
    pB = psb(6)
    pBg = psb(7)[:, 0:32].rearrange("p (m v) -> p m v", v=2)
    pB1 = psb(7)[:, 32:96].rearrange("p (m v) -> p m v", v=2)

    PREP = Stage()
    pcnt = [0]

    def prep_piece(src_ap, kc, ncols, dst_ap, row_scale=None, col_scale=None, bias=None):
        i = pcnt[0] % 2
        pcnt[0] += 1
        view = pstg[i][:, 0:kc * ncols].rearrange("p (c n) -> p c n", c=kc)
        cview = pcast[i][:, 0:kc * ncols].rearrange("p (c n) -> p c n", c=kc)
        PREP.op('sp', f_dma(view, src_ap), writes=[('pstg', i)], dma=('pstg', i))
        for c in range(kc):
            eng = 'dve' if c % 2 == 0 else 'pool'
            if col_scale is not None:
                fn = f_tt(cview[:, c, :], view[:, c, :], col_scale, ALU.mult)
            elif row_scale is not None:
                fn = f_ts(cview[:, c, :], view[:, c, :], row_scale(c), ALU.mult, 1.0, ALU.mult)
            else:
                fn = f_copy(cview[:, c, :], view[:, c, :])
            PREP.op(eng, fn, reads=[('pstg', i)], writes=[('pcast', i, c)])
        if bias is not None:
            ptile, m0, rhs_of, key = bias
            for m in range(ncols // 128):
                for c in range(kc):
                    PREP.op('pe', f_mm(ptile[:, m0 + m, :], view[:, c, m * 128:(m + 1) * 128], rhs_of(c), c == 0, c == kc - 1),
                            reads=[('pstg', i)], writes=[key])
        PREP.op('pool', f_dma(dst_ap, cview), reads=[('pcast', i, c) for c in range(kc)], writes=[('WB', pcnt[0])], dma=('pcst', i))

    WBg3 = WB_wg.rearrange("p (c n) -> p c n", c=8)
    for pi in range(4):
        n0 = 1184 + pi * 512
        prep_piece(w_in[:, n0:n0 + 512].rearrange("(c p) n -> p c n", p=128), 8, 512, WBg3[:, :, pi * 512:(pi + 1) * 512],
                   row_scale=lambda c: A1[:, c:c + 1], bias=(pBg, pi * 4, lambda c: vcol[:, 0, c, :], 'pBg'))
    PREP.op('dve', f_copy(bias_g[:], pBg[:, :, 0]), reads=['pBg'], writes=['bias_g'])
    prep_piece(w_br_mla.rearrange("(c p) n -> p c n", p=128), 4, 1024, WB_wbm.rearrange("p (c n) -> p c n", c=4))
    prep_piece(w_br_pool.rearrange("(c p) n -> p c n", p=128), 4, 1024, WB_wbp.rearrange("p (c n) -> p c n", c=4),
               row_scale=lambda c: sm[:, 85 + c:86 + c])
    prep_piece(pool_w.rearrange("g c d -> c g d"), 4, 128, WB_pw.rearrange("p (c n) -> p c n", c=4))
    WBo3 = WB_wo.rearrange("p (c n) -> p c n", c=8)
    for pi in range(2):
        prep_piece(w_out[:, pi * 512:(pi + 1) * 512].rearrange("(c p) n -> p c n", p=128), 8, 512, WBo3[:, :, pi * 512:(pi + 1) * 512],
                   col_scale=gtbc[:, 0, pi * 512:(pi + 1) * 512])
    WB13 = WB_w1.rearrange("p (c n) -> p c n", c=8)
    for pi in range(8):
        prep_piece(w_mlp1[:, pi * 512:(pi + 1) * 512].rearrange("(c p) n -> p c n", p=128), 8, 512, WB13[:, :, pi * 512:(pi + 1) * 512],
                   row_scale=lambda c: A2[:, c:c + 1], bias=(pB1, pi * 4, lambda c: vcol[:, 2, c, :], 'pB1'))
    PREP.op('dve', f_copy(b1p[:], pB1[:, :, 0]), reads=['pB1'], writes=['b1p'])
    WB23 = WB_w2.rearrange("p (c n) -> p c n", c=32)
    for pi in range(8):
        prep_piece(w_mlp2[pi * 512:(pi + 1) * 512, :].rearrange("(c p) n -> p c n", p=128), 4, 1024, WB23[:, pi * 4:(pi + 1) * 4, :],
                   col_scale=gtbc[:, 1, :])
    prepq = list(PREP.l)
    PSTEP = 6

    def emit_prep(k=1):
        for _ in range(k):
            if prepq:
                o = prepq.pop(0)
                R.op(o[0], o[1], reads=o[2], writes=o[3], dma=o[4])

    for g in range(2):
        for hh in range(4):
            R.op('sp', f_dma(kt_sb[0:96, hh, :], KT[g * 4 + hh]), writes=[('kt', hh)], dma=('kt', hh))
        R.op('sp', f_dma(v_sb.rearrange("p a b c -> p (a b c)"), Vs[g].rearrange("p a b c -> p (a b c)")), writes=['v'], dma='v')
        its = [(qb, hh, kt) for qb in range(16) for hh in range(4) for kt in range(NKT)]
        n = len(its)
        pend = []

        def emit_S(i):
            qb, hh, kt = its[i]
            if hh == 0 and kt == 0:
                R.op('sp', f_dma(qt_sb[qb % 2][0:96], QTv[:, g * 4:(g + 1) * 4, qb * 512:(qb + 1) * 512]),
                     writes=[('qt', qb % 2)], dma=('qt', qb % 2))
            R.op('pe', f_mm(pS[i % NS], kt_sb[0:96, hh, kt * 128:(kt + 1) * 128], qt_sb[qb % 2][0:96, hh, :], True, True),
                 reads=[('kt', hh), ('qt', qb % 2)], writes=[('pS', i % NS)])
            R.op('act', f_act(PT[i % NP], pS[i % NS], AF.Exp, scale=SCALE), reads=[('pS', i % NS)], writes=[('PT', i % NP)])

        def emit_PV(i):
            qb, hh, kt = its[i]
            hidx = qb * 4 + hh
            o = hidx % 2
            R.op('pe', f_mm(pO[o][0:65, :], v_sb[:, kt, hh, 0:65], PT[i % NP], kt == 0, kt == NKT - 1),
                 reads=['v', ('PT', i % NP)], writes=[('pO', o)])
            if kt == NKT - 1:
                R.op('dve', f_recip(rden[o][64:65, :], pO[o][64:65, :]), reads=[('pO', o)], writes=[('rden', o)])
                pend.append((i + 4, hidx, qb, hh))

        def emit_epi(hidx, qb, hh):
            o = hidx % 2
            R.op('pe', f_mm(pB[0:64, :], ones_f[64:65, 0:64], rden[o][64:65, :], True, True), reads=[('rden', o), 'ones'], writes=['pB'])
            R.op('dve', f_copy(bcs[o][0:64], pB[0:64, :]), reads=['pB'], writes=[('bcs', o)])
            R.op('dve', f_tt(OTn[o][0:64], pO[o][0:64, :], bcs[o][0:64], ALU.mult), reads=[('pO', o), ('bcs', o)], writes=[('OTn', o)])
            r0 = (g * 4 + hh) * 64
            R.op('pool', f_dma(OT[r0:r0 + 64, qb * 512:(qb + 1) * 512], OTn[o][0:64]), reads=[('OTn', o)], writes=[('OT', g, hidx)], dma=('otst', o))

        for i in range(n + LOOK):
            if i < n:
                emit_S(i)
            if i >= LOOK:
                emit_PV(i - LOOK)
            while pend and (pend[0][0] <= i - LOOK or i == n + LOOK - 1):
                _, hidx, qb, hh = pend.pop(0)
                emit_epi(hidx, qb, hh)
            if i % PSTEP == 0:
                emit_prep()
    emit_prep(len(prepq))

    R.barrier()
    if stop == 2:
        R.finalize()
        return nc
    AR.release(0)

    R.bank = {'pT': 0, ('pG', 0): 1, ('pG', 1): 2, ('pD', 0): 3, ('pD', 1): 5, ('pY', 0): 4, ('pY', 1): 6, ('pM', 0): 5, ('pM', 1): 1, ('pP', 0): 6, ('pP', 1): 2, ('pW', 0): 7, ('pW', 1): 3}
    wg = AR.alloc([8, 2048], BF16)
    wbm = AR.alloc([4, 1024], BF16)
    wbp = AR.alloc([4, 1024], BF16)
    poolw = AR.alloc([4, 128], BF16)
    wo = AR.alloc([8, 1024], BF16)
    band = AR.alloc([20, 128], BF16)
    R.op('sp', f_dma(band.rearrange("p a b -> p (a b)"), band_d), writes=['band'], dma='band')
    for nm, dst, src in [('wg', wg, WB_wg), ('wbm', wbm, WB_wbm), ('wbp', wbp, WB_wbp), ('poolw', poolw, WB_pw), ('wo', wo, WB_wo)]:
        R.op('sp', f_dma(dst.rearrange("p a b -> p (a b)"), src), writes=[nm], dma=nm)
    R.barrier()

    xs3 = [AR.alloc([4, 1024], F32) for _ in range(2)]
    xn3 = [AR.alloc([1024], BF16) for _ in range(2)]
    xT3 = [AR.alloc([8, 512], BF16) for _ in range(2)]
    sig = AR.alloc([16, 512], BF16)
    us = [AR.alloc([6, 512], BF16) for _ in range(2)]
    dT = [AR.alloc([512], BF16) for _ in range(2)]
    yT = AR.alloc([4, 512], BF16)
    ot_sb = [AR.alloc([4, 512], BF16) for _ in range(2)]
    tA = [AR.alloc([512], F32) for _ in range(2)]
    tB = [AR.alloc([512], F32) for _ in range(2)]
    mT = AR.alloc([8, 512], BF16)

    pT = psb(0, BF16).rearrange("p (c n) -> p c n", c=8)
    pG = [psb(1), psb(2)]
    pDs = [psb(3), psb(5)]
    pYs = [psb(4), psb(6)]
    pMs = [psb(5), psb(1)]
    pPs = [psb(6), psb(2)]
    pWs = [psb(7), psb(3)]

    def F3(G, b):
        s = b % 2
        G.op('sp', f_dma(xs3[s], x[b * 512:(b + 1) * 512, :].rearrange("(j p) f -> p j f", p=128)),
             writes=[('xs3', s, j) for j in range(4)], dma=('xs3', s))
        G.op('sp', f_dma(ot_sb[s], OT[:, b * 512:(b + 1) * 512].rearrange("(c p) t -> p c t", p=128)), writes=[('ot', s)], dma=('ot', s))
        lo = max(4 * b - 1, 0)
        hi = min(4 * b + 5, 64)
        d0 = lo - (4 * b - 1)
        G.op('sp', f_dma(us[s][:, d0:d0 + hi - lo, :], U[lo * 128:hi * 128, :].rearrange("(j p) f -> p j f", p=128)),
             writes=[('us', s)], dma=('us', s))
        for j in range(4):
            T = 4 * b + j
            G.op('act', f_act(xn3[j % 2], xs3[s][:, j, :], AF.Copy, scale=rstd1[:, T:T + 1]), reads=[('xs3', s, j)], writes=[('xn3', j % 2)])
            for c in range(8):
                G.op('pe', f_tr(pT[:, c, :], xn3[j % 2][:, c * 128:(c + 1) * 128], ident[:]), reads=[('xn3', j % 2)], writes=['pT'])
            G.op('dve', f_copy(xT3[s][:, :, j * 128:(j + 1) * 128], pT), reads=['pT'], writes=[('xT3', s, j)])

    def G3(G, b):
        s = b % 2
        xT3keys = [('xT3', s, j) for j in range(4)]
        for m in range(16):
            for c in range(8):
                G.op('pe', f_mm(pG[m % 2], wg[:, c, m * 128:(m + 1) * 128], xT3[s][:, c, :], c == 0, c == 7), reads=xT3keys, writes=[('pG', m % 2)])
            G.op('act', f_act(sig[:, m, :], pG[m % 2], AF.Sigmoid, bias=bias_g[:, m:m + 1]), reads=[('pG', m % 2)], writes=[('sig', m)])
        for g in range(4):
            for j in range(4):
                T = 4 * b + j
                parts = []
                if T > 0:
                    parts.append((j, g * 5 + 0))
                parts.append((j + 1, g * 5 + (3 if T == 0 else (4 if T == 63 else 1))))
                if T < 63:
                    parts.append((j + 2, g * 5 + 2))
                for k, (slot, bi) in enumerate(parts):
                    G.op('pe', f_mm(pDs[g % 2][:, j * 128:(j + 1) * 128], us[s][:, slot, g * 128:(g + 1) * 128], band[:, bi, :], k == 0, k == len(parts) - 1),
                         reads=[('us', s), 'band'], writes=[('pD', g % 2)])
            G.op('dve', f_copy(dT[g % 2], pDs[g % 2]), reads=[('pD', g % 2)], writes=[('dT', g % 2)])
            G.op('pe', f_mm(pYs[g % 2], poolw[:, g, :], dT[g % 2], True, True), reads=[('dT', g % 2)], writes=[('pY', g % 2)])
            G.op('act', f_copy_act(yT[:, g, :], pYs[g % 2]), reads=[('pY', g % 2)], writes=[('yT', g)])
        yTkeys = [('yT', g) for g in range(4)]
        for m in range(8):
            k2 = m % 2
            for c in range(4):
                G.op('pe', f_mm(pMs[k2], wbm[:, c, m * 128:(m + 1) * 128], ot_sb[s][:, c, :], c == 0, c == 3), reads=[('ot', s)], writes=[('pM', k2)])
            for g in range(4):
                G.op('pe', f_mm(pPs[k2], wbp[:, g, m * 128:(m + 1) * 128], yT[:, g, :], g == 0, g == 3), reads=yTkeys, writes=[('pP', k2)])
            G.op('dve', f_tt(tA[k2], pMs[k2], sig[:, m, :], ALU.mult), reads=[('pM', k2), ('sig', m)], writes=[('tA', k2)])
            G.op('dve', f_tt(tB[k2], pPs[k2], sig[:, 8 + m, :], ALU.mult), reads=[('pP', k2), ('sig', 8 + m)], writes=[('tB', k2)])
            G.op('pool', f_tt(mT[:, m, :], tA[k2], tB[k2], ALU.add), reads=[('tA', k2), ('tB', k2)], writes=[('mT', m)])
        mTkeys = [('mT', m) for m in range(8)]
        for j in range(4):
            for half in range(2):
                k2 = half
                for c in range(8):
                    G.op('pe', f_mm(pWs[k2], mT[:, c, j * 128:(j + 1) * 128], wo[:, c, half * 512:(half + 1) * 512], c == 0, c == 7), reads=mTkeys, writes=[('pW', k2)])
                G.op('dve', f_tt(xs3[s][:, j, half * 512:(half + 1) * 512], pWs[k2], xs3[s][:, j, half * 512:(half + 1) * 512], ALU.add),
                     reads=[('pW', k2), ('xs3', s, j)], writes=[('xs3', s, j)])
        G.op('pool', f_dma(X1[b * 512:(b + 1) * 512, :].rearrange("(j p) f -> p j f", p=128), xs3[s]),
             reads=[('xs3', s, j) for j in range(4)], writes=[('X1', b)], dma=('x1st', s))

    g0 = Stage(); F3(g0, 0); emit_merged([g0])
    for b in range(16):
        stages = [Stage()]
        G3(stages[0], b)
        if b + 1 < 16:
            g1 = Stage(); F3(g1, b + 1); stages.append(g1)
        emit_merged(stages)

    R.barrier()
    if stop == 3:
        R.finalize()
        return nc
    AR.release(0)

    R.bank = {'pB1': 0, 'pT': 0, ('pH', 0): 1, ('pH', 1): 2, ('pH', 2): 3, ('pY2', 0): 4, ('pY2', 1): 5}
    w1 = AR.alloc([8, 4096], BF16)
    w2 = AR.alloc([32, 1024], BF16)
    fg = AR.alloc([1024], F32)
    b1 = b1p
    R.op('sp', f_dma(fg, bigbc_d[:, 2048:3072]), writes=['fg'], dma='fg')
    R.op('sp', f_dma(w1.rearrange("p a b -> p (a b)"), WB_w1), writes=['w1'], dma='w1')
    R.op('sp', f_dma(w2.rearrange("p a b -> p (a b)"), WB_w2), writes=['w2'], dma='w2')
    R.barrier()

    x1s = [AR.alloc([2, 1024], F32) for _ in range(2)]
    xn4 = [AR.alloc([1024], BF16) for _ in range(2)]
    h2T = [AR.alloc([8, 256], BF16) for _ in range(2)]
    rr = [AR.alloc([256], F32) for _ in range(2)]
    hidT = AR.alloc([32, 256], BF16)
    x2 = [AR.alloc([1024], F32) for _ in range(2)]
    outs = x2
    junk4 = AR.alloc([1024], BF16)

    pT = psb(0, BF16).rearrange("p (c n) -> p c n", c=8)
    pH = [psb(1), psb(2), psb(3)]
    pY2 = [psb(4), psb(5)]
    outkeys = []
    def F4(G, b):
        s = b % 2
        G.op('sp', f_dma(x1s[s], X1[b * 256:(b + 1) * 256, :].rearrange("(j p) f -> p j f", p=128)),
             writes=[('x1s', s, 0), ('x1s', s, 1)], dma=('x1s', s))
        for j in range(2):
            G.op('act', f_act(junk4, x1s[s][:, j, :], AF.Square, accum_out=st(9, j)), reads=[('x1s', s, j)], writes=[('ss2', j)])
            G.op('act', f_act(st(10, j), st(9, j), AF.Sqrt, scale=1.0 / D, bias=epsb[:]), reads=[('ss2', j)], writes=[('sd2', j)])
            G.op('dve', f_recip(st(11, j), st(10, j)), reads=[('sd2', j)], writes=[('r2', j)])
            G.op('dve', f_ts(xn4[j], x1s[s][:, j, :], st(11, j), ALU.mult), reads=[('x1s', s, j), ('r2', j)], writes=[('xn4', j)])
            for c in range(8):
                G.op('pe', f_tr(pT[:, c, :], xn4[j][:, c * 128:(c + 1) * 128], ident[:]), reads=[('xn4', j)], writes=['pT'])
            G.op('act', f_copy_act(h2T[s][:, :, j * 128:(j + 1) * 128], pT), reads=['pT'], writes=[('h2T', s, j)])

    def M1(G, b):
        s = b % 2
        for m in range(32):
            for c in range(8):
                G.op('pe', f_mm(pH[m % 3][:, 0:256], w1[:, c, m * 128:(m + 1) * 128], h2T[s][:, c, :], c == 0, c == 7),
                     reads=[('h2T', s, 0), ('h2T', s, 1)], writes=[('pH', m % 3)])
            G.op('act', f_act(rr[m % 2], pH[m % 3][:, 0:256], AF.Relu, bias=b1[:, m:m + 1]), reads=[('pH', m % 3)], writes=[('rr', m % 2)])
            G.op('dve' if m % 2 == 0 else 'pool', f_tt(hidT[:, m, :], rr[m % 2], rr[m % 2], ALU.mult), reads=[('rr', m % 2)], writes=[('hidT', m)])

    def M2(G, b):
        s = b % 2
        hkeys = [('hidT', m) for m in range(32)]
        for j in range(2):
            for half in range(2):
                pk = (j * 2 + half) % 2
                for m in range(32):
                    G.op('pe', f_mm(pY2[pk], hidT[:, m, j * 128:(j + 1) * 128], w2[:, m, half * 512:(half + 1) * 512], m == 0, m == 31),
                         reads=hkeys, writes=[('pY2', pk)])
                G.op('dve', f_tt(x2[j][:, half * 512:(half + 1) * 512], pY2[pk], x1s[s][:, j, half * 512:(half + 1) * 512], ALU.add),
                     reads=[('pY2', pk), ('x1s', s, j)], writes=[('x2', j, half)])
            G.op('act', f_act(junk4, x2[j], AF.Square, accum_out=st(12, j)), reads=[('x2', j, 0), ('x2', j, 1)], writes=[('ss3', j)])
            G.op('act', f_act(st(13, j), st(12, j), AF.Sqrt, scale=1.0 / D, bias=epsb[:]), reads=[('ss3', j)], writes=[('sd3', j)])
            G.op('dve', f_recip(st(14, j), st(13, j)), reads=[('sd3', j)], writes=[('r3', j)])
            G.op('dve', f_stt(outs[j], x2[j], st(14, j), fg, ALU.mult, ALU.mult), reads=[('x2', j, 0), ('x2', j, 1), ('r3', j), 'fg'],
                 writes=[('x2', j, 0), ('x2', j, 1)])
            row = (b * 2 + j) * 128
            G.op('pool', f_dma(out[row:row + 128, :], outs[j]), reads=[('x2', j, 0), ('x2', j, 1)], writes=[('out', b, j)], dma=('outst', j))
            outkeys.append(('out', b, j))

    g0 = Stage(); F4(g0, 0); emit_merged([g0])
    for b in range(32):
        stages = [Stage()]
        M1(stages[0], b)
        if b + 1 < 32:
            g1 = Stage(); F4(g1, b + 1); stages.append(g1)
        emit_merged(stages)
        g2 = Stage(); M2(g2, b); emit_merged([g2])
    R.op('sp', None, reads=outkeys)
    R.finalize()
    return nc


def f_copy_act(out, in_):
    return lambda e: e.activation(out=out, in_=in_, func=AF.Copy)


def _host_consts():
    bf = ml_dtypes.bfloat16
    ident = np.eye(128, dtype=np.float32).astype(bf)
    t = np.arange(L)
    row = (t // 64).astype(np.float32)
    col = (t % 64).astype(np.float32)
    inv_freq = (np.float32(10000.0) ** (-np.arange(0, 16, 2, dtype=np.float32) / np.float32(16))).astype(np.float32)
    ar = (row[:, None] * inv_freq).astype(np.float32)
    ac = (col[:, None] * inv_freq).astype(np.float32)
    cr, sr, cc, sc = np.cos(ar), np.sin(ar), np.cos(ac), np.sin(ac)
    cosF = np.concatenate([cr, cr, cc, cc], axis=1).astype(np.float32)
    sinS = np.concatenate([-sr, sr, -sc, sc], axis=1).astype(np.float32)
    cosF = np.ascontiguousarray(cosF.reshape(64, 128, 32).transpose(1, 0, 2).reshape(128, 64 * 32))
    sinS = np.ascontiguousarray(sinS.reshape(64, 128, 32).transpose(1, 0, 2).reshape(128, 64 * 32))
    band = np.zeros((128, 20, 128), np.float32)
    for g, w in enumerate((2, 4, 8, 16)):
        hw = w // 2
        for kind in range(5):
            T = {0: 5, 1: 5, 2: 5, 3: 0, 4: 63}[kind]
            dT = {0: -1, 1: 0, 2: 1, 3: 0, 4: 0}[kind]
            m = np.zeros((128, 128), np.float32)
            for tl in range(128):
                tg = T * 128 + tl
                lo = max(tg - hw, 0)
                hi = min(tg + hw, L)
                cnt = hi - lo
                for tp in range(lo, hi):
                    p = tp - (T + dT) * 128
                    if 0 <= p < 128:
                        m[p, tl] += 1.0 / cnt
                p = tg - (T + dT) * 128
                if 0 <= p < 128:
                    m[p, tl] -= 1.0
            band[:, g * 5 + kind, :] = m
    band = np.ascontiguousarray(band.reshape(128, 20 * 128)).astype(bf)
    return ident, cosF, sinS, band


_CACHE = {}


def prep_inputs(x, c, ctx, c_ctx, w_ada, b_ada, norm1_g, w_in, q_norm_g, kv_norm_g, w_uq, w_ukv,
                w_br_mla, pool_w, pool_scale, w_br_pool, w_out, norm2_g, w_mlp1, w_mlp2, final_g):
    f = lambda a: np.ascontiguousarray(np.asarray(a, dtype=np.float32))
    x, c, ctx, c_ctx = f(x), f(c), f(ctx), f(c_ctx)
    w_ada, b_ada, w_in, w_uq, w_ukv = f(w_ada)[0], f(b_ada)[0], f(w_in)[0], f(w_uq)[0], f(w_ukv)[0]
    w_br_mla, pool_w, w_br_pool, w_out = f(w_br_mla)[0], f(pool_w)[0], f(w_br_pool)[0], f(w_out)[0]
    w_mlp1, w_mlp2 = f(w_mlp1)[0], f(w_mlp2)[0]
    norm1_g, norm2_g, q_norm_g, kv_norm_g, pool_scale, final_g = (f(norm1_g)[0], f(norm2_g)[0], f(q_norm_g)[0],
                                                                   f(kv_norm_g)[0], f(pool_scale)[0], f(final_g))
    if 'consts' not in _CACHE:
        _CACHE['consts'] = _host_consts()
    ident, cosF, sinS, band = _CACHE['consts']
    col = lambda v: v.reshape(-1, 128).T
    bigbc = np.ascontiguousarray(np.broadcast_to(
        np.concatenate([b_ada[2 * D:3 * D], b_ada[5 * D:6 * D], final_g])[None, :], (128, 3 * D)))
    shared = dict(w_ada=w_ada, w_in=w_in, w_uq=w_uq, w_ukv=w_ukv, w_br_mla=w_br_mla, pool_w=pool_w,
                  w_br_pool=w_br_pool, w_out=w_out, w_mlp1=w_mlp1, w_mlp2=w_mlp2, ident=ident,
                  cosF=cosF, sinS=sinS, band=band, bigbc=bigbc)
    in_maps = []
    for b in range(x.shape[0]):
        ccol = np.stack([col(c[b]), col(c_ctx)], axis=2).reshape(128, 16)
        smallf = np.ascontiguousarray(np.concatenate(
            [ccol, col(b_ada), col(norm1_g), col(norm2_g), col(q_norm_g), col(kv_norm_g), col(pool_scale)], axis=1).astype(np.float32))
        assert smallf.shape == (128, 89)
        m = dict(shared)
        m.update(x=x[b], ctx=ctx[b], smallf=smallf)
        in_maps.append(m)
    return in_maps


def kernel(**inputs):
    in_maps = prep_inputs(**inputs)
    if 'nc' not in _CACHE:
        _CACHE['nc'] = build_program()
    nc = _CACHE['nc']
    res = run_bass_kernel_spmd(nc, in_maps, core_ids=list(range(8)))
    return np.stack([np.asarray(r["out"], dtype=np.float32) for r in res.results], axis=0)
```

```python
import numpy as np
import ml_dtypes
import concourse.bass as bass
import concourse.mybir as mybir
from concourse.bass_utils import run_bass_kernel_spmd

F32 = mybir.dt.float32
BF16 = mybir.dt.bfloat16
AF = mybir.ActivationFunctionType
ALU = mybir.AluOpType

D = 1024
L = 8192
CTX = 256
NKT = (L + CTX) // 128
H = 8
EPS = 1e-6
SCALE = 96.0 ** -0.5
ENGS = ['pe', 'act', 'dve', 'pool', 'sp']


class Rec:
    def __init__(self, nc):
        self.nc = nc
        self.ops = []
        self.lastw = {}
        self.rd_eng = {}
        self.rd_dma = {}
        self.bank = {}
        self.bank_last = {}

    def op(self, eng, fn, reads=(), writes=(), dma=None):
        j = len(self.ops)
        deps = {}
        bset = set()
        for k in list(reads) + list(writes):
            if k in self.bank:
                bv = self.bank[k]
                bset.update(bv if isinstance(bv, tuple) else (bv,))
        for bnk in bset:
            la = self.bank_last.setdefault(bnk, {})
            for e2, i in la.items():
                if e2 != eng:
                    deps.setdefault(i, False)
            la[eng] = j
        for k in reads:
            i = self.lastw.get(k)
            if i is not None:
                deps[i] = True
        for k in writes:
            for i in self.rd_eng.get(k, {}).values():
                deps.setdefault(i, False)
            for i in self.rd_dma.get(k, ()):
                deps.setdefault(i, False)
            i = self.lastw.get(k)
            if i is not None:
                deps.setdefault(i, False)
        for k in reads:
            if dma is None:
                self.rd_eng.setdefault(k, {})[eng] = j
            else:
                self.rd_dma.setdefault(k, []).append(j)
        for k in writes:
            self.lastw[k] = j
            self.rd_eng[k] = {}
            self.rd_dma[k] = []
        keep = []
        for i, raw in deps.items():
            oi = self.ops[i]
            if oi['dma'] is None and dma is None and oi['eng'] == eng:
                if not raw or eng == 'pe':
                    continue
            keep.append(i)
        self.ops.append(dict(eng=eng, fn=fn, dma=dma, deps=keep, sig=False))
        return j

    def barrier(self):
        last = {}
        for idx, o in enumerate(self.ops):
            if o['fn'] is None:
                continue
            if o['dma'] is not None:
                last[('d', o['dma'])] = idx
            else:
                last[('e', o['eng'])] = idx
        deps = list(last.values())
        for e in ENGS:
            self.ops.append(dict(eng=e, fn=None, dma=None, deps=list(deps), sig=False))
        self.lastw.clear()
        self.rd_eng.clear()
        self.rd_dma.clear()
        self.bank_last.clear()

    def finalize(self):
        nc = self.nc
        ops = self.ops
        for o in ops:
            for i in o['deps']:
                ops[i]['sig'] = True
        esem = {e: nc.alloc_semaphore("s_" + e) for e in ENGS}
        dsem = {}
        ecnt = {e: 0 for e in ENGS}
        dcnt = {}
        for o in ops:
            if o['fn'] is None:
                continue
            if o['dma'] is not None:
                k = o['dma']
                if k not in dsem:
                    dsem[k] = nc.alloc_semaphore("d_%d" % len(dsem))
                    dcnt[k] = 0
                dcnt[k] += 16
                o['sem'] = dsem[k]
                o['semid'] = ('d', k)
                o['val'] = dcnt[k]
            elif o['sig']:
                ecnt[o['eng']] += 1
                o['sem'] = esem[o['eng']]
                o['semid'] = ('e', o['eng'])
                o['val'] = ecnt[o['eng']]
        streams = {e: [] for e in ENGS}
        for idx, o in enumerate(ops):
            streams[o['eng']].append(idx)

        def run(eng_name, engine):
            seen = {}
            for idx in streams[eng_name]:
                o = ops[idx]
                need = {}
                for i in o['deps']:
                    d = ops[i]
                    sid, v = d['semid'], d['val']
                    if seen.get(sid, 0) >= v:
                        continue
                    if need.get(sid, (None, 0))[1] < v:
                        need[sid] = (d['sem'], v)
                for sid, (s, v) in need.items():
                    seen[sid] = v
                    engine.wait_ge(s, v)
                if o['fn'] is None:
                    continue
                ins = o['fn'](engine)
                if o['dma'] is not None:
                    ins.then_inc(o['sem'], 16)
                elif o['sig']:
                    ins.then_inc(o['sem'], 1)

        with nc.Block() as block:
            @block.tensor
            def _(e):
                run('pe', e)

            @block.scalar
            def _(e):
                run('act', e)

            @block.vector
            def _(e):
                run('dve', e)

            @block.gpsimd
            def _(e):
                run('pool', e)

            @block.sync
            def _(e):
                run('sp', e)


class Arena:
    def __init__(self, nc, nbytes):
        self.t = nc.alloc_sbuf_tensor("arena", [128, nbytes // 2], BF16)
        self.cap = nbytes
        self.off = 0

    def mark(self):
        return self.off

    def release(self, m):
        self.off = m

    def alloc(self, shape, dtype, parts=128):
        esz = 4 if dtype == F32 else 2
        n = 1
        for s in shape:
            n *= s
        nb = n * esz
        off = (self.off + 63) // 64 * 64
        assert off + nb <= self.cap, ("arena overflow", off, nb, self.cap)
        self.off = off + nb
        ap = self.t[0:parts, off // 2:(off + nb) // 2]
        if dtype == F32:
            ap = ap.bitcast(F32)
        if len(shape) > 1:
            names = ["a%d" % i for i in range(len(shape))]
            pat = "p (" + " ".join(names) + ") -> p " + " ".join(names)
            ap = ap.rearrange(pat, **{nm: s for nm, s in zip(names[1:], shape[1:])})
        return ap


def f_dma(out, in_):
    return lambda e: e.dma_start(out=out, in_=in_)


def f_mm(out, lhsT, rhs, start, stop):
    return lambda e: e.matmul(out, lhsT=lhsT, rhs=rhs, start=start, stop=stop)


def f_tr(out, in_, ident):
    return lambda e: e.transpose(out=out, in_=in_, identity=ident)


def f_act(out, in_, func, **kw):
    return lambda e: e.activation(out=out, in_=in_, func=func, **kw)


def f_tt(out, in0, in1, op):
    return lambda e: e.tensor_tensor(out=out, in0=in0, in1=in1, op=op)


def f_ts(out, in0, s1, op0, s2=None, op1=None):
    if op1 is None:
        return lambda e: e.tensor_scalar(out=out, in0=in0, scalar1=s1, scalar2=None, op0=op0)
    return lambda e: e.tensor_scalar(out=out, in0=in0, scalar1=s1, scalar2=s2, op0=op0, op1=op1)


def f_stt(out, in0, scalar, in1, op0, op1):
    return lambda e: e.scalar_tensor_tensor(out=out, in0=in0, scalar=scalar, in1=in1, op0=op0, op1=op1)


def f_copy(out, in_):
    return lambda e: e.tensor_copy(out=out, in_=in_)


def f_recip(out, in_):
    return lambda e: e.reciprocal(out=out, in_=in_)


def f_memset(out, v):
    return lambda e: e.memset(out, v)


def cast_scaled(R, k, out, in_, scal, reads, writes):
    eng = ('dve', 'act', 'pool')[k % 3]
    if scal is None:
        if eng == 'act':
            R.op('act', f_act(out, in_, AF.Copy), reads, writes)
        else:
            R.op(eng, f_copy(out, in_), reads, writes)
    elif eng == 'act':
        R.op('act', f_act(out, in_, AF.Copy, scale=scal), reads, writes)
    elif eng == 'dve':
        R.op('dve', f_ts(out, in_, scal, ALU.mult), reads, writes)
    else:
        R.op('pool', f_ts(out, in_, scal, ALU.mult, 1.0, ALU.mult), reads, writes)


def build_program(stop=9, dbg=False):
    nc = bass.Bass("TRN2", target_bir_lowering=False)
    skind = "ExternalOutput" if dbg else "Internal"

    def din(name, shape, dt=F32):
        return nc.dram_tensor(name, shape, dt, kind="ExternalInput").ap()

    x = din("x", [L, D])
    ctx = din("ctx", [CTX, D])
    smallf = din("smallf", [128, 89])
    bigbc_d = din("bigbc", [128, 3072])
    w_ada = din("w_ada", [D, 6 * D])
    w_in = din("w_in", [D, 3232])
    w_uq = din("w_uq", [384, 768])
    w_ukv = din("w_ukv", [256, 1024])
    w_br_mla = din("w_br_mla", [512, 1024])
    pool_w = din("pool_w", [4, 128, 128])
    w_br_pool = din("w_br_pool", [512, 1024])
    w_out = din("w_out", [D, D])
    w_mlp1 = din("w_mlp1", [D, 4096])
    w_mlp2 = din("w_mlp2", [4096, D])
    ident_d = din("ident", [128, 128], BF16)
    cosF_d = din("cosF", [128, 64 * 32])
    sinS_d = din("sinS", [128, 64 * 32])
    band_d = din("band", [128, 20 * 128], BF16)
    out = nc.dram_tensor("out", [L, D], F32, kind="ExternalOutput").ap()

    QT = nc.dram_tensor("QT", [H, 96, L], BF16, kind=skind).ap()
    KT = nc.dram_tensor("KT", [H, 96, NKT * 128], BF16, kind=skind).ap()
    Vs = nc.dram_tensor("Vs", [2, 128, NKT, 4, 66], BF16, kind=skind).ap()
    OT = nc.dram_tensor("OT", [512, L], BF16, kind=skind).ap()
    U = nc.dram_tensor("U", [L, 512], BF16, kind=skind).ap()
    X1 = nc.dram_tensor("X1", [L, D], F32, kind=skind).ap()
    WB_wg = nc.dram_tensor("WB_wg", [128, 8 * 2048], BF16, kind="Internal").ap()
    WB_wbm = nc.dram_tensor("WB_wbm", [128, 4 * 1024], BF16, kind="Internal").ap()
    WB_wbp = nc.dram_tensor("WB_wbp", [128, 4 * 1024], BF16, kind="Internal").ap()
    WB_pw = nc.dram_tensor("WB_pw", [128, 4 * 128], BF16, kind="Internal").ap()
    WB_wo = nc.dram_tensor("WB_wo", [128, 8 * 1024], BF16, kind="Internal").ap()
    WB_w1 = nc.dram_tensor("WB_w1", [128, 8 * 4096], BF16, kind="Internal").ap()
    WB_w2 = nc.dram_tensor("WB_w2", [128, 32 * 1024], BF16, kind="Internal").ap()

    R = Rec(nc)

    def gt(name, shape, dt=F32):
        return nc.alloc_sbuf_tensor(name, shape, dt)

    sm = gt("sm", [128, 89])
    ident = gt("ident_s", [128, 128], BF16)
    ones_f = gt("ones_f", [128, 128])
    epsb = gt("epsb", [128, 1])
    sil = gt("sil", [128, 8, 2])
    vcol = gt("vcol", [128, 4, 8, 2])
    gtbc = gt("gtbc", [128, 2, 1024])
    A1 = gt("A1", [128, 8])
    cA1 = gt("cA1", [128, 8])
    A2 = gt("A2", [128, 8])
    rstd1 = gt("rstd1", [128, 64])
    stat = gt("stat", [128, 64])
    bias_g = gt("bias_g", [128, 16])
    b1p = gt("b1p", [128, 32])
    AR = Arena(nc, 196 * 1024)

    PSall = nc.alloc_psum_tensor("psall", [128, 8, 512], F32)

    def psb(i, dt=F32):
        ap = PSall[:, i, :]
        if dt == BF16:
            ap = ap.bitcast(BF16)
        return ap

    R.bank = {('pV', 0): 0, ('pV', 1): 1, ('pR', 0): 2, ('pR', 1): 3, 'pBias0': 4, 'pBias1': 5}
    R.op('sp', f_dma(sm[:], smallf), writes=['sm'], dma='sm')
    R.op('sp', f_dma(ident[:], ident_d), writes=['ident'], dma='ident')
    R.op('dve', f_memset(ones_f[:], 1.0), writes=['ones'])
    R.op('dve', f_memset(epsb[:], EPS), writes=['epsb'])
    R.op('act', f_act(sil[:].rearrange("p c v -> p (c v)"), sm[:, 0:16], AF.Silu), reads=['sm'], writes=['sil'])

    w1x = AR.alloc([8, 1184], BF16)
    w1c = AR.alloc([8, 288], BF16)
    wuq = AR.alloc([3, 768], BF16)
    wukv = AR.alloc([2, 1024], BF16)
    bias1x = AR.alloc([1184], F32)
    bias1c = AR.alloc([288], F32)
    cosF = AR.alloc([64, 32], F32)
    sinS = AR.alloc([64, 32], F32)
    m_p1 = AR.mark()
    stgh = {'b': [AR.alloc([8192], F32) for _ in range(2)]}
    bigbc = AR.alloc([2048], F32)
    R.op('sp', f_dma(bigbc, bigbc_d[:, 0:2048]), writes=['bigbc'], dma='bigbc')
    sil_rep = AR.alloc([8, 128], F32)
    sh1_rep = AR.alloc([8, 128], F32)
    csh1_rep = AR.alloc([8, 128], F32)

    R.op('sp', f_dma(cosF.rearrange("p a b -> p (a b)"), cosF_d), writes=['cosF'], dma='cosF')
    R.op('sp', f_dma(sinS.rearrange("p a b -> p (a b)"), sinS_d), writes=['sinS'], dma='sinS')

    for c in range(8):
        R.op('dve', f_ts(sil_rep[:, c, :], ones_f[:], sil[:, c, 0:1], ALU.mult),
             reads=['ones', 'sil'], writes=[('silrep', c)])

    stg_n = [0]

    def stage_load(src_ap, kc, ncols):
        i = stg_n[0] % 2
        stg_n[0] += 1
        view = stgh['b'][i][:, 0:kc * ncols].rearrange("p (c n) -> p c n", c=kc)
        R.op('sp', f_dma(view, src_ap), writes=[('stg', i)], dma=('stg', i))
        return view, ('stg', i)

    pV = [psb(0)[:, 0:16].rearrange("p (m v) -> p m v", v=2), psb(1)[:, 0:16].rearrange("p (m v) -> p m v", v=2)]
    pR = [psb(2), psb(3)]
    vmap = {0: 0, 1: 1, 3: 2, 4: 3}
    for v in range(6):
        view, skey = stage_load(w_ada[:, v * 1024:(v + 1) * 1024].rearrange("(c p) n -> p c n", p=128), 8, 1024)
        if v in vmap:
            vi = vmap[v]
            pk = ('pV', vi % 2)
            for m in range(8):
                for c in range(8):
                    R.op('pe', f_mm(pV[vi % 2][:, m, :], view[:, c, m * 128:(m + 1) * 128], sil[:, c, :], c == 0, c == 7),
                         reads=[skey, 'sil'], writes=[pk])
            R.op('dve', f_tt(vcol[:, vi], pV[vi % 2],
                             sm[:, 16 + v * 8:16 + (v + 1) * 8].unsqueeze(2).broadcast_to([128, 8, 2]), ALU.add),
                 reads=[pk, 'sm'], writes=[('vcol', vi)])
        else:
            gi = 0 if v == 2 else 1
            for half in range(2):
                pk = ('pR', half)
                for c in range(8):
                    R.op('pe', f_mm(pR[half], sil_rep[:, c, :], view[:, c, half * 512:(half + 1) * 512], c == 0, c == 7),
                         reads=[skey, ('silrep', c)], writes=[pk])
                R.op('dve', f_tt(gtbc[:, gi, half * 512:(half + 1) * 512], pR[half],
                                 bigbc[:, gi * 1024 + half * 512:gi * 1024 + (half + 1) * 512], ALU.add),
                     reads=[pk, 'bigbc'], writes=[('gtbc', gi)])

    R.op('dve', f_stt(A1[:], vcol[:, 1, :, 0], 1.0, sm[:, 64:72], ALU.add, ALU.mult), reads=[('vcol', 1), 'sm'], writes=['A1'])
    R.op('dve', f_stt(cA1[:], vcol[:, 1, :, 1], 1.0, sm[:, 64:72], ALU.add, ALU.mult), reads=[('vcol', 1), 'sm'], writes=['cA1'])
    R.op('dve', f_stt(A2[:], vcol[:, 3, :, 0], 1.0, sm[:, 72:80], ALU.add, ALU.mult), reads=[('vcol', 3), 'sm'], writes=['A2'])
    for c in range(8):
        R.op('dve', f_ts(sh1_rep[:, c, :], ones_f[:], vcol[:, 0, c, 0:1], ALU.mult), reads=['ones', ('vcol', 0)], writes=[('sh1rep', c)])
        R.op('dve', f_ts(csh1_rep[:, c, :], ones_f[:], vcol[:, 0, c, 1:2], ALU.mult), reads=['ones', ('vcol', 0)], writes=[('csh1rep', c)])

    kk = 0
    pBias = [psb(4), psb(5)]
    for pi, (n0, n1) in enumerate([(0, 384), (384, 672), (672, 1184)]):
        n = n1 - n0
        view, skey = stage_load(w_in[:, n0:n1].rearrange("(c p) n -> p c n", p=128), 8, n)
        for c in range(8):
            cast_scaled(R, kk, w1x[:, c, n0:n1], view[:, c, :], A1[:, c:c + 1], [skey, 'A1'], [('w1x', pi, c)])
            kk += 1
        for c in range(8):
            R.op('pe', f_mm(pBias[0][:, 0:n], sh1_rep[:, c, :], view[:, c, :], c == 0, c == 7),
                 reads=[skey, ('sh1rep', c)], writes=['pBias0'])
        R.op('dve', f_copy(bias1x[:, n0:n1], pBias[0][:, 0:n]), reads=['pBias0'], writes=[('bias1x', pi)])
        if pi == 1:
            for c in range(8):
                cast_scaled(R, kk, w1c[:, c, :], view[:, c, :], cA1[:, c:c + 1], [skey, 'cA1'], [('w1c', c)])
                kk += 1
            for c in range(8):
                R.op('pe', f_mm(pBias[1][:, 0:n], csh1_rep[:, c, :], view[:, c, :], c == 0, c == 7),
                     reads=[skey, ('csh1rep', c)], writes=['pBias1'])
            R.op('dve', f_copy(bias1c[:], pBias[1][:, 0:n]), reads=['pBias1'], writes=['bias1c'])
    view, skey = stage_load(w_uq.rearrange("(c p) n -> p c n", p=128), 3, 768)
    for c in range(3):
        cast_scaled(R, kk, wuq[:, c, :], view[:, c, :], sm[:, 80 + c:81 + c], [skey, 'sm'], [('wuq', c)])
        kk += 1
    view, skey = stage_load(w_ukv.rearrange("(c p) n -> p c n", p=128), 2, 1024)
    for c in range(2):
        cast_scaled(R, kk, wukv[:, c, :], view[:, c, :], sm[:, 83 + c:84 + c], [skey, 'sm'], [('wukv', c)])
        kk += 1

    if dbg:
        dbg0 = nc.dram_tensor("dbg0", [128, 4 * 16 + 2048 + 24 + 1184 + 288], F32, kind="ExternalOutput").ap()
        R.op('sp', f_dma(dbg0[:, 0:64], vcol[:].rearrange("p a b c -> p (a b c)")), reads=[('vcol', i) for i in range(4)], writes=['dbg0a'], dma='dbg0a')
        R.op('sp', f_dma(dbg0[:, 64:2112], gtbc[:].rearrange("p a b -> p (a b)")), reads=[('gtbc', 0), ('gtbc', 1)], writes=['dbg0b'], dma='dbg0b')
        R.op('sp', f_dma(dbg0[:, 2112:2120], A1[:]), reads=['A1'], writes=['dbg0c'], dma='dbg0c')
        R.op('sp', f_dma(dbg0[:, 2120:2128], cA1[:]), reads=['cA1'], writes=['dbg0d'], dma='dbg0d')
        R.op('sp', f_dma(dbg0[:, 2128:2136], A2[:]), reads=['A2'], writes=['dbg0e'], dma='dbg0e')
        R.op('sp', f_dma(dbg0[:, 2136:2136 + 1184], bias1x), reads=[('bias1x', i) for i in range(3)], writes=['dbg0f'], dma='dbg0f')
        R.op('sp', f_dma(dbg0[:, 2136 + 1184:2136 + 1184 + 288], bias1c), reads=['bias1c'], writes=['dbg0g'], dma='dbg0g')
        dbg1 = nc.dram_tensor("dbg1", [128, 8 * 1184], BF16, kind="ExternalOutput").ap()
        R.op('sp', f_dma(dbg1, w1x.rearrange("p a b -> p (a b)")), reads=[('w1x', pi, c) for pi in range(3) for c in range(8)], writes=['dbg1'], dma='dbg1')
    R.barrier()
    if stop == 0:
        R.finalize()
        return nc
    AR.release(m_p1)

    R.bank = {'pT': 0, 'psA': 1, 'psB': 2, 'psC': 3, 'pT2q': 4, 'pT2k': 4, 'psWk': 5, 'psWq': 6, 'pXTk': 7, 'pXTq': 7}
    NX, NB = 4, 4
    xs = [AR.alloc([1024], F32) for _ in range(NX)]
    junk = [AR.alloc([1024], BF16) for _ in range(3)]
    xn = [AR.alloc([1024], BF16) for _ in range(NB)]
    xT = [AR.alloc([8, 128], BF16) for _ in range(NB)]
    u_sb = [AR.alloc([512], BF16) for _ in range(NB)]
    cqn = [AR.alloc([384], BF16) for _ in range(NB)]
    cqT = [AR.alloc([3, 128], BF16) for _ in range(NB)]
    ckvn = [AR.alloc([256], BF16) for _ in range(NB)]
    ckvT = [AR.alloc([2, 128], BF16) for _ in range(NB)]
    krs_sb = [AR.alloc([32], F32) for _ in range(NB)]
    t1 = [AR.alloc([8, 32], F32) for _ in range(NB)]
    t2 = [AR.alloc([8, 32], F32) for _ in range(NB)]
    qf = [AR.alloc([8, 96], BF16) for _ in range(NB)]
    kf = [AR.alloc([8, 96], BF16) for _ in range(NB)]
    vaug = [AR.alloc([8, 66], BF16) for _ in range(NB)]
    qT_sb = [AR.alloc([8, 128], BF16) for _ in range(NB)]
    kT_sb = [AR.alloc([8, 128], BF16) for _ in range(NB)]
    t1k = [AR.alloc([32], F32) for _ in range(NB)]
    t2k = [AR.alloc([32], F32) for _ in range(NB)]
    kr = [AR.alloc([32], F32) for _ in range(NB)]
    ones2 = AR.alloc([128], BF16)
    bias2x = AR.alloc([1184], BF16)
    bias2c = AR.alloc([288], BF16)
    bhi32 = AR.alloc([1184], BF16)
    btmp = AR.alloc([1184], F32)
    R.op('dve', f_memset(ones2, 0.0), writes=['ones2'])
    R.op('dve', f_memset(ones2[0:1], 1.0), writes=['ones2'])
    R.op('dve', f_memset(ones2[32:33], 1.0), writes=['ones2'])
    for (b2, src_, n_, kn) in [(bias2x, bias1x, 1184, 'x'), (bias2c, bias1c, 288, 'c')]:
        R.op('dve', f_memset(b2, 0.0), writes=[('b2', kn)])
        R.op('dve', f_copy(b2[0:1], src_[0:1]), writes=[('b2', kn)])
        R.op('dve', f_copy(bhi32[32:33, 0:n_], src_[32:33]), writes=[('bhi32', kn)])
        R.op('dve', f_tt(btmp[32:33, 0:n_], src_[32:33], bhi32[32:33, 0:n_], ALU.subtract), reads=[('bhi32', kn)], writes=[('btmp', kn)])
        R.op('dve', f_copy(b2[32:33], btmp[32:33, 0:n_]), reads=[('btmp', kn)], writes=[('b2', kn)])
    for s in range(NB):
        R.op('dve', f_memset(vaug[s], 1.0), writes=[('vaug', s)])

    pT = psb(0, BF16).rearrange("p (c n) -> p c n", c=8)
    psA, psB, psC = psb(1), psb(2), psb(3)
    pT2 = psb(4, BF16).rearrange("p (c n) -> p c n", c=8)
    psWk = psb(5).rearrange("p (h d) -> p h d", d=128)
    psWq = psb(6)[:, 0:384].rearrange("p (h d) -> p h d", d=96)
    pXT = psb(7, BF16).rearrange("p (c n) -> p c n", c=8)
    QTv = QT.rearrange("h d t -> d h t")
    KTv = KT.rearrange("h d t -> d h t")

    def st(col, s):
        return stat[:, col * 4 + s:col * 4 + s + 1]

    class Stage:
        def __init__(self):
            self.l = []

        def op(self, eng, fn, reads=(), writes=(), dma=None):
            self.l.append((eng, fn, reads, writes, dma))

    def emit_merged(stages, group_pe=False):
        items = []
        for si, sg in enumerate(stages):
            n = len(sg.l)
            k = 0
            while k < n:
                k1 = k + 1
                if group_pe and sg.l[k][0] == 'pe':
                    while k1 < n and sg.l[k1][0] == 'pe':
                        k1 += 1
                items.append(((k + 0.5) / n, si, k, sg.l[k:k1]))
                k = k1
        items.sort(key=lambda z: (z[0], z[1]))
        for _, _, _, grp in items:
            for o in grp:
                R.op(o[0], o[1], reads=o[2], writes=o[3], dma=o[4])

    def tile_src(t):
        return ctx[t * 128:(t + 1) * 128, :] if t < 2 else x[(t - 2) * 128:(t - 1) * 128, :]

    def SL(G, t):
        G.op('sp', f_dma(xs[t % NX], tile_src(t)), writes=[('xs', t % NX)], dma=('xs', t % NX))

    def S1(G, t):
        is_ctx = t < 2
        xi = t - 2
        sx, s, s4_ = t % NX, t % NB, t % 4
        G.op('act', f_act(junk[0], xs[sx], AF.Square, accum_out=st(0, s4_)), reads=[('xs', sx)], writes=[('ss', s4_)])
        G.op('act', f_act(st(1, s4_), st(0, s4_), AF.Sqrt, scale=1.0 / D, bias=epsb[:]), reads=[('ss', s4_)], writes=[('sd', s4_)])
        if is_ctx:
            rst, rkey = st(2, s4_), ('rstc', s4_)
        else:
            rst, rkey = rstd1[:, xi:xi + 1], ('rstd1', xi)
        G.op('dve', f_recip(rst, st(1, s4_)), reads=[('sd', s4_)], writes=[rkey])
        G.op('dve', f_ts(xn[s], xs[sx], rst, ALU.mult), reads=[('xs', sx), rkey], writes=[('xn', s)])
        for c in range(8):
            G.op('pe', f_tr(pT[:, c, :], xn[s][:, c * 128:(c + 1) * 128], ident[:]), reads=[('xn', s)], writes=['pT'])
        G.op('act', f_copy_act(xT[s].rearrange("p c n -> p (c n)"), psb(0, BF16)), reads=['pT'], writes=[('xT', s)])

    def S1b(G, t):
        is_ctx = t < 2
        xi = t - 2
        sx, s, s4_ = t % NX, t % NB, t % 4
        if not is_ctx:
            groups = [(psA, 'psA', 0, 384), (psB, 'psB', 384, 672), (psC, 'psC', 672, 1184)]
            for (ps, key, n0, n1) in groups:
                for c in range(8):
                    G.op('pe', f_mm(ps[:, 0:n1 - n0], xT[s][:, c, :], w1x[:, c, n0:n1], c == 0, False), reads=[('xT', s)], writes=[key])
                G.op('pe', f_mm(ps[:, 0:n1 - n0], ones2, bias2x[:, n0:n1], False, True), writes=[key])
        else:
            for c in range(8):
                G.op('pe', f_mm(psB[:, 0:288], xT[s][:, c, :], w1c[:, c, :], c == 0, False), reads=[('xT', s)], writes=['psB'])
            G.op('pe', f_mm(psB[:, 0:288], ones2, bias2c, False, True), writes=['psB'])
        G.op('act', f_act(junk[0][:, 0:256], psB[:, 0:256], AF.Square, accum_out=st(3, s4_)), reads=['psB'], writes=[('sskv', s4_)])
        G.op('act', f_act(st(4, s4_), st(3, s4_), AF.Sqrt, scale=1.0 / 256, bias=epsb[:]), reads=[('sskv', s4_)], writes=[('sdkv', s4_)])
        G.op('dve', f_recip(st(5, s4_), st(4, s4_)), reads=[('sdkv', s4_)], writes=[('rkv', s4_)])
        G.op('dve', f_ts(ckvn[s], psB[:, 0:256], st(5, s4_), ALU.mult), reads=['psB', ('rkv', s4_)], writes=[('ckvn', s)])
        G.op('dve', f_copy(krs_sb[s], psB[:, 256:288]), reads=['psB'], writes=[('krs', s)])
        if not is_ctx:
            G.op('act', f_act(junk[0][:, 0:384], psA[:, 0:384], AF.Square, accum_out=st(6, s4_)), reads=['psA'], writes=[('ssq', s4_)])
            G.op('act', f_act(st(7, s4_), st(6, s4_), AF.Sqrt, scale=1.0 / 384, bias=epsb[:]), reads=[('ssq', s4_)], writes=[('sdq', s4_)])
            G.op('dve', f_recip(st(8, s4_), st(7, s4_)), reads=[('sdq', s4_)], writes=[('rq', s4_)])
            G.op('dve', f_ts(cqn[s], psA[:, 0:384], st(8, s4_), ALU.mult), reads=['psA', ('rq', s4_)], writes=[('cqn', s)])
            G.op('act', f_copy_act(u_sb[s], psC[:, 0:512]), reads=['psC'], writes=[('u', s)])
            G.op('pool', f_dma(U[xi * 128:(xi + 1) * 128, :], u_sb[s]), reads=[('u', s)], writes=[('U', xi)], dma=('ust', s))

    def S2(G, t):
        is_ctx = t < 2
        xi = t - 2
        s = t % NB
        for c in range(2):
            G.op('pe', f_tr(pT2[:, 3 + c, :], ckvn[s][:, c * 128:(c + 1) * 128], ident[:]), reads=[('ckvn', s)], writes=['pT2k'])
        G.op('dve', f_copy(ckvT[s], pT2[:, 3:5, :]), reads=['pT2k'], writes=[('ckvT', s)])
        if is_ctx:
            krsrc, krkey = krs_sb[s], ('krs', s)
        else:
            cos_t = cosF[:, xi, :]
            sin_t = sinS[:, xi, :]
            G.op('pool', f_tt(t1k[s], krs_sb[s], cos_t, ALU.mult), reads=[('krs', s)], writes=[('t1k', s)])
            kv4 = krs_sb[s].rearrange("p (a f i) -> p a f i", a=2, f=2)
            o4 = t2k[s].rearrange("p (a f i) -> p a f i", a=2, f=2)
            s4 = sin_t.rearrange("p (a f i) -> p a f i", a=2, f=2)
            for f in range(2):
                G.op('pool', f_tt(o4[:, :, f, :], kv4[:, :, 1 - f, :], s4[:, :, f, :], ALU.mult), reads=[('krs', s)], writes=[('t2k', s, f)])
            G.op('pool', f_tt(kr[s], t1k[s], t2k[s], ALU.add), reads=[('t1k', s), ('t2k', s, 0), ('t2k', s, 1)], writes=[('kr', s)])
            krsrc, krkey = kr[s], ('kr', s)
        G.op('pool', f_copy(kf[s][:, :, 64:96], krsrc.unsqueeze(1).broadcast_to([128, 8, 32])), reads=[krkey], writes=[('kfr', s)])
        for hb in range(2):
            hs = slice(hb * 4, (hb + 1) * 4)
            for c in range(2):
                G.op('pe', f_mm(psb(5), ckvT[s][:, c, :], wukv[:, c, hb * 512:(hb + 1) * 512], c == 0, c == 1), reads=[('ckvT', s)], writes=['psWk'])
            G.op('act', f_copy_act(kf[s][:, hs, 0:64], psWk[:, :, 0:64]), reads=['psWk'], writes=[('kfn', s, hb)])
            G.op('dve', f_copy(vaug[s][:, hs, 0:64], psWk[:, :, 64:128]), reads=['psWk'], writes=[('vaug', s, hb)])
        for hb in range(2):
            for h4 in range(4):
                h = hb * 4 + h4
                G.op('pe', f_tr(pXT[0:96, h4, :], kf[s][:, h, :], ident[:]), reads=[('kfr', s), ('kfn', s, hb)], writes=['pXTk'])
            G.op('dve' if hb == 0 else 'act', (f_copy if hb == 0 else f_copy_act)(kT_sb[s][0:96, hb * 4:(hb + 1) * 4, :], pXT[0:96, 0:4, :]),
                 reads=['pXTk'], writes=[('kT', s, hb)])
        G.op('sp', f_dma(KTv[:, :, t * 128:(t + 1) * 128], kT_sb[s][0:96]), reads=[('kT', s, 0), ('kT', s, 1)], writes=[('KT', t)], dma=('ktst', s))
        G.op('pool', f_dma(Vs[:, :, t].rearrange("g p hh e -> p g hh e"), vaug[s].rearrange("p (g hh) e -> p g hh e", g=2)),
             reads=[('vaug', s, 0), ('vaug', s, 1)], writes=[('Vs', t)], dma=('vst', s))

    def S3(G, t):
        xi = t - 2
        s = t % NB
        cos_t = cosF[:, xi, :]
        sin_t = sinS[:, xi, :]
        s4 = sin_t.rearrange("p (a f i) -> p a f i", a=2, f=2)
        for c in range(3):
            G.op('pe', f_tr(pT2[:, c, :], cqn[s][:, c * 128:(c + 1) * 128], ident[:]), reads=[('cqn', s)], writes=['pT2q'])
        G.op('dve', f_copy(cqT[s], pT2[:, 0:3, :]), reads=['pT2q'], writes=[('cqT', s)])
        for hb in range(2):
            hs = slice(hb * 4, (hb + 1) * 4)
            for c in range(3):
                G.op('pe', f_mm(psb(6)[:, 0:384], cqT[s][:, c, :], wuq[:, c, hb * 384:(hb + 1) * 384], c == 0, c == 2),
                     reads=[('cqT', s)], writes=['psWq'])
            qr = psWq[:, :, 64:96]
            G.op('dve', f_tt(t1[s][:, hs, :], qr, cos_t.unsqueeze(1).broadcast_to([128, 4, 32]), ALU.mult), reads=['psWq'], writes=[('t1', s, hb)])
            q5 = qr.rearrange("p h (a f i) -> p h a f i", a=2, f=2)
            o5 = t2[s][:, hs, :].rearrange("p h (a f i) -> p h a f i", a=2, f=2)
            for f in range(2):
                G.op('dve', f_tt(o5[:, :, :, f, :], q5[:, :, :, 1 - f, :], s4[:, :, f, :].unsqueeze(1).broadcast_to([128, 4, 2, 8]), ALU.mult),
                     reads=['psWq'], writes=[('t2', s, hb, f)])
            G.op('act', f_copy_act(qf[s][:, hs, 0:64], psWq[:, :, 0:64]), reads=['psWq'], writes=[('qfn', s, hb)])
        G.op('pool', f_tt(qf[s][:, :, 64:96], t1[s], t2[s], ALU.add),
             reads=[('t1', s, 0), ('t1', s, 1)] + [('t2', s, hb, f) for hb in range(2) for f in range(2)], writes=[('qfr', s)])
        for hb in range(2):
            for h4 in range(4):
                h = hb * 4 + h4
                G.op('pe', f_tr(pXT[0:96, 4 + h4, :], qf[s][:, h, :], ident[:]), reads=[('qfr', s), ('qfn', s, hb)], writes=['pXTq'])
            G.op('dve', f_copy(qT_sb[s][0:96, hb * 4:(hb + 1) * 4, :], pXT[0:96, 4:8, :]), reads=['pXTq'], writes=[('qT', s, hb)])
        G.op('sp', f_dma(QTv[:, :, xi * 128:(xi + 1) * 128], qT_sb[s][0:96]), reads=[('qT', s, 0), ('qT', s, 1)], writes=[('QT', xi)], dma=('qtst', s))

    _nt = NKT
    G0 = Stage()
    SL(G0, 0)
    if _nt > 1:
        SL(G0, 1)
    emit_merged([G0])
    for i in range(_nt + 3):
        stages = []
        if i + 2 < _nt:
            g = Stage(); SL(g, i + 2); stages.append(g)
        if i < _nt:
            g = Stage(); S1(g, i); stages.append(g)
        if 0 <= i - 1 < _nt:
            g = Stage(); S1b(g, i - 1); stages.append(g)
        if 0 <= i - 2 < _nt:
            g = Stage(); S2(g, i - 2); stages.append(g)
        if 2 <= i - 3 < _nt:
            g = Stage(); S3(g, i - 3); stages.append(g)
        emit_merged(stages)

    R.barrier()
    if stop == 1:
        R.finalize()
        return nc
    AR.release(0)

    R.bank = {('pS', 0): 0, ('pS', 1): 1, ('pS', 2): 2, ('pS', 3): 3, ('pO', 0): 4, ('pO', 1): 5, 'pB': 6, 'pBg': 7, 'pB1': 7}
    kt_sb = AR.alloc([4, NKT * 128], BF16)
    v_sb = AR.alloc([NKT, 4, 66], BF16)
    qt_sb = [AR.alloc([4, 512], BF16) for _ in range(2)]
    NS, NP, LOOK = 4, 4, 2
    PT = [AR.alloc([512], BF16) for _ in range(NP)]
    rden = [AR.alloc([512], F32) for _ in range(2)]
    bcs = [AR.alloc([512], F32) for _ in range(2)]
    OTn = [AR.alloc([512], BF16) for _ in range(2)]
    ones_sel = AR.alloc([128], BF16)
    rhi = [AR.alloc([512], BF16) for _ in range(2)]
    rlo = [AR.alloc([512], BF16) for _ in range(2)]
    rtmp = AR.alloc([512], F32)
    R.op('dve', f_memset(ones_sel, 0.0), writes=['ones_sel'])
    R.op('dve', f_memset(ones_sel[64:65], 1.0), writes=['ones_sel'])
    for o_ in range(2):
        R.op('dve', f_memset(rhi[o_], 0.0), writes=[('rhi', o_)])
        R.op('dve', f_memset(rlo[o_], 0.0), writes=[('rlo', o_)])
    pstg = [AR.alloc([4096], F32) for _ in range(2)]
    pcast = [AR.alloc([4096], BF16) for _ in range(2)]
    pS = [psb(i) for i in range(NS)]
    pO = [psb(4), psb(5)]
    pB = psb(6)
    pBg = psb(7)[:, 0:32].rearrange("p (m v) -> p m v", v=2)
    pB1 = psb(7)[:, 32:96].rearrange("p (m v) -> p m v", v=2)

    PREP = Stage()
    pcnt = [0]

    def prep_piece(src_ap, kc, ncols, dst_ap, row_scale=None, col_scale=None, bias=None):
        i = pcnt[0] % 2
        pcnt[0] += 1
        view = pstg[i][:, 0:kc * ncols].rearrange("p (c n) -> p c n", c=kc)
        cview = pcast[i][:, 0:kc * ncols].rearrange("p (c n) -> p c n", c=kc)
        PREP.op('sp', f_dma(view, src_ap), writes=[('pstg', i)], dma=('pstg', i))
        for c in range(kc):
            eng = 'dve' if c % 2 == 0 else 'pool'
            if col_scale is not None:
                fn = f_tt(cview[:, c, :], view[:, c, :], col_scale, ALU.mult)
            elif row_scale is not None:
                fn = f_ts(cview[:, c, :], view[:, c, :], row_scale(c), ALU.mult, 1.0, ALU.mult)
            else:
                fn = f_copy(cview[:, c, :], view[:, c, :])
            PREP.op(eng, fn, reads=[('pstg', i)], writes=[('pcast', i, c)])
        if bias is not None:
            ptile, m0, rhs_of, key = bias
            for m in range(ncols // 128):
                for c in range(kc):
                    PREP.op('pe', f_mm(ptile[:, m0 + m, :], view[:, c, m * 128:(m + 1) * 128], rhs_of(c), c == 0, c == kc - 1),
                            reads=[('pstg', i)], writes=[key])
        PREP.op('pool', f_dma(dst_ap, cview), reads=[('pcast', i, c) for c in range(kc)], writes=[('WB', pcnt[0])], dma=('pcst', i))

    WBg3 = WB_wg.rearrange("p (c n) -> p c n", c=8)
    for pi in range(4):
        n0 = 1184 + pi * 512
        prep_piece(w_in[:, n0:n0 + 512].rearrange("(c p) n -> p c n", p=128), 8, 512, WBg3[:, :, pi * 512:(pi + 1) * 512],
                   row_scale=lambda c: A1[:, c:c + 1], bias=(pBg, pi * 4, lambda c: vcol[:, 0, c, :], 'pBg'))
    PREP.op('dve', f_copy(bias_g[:], pBg[:, :, 0]), reads=['pBg'], writes=['bias_g'])
    prep_piece(w_br_mla.rearrange("(c p) n -> p c n", p=128), 4, 1024, WB_wbm.rearrange("p (c n) -> p c n", c=4))
    prep_piece(w_br_pool.rearrange("(c p) n -> p c n", p=128), 4, 1024, WB_wbp.rearrange("p (c n) -> p c n", c=4),
               row_scale=lambda c: sm[:, 85 + c:86 + c])
    prep_piece(pool_w.rearrange("g c d -> c g d"), 4, 128, WB_pw.rearrange("p (c n) -> p c n", c=4))
    WBo3 = WB_wo.rearrange("p (c n) -> p c n", c=8)
    for pi in range(2):
        prep_piece(w_out[:, pi * 512:(pi + 1) * 512].rearrange("(c p) n -> p c n", p=128), 8, 512, WBo3[:, :, pi * 512:(pi + 1) * 512],
                   col_scale=gtbc[:, 0, pi * 512:(pi + 1) * 512])
    WB13 = WB_w1.rearrange("p (c n) -> p c n", c=8)
    for pi in range(8):
        prep_piece(w_mlp1[:, pi * 512:(pi + 1) * 512].rearrange("(c p) n -> p c n", p=128), 8, 512, WB13[:, :, pi * 512:(pi + 1) * 512],
                   row_scale=lambda c: A2[:, c:c + 1], bias=(pB1, pi * 4, lambda c: vcol[:, 2, c, :], 'pB1'))
    PREP.op('dve', f_copy(b1p[:], pB1[:, :, 0]), reads=['pB1'], writes=['b1p'])
    WB23 = WB_w2.rearrange("p (c n) -> p c n", c=32)
    for pi in range(8):
        prep_piece(w_mlp2[pi * 512:(pi + 1) * 512, :].rearrange("(c p) n -> p c n", p=128), 4, 1024, WB23[:, pi * 4:(pi + 1) * 4, :],
                   col_scale=gtbc[:, 1, :])
    prepq = list(PREP.l)
    PSTEP = 6

    def emit_prep(k=1):
        for _ in range(k):
            if prepq:
                o = prepq.pop(0)
                R.op(o[0], o[1], reads=o[2], writes=o[3], dma=o[4])

    for g in range(2):
        for hh in range(4):
            R.op('sp', f_dma(kt_sb[0:96, hh, :], KT[g * 4 + hh]), writes=[('kt', hh)], dma=('kt', hh))
        R.op('sp', f_dma(v_sb.rearrange("p a b c -> p (a b c)"), Vs[g].rearrange("p a b c -> p (a b c)")), writes=['v'], dma='v')
        its = [(qb, hh, kt) for qb in range(16) for hh in range(4) for kt in range(NKT)]
        n = len(its)
        pend = []

        def emit_S(i):
            qb, hh, kt = its[i]
            if hh == 0 and kt == 0:
                R.op('sp', f_dma(qt_sb[qb % 2][0:96], QTv[:, g * 4:(g + 1) * 4, qb * 512:(qb + 1) * 512]),
                     writes=[('qt', qb % 2)], dma=('qt', qb % 2))
            R.op('pe', f_mm(pS[i % NS], kt_sb[0:96, hh, kt * 128:(kt + 1) * 128], qt_sb[qb % 2][0:96, hh, :], True, True),
                 reads=[('kt', hh), ('qt', qb % 2)], writes=[('pS', i % NS)])
            R.op('act', f_act(PT[i % NP], pS[i % NS], AF.Exp, scale=SCALE), reads=[('pS', i % NS)], writes=[('PT', i % NP)])

        def emit_PV(i):
            qb, hh, kt = its[i]
            hidx = qb * 4 + hh
            o = hidx % 2
            R.op('pe', f_mm(pO[o][0:65, :], v_sb[:, kt, hh, 0:65], PT[i % NP], kt == 0, kt == NKT - 1),
                 reads=['v', ('PT', i % NP)], writes=[('pO', o)])
            if kt == NKT - 1:
                R.op('dve', f_recip(rden[o][64:65, :], pO[o][64:65, :]), reads=[('pO', o)], writes=[('rden', o)])
                R.op('dve', f_copy(rhi[o][64:65, :], rden[o][64:65, :]), reads=[('rden', o)], writes=[('rhi', o)])
                R.op('dve', f_tt(rtmp[64:65, :], rden[o][64:65, :], rhi[o][64:65, :], ALU.subtract), reads=[('rden', o), ('rhi', o)], writes=['rtmp'])
                R.op('dve', f_copy(rlo[o][64:65, :], rtmp[64:65, :]), reads=['rtmp'], writes=[('rlo', o)])
                pend.append((i + 6, hidx, qb, hh))

        def emit_epi(hidx, qb, hh):
            o = hidx % 2
            R.op('pe', f_mm(pB, ones_sel, rhi[o], True, False), reads=[('rhi', o), 'ones_sel'], writes=['pB'])
            R.op('pe', f_mm(pB, ones_sel, rlo[o], False, True), reads=[('rlo', o), 'ones_sel'], writes=['pB'])
            R.op('dve', f_copy(bcs[o][0:64], pB[0:64, :]), reads=['pB'], writes=[('bcs', o)])
            R.op('dve', f_tt(OTn[o][0:64], pO[o][0:64, :], bcs[o][0:64], ALU.mult), reads=[('pO', o), ('bcs', o)], writes=[('OTn', o)])
            r0 = (g * 4 + hh) * 64
            R.op('pool', f_dma(OT[r0:r0 + 64, qb * 512:(qb + 1) * 512], OTn[o][0:64]), reads=[('OTn', o)], writes=[('OT', g, hidx)], dma=('otst', o))

        for i in range(n + LOOK):
            if i < n:
                emit_S(i)
            if i >= LOOK:
                emit_PV(i - LOOK)
            while pend and (pend[0][0] <= i - LOOK or i == n + LOOK - 1):
                _, hidx, qb, hh = pend.pop(0)
                emit_epi(hidx, qb, hh)
            if i % PSTEP == 0:
                emit_prep()
    emit_prep(len(prepq))

    R.barrier()
    if stop == 2:
        R.finalize()
        return nc
    AR.release(0)

    R.bank = {'pT': 0, ('pG', 0): 1, ('pG', 1): 2, ('pD', 0): 3, ('pD', 1): 5, ('pY', 0): 4, ('pY', 1): 6, ('pM', 0): 5, ('pM', 1): 1, ('pP', 0): 6, ('pP', 1): 2, ('pW', 0): 7, ('pW', 1): 3}
    wg = AR.alloc([8, 2048], BF16)
    wbm = AR.alloc([4, 1024], BF16)
    wbp = AR.alloc([4, 1024], BF16)
    poolw = AR.alloc([4, 128], BF16)
    wo = AR.alloc([8, 1024], BF16)
    band = AR.alloc([20, 128], BF16)
    R.op('sp', f_dma(band.rearrange("p a b -> p (a b)"), band_d), writes=['band'], dma='band')
    for nm, dst, src in [('wg', wg, WB_wg), ('wbm', wbm, WB_wbm), ('wbp', wbp, WB_wbp), ('poolw', poolw, WB_pw), ('wo', wo, WB_wo)]:
        R.op('sp', f_dma(dst.rearrange("p a b -> p (a b)"), src), writes=[nm], dma=nm)
    R.barrier()

    xs3 = [AR.alloc([4, 1024], F32) for _ in range(2)]
    xn3 = [AR.alloc([1024], BF16) for _ in range(2)]
    xT3 = [AR.alloc([8, 512], BF16) for _ in range(2)]
    sig = AR.alloc([16, 512], BF16)
    us = [AR.alloc([6, 512], BF16) for _ in range(2)]
    dT = [AR.alloc([512], BF16) for _ in range(2)]
    yT = AR.alloc([4, 512], BF16)
    ot_sb = [AR.alloc([4, 512], BF16) for _ in range(2)]
    tA = [AR.alloc([512], F32) for _ in range(2)]
    tB = [AR.alloc([512], F32) for _ in range(2)]
    mT = AR.alloc([8, 512], BF16)

    pT = psb(0, BF16).rearrange("p (c n) -> p c n", c=8)
    pG = [psb(1), psb(2)]
    pDs = [psb(3), psb(5)]
    pYs = [psb(4), psb(6)]
    pMs = [psb(5), psb(1)]
    pPs = [psb(6), psb(2)]
    pWs = [psb(7), psb(3)]

    def F3(G, b):
        s = b % 2
        G.op('sp', f_dma(xs3[s], x[b * 512:(b + 1) * 512, :].rearrange("(j p) f -> p j f", p=128)),
             writes=[('xs3', s, j) for j in range(4)], dma=('xs3', s))
        G.op('sp', f_dma(ot_sb[s], OT[:, b * 512:(b + 1) * 512].rearrange("(c p) t -> p c t", p=128)), writes=[('ot', s)], dma=('ot', s))
        lo = max(4 * b - 1, 0)
        hi = min(4 * b + 5, 64)
        d0 = lo - (4 * b - 1)
        G.op('sp', f_dma(us[s][:, d0:d0 + hi - lo, :], U[lo * 128:hi * 128, :].rearrange("(j p) f -> p j f", p=128)),
             writes=[('us', s)], dma=('us', s))
        for j in range(4):
            T = 4 * b + j
            G.op('act', f_act(xn3[j % 2], xs3[s][:, j, :], AF.Copy, scale=rstd1[:, T:T + 1]), reads=[('xs3', s, j)], writes=[('xn3', j % 2)])
            for c in range(8):
                G.op('pe', f_tr(pT[:, c, :], xn3[j % 2][:, c * 128:(c + 1) * 128], ident[:]), reads=[('xn3', j % 2)], writes=['pT'])
            G.op('dve', f_copy(xT3[s][:, :, j * 128:(j + 1) * 128], pT), reads=['pT'], writes=[('xT3', s, j)])

    def G3(G, b):
        s = b % 2
        xT3keys = [('xT3', s, j) for j in range(4)]
        for m in range(16):
            for c in range(8):
                G.op('pe', f_mm(pG[m % 2], wg[:, c, m * 128:(m + 1) * 128], xT3[s][:, c, :], c == 0, c == 7), reads=xT3keys, writes=[('pG', m % 2)])
            G.op('act', f_act(sig[:, m, :], pG[m % 2], AF.Sigmoid, bias=bias_g[:, m:m + 1]), reads=[('pG', m % 2)], writes=[('sig', m)])
        for g in range(4):
            for j in range(4):
                T = 4 * b + j
                parts = []
                if T > 0:
                    parts.append((j, g * 5 + 0))
                parts.append((j + 1, g * 5 + (3 if T == 0 else (4 if T == 63 else 1))))
                if T < 63:
                    parts.append((j + 2, g * 5 + 2))
                for k, (slot, bi) in enumerate(parts):
                    G.op('pe', f_mm(pDs[g % 2][:, j * 128:(j + 1) * 128], us[s][:, slot, g * 128:(g + 1) * 128], band[:, bi, :], k == 0, k == len(parts) - 1),
                         reads=[('us', s), 'band'], writes=[('pD', g % 2)])
            G.op('dve', f_copy(dT[g % 2], pDs[g % 2]), reads=[('pD', g % 2)], writes=[('dT', g % 2)])
            G.op('pe', f_mm(pYs[g % 2], poolw[:, g, :], dT[g % 2], True, True), reads=[('dT', g % 2)], writes=[('pY', g % 2)])
            G.op('act', f_copy_act(yT[:, g, :], pYs[g % 2]), reads=[('pY', g % 2)], writes=[('yT', g)])
        yTkeys = [('yT', g) for g in range(4)]
        for m in range(8):
            k2 = m % 2
            for c in range(4):
                G.op('pe', f_mm(pMs[k2], wbm[:, c, m * 128:(m + 1) * 128], ot_sb[s][:, c, :], c == 0, c == 3), reads=[('ot', s)], writes=[('pM', k2)])
            for g in range(4):
                G.op('pe', f_mm(pPs[k2], wbp[:, g, m * 128:(m + 1) * 128], yT[:, g, :], g == 0, g == 3), reads=yTkeys, writes=[('pP', k2)])
            G.op('dve', f_tt(tA[k2], pMs[k2], sig[:, m, :], ALU.mult), reads=[('pM', k2), ('sig', m)], writes=[('tA', k2)])
            G.op('dve', f_tt(tB[k2], pPs[k2], sig[:, 8 + m, :], ALU.mult), reads=[('pP', k2), ('sig', 8 + m)], writes=[('tB', k2)])
            G.op('pool', f_tt(mT[:, m, :], tA[k2], tB[k2], ALU.add), reads=[('tA', k2), ('tB', k2)], writes=[('mT', m)])
        mTkeys = [('mT', m) for m in range(8)]
        for j in range(4):
            for half in range(2):
                k2 = half
                for c in range(8):
                    G.op('pe', f_mm(pWs[k2], mT[:, c, j * 128:(j + 1) * 128], wo[:, c, half * 512:(half + 1) * 512], c == 0, c == 7), reads=mTkeys, writes=[('pW', k2)])
                G.op('dve', f_tt(xs3[s][:, j, half * 512:(half + 1) * 512], pWs[k2], xs3[s][:, j, half * 512:(half + 1) * 512], ALU.add),
                     reads=[('pW', k2), ('xs3', s, j)], writes=[('xs3', s, j)])
        G.op('pool', f_dma(X1[b * 512:(b + 1) * 512, :].rearrange("(j p) f -> p j f", p=128), xs3[s]),
             reads=[('xs3', s, j) for j in range(4)], writes=[('X1', b)], dma=('x1st', s))

    g0 = Stage(); F3(g0, 0); emit_merged([g0])
    for b in range(16):
        stages = [Stage()]
        G3(stages[0], b)
        if b + 1 < 16:
            g1 = Stage(); F3(g1, b + 1); stages.append(g1)
        emit_merged(stages)

    R.barrier()
    if stop == 3:
        R.finalize()
        return nc
    AR.release(0)

    R.bank = {'pB1': 0, 'pT': 0, ('pH', 0): 1, ('pH', 1): 2, ('pH', 2): 3, ('pY2', 0): 4, ('pY2', 1): 5}
    w1 = AR.alloc([8, 4096], BF16)
    w2 = AR.alloc([32, 1024], BF16)
    fg = AR.alloc([1024], F32)
    b1 = b1p
    R.op('sp', f_dma(fg, bigbc_d[:, 2048:3072]), writes=['fg'], dma='fg')
    R.op('sp', f_dma(w1.rearrange("p a b -> p (a b)"), WB_w1), writes=['w1'], dma='w1')
    R.op('sp', f_dma(w2.rearrange("p a b -> p (a b)"), WB_w2), writes=['w2'], dma='w2')
    R.barrier()

    x1s = [AR.alloc([2, 1024], F32) for _ in range(2)]
    xn4 = [AR.alloc([1024], BF16) for _ in range(2)]
    h2T = [AR.alloc([8, 256], BF16) for _ in range(2)]
    rr = [AR.alloc([256], F32) for _ in range(2)]
    hidT = AR.alloc([32, 256], BF16)
    x2 = [AR.alloc([1024], F32) for _ in range(2)]
    outs = x2
    junk4 = AR.alloc([1024], BF16)

    pT = psb(0, BF16).rearrange("p (c n) -> p c n", c=8)
    pH = [psb(1), psb(2), psb(3)]
    pY2 = [psb(4), psb(5)]
    outkeys = []
    def F4(G, b):
        s = b % 2
        G.op('sp', f_dma(x1s[s], X1[b * 256:(b + 1) * 256, :].rearrange("(j p) f -> p j f", p=128)),
             writes=[('x1s', s, 0), ('x1s', s, 1)], dma=('x1s', s))
        for j in range(2):
            G.op('act', f_act(junk4, x1s[s][:, j, :], AF.Square, accum_out=st(9, j)), reads=[('x1s', s, j)], writes=[('ss2', j)])
            G.op('act', f_act(st(10, j), st(9, j), AF.Sqrt, scale=1.0 / D, bias=epsb[:]), reads=[('ss2', j)], writes=[('sd2', j)])
            G.op('dve', f_recip(st(11, j), st(10, j)), reads=[('sd2', j)], writes=[('r2', j)])
            G.op('dve', f_ts(xn4[j], x1s[s][:, j, :], st(11, j), ALU.mult), reads=[('x1s', s, j), ('r2', j)], writes=[('xn4', j)])
            for c in range(8):
                G.op('pe', f_tr(pT[:, c, :], xn4[j][:, c * 128:(c + 1) * 128], ident[:]), reads=[('xn4', j)], writes=['pT'])
            G.op('act', f_copy_act(h2T[s][:, :, j * 128:(j + 1) * 128], pT), reads=['pT'], writes=[('h2T', s, j)])

    def M1(G, b):
        s = b % 2
        for m in range(32):
            for c in range(8):
                G.op('pe', f_mm(pH[m % 3][:, 0:256], w1[:, c, m * 128:(m + 1) * 128], h2T[s][:, c, :], c == 0, c == 7),
                     reads=[('h2T', s, 0), ('h2T', s, 1)], writes=[('pH', m % 3)])
            G.op('act', f_act(rr[m % 2], pH[m % 3][:, 0:256], AF.Relu, bias=b1[:, m:m + 1]), reads=[('pH', m % 3)], writes=[('rr', m % 2)])
            G.op('dve' if m % 2 == 0 else 'pool', f_tt(hidT[:, m, :], rr[m % 2], rr[m % 2], ALU.mult), reads=[('rr', m % 2)], writes=[('hidT', m)])

    def M2(G, b):
        s = b % 2
        hkeys = [('hidT', m) for m in range(32)]
        for j in range(2):
            for half in range(2):
                pk = (j * 2 + half) % 2
                for m in range(32):
                    G.op('pe', f_mm(pY2[pk], hidT[:, m, j * 128:(j + 1) * 128], w2[:, m, half * 512:(half + 1) * 512], m == 0, m == 31),
                         reads=hkeys, writes=[('pY2', pk)])
                G.op('dve', f_tt(x2[j][:, half * 512:(half + 1) * 512], pY2[pk], x1s[s][:, j, half * 512:(half + 1) * 512], ALU.add),
                     reads=[('pY2', pk), ('x1s', s, j)], writes=[('x2', j, half)])
            G.op('act', f_act(junk4, x2[j], AF.Square, accum_out=st(12, j)), reads=[('x2', j, 0), ('x2', j, 1)], writes=[('ss3', j)])
            G.op('act', f_act(st(13, j), st(12, j), AF.Sqrt, scale=1.0 / D, bias=epsb[:]), reads=[('ss3', j)], writes=[('sd3', j)])
            G.op('dve', f_recip(st(14, j), st(13, j)), reads=[('sd3', j)], writes=[('r3', j)])
            G.op('dve', f_stt(outs[j], x2[j], st(14, j), fg, ALU.mult, ALU.mult), reads=[('x2', j, 0), ('x2', j, 1), ('r3', j), 'fg'],
                 writes=[('x2', j, 0), ('x2', j, 1)])
            row = (b * 2 + j) * 128
            G.op('pool', f_dma(out[row:row + 128, :], outs[j]), reads=[('x2', j, 0), ('x2', j, 1)], writes=[('out', b, j)], dma=('outst', j))
            outkeys.append(('out', b, j))

    g0 = Stage(); F4(g0, 0); emit_merged([g0])
    for b in range(32):
        stages = [Stage()]
        M1(stages[0], b)
        if b + 1 < 32:
            g1 = Stage(); F4(g1, b + 1); stages.append(g1)
        emit_merged(stages)
        g2 = Stage(); M2(g2, b); emit_merged([g2])
    R.op('sp', None, reads=outkeys)
    R.finalize()
    return nc


def f_copy_act(out, in_):
    return lambda e: e.activation(out=out, in_=in_, func=AF.Copy)


def _host_consts():
    bf = ml_dtypes.bfloat16
    ident = np.eye(128, dtype=np.float32).astype(bf)
    t = np.arange(L)
    row = (t // 64).astype(np.float32)
    col = (t % 64).astype(np.float32)
    inv_freq = (np.float32(10000.0) ** (-np.arange(0, 16, 2, dtype=np.float32) / np.float32(16))).astype(np.float32)
    ar = (row[:, None] * inv_freq).astype(np.float32)
    ac = (col[:, None] * inv_freq).astype(np.float32)
    cr, sr, cc, sc = np.cos(ar), np.sin(ar), np.cos(ac), np.sin(ac)
    cosF = np.concatenate([cr, cr, cc, cc], axis=1).astype(np.float32)
    sinS = np.concatenate([-sr, sr, -sc, sc], axis=1).astype(np.float32)
    cosF = np.ascontiguousarray(cosF.reshape(64, 128, 32).transpose(1, 0, 2).reshape(128, 64 * 32))
    sinS = np.ascontiguousarray(sinS.reshape(64, 128, 32).transpose(1, 0, 2).reshape(128, 64 * 32))
    band = np.zeros((128, 20, 128), np.float32)
    for g, w in enumerate((2, 4, 8, 16)):
        hw = w // 2
        for kind in range(5):
            T = {0: 5, 1: 5, 2: 5, 3: 0, 4: 63}[kind]
            dT = {0: -1, 1: 0, 2: 1, 3: 0, 4: 0}[kind]
            m = np.zeros((128, 128), np.float32)
            for tl in range(128):
                tg = T * 128 + tl
                lo = max(tg - hw, 0)
                hi = min(tg + hw, L)
                cnt = hi - lo
                for tp in range(lo, hi):
                    p = tp - (T + dT) * 128
                    if 0 <= p < 128:
                        m[p, tl] += 1.0 / cnt
                p = tg - (T + dT) * 128
                if 0 <= p < 128:
                    m[p, tl] -= 1.0
            band[:, g * 5 + kind, :] = m
    band = np.ascontiguousarray(band.reshape(128, 20 * 128)).astype(bf)
    return ident, cosF, sinS, band


_CACHE = {}


def prep_inputs(x, c, ctx, c_ctx, w_ada, b_ada, norm1_g, w_in, q_norm_g, kv_norm_g, w_uq, w_ukv,
                w_br_mla, pool_w, pool_scale, w_br_pool, w_out, norm2_g, w_mlp1, w_mlp2, final_g):
    f = lambda a: np.ascontiguousarray(np.asarray(a, dtype=np.float32))
    x, c, ctx, c_ctx = f(x), f(c), f(ctx), f(c_ctx)
    w_ada, b_ada, w_in, w_uq, w_ukv = f(w_ada)[0], f(b_ada)[0], f(w_in)[0], f(w_uq)[0], f(w_ukv)[0]
    w_br_mla, pool_w, w_br_pool, w_out = f(w_br_mla)[0], f(pool_w)[0], f(w_br_pool)[0], f(w_out)[0]
    w_mlp1, w_mlp2 = f(w_mlp1)[0], f(w_mlp2)[0]
    norm1_g, norm2_g, q_norm_g, kv_norm_g, pool_scale, final_g = (f(norm1_g)[0], f(norm2_g)[0], f(q_norm_g)[0],
                                                                   f(kv_norm_g)[0], f(pool_scale)[0], f(final_g))
    if 'consts' not in _CACHE:
        _CACHE['consts'] = _host_consts()
    ident, cosF, sinS, band = _CACHE['consts']
    col = lambda v: v.reshape(-1, 128).T
    bigbc = np.ascontiguousarray(np.broadcast_to(
        np.concatenate([b_ada[2 * D:3 * D], b_ada[5 * D:6 * D], final_g])[None, :], (128, 3 * D)))
    shared = dict(w_ada=w_ada, w_in=w_in, w_uq=w_uq, w_ukv=w_ukv, w_br_mla=w_br_mla, pool_w=pool_w,
                  w_br_pool=w_br_pool, w_out=w_out, w_mlp1=w_mlp1, w_mlp2=w_mlp2, ident=ident,
                  cosF=cosF, sinS=sinS, band=band, bigbc=bigbc)
    in_maps = []
    for b in range(x.shape[0]):
        ccol = np.stack([col(c[b]), col(c_ctx)], axis=2).reshape(128, 16)
        smallf = np.ascontiguousarray(np.concatenate(
            [ccol, col(b_ada), col(norm1_g), col(norm2_g), col(q_norm_g), col(kv_norm_g), col(pool_scale)], axis=1).astype(np.float32))
        assert smallf.shape == (128, 89)
        m = dict(shared)
        m.update(x=x[b], ctx=ctx[b], smallf=smallf)
        in_maps.append(m)
    return in_maps


def kernel(**inputs):
    in_maps = prep_inputs(**inputs)
    if 'nc' not in _CACHE:
        _CACHE['nc'] = build_program()
    nc = _CACHE['nc']
    res = run_bass_kernel_spmd(nc, in_maps, core_ids=list(range(8)))
    return np.stack([np.asarray(r["out"], dtype=np.float32) for r in res.results], axis=0)
```

```python
import numpy as np
import ml_dtypes
import concourse.bass as bass
import concourse.mybir as mybir
from concourse.bass_utils import run_bass_kernel_spmd

F32 = mybir.dt.float32
BF16 = mybir.dt.bfloat16
AF = mybir.ActivationFunctionType
ALU = mybir.AluOpType

D = 1024
L = 8192
CTX = 256
NKT = (L + CTX) // 128
H = 8
EPS = 1e-6
SCALE = 96.0 ** -0.5
ENGS = ['pe', 'act', 'dve', 'pool', 'sp']


class Rec:
    def __init__(self, nc):
        self.nc = nc
        self.ops = []
        self.lastw = {}
        self.rd_eng = {}
        self.rd_dma = {}
        self.bank = {}
        self.bank_last = {}

    def op(self, eng, fn, reads=(), writes=(), dma=None):
        j = len(self.ops)
        deps = {}
        bset = set()
        for k in list(reads) + list(writes):
            if k in self.bank:
                bv = self.bank[k]
                bset.update(bv if isinstance(bv, tuple) else (bv,))
        for bnk in bset:
            la = self.bank_last.setdefault(bnk, {})
            for e2, i in la.items():
                if e2 != eng:
                    deps.setdefault(i, False)
            la[eng] = j
        for k in reads:
            i = self.lastw.get(k)
            if i is not None:
                deps[i] = True
        for k in writes:
            for i in self.rd_eng.get(k, {}).values():
                deps.setdefault(i, False)
            for i in self.rd_dma.get(k, ()):
                deps.setdefault(i, False)
            i = self.lastw.get(k)
            if i is not None:
                deps.setdefault(i, False)
        for k in reads:
            if dma is None:
                self.rd_eng.setdefault(k, {})[eng] = j
            else:
                self.rd_dma.setdefault(k, []).append(j)
        for k in writes:
            self.lastw[k] = j
            self.rd_eng[k] = {}
            self.rd_dma[k] = []
        keep = []
        for i, raw in deps.items():
            oi = self.ops[i]
            if oi['dma'] is None and dma is None and oi['eng'] == eng:
                if not raw or eng == 'pe':
                    continue
            keep.append(i)
        self.ops.append(dict(eng=eng, fn=fn, dma=dma, deps=keep, sig=False))
        return j

    def barrier(self):
        last = {}
        for idx, o in enumerate(self.ops):
            if o['fn'] is None:
                continue
            if o['dma'] is not None:
                last[('d', o['dma'])] = idx
            else:
                last[('e', o['eng'])] = idx
        deps = list(last.values())
        for e in ENGS:
            self.ops.append(dict(eng=e, fn=None, dma=None, deps=list(deps), sig=False))
        self.lastw.clear()
        self.rd_eng.clear()
        self.rd_dma.clear()
        self.bank_last.clear()

    def finalize(self):
        nc = self.nc
        ops = self.ops
        for o in ops:
            for i in o['deps']:
                ops[i]['sig'] = True
        esem = {e: nc.alloc_semaphore("s_" + e) for e in ENGS}
        dsem = {}
        ecnt = {e: 0 for e in ENGS}
        dcnt = {}
        for o in ops:
            if o['fn'] is None:
                continue
            if o['dma'] is not None:
                k = o['dma']
                if k not in dsem:
                    dsem[k] = nc.alloc_semaphore("d_%d" % len(dsem))
                    dcnt[k] = 0
                dcnt[k] += 16
                o['sem'] = dsem[k]
                o['semid'] = ('d', k)
                o['val'] = dcnt[k]
            elif o['sig']:
                ecnt[o['eng']] += 1
                o['sem'] = esem[o['eng']]
                o['semid'] = ('e', o['eng'])
                o['val'] = ecnt[o['eng']]
        streams = {e: [] for e in ENGS}
        for idx, o in enumerate(ops):
            streams[o['eng']].append(idx)

        def run(eng_name, engine):
            seen = {}
            for idx in streams[eng_name]:
                o = ops[idx]
                need = {}
                for i in o['deps']:
                    d = ops[i]
                    sid, v = d['semid'], d['val']
                    if seen.get(sid, 0) >= v:
                        continue
                    if need.get(sid, (None, 0))[1] < v:
                        need[sid] = (d['sem'], v)
                for sid, (s, v) in need.items():
                    seen[sid] = v
                    engine.wait_ge(s, v)
                if o['fn'] is None:
                    continue
                ins = o['fn'](engine)
                if o['dma'] is not None:
                    ins.then_inc(o['sem'], 16)
                elif o['sig']:
                    ins.then_inc(o['sem'], 1)

        with nc.Block() as block:
            @block.tensor
            def _(e):
                run('pe', e)

            @block.scalar
            def _(e):
                run('act', e)

            @block.vector
            def _(e):
                run('dve', e)

            @block.gpsimd
            def _(e):
                run('pool', e)

            @block.sync
            def _(e):
                run('sp', e)


class Arena:
    def __init__(self, nc, nbytes):
        self.t = nc.alloc_sbuf_tensor("arena", [128, nbytes // 2], BF16)
        self.cap = nbytes
        self.off = 0

    def mark(self):
        return self.off

    def release(self, m):
        self.off = m

    def alloc(self, shape, dtype, parts=128):
        esz = 4 if dtype == F32 else 2
        n = 1
        for s in shape:
            n *= s
        nb = n * esz
        off = (self.off + 63) // 64 * 64
        assert off + nb <= self.cap, ("arena overflow", off, nb, self.cap)
        self.off = off + nb
        ap = self.t[0:parts, off // 2:(off + nb) // 2]
        if dtype == F32:
            ap = ap.bitcast(F32)
        if len(shape) > 1:
            names = ["a%d" % i for i in range(len(shape))]
            pat = "p (" + " ".join(names) + ") -> p " + " ".join(names)
            ap = ap.rearrange(pat, **{nm: s for nm, s in zip(names[1:], shape[1:])})
        return ap


def f_dma(out, in_):
    return lambda e: e.dma_start(out=out, in_=in_)


def f_mm(out, lhsT, rhs, start, stop):
    return lambda e: e.matmul(out, lhsT=lhsT, rhs=rhs, start=start, stop=stop)


def f_tr(out, in_, ident):
    return lambda e: e.transpose(out=out, in_=in_, identity=ident)


def f_act(out, in_, func, **kw):
    return lambda e: e.activation(out=out, in_=in_, func=func, **kw)


def f_tt(out, in0, in1, op):
    return lambda e: e.tensor_tensor(out=out, in0=in0, in1=in1, op=op)


def f_ts(out, in0, s1, op0, s2=None, op1=None):
    if op1 is None:
        return lambda e: e.tensor_scalar(out=out, in0=in0, scalar1=s1, scalar2=None, op0=op0)
    return lambda e: e.tensor_scalar(out=out, in0=in0, scalar1=s1, scalar2=s2, op0=op0, op1=op1)


def f_stt(out, in0, scalar, in1, op0, op1):
    return lambda e: e.scalar_tensor_tensor(out=out, in0=in0, scalar=scalar, in1=in1, op0=op0, op1=op1)


def f_copy(out, in_):
    return lambda e: e.tensor_copy(out=out, in_=in_)


def f_recip(out, in_):
    return lambda e: e.reciprocal(out=out, in_=in_)


def f_memset(out, v):
    return lambda e: e.memset(out, v)


def cast_scaled(R, k, out, in_, scal, reads, writes):
    eng = ('dve', 'act', 'pool')[k % 3]
    if scal is None:
        if eng == 'act':
            R.op('act', f_act(out, in_, AF.Copy), reads, writes)
        else:
            R.op(eng, f_copy(out, in_), reads, writes)
    elif eng == 'act':
        R.op('act', f_act(out, in_, AF.Copy, scale=scal), reads, writes)
    elif eng == 'dve':
        R.op('dve', f_ts(out, in_, scal, ALU.mult), reads, writes)
    else:
        R.op('pool', f_ts(out, in_, scal, ALU.mult, 1.0, ALU.mult), reads, writes)


def build_program(stop=9, dbg=False):
    nc = bass.Bass("TRN2", target_bir_lowering=False)
    skind = "ExternalOutput" if dbg else "Internal"

    def din(name, shape, dt=F32):
        return nc.dram_tensor(name, shape, dt, kind="ExternalInput").ap()

    x = din("x", [L, D])
    ctx = din("ctx", [CTX, D])
    smallf = din("smallf", [128, 89])
    bigbc_d = din("bigbc", [128, 3072])
    w_ada = din("w_ada", [D, 6 * D])
    w_in = din("w_in", [D, 3232])
    w_uq = din("w_uq", [384, 768])
    w_ukv = din("w_ukv", [256, 1024])
    w_br_mla = din("w_br_mla", [512, 1024])
    pool_w = din("pool_w", [4, 128, 128])
    w_br_pool = din("w_br_pool", [512, 1024])
    w_out = din("w_out", [D, D])
    w_mlp1 = din("w_mlp1", [D, 4096])
    w_mlp2 = din("w_mlp2", [4096, D])
    ident_d = din("ident", [128, 128], BF16)
    cosF_d = din("cosF", [128, 64 * 32])
    sinS_d = din("sinS", [128, 64 * 32])
    band_d = din("band", [128, 20 * 128], BF16)
    out = nc.dram_tensor("out", [L, D], F32, kind="ExternalOutput").ap()

    QT = nc.dram_tensor("QT", [H, 96, L], BF16, kind=skind).ap()
    KT = nc.dram_tensor("KT", [H, 96, NKT * 128], BF16, kind=skind).ap()
    Vs = nc.dram_tensor("Vs", [2, 128, NKT, 4, 66], BF16, kind=skind).ap()
    OT = nc.dram_tensor("OT", [512, L], BF16, kind=skind).ap()
    U = nc.dram_tensor("U", [L, 512], BF16, kind=skind).ap()
    X1 = nc.dram_tensor("X1", [L, D], F32, kind=skind).ap()
    WB_wg = nc.dram_tensor("WB_wg", [128, 8 * 2048], BF16, kind="Internal").ap()
    WB_wbm = nc.dram_tensor("WB_wbm", [128, 4 * 1024], BF16, kind="Internal").ap()
    WB_wbp = nc.dram_tensor("WB_wbp", [128, 4 * 1024], BF16, kind="Internal").ap()
    WB_pw = nc.dram_tensor("WB_pw", [128, 4 * 128], BF16, kind="Internal").ap()
    WB_wo = nc.dram_tensor("WB_wo", [128, 8 * 1024], BF16, kind="Internal").ap()
    WB_w1 = nc.dram_tensor("WB_w1", [128, 8 * 4096], BF16, kind="Internal").ap()
    WB_w2 = nc.dram_tensor("WB_w2", [128, 32 * 1024], BF16, kind="Internal").ap()

    R = Rec(nc)

    def gt(name, shape, dt=F32):
        return nc.alloc_sbuf_tensor(name, shape, dt)

    sm = gt("sm", [128, 89])
    ident = gt("ident_s", [128, 128], BF16)
    ones_f = gt("ones_f", [128, 128])
    epsb = gt("epsb", [128, 1])
    sil = gt("sil", [128, 8, 2])
    vcol = gt("vcol", [128, 4, 8, 2])
    gtbc = gt("gtbc", [128, 2, 1024])
    A1 = gt("A1", [128, 8])
    cA1 = gt("cA1", [128, 8])
    A2 = gt("A2", [128, 8])
    rstd1 = gt("rstd1", [128, 64])
    stat = gt("stat", [128, 64])
    bias_g = gt("bias_g", [128, 16])
    b1p = gt("b1p", [128, 32])
    AR = Arena(nc, 196 * 1024)

    PSall = nc.alloc_psum_tensor("psall", [128, 8, 512], F32)

    def psb(i, dt=F32):
        ap = PSall[:, i, :]
        if dt == BF16:
            ap = ap.bitcast(BF16)
        return ap

    R.bank = {('pV', 0): 0, ('pV', 1): 1, ('pR', 0): 2, ('pR', 1): 3, 'pBias0': 4, 'pBias1': 5}
    R.op('sp', f_dma(sm[:], smallf), writes=['sm'], dma='sm')
    R.op('sp', f_dma(ident[:], ident_d), writes=['ident'], dma='ident')
    R.op('dve', f_memset(ones_f[:], 1.0), writes=['ones'])
    R.op('dve', f_memset(epsb[:], EPS), writes=['epsb'])
    R.op('act', f_act(sil[:].rearrange("p c v -> p (c v)"), sm[:, 0:16], AF.Silu), reads=['sm'], writes=['sil'])

    w1x = AR.alloc([8, 1184], BF16)
    w1c = AR.alloc([8, 288], BF16)
    wuq = AR.alloc([3, 768], BF16)
    wukv = AR.alloc([2, 1024], BF16)
    bias1x = AR.alloc([1184], F32)
    bias1c = AR.alloc([288], F32)
    cosF = AR.alloc([64, 32], F32)
    sinS = AR.alloc([64, 32], F32)
    m_p1 = AR.mark()
    stgh = {'b': [AR.alloc([8192], F32) for _ in range(2)]}
    bigbc = AR.alloc([2048], F32)
    R.op('sp', f_dma(bigbc, bigbc_d[:, 0:2048]), writes=['bigbc'], dma='bigbc')
    sil_rep = AR.alloc([8, 128], F32)
    sh1_rep = AR.alloc([8, 128], F32)
    csh1_rep = AR.alloc([8, 128], F32)

    R.op('sp', f_dma(cosF.rearrange("p a b -> p (a b)"), cosF_d), writes=['cosF'], dma='cosF')
    R.op('sp', f_dma(sinS.rearrange("p a b -> p (a b)"), sinS_d), writes=['sinS'], dma='sinS')

    for c in range(8):
        R.op('dve', f_ts(sil_rep[:, c, :], ones_f[:], sil[:, c, 0:1], ALU.mult),
             reads=['ones', 'sil'], writes=[('silrep', c)])

    stg_n = [0]

    def stage_load(src_ap, kc, ncols):
        i = stg_n[0] % 2
        stg_n[0] += 1
        view = stgh['b'][i][:, 0:kc * ncols].rearrange("p (c n) -> p c n", c=kc)
        R.op('sp', f_dma(view, src_ap), writes=[('stg', i)], dma=('stg', i))
        return view, ('stg', i)

    pV = [psb(0)[:, 0:16].rearrange("p (m v) -> p m v", v=2), psb(1)[:, 0:16].rearrange("p (m v) -> p m v", v=2)]
    pR = [psb(2), psb(3)]
    vmap = {0: 0, 1: 1, 3: 2, 4: 3}
    for v in range(6):
        view, skey = stage_load(w_ada[:, v * 1024:(v + 1) * 1024].rearrange("(c p) n -> p c n", p=128), 8, 1024)
        if v in vmap:
            vi = vmap[v]
            pk = ('pV', vi % 2)
            for m in range(8):
                for c in range(8):
                    R.op('pe', f_mm(pV[vi % 2][:, m, :], view[:, c, m * 128:(m + 1) * 128], sil[:, c, :], c == 0, c == 7),
                         reads=[skey, 'sil'], writes=[pk])
            R.op('dve', f_tt(vcol[:, vi], pV[vi % 2],
                             sm[:, 16 + v * 8:16 + (v + 1) * 8].unsqueeze(2).broadcast_to([128, 8, 2]), ALU.add),
                 reads=[pk, 'sm'], writes=[('vcol', vi)])
        else:
            gi = 0 if v == 2 else 1
            for half in range(2):
                pk = ('pR', half)
                for c in range(8):
                    R.op('pe', f_mm(pR[half], sil_rep[:, c, :], view[:, c, half * 512:(half + 1) * 512], c == 0, c == 7),
                         reads=[skey, ('silrep', c)], writes=[pk])
                R.op('dve', f_tt(gtbc[:, gi, half * 512:(half + 1) * 512], pR[half],
                                 bigbc[:, gi * 1024 + half * 512:gi * 1024 + (half + 1) * 512], ALU.add),
                     reads=[pk, 'bigbc'], writes=[('gtbc', gi)])

    R.op('dve', f_stt(A1[:], vcol[:, 1, :, 0], 1.0, sm[:, 64:72], ALU.add, ALU.mult), reads=[('vcol', 1), 'sm'], writes=['A1'])
    R.op('dve', f_stt(cA1[:], vcol[:, 1, :, 1], 1.0, sm[:, 64:72], ALU.add, ALU.mult), reads=[('vcol', 1), 'sm'], writes=['cA1'])
    R.op('dve', f_stt(A2[:], vcol[:, 3, :, 0], 1.0, sm[:, 72:80], ALU.add, ALU.mult), reads=[('vcol', 3), 'sm'], writes=['A2'])
    for c in range(8):
        R.op('dve', f_ts(sh1_rep[:, c, :], ones_f[:], vcol[:, 0, c, 0:1], ALU.mult), reads=['ones', ('vcol', 0)], writes=[('sh1rep', c)])
        R.op('dve', f_ts(csh1_rep[:, c, :], ones_f[:], vcol[:, 0, c, 1:2], ALU.mult), reads=['ones', ('vcol', 0)], writes=[('csh1rep', c)])

    kk = 0
    pBias = [psb(4), psb(5)]
    for pi, (n0, n1) in enumerate([(0, 384), (384, 672), (672, 1184)]):
        n = n1 - n0
        view, skey = stage_load(w_in[:, n0:n1].rearrange("(c p) n -> p c n", p=128), 8, n)
        for c in range(8):
            cast_scaled(R, kk, w1x[:, c, n0:n1], view[:, c, :], A1[:, c:c + 1], [skey, 'A1'], [('w1x', pi, c)])
            kk += 1
        for c in range(8):
            R.op('pe', f_mm(pBias[0][:, 0:n], sh1_rep[:, c, :], view[:, c, :], c == 0, c == 7),
                 reads=[skey, ('sh1rep', c)], writes=['pBias0'])
        R.op('dve', f_copy(bias1x[:, n0:n1], pBias[0][:, 0:n]), reads=['pBias0'], writes=[('bias1x', pi)])
        if pi == 1:
            for c in range(8):
                cast_scaled(R, kk, w1c[:, c, :], view[:, c, :], cA1[:, c:c + 1], [skey, 'cA1'], [('w1c', c)])
                kk += 1
            for c in range(8):
                R.op('pe', f_mm(pBias[1][:, 0:n], csh1_rep[:, c, :], view[:, c, :], c == 0, c == 7),
                     reads=[skey, ('csh1rep', c)], writes=['pBias1'])
            R.op('dve', f_copy(bias1c[:], pBias[1][:, 0:n]), reads=['pBias1'], writes=['bias1c'])
    view, skey = stage_load(w_uq.rearrange("(c p) n -> p c n", p=128), 3, 768)
    for c in range(3):
        cast_scaled(R, kk, wuq[:, c, :], view[:, c, :], sm[:, 80 + c:81 + c], [skey, 'sm'], [('wuq', c)])
        kk += 1
    view, skey = stage_load(w_ukv.rearrange("(c p) n -> p c n", p=128), 2, 1024)
    for c in range(2):
        cast_scaled(R, kk, wukv[:, c, :], view[:, c, :], sm[:, 83 + c:84 + c], [skey, 'sm'], [('wukv', c)])
        kk += 1

    if dbg:
        dbg0 = nc.dram_tensor("dbg0", [128, 4 * 16 + 2048 + 24 + 1184 + 288], F32, kind="ExternalOutput").ap()
        R.op('sp', f_dma(dbg0[:, 0:64], vcol[:].rearrange("p a b c -> p (a b c)")), reads=[('vcol', i) for i in range(4)], writes=['dbg0a'], dma='dbg0a')
        R.op('sp', f_dma(dbg0[:, 64:2112], gtbc[:].rearrange("p a b -> p (a b)")), reads=[('gtbc', 0), ('gtbc', 1)], writes=['dbg0b'], dma='dbg0b')
        R.op('sp', f_dma(dbg0[:, 2112:2120], A1[:]), reads=['A1'], writes=['dbg0c'], dma='dbg0c')
        R.op('sp', f_dma(dbg0[:, 2120:2128], cA1[:]), reads=['cA1'], writes=['dbg0d'], dma='dbg0d')
        R.op('sp', f_dma(dbg0[:, 2128:2136], A2[:]), reads=['A2'], writes=['dbg0e'], dma='dbg0e')
        R.op('sp', f_dma(dbg0[:, 2136:2136 + 1184], bias1x), reads=[('bias1x', i) for i in range(3)], writes=['dbg0f'], dma='dbg0f')
        R.op('sp', f_dma(dbg0[:, 2136 + 1184:2136 + 1184 + 288], bias1c), reads=['bias1c'], writes=['dbg0g'], dma='dbg0g')
        dbg1 = nc.dram_tensor("dbg1", [128, 8 * 1184], BF16, kind="ExternalOutput").ap()
        R.op('sp', f_dma(dbg1, w1x.rearrange("p a b -> p (a b)")), reads=[('w1x', pi, c) for pi in range(3) for c in range(8)], writes=['dbg1'], dma='dbg1')
    R.barrier()
    if stop == 0:
        R.finalize()
        return nc
    AR.release(m_p1)

    R.bank = {'pT': 0, 'psA': 1, 'psB': 2, 'psC': 3, 'pT2q': 4, 'pT2k': 4, 'psWk': 5, 'psWq': 6, 'pXTk': 7, 'pXTq': 7}
    NX, NB = 4, 4
    xs = [AR.alloc([1024], F32) for _ in range(NX)]
    junk = [AR.alloc([1024], BF16) for _ in range(3)]
    xn = [AR.alloc([1024], BF16) for _ in range(NB)]
    xT = [AR.alloc([8, 128], BF16) for _ in range(NB)]
    u_sb = [AR.alloc([512], BF16) for _ in range(NB)]
    cqn = [AR.alloc([384], BF16) for _ in range(NB)]
    cqT = [AR.alloc([3, 128], BF16) for _ in range(NB)]
    ckvn = [AR.alloc([256], BF16) for _ in range(NB)]
    ckvT = [AR.alloc([2, 128], BF16) for _ in range(NB)]
    krs_sb = [AR.alloc([32], F32) for _ in range(NB)]
    t1 = [AR.alloc([8, 32], F32) for _ in range(NB)]
    t2 = [AR.alloc([8, 32], F32) for _ in range(NB)]
    qf = [AR.alloc([8, 96], BF16) for _ in range(NB)]
    kf = [AR.alloc([8, 96], BF16) for _ in range(NB)]
    vaug = [AR.alloc([8, 66], BF16) for _ in range(NB)]
    qT_sb = [AR.alloc([8, 128], BF16) for _ in range(NB)]
    kT_sb = [AR.alloc([8, 128], BF16) for _ in range(NB)]
    t1k = [AR.alloc([32], F32) for _ in range(NB)]
    t2k = [AR.alloc([32], F32) for _ in range(NB)]
    kr = [AR.alloc([32], F32) for _ in range(NB)]
    ones2 = AR.alloc([128], BF16)
    bias2x = AR.alloc([1184], BF16)
    bias2c = AR.alloc([288], BF16)
    bhi32 = AR.alloc([1184], BF16)
    btmp = AR.alloc([1184], F32)
    R.op('dve', f_memset(ones2, 0.0), writes=['ones2'])
    R.op('dve', f_memset(ones2[0:1], 1.0), writes=['ones2'])
    R.op('dve', f_memset(ones2[32:33], 1.0), writes=['ones2'])
    for (b2, src_, n_, kn) in [(bias2x, bias1x, 1184, 'x'), (bias2c, bias1c, 288, 'c')]:
        R.op('dve', f_memset(b2, 0.0), writes=[('b2', kn)])
        R.op('dve', f_copy(b2[0:1], src_[0:1]), writes=[('b2', kn)])
        R.op('dve', f_copy(bhi32[32:33, 0:n_], src_[32:33]), writes=[('bhi32', kn)])
        R.op('dve', f_tt(btmp[32:33, 0:n_], src_[32:33], bhi32[32:33, 0:n_], ALU.subtract), reads=[('bhi32', kn)], writes=[('btmp', kn)])
        R.op('dve', f_copy(b2[32:33], btmp[32:33, 0:n_]), reads=[('btmp', kn)], writes=[('b2', kn)])
    for s in range(NB):
        R.op('dve', f_memset(vaug[s], 1.0), writes=[('vaug', s)])

    pT = psb(0, BF16).rearrange("p (c n) -> p c n", c=8)
    psA, psB, psC = psb(1), psb(2), psb(3)
    pT2 = psb(4, BF16).rearrange("p (c n) -> p c n", c=8)
    psWk = psb(5).rearrange("p (h d) -> p h d", d=128)
    psWq = psb(6)[:, 0:384].rearrange("p (h d) -> p h d", d=96)
    pXT = psb(7, BF16).rearrange("p (c n) -> p c n", c=8)
    QTv = QT.rearrange("h d t -> d h t")
    KTv = KT.rearrange("h d t -> d h t")

    def st(col, s):
        return stat[:, col * 4 + s:col * 4 + s + 1]

    class Stage:
        def __init__(self):
            self.l = []

        def op(self, eng, fn, reads=(), writes=(), dma=None):
            self.l.append((eng, fn, reads, writes, dma))

    def emit_merged(stages, group_pe=False):
        items = []
        for si, sg in enumerate(stages):
            n = len(sg.l)
            k = 0
            while k < n:
                k1 = k + 1
                if group_pe and sg.l[k][0] == 'pe':
                    while k1 < n and sg.l[k1][0] == 'pe':
                        k1 += 1
                items.append(((k + 0.5) / n, si, k, sg.l[k:k1]))
                k = k1
        items.sort(key=lambda z: (z[0], z[1]))
        for _, _, _, grp in items:
            for o in grp:
                R.op(o[0], o[1], reads=o[2], writes=o[3], dma=o[4])

    def tile_src(t):
        return ctx[t * 128:(t + 1) * 128, :] if t < 2 else x[(t - 2) * 128:(t - 1) * 128, :]

    def SL(G, t):
        G.op('sp', f_dma(xs[t % NX], tile_src(t)), writes=[('xs', t % NX)], dma=('xs', t % NX))

    def S1(G, t):
        is_ctx = t < 2
        xi = t - 2
        sx, s, s4_ = t % NX, t % NB, t % 4
        G.op('act', f_act(junk[0], xs[sx], AF.Square, accum_out=st(0, s4_)), reads=[('xs', sx)], writes=[('ss', s4_)])
        G.op('act', f_act(st(1, s4_), st(0, s4_), AF.Sqrt, scale=1.0 / D, bias=epsb[:]), reads=[('ss', s4_)], writes=[('sd', s4_)])
        if is_ctx:
            rst, rkey = st(2, s4_), ('rstc', s4_)
        else:
            rst, rkey = rstd1[:, xi:xi + 1], ('rstd1', xi)
        G.op('dve', f_recip(rst, st(1, s4_)), reads=[('sd', s4_)], writes=[rkey])
        G.op('dve', f_ts(xn[s], xs[sx], rst, ALU.mult), reads=[('xs', sx), rkey], writes=[('xn', s)])
        for c in range(8):
            G.op('pe', f_tr(pT[:, c, :], xn[s][:, c * 128:(c + 1) * 128], ident[:]), reads=[('xn', s)], writes=['pT'])
        G.op('act', f_copy_act(xT[s].rearrange("p c n -> p (c n)"), psb(0, BF16)), reads=['pT'], writes=[('xT', s)])

    def S1b(G, t):
        is_ctx = t < 2
        xi = t - 2
        sx, s, s4_ = t % NX, t % NB, t % 4
        if not is_ctx:
            groups = [(psA, 'psA', 0, 384), (psB, 'psB', 384, 672), (psC, 'psC', 672, 1184)]
            for (ps, key, n0, n1) in groups:
                for c in range(8):
                    G.op('pe', f_mm(ps[:, 0:n1 - n0], xT[s][:, c, :], w1x[:, c, n0:n1], c == 0, False), reads=[('xT', s)], writes=[key])
                G.op('pe', f_mm(ps[:, 0:n1 - n0], ones2, bias2x[:, n0:n1], False, True), writes=[key])
        else:
            for c in range(8):
                G.op('pe', f_mm(psB[:, 0:288], xT[s][:, c, :], w1c[:, c, :], c == 0, False), reads=[('xT', s)], writes=['psB'])
            G.op('pe', f_mm(psB[:, 0:288], ones2, bias2c, False, True), writes=['psB'])
        G.op('act', f_act(junk[0][:, 0:256], psB[:, 0:256], AF.Square, accum_out=st(3, s4_)), reads=['psB'], writes=[('sskv', s4_)])
        G.op('act', f_act(st(4, s4_), st(3, s4_), AF.Sqrt, scale=1.0 / 256, bias=epsb[:]), reads=[('sskv', s4_)], writes=[('sdkv', s4_)])
        G.op('dve', f_recip(st(5, s4_), st(4, s4_)), reads=[('sdkv', s4_)], writes=[('rkv', s4_)])
        G.op('dve', f_ts(ckvn[s], psB[:, 0:256], st(5, s4_), ALU.mult), reads=['psB', ('rkv', s4_)], writes=[('ckvn', s)])
        G.op('dve', f_copy(krs_sb[s], psB[:, 256:288]), reads=['psB'], writes=[('krs', s)])
        if not is_ctx:
            G.op('act', f_act(junk[0][:, 0:384], psA[:, 0:384], AF.Square, accum_out=st(6, s4_)), reads=['psA'], writes=[('ssq', s4_)])
            G.op('act', f_act(st(7, s4_), st(6, s4_), AF.Sqrt, scale=1.0 / 384, bias=epsb[:]), reads=[('ssq', s4_)], writes=[('sdq', s4_)])
            G.op('dve', f_recip(st(8, s4_), st(7, s4_)), reads=[('sdq', s4_)], writes=[('rq', s4_)])
            G.op('dve', f_ts(cqn[s], psA[:, 0:384], st(8, s4_), ALU.mult), reads=['psA', ('rq', s4_)], writes=[('cqn', s)])
            G.op('act', f_copy_act(u_sb[s], psC[:, 0:512]), reads=['psC'], writes=[('u', s)])
            G.op('pool', f_dma(U[xi * 128:(xi + 1) * 128, :], u_sb[s]), reads=[('u', s)], writes=[('U', xi)], dma=('ust', s))

    def S2(G, t):
        is_ctx = t < 2
        xi = t - 2
        s = t % NB
        for c in range(2):
            G.op('pe', f_tr(pT2[:, 3 + c, :], ckvn[s][:, c * 128:(c + 1) * 128], ident[:]), reads=[('ckvn', s)], writes=['pT2k'])
        G.op('dve', f_copy(ckvT[s], pT2[:, 3:5, :]), reads=['pT2k'], writes=[('ckvT', s)])
        if is_ctx:
            krsrc, krkey = krs_sb[s], ('krs', s)
        else:
            cos_t = cosF[:, xi, :]
            sin_t = sinS[:, xi, :]
            G.op('pool', f_tt(t1k[s], krs_sb[s], cos_t, ALU.mult), reads=[('krs', s)], writes=[('t1k', s)])
            kv4 = krs_sb[s].rearrange("p (a f i) -> p a f i", a=2, f=2)
            o4 = t2k[s].rearrange("p (a f i) -> p a f i", a=2, f=2)
            s4 = sin_t.rearrange("p (a f i) -> p a f i", a=2, f=2)
            for f in range(2):
                G.op('pool', f_tt(o4[:, :, f, :], kv4[:, :, 1 - f, :], s4[:, :, f, :], ALU.mult), reads=[('krs', s)], writes=[('t2k', s, f)])
            G.op('pool', f_tt(kr[s], t1k[s], t2k[s], ALU.add), reads=[('t1k', s), ('t2k', s, 0), ('t2k', s, 1)], writes=[('kr', s)])
            krsrc, krkey = kr[s], ('kr', s)
        G.op('pool', f_copy(kf[s][:, :, 64:96], krsrc.unsqueeze(1).broadcast_to([128, 8, 32])), reads=[krkey], writes=[('kfr', s)])
        for hb in range(2):
            hs = slice(hb * 4, (hb + 1) * 4)
            for c in range(2):
                G.op('pe', f_mm(psb(5), ckvT[s][:, c, :], wukv[:, c, hb * 512:(hb + 1) * 512], c == 0, c == 1), reads=[('ckvT', s)], writes=['psWk'])
            G.op('act', f_copy_act(kf[s][:, hs, 0:64], psWk[:, :, 0:64]), reads=['psWk'], writes=[('kfn', s, hb)])
            G.op('dve', f_copy(vaug[s][:, hs, 0:64], psWk[:, :, 64:128]), reads=['psWk'], writes=[('vaug', s, hb)])
        for hb in range(2):
            for h4 in range(4):
                h = hb * 4 + h4
                G.op('pe', f_tr(pXT[0:96, h4, :], kf[s][:, h, :], ident[:]), reads=[('kfr', s), ('kfn', s, hb)], writes=['pXTk'])
            G.op('dve' if hb == 0 else 'act', (f_copy if hb == 0 else f_copy_act)(kT_sb[s][0:96, hb * 4:(hb + 1) * 4, :], pXT[0:96, 0:4, :]),
                 reads=['pXTk'], writes=[('kT', s, hb)])
        G.op('sp', f_dma(KTv[:, :, t * 128:(t + 1) * 128], kT_sb[s][0:96]), reads=[('kT', s, 0), ('kT', s, 1)], writes=[('KT', t)], dma=('ktst', s))
        G.op('pool', f_dma(Vs[:, :, t].rearrange("g p hh e -> p g hh e"), vaug[s].rearrange("p (g hh) e -> p g hh e", g=2)),
             reads=[('vaug', s, 0), ('vaug', s, 1)], writes=[('Vs', t)], dma=('vst', s))

    def S3(G, t):
        xi = t - 2
        s = t % NB
        cos_t = cosF[:, xi, :]
        sin_t = sinS[:, xi, :]
        s4 = sin_t.rearrange("p (a f i) -> p a f i", a=2, f=2)
        for c in range(3):
            G.op('pe', f_tr(pT2[:, c, :], cqn[s][:, c * 128:(c + 1) * 128], ident[:]), reads=[('cqn', s)], writes=['pT2q'])
        G.op('dve', f_copy(cqT[s], pT2[:, 0:3, :]), reads=['pT2q'], writes=[('cqT', s)])
        for hb in range(2):
            hs = slice(hb * 4, (hb + 1) * 4)
            for c in range(3):
                G.op('pe', f_mm(psb(6)[:, 0:384], cqT[s][:, c, :], wuq[:, c, hb * 384:(hb + 1) * 384], c == 0, c == 2),
                     reads=[('cqT', s)], writes=['psWq'])
            qr = psWq[:, :, 64:96]
            G.op('dve', f_tt(t1[s][:, hs, :], qr, cos_t.unsqueeze(1).broadcast_to([128, 4, 32]), ALU.mult), reads=['psWq'], writes=[('t1', s, hb)])
            q5 = qr.rearrange("p h (a f i) -> p h a f i", a=2, f=2)
            o5 = t2[s][:, hs, :].rearrange("p h (a f i) -> p h a f i", a=2, f=2)
            for f in range(2):
                G.op('dve', f_tt(o5[:, :, :, f, :], q5[:, :, :, 1 - f, :], s4[:, :, f, :].unsqueeze(1).broadcast_to([128, 4, 2, 8]), ALU.mult),
                     reads=['psWq'], writes=[('t2', s, hb, f)])
            G.op('act', f_copy_act(qf[s][:, hs, 0:64], psWq[:, :, 0:64]), reads=['psWq'], writes=[('qfn', s, hb)])
        G.op('pool', f_tt(qf[s][:, :, 64:96], t1[s], t2[s], ALU.add),
             reads=[('t1', s, 0), ('t1', s, 1)] + [('t2', s, hb, f) for hb in range(2) for f in range(2)], writes=[('qfr', s)])
        for hb in range(2):
            for h4 in range(4):
                h = hb * 4 + h4
                G.op('pe', f_tr(pXT[0:96, 4 + h4, :], qf[s][:, h, :], ident[:]), reads=[('qfr', s), ('qfn', s, hb)], writes=['pXTq'])
            G.op('dve', f_copy(qT_sb[s][0:96, hb * 4:(hb + 1) * 4, :], pXT[0:96, 4:8, :]), reads=['pXTq'], writes=[('qT', s, hb)])
        G.op('sp', f_dma(QTv[:, :, xi * 128:(xi + 1) * 128], qT_sb[s][0:96]), reads=[('qT', s, 0), ('qT', s, 1)], writes=[('QT', xi)], dma=('qtst', s))

    _nt = NKT
    G0 = Stage()
    SL(G0, 0)
    if _nt > 1:
        SL(G0, 1)
    emit_merged([G0])
    for i in range(_nt + 3):
        stages = []
        if i + 2 < _nt:
            g = Stage(); SL(g, i + 2); stages.append(g)
        if i < _nt:
            g = Stage(); S1(g, i); stages.append(g)
        if 0 <= i - 1 < _nt:
            g = Stage(); S1b(g, i - 1); stages.append(g)
        if 0 <= i - 2 < _nt:
            g = Stage(); S2(g, i - 2); stages.append(g)
        if 2 <= i - 3 < _nt:
            g = Stage(); S3(g, i - 3); stages.append(g)
        emit_merged(stages)

    R.barrier()
    if stop == 1:
        R.finalize()
        return nc
    AR.release(0)

    R.bank = {('pS', 0): 0, ('pS', 1): 1, ('pS', 2): 2, ('pS', 3): 3, ('pO', 0): 4, ('pO', 1): 5, 'pB': 6, 'pBg': 7, 'pB1': 7}
    kt_sb = AR.alloc([4, NKT * 128], BF16)
    v_sb = AR.alloc([NKT, 4, 66], BF16)
    qt_sb = [AR.alloc([4, 512], BF16) for _ in range(2)]
    NS, NP, LOOK = 4, 6, 3
    PT = [AR.alloc([512], BF16) for _ in range(NP)]
    rden = [AR.alloc([512], F32) for _ in range(2)]
    bcs = [AR.alloc([512], F32) for _ in range(2)]
    OTn = [AR.alloc([512], BF16) for _ in range(2)]
    ones_sel = AR.alloc([128], BF16)
    rhi = [AR.alloc([512], BF16) for _ in range(2)]
    rlo = [AR.alloc([512], BF16) for _ in range(2)]
    rtmp = AR.alloc([512], F32)
    R.op('dve', f_memset(ones_sel, 0.0), writes=['ones_sel'])
    R.op('dve', f_memset(ones_sel[64:65], 1.0), writes=['ones_sel'])
    for o_ in range(2):
        R.op('dve', f_memset(rhi[o_], 0.0), writes=[('rhi', o_)])
        R.op('dve', f_memset(rlo[o_], 0.0), writes=[('rlo', o_)])
    pstg = [AR.alloc([4096], F32) for _ in range(2)]
    pcast = [AR.alloc([4096], BF16) for _ in range(2)]
    pS = [psb(i) for i in range(NS)]
    pO = [psb(4), psb(5)]
    pB = psb(6)
    pBg = psb(7)[:, 0:32].rearrange("p (m v) -> p m v", v=2)
    pB1 = psb(7)[:, 32:96].rearrange("p (m v) -> p m v", v=2)

    PREP = Stage()
    pcnt = [0]

    def prep_piece(src_ap, kc, ncols, dst_ap, row_scale=None, col_scale=None, bias=None):
        i = pcnt[0] % 2
        pcnt[0] += 1
        view = pstg[i][:, 0:kc * ncols].rearrange("p (c n) -> p c n", c=kc)
        cview = pcast[i][:, 0:kc * ncols].rearrange("p (c n) -> p c n", c=kc)
        PREP.op('sp', f_dma(view, src_ap), writes=[('pstg', i)], dma=('pstg', i))
        for c in range(kc):
            eng = 'dve' if c % 2 == 0 else 'pool'
            if col_scale is not None:
                fn = f_tt(cview[:, c, :], view[:, c, :], col_scale, ALU.mult)
            elif row_scale is not None:
                fn = f_ts(cview[:, c, :], view[:, c, :], row_scale(c), ALU.mult, 1.0, ALU.mult)
            else:
                fn = f_copy(cview[:, c, :], view[:, c, :])
            PREP.op(eng, fn, reads=[('pstg', i)], writes=[('pcast', i, c)])
        if bias is not None:
            ptile, m0, rhs_of, key = bias
            for m in range(ncols // 128):
                for c in range(kc):
                    PREP.op('pe', f_mm(ptile[:, m0 + m, :], view[:, c, m * 128:(m + 1) * 128], rhs_of(c), c == 0, c == kc - 1),
                            reads=[('pstg', i)], writes=[key])
        PREP.op('pool', f_dma(dst_ap, cview), reads=[('pcast', i, c) for c in range(kc)], writes=[('WB', pcnt[0])], dma=('pcst', i))

    WBg3 = WB_wg.rearrange("p (c n) -> p c n", c=8)
    for pi in range(4):
        n0 = 1184 + pi * 512
        prep_piece(w_in[:, n0:n0 + 512].rearrange("(c p) n -> p c n", p=128), 8, 512, WBg3[:, :, pi * 512:(pi + 1) * 512],
                   row_scale=lambda c: A1[:, c:c + 1], bias=(pBg, pi * 4, lambda c: vcol[:, 0, c, :], 'pBg'))
    PREP.op('dve', f_copy(bias_g[:], pBg[:, :, 0]), reads=['pBg'], writes=['bias_g'])
    prep_piece(w_br_mla.rearrange("(c p) n -> p c n", p=128), 4, 1024, WB_wbm.rearrange("p (c n) -> p c n", c=4))
    prep_piece(w_br_pool.rearrange("(c p) n -> p c n", p=128), 4, 1024, WB_wbp.rearrange("p (c n) -> p c n", c=4),
               row_scale=lambda c: sm[:, 85 + c:86 + c])
    prep_piece(pool_w.rearrange("g c d -> c g d"), 4, 128, WB_pw.rearrange("p (c n) -> p c n", c=4))
    WBo3 = WB_wo.rearrange("p (c n) -> p c n", c=8)
    for pi in range(2):
        prep_piece(w_out[:, pi * 512:(pi + 1) * 512].rearrange("(c p) n -> p c n", p=128), 8, 512, WBo3[:, :, pi * 512:(pi + 1) * 512],
                   col_scale=gtbc[:, 0, pi * 512:(pi + 1) * 512])
    WB13 = WB_w1.rearrange("p (c n) -> p c n", c=8)
    for pi in range(8):
        prep_piece(w_mlp1[:, pi * 512:(pi + 1) * 512].rearrange("(c p) n -> p c n", p=128), 8, 512, WB13[:, :, pi * 512:(pi + 1) * 512],
                   row_scale=lambda c: A2[:, c:c + 1], bias=(pB1, pi * 4, lambda c: vcol[:, 2, c, :], 'pB1'))
    PREP.op('dve', f_copy(b1p[:], pB1[:, :, 0]), reads=['pB1'], writes=['b1p'])
    WB23 = WB_w2.rearrange("p (c n) -> p c n", c=32)
    for pi in range(8):
        prep_piece(w_mlp2[pi * 512:(pi + 1) * 512, :].rearrange("(c p) n -> p c n", p=128), 4, 1024, WB23[:, pi * 4:(pi + 1) * 4, :],
                   col_scale=gtbc[:, 1, :])
    prepq = list(PREP.l)
    PSTEP = 6

    def emit_prep(k=1):
        for _ in range(k):
            if prepq:
                o = prepq.pop(0)
                R.op(o[0], o[1], reads=o[2], writes=o[3], dma=o[4])

    for g in range(2):
        for hh in range(4):
            R.op('sp', f_dma(kt_sb[0:96, hh, :], KT[g * 4 + hh]), writes=[('kt', hh)], dma=('kt', hh))
        R.op('sp', f_dma(v_sb.rearrange("p a b c -> p (a b c)"), Vs[g].rearrange("p a b c -> p (a b c)")), writes=['v'], dma='v')
        its = [(qb, hh, kt) for qb in range(16) for hh in range(4) for kt in range(NKT)]
        n = len(its)
        pend = []

        def emit_S(i):
            qb, hh, kt = its[i]
            if hh == 0 and kt == 0:
                R.op('sp', f_dma(qt_sb[qb % 2][0:96], QTv[:, g * 4:(g + 1) * 4, qb * 512:(qb + 1) * 512]),
                     writes=[('qt', qb % 2)], dma=('qt', qb % 2))
            R.op('pe', f_mm(pS[i % NS], kt_sb[0:96, hh, kt * 128:(kt + 1) * 128], qt_sb[qb % 2][0:96, hh, :], True, True),
                 reads=[('kt', hh), ('qt', qb % 2)], writes=[('pS', i % NS)])
            R.op('act', f_act(PT[i % NP], pS[i % NS], AF.Exp, scale=SCALE), reads=[('pS', i % NS)], writes=[('PT', i % NP)])

        def emit_PV(i):
            qb, hh, kt = its[i]
            hidx = qb * 4 + hh
            o = hidx % 2
            R.op('pe', f_mm(pO[o][0:65, :], v_sb[:, kt, hh, 0:65], PT[i % NP], kt == 0, kt == NKT - 1),
                 reads=['v', ('PT', i % NP)], writes=[('pO', o)])
            if kt == NKT - 1:
                R.op('dve', f_recip(rden[o][64:65, :], pO[o][64:65, :]), reads=[('pO', o)], writes=[('rden', o)])
                R.op('dve', f_copy(rhi[o][64:65, :], rden[o][64:65, :]), reads=[('rden', o)], writes=[('rhi', o)])
                R.op('dve', f_tt(rtmp[64:65, :], rden[o][64:65, :], rhi[o][64:65, :], ALU.subtract), reads=[('rden', o), ('rhi', o)], writes=['rtmp'])
                R.op('dve', f_copy(rlo[o][64:65, :], rtmp[64:65, :]), reads=['rtmp'], writes=[('rlo', o)])
                pend.append((i + 6, hidx, qb, hh))

        def emit_epi(hidx, qb, hh):
            o = hidx % 2
            R.op('pe', f_mm(pB, ones_sel, rhi[o], True, False), reads=[('rhi', o), 'ones_sel'], writes=['pB'])
            R.op('pe', f_mm(pB, ones_sel, rlo[o], False, True), reads=[('rlo', o), 'ones_sel'], writes=['pB'])
            R.op('dve', f_copy(bcs[o][0:64], pB[0:64, :]), reads=['pB'], writes=[('bcs', o)])
            R.op('dve', f_tt(OTn[o][0:64], pO[o][0:64, :], bcs[o][0:64], ALU.mult), reads=[('pO', o), ('bcs', o)], writes=[('OTn', o)])
            r0 = (g * 4 + hh) * 64
            R.op('pool', f_dma(OT[r0:r0 + 64, qb * 512:(qb + 1) * 512], OTn[o][0:64]), reads=[('OTn', o)], writes=[('OT', g, hidx)], dma=('otst', o))

        for i in range(n + LOOK):
            if i < n:
                emit_S(i)
            if i >= LOOK:
                emit_PV(i - LOOK)
            while pend and (pend[0][0] <= i - LOOK or i == n + LOOK - 1):
                _, hidx, qb, hh = pend.pop(0)
                emit_epi(hidx, qb, hh)
            if i % PSTEP == 0:
                emit_prep()
    emit_prep(len(prepq))

    R.barrier()
    if stop == 2:
        R.finalize()
        return nc
    AR.release(0)

    R.bank = {'pT': 0, ('pG', 0): 1, ('pG', 1): 2, ('pD', 0): 3, ('pD', 1): 5, ('pY', 0): 4, ('pY', 1): 6, ('pM', 0): 5, ('pM', 1): 1, ('pP', 0): 6, ('pP', 1): 2, ('pW', 0): 7, ('pW', 1): 3}
    wg = AR.alloc([8, 2048], BF16)
    wbm = AR.alloc([4, 1024], BF16)
    wbp = AR.alloc([4, 1024], BF16)
    poolw = AR.alloc([4, 128], BF16)
    wo = AR.alloc([8, 1024], BF16)
    band = AR.alloc([20, 128], BF16)
    R.op('sp', f_dma(band.rearrange("p a b -> p (a b)"), band_d), writes=['band'], dma='band')
    for nm, dst, src in [('wg', wg, WB_wg), ('wbm', wbm, WB_wbm), ('wbp', wbp, WB_wbp), ('poolw', poolw, WB_pw), ('wo', wo, WB_wo)]:
        R.op('sp', f_dma(dst.rearrange("p a b -> p (a b)"), src), writes=[nm], dma=nm)
    R.barrier()

    xs3 = [AR.alloc([4, 1024], F32) for _ in range(2)]
    xn3 = [AR.alloc([1024], BF16) for _ in range(2)]
    xT3 = [AR.alloc([8, 512], BF16) for _ in range(2)]
    sig = AR.alloc([16, 512], BF16)
    us = [AR.alloc([6, 512], BF16) for _ in range(2)]
    dT = [AR.alloc([512], BF16) for _ in range(2)]
    yT = AR.alloc([4, 512], BF16)
    ot_sb = [AR.alloc([4, 512], BF16) for _ in range(2)]
    tA = [AR.alloc([512], F32) for _ in range(2)]
    tB = [AR.alloc([512], F32) for _ in range(2)]
    mT = AR.alloc([8, 512], BF16)

    pT = psb(0, BF16).rearrange("p (c n) -> p c n", c=8)
    pG = [psb(1), psb(2)]
    pDs = [psb(3), psb(5)]
    pYs = [psb(4), psb(6)]
    pMs = [psb(5), psb(1)]
    pPs = [psb(6), psb(2)]
    pWs = [psb(7), psb(3)]

    def F3(G, b):
        s = b % 2
        G.op('sp', f_dma(xs3[s], x[b * 512:(b + 1) * 512, :].rearrange("(j p) f -> p j f", p=128)),
             writes=[('xs3', s, j) for j in range(4)], dma=('xs3', s))
        G.op('sp', f_dma(ot_sb[s], OT[:, b * 512:(b + 1) * 512].rearrange("(c p) t -> p c t", p=128)), writes=[('ot', s)], dma=('ot', s))
        lo = max(4 * b - 1, 0)
        hi = min(4 * b + 5, 64)
        d0 = lo - (4 * b - 1)
        G.op('sp', f_dma(us[s][:, d0:d0 + hi - lo, :], U[lo * 128:hi * 128, :].rearrange("(j p) f -> p j f", p=128)),
             writes=[('us', s)], dma=('us', s))
        for j in range(4):
            T = 4 * b + j
            G.op('act', f_act(xn3[j % 2], xs3[s][:, j, :], AF.Copy, scale=rstd1[:, T:T + 1]), reads=[('xs3', s, j)], writes=[('xn3', j % 2)])
            for c in range(8):
                G.op('pe', f_tr(pT[:, c, :], xn3[j % 2][:, c * 128:(c + 1) * 128], ident[:]), reads=[('xn3', j % 2)], writes=['pT'])
            G.op('dve', f_copy(xT3[s][:, :, j * 128:(j + 1) * 128], pT), reads=['pT'], writes=[('xT3', s, j)])

    def G3(G, b):
        s = b % 2
        xT3keys = [('xT3', s, j) for j in range(4)]
        for m in range(16):
            for c in range(8):
                G.op('pe', f_mm(pG[m % 2], wg[:, c, m * 128:(m + 1) * 128], xT3[s][:, c, :], c == 0, c == 7), reads=xT3keys, writes=[('pG', m % 2)])
            G.op('act', f_act(sig[:, m, :], pG[m % 2], AF.Sigmoid, bias=bias_g[:, m:m + 1]), reads=[('pG', m % 2)], writes=[('sig', m)])
        for g in range(4):
            for j in range(4):
                T = 4 * b + j
                parts = []
                if T > 0:
                    parts.append((j, g * 5 + 0))
                parts.append((j + 1, g * 5 + (3 if T == 0 else (4 if T == 63 else 1))))
                if T < 63:
                    parts.append((j + 2, g * 5 + 2))
                for k, (slot, bi) in enumerate(parts):
                    G.op('pe', f_mm(pDs[g % 2][:, j * 128:(j + 1) * 128], us[s][:, slot, g * 128:(g + 1) * 128], band[:, bi, :], k == 0, k == len(parts) - 1),
                         reads=[('us', s), 'band'], writes=[('pD', g % 2)])
            G.op('dve', f_copy(dT[g % 2], pDs[g % 2]), reads=[('pD', g % 2)], writes=[('dT', g % 2)])
            G.op('pe', f_mm(pYs[g % 2], poolw[:, g, :], dT[g % 2], True, True), reads=[('dT', g % 2)], writes=[('pY', g % 2)])
            G.op('act', f_copy_act(yT[:, g, :], pYs[g % 2]), reads=[('pY', g % 2)], writes=[('yT', g)])
        yTkeys = [('yT', g) for g in range(4)]
        for m in range(8):
            k2 = m % 2
            for c in range(4):
                G.op('pe', f_mm(pMs[k2], wbm[:, c, m * 128:(m + 1) * 128], ot_sb[s][:, c, :], c == 0, c == 3), reads=[('ot', s)], writes=[('pM', k2)])
            for g in range(4):
                G.op('pe', f_mm(pPs[k2], wbp[:, g, m * 128:(m + 1) * 128], yT[:, g, :], g == 0, g == 3), reads=yTkeys, writes=[('pP', k2)])
            G.op('dve', f_tt(tA[k2], pMs[k2], sig[:, m, :], ALU.mult), reads=[('pM', k2), ('sig', m)], writes=[('tA', k2)])
            G.op('dve', f_tt(tB[k2], pPs[k2], sig[:, 8 + m, :], ALU.mult), reads=[('pP', k2), ('sig', 8 + m)], writes=[('tB', k2)])
            G.op('pool', f_tt(mT[:, m, :], tA[k2], tB[k2], ALU.add), reads=[('tA', k2), ('tB', k2)], writes=[('mT', m)])
        mTkeys = [('mT', m) for m in range(8)]
        for j in range(4):
            for half in range(2):
                k2 = half
                for c in range(8):
                    G.op('pe', f_mm(pWs[k2], mT[:, c, j * 128:(j + 1) * 128], wo[:, c, half * 512:(half + 1) * 512], c == 0, c == 7), reads=mTkeys, writes=[('pW', k2)])
                G.op('dve', f_tt(xs3[s][:, j, half * 512:(half + 1) * 512], pWs[k2], xs3[s][:, j, half * 512:(half + 1) * 512], ALU.add),
                     reads=[('pW', k2), ('xs3', s, j)], writes=[('xs3', s, j)])
        G.op('pool', f_dma(X1[b * 512:(b + 1) * 512, :].rearrange("(j p) f -> p j f", p=128), xs3[s]),
             reads=[('xs3', s, j) for j in range(4)], writes=[('X1', b)], dma=('x1st', s))

    g0 = Stage(); F3(g0, 0); emit_merged([g0])
    for b in range(16):
        stages = [Stage()]
        G3(stages[0], b)
        if b + 1 < 16:
            g1 = Stage(); F3(g1, b + 1); stages.append(g1)
        emit_merged(stages)

    R.barrier()
    if stop == 3:
        R.finalize()
        return nc
    AR.release(0)

    R.bank = {'pB1': 0, 'pT': 0, ('pH', 0): 1, ('pH', 1): 2, ('pH', 2): 3, ('pY2', 0): 4, ('pY2', 1): 5}
    w1 = AR.alloc([8, 4096], BF16)
    w2 = AR.alloc([32, 1024], BF16)
    fg = AR.alloc([1024], F32)
    b1 = b1p
    R.op('sp', f_dma(fg, bigbc_d[:, 2048:3072]), writes=['fg'], dma='fg')
    R.op('sp', f_dma(w1.rearrange("p a b -> p (a b)"), WB_w1), writes=['w1'], dma='w1')
    R.op('sp', f_dma(w2.rearrange("p a b -> p (a b)"), WB_w2), writes=['w2'], dma='w2')
    R.barrier()

    x1s = [AR.alloc([2, 1024], F32) for _ in range(2)]
    xn4 = [AR.alloc([1024], BF16) for _ in range(2)]
    h2T = [AR.alloc([8, 256], BF16) for _ in range(2)]
    rr = [AR.alloc([256], F32) for _ in range(2)]
    hidT = AR.alloc([32, 256], BF16)
    x2 = [AR.alloc([1024], F32) for _ in range(2)]
    outs = x2
    junk4 = AR.alloc([1024], BF16)

    pT = psb(0, BF16).rearrange("p (c n) -> p c n", c=8)
    pH = [psb(1), psb(2), psb(3)]
    pY2 = [psb(4), psb(5)]
    outkeys = []
    def F4(G, b):
        s = b % 2
        G.op('sp', f_dma(x1s[s], X1[b * 256:(b + 1) * 256, :].rearrange("(j p) f -> p j f", p=128)),
             writes=[('x1s', s, 0), ('x1s', s, 1)], dma=('x1s', s))
        for j in range(2):
            G.op('act', f_act(junk4, x1s[s][:, j, :], AF.Square, accum_out=st(9, j)), reads=[('x1s', s, j)], writes=[('ss2', j)])
            G.op('act', f_act(st(10, j), st(9, j), AF.Sqrt, scale=1.0 / D, bias=epsb[:]), reads=[('ss2', j)], writes=[('sd2', j)])
            G.op('dve', f_recip(st(11, j), st(10, j)), reads=[('sd2', j)], writes=[('r2', j)])
            G.op('dve', f_ts(xn4[j], x1s[s][:, j, :], st(11, j), ALU.mult), reads=[('x1s', s, j), ('r2', j)], writes=[('xn4', j)])
            for c in range(8):
                G.op('pe', f_tr(pT[:, c, :], xn4[j][:, c * 128:(c + 1) * 128], ident[:]), reads=[('xn4', j)], writes=['pT'])
            G.op('act', f_copy_act(h2T[s][:, :, j * 128:(j + 1) * 128], pT), reads=['pT'], writes=[('h2T', s, j)])

    def M1(G, b):
        s = b % 2
        for m in range(32):
            for c in range(8):
                G.op('pe', f_mm(pH[m % 3][:, 0:256], w1[:, c, m * 128:(m + 1) * 128], h2T[s][:, c, :], c == 0, c == 7),
                     reads=[('h2T', s, 0), ('h2T', s, 1)], writes=[('pH', m % 3)])
            G.op('act', f_act(rr[m % 2], pH[m % 3][:, 0:256], AF.Relu, bias=b1[:, m:m + 1]), reads=[('pH', m % 3)], writes=[('rr', m % 2)])
            G.op('dve' if m % 2 == 0 else 'pool', f_tt(hidT[:, m, :], rr[m % 2], rr[m % 2], ALU.mult), reads=[('rr', m % 2)], writes=[('hidT', m)])

    def M2(G, b):
        s = b % 2
        hkeys = [('hidT', m) for m in range(32)]
        for j in range(2):
            for half in range(2):
                pk = (j * 2 + half) % 2
                for m in range(32):
                    G.op('pe', f_mm(pY2[pk], hidT[:, m, j * 128:(j + 1) * 128], w2[:, m, half * 512:(half + 1) * 512], m == 0, m == 31),
                         reads=hkeys, writes=[('pY2', pk)])
                G.op('dve', f_tt(x2[j][:, half * 512:(half + 1) * 512], pY2[pk], x1s[s][:, j, half * 512:(half + 1) * 512], ALU.add),
                     reads=[('pY2', pk), ('x1s', s, j)], writes=[('x2', j, half)])
            G.op('act', f_act(junk4, x2[j], AF.Square, accum_out=st(12, j)), reads=[('x2', j, 0), ('x2', j, 1)], writes=[('ss3', j)])
            G.op('act', f_act(st(13, j), st(12, j), AF.Sqrt, scale=1.0 / D, bias=epsb[:]), reads=[('ss3', j)], writes=[('sd3', j)])
            G.op('dve', f_recip(st(14, j), st(13, j)), reads=[('sd3', j)], writes=[('r3', j)])
            G.op('dve', f_stt(outs[j], x2[j], st(14, j), fg, ALU.mult, ALU.mult), reads=[('x2', j, 0), ('x2', j, 1), ('r3', j), 'fg'],
                 writes=[('x2', j, 0), ('x2', j, 1)])
            row = (b * 2 + j) * 128
            G.op('pool', f_dma(out[row:row + 128, :], outs[j]), reads=[('x2', j, 0), ('x2', j, 1)], writes=[('out', b, j)], dma=('outst', j))
            outkeys.append(('out', b, j))

    g0 = Stage(); F4(g0, 0); emit_merged([g0])
    for b in range(32):
        stages = [Stage()]
        M1(stages[0], b)
        if b + 1 < 32:
            g1 = Stage(); F4(g1, b + 1); stages.append(g1)
        emit_merged(stages)
        g2 = Stage(); M2(g2, b); emit_merged([g2])
    R.op('sp', None, reads=outkeys)
    R.finalize()
    return nc


def f_copy_act(out, in_):
    return lambda e: e.activation(out=out, in_=in_, func=AF.Copy)


def _host_consts():
    bf = ml_dtypes.bfloat16
    ident = np.eye(128, dtype=np.float32).astype(bf)
    t = np.arange(L)
    row = (t // 64).astype(np.float32)
    col = (t % 64).astype(np.float32)
    inv_freq = (np.float32(10000.0) ** (-np.arange(0, 16, 2, dtype=np.float32) / np.float32(16))).astype(np.float32)
    ar = (row[:, None] * inv_freq).astype(np.float32)
    ac = (col[:, None] * inv_freq).astype(np.float32)
    cr, sr, cc, sc = np.cos(ar), np.sin(ar), np.cos(ac), np.sin(ac)
    cosF = np.concatenate([cr, cr, cc, cc], axis=1).astype(np.float32)
    sinS = np.concatenate([-sr, sr, -sc, sc], axis=1).astype(np.float32)
    cosF = np.ascontiguousarray(cosF.reshape(64, 128, 32).transpose(1, 0, 2).reshape(128, 64 * 32))
    sinS = np.ascontiguousarray(sinS.reshape(64, 128, 32).transpose(1, 0, 2).reshape(128, 64 * 32))
    band = np.zeros((128, 20, 128), np.float32)
    for g, w in enumerate((2, 4, 8, 16)):
        hw = w // 2
        for kind in range(5):
            T = {0: 5, 1: 5, 2: 5, 3: 0, 4: 63}[kind]
            dT = {0: -1, 1: 0, 2: 1, 3: 0, 4: 0}[kind]
            m = np.zeros((128, 128), np.float32)
            for tl in range(128):
                tg = T * 128 + tl
                lo = max(tg - hw, 0)
                hi = min(tg + hw, L)
                cnt = hi - lo
                for tp in range(lo, hi):
                    p = tp - (T + dT) * 128
                    if 0 <= p < 128:
                        m[p, tl] += 1.0 / cnt
                p = tg - (T + dT) * 128
                if 0 <= p < 128:
                    m[p, tl] -= 1.0
            band[:, g * 5 + kind, :] = m
    band = np.ascontiguousarray(band.reshape(128, 20 * 128)).astype(bf)
    return ident, cosF, sinS, band


_CACHE = {}


def prep_inputs(x, c, ctx, c_ctx, w_ada, b_ada, norm1_g, w_in, q_norm_g, kv_norm_g, w_uq, w_ukv,
                w_br_mla, pool_w, pool_scale, w_br_pool, w_out, norm2_g, w_mlp1, w_mlp2, final_g):
    f = lambda a: np.ascontiguousarray(np.asarray(a, dtype=np.float32))
    x, c, ctx, c_ctx = f(x), f(c), f(ctx), f(c_ctx)
    w_ada, b_ada, w_in, w_uq, w_ukv = f(w_ada)[0], f(b_ada)[0], f(w_in)[0], f(w_uq)[0], f(w_ukv)[0]
    w_br_mla, pool_w, w_br_pool, w_out = f(w_br_mla)[0], f(pool_w)[0], f(w_br_pool)[0], f(w_out)[0]
    w_mlp1, w_mlp2 = f(w_mlp1)[0], f(w_mlp2)[0]
    norm1_g, norm2_g, q_norm_g, kv_norm_g, pool_scale, final_g = (f(norm1_g)[0], f(norm2_g)[0], f(q_norm_g)[0],
                                                                   f(kv_norm_g)[0], f(pool_scale)[0], f(final_g))
    if 'consts' not in _CACHE:
        _CACHE['consts'] = _host_consts()
    ident, cosF, sinS, band = _CACHE['consts']
    col = lambda v: v.reshape(-1, 128).T
    bigbc = np.ascontiguousarray(np.broadcast_to(
        np.concatenate([b_ada[2 * D:3 * D], b_ada[5 * D:6 * D], final_g])[None, :], (128, 3 * D)))
    shared = dict(w_ada=w_ada, w_in=w_in, w_uq=w_uq, w_ukv=w_ukv, w_br_mla=w_br_mla, pool_w=pool_w,
                  w_br_pool=w_br_pool, w_out=w_out, w_mlp1=w_mlp1, w_mlp2=w_mlp2, ident=ident,
                  cosF=cosF, sinS=sinS, band=band, bigbc=bigbc)
    in_maps = []
    for b in range(x.shape[0]):
        ccol = np.stack([col(c[b]), col(c_ctx)], axis=2).reshape(128, 16)
        smallf = np.ascontiguousarray(np.concatenate(
            [ccol, col(b_ada), col(norm1_g), col(norm2_g), col(q_norm_g), col(kv_norm_g), col(pool_scale)], axis=1).astype(np.float32))
        assert smallf.shape == (128, 89)
        m = dict(shared)
        m.update(x=x[b], ctx=ctx[b], smallf=smallf)
        in_maps.append(m)
    return in_maps


def kernel(**inputs):
    in_maps = prep_inputs(**inputs)
    if 'nc' not in _CACHE:
        _CACHE['nc'] = build_program()
    nc = _CACHE['nc']
    res = run_bass_kernel_spmd(nc, in_maps, core_ids=list(range(8)))
    return np.stack([np.asarray(r["out"], dtype=np.float32) for r in res.results], axis=0)
```

```python
import numpy as np
import ml_dtypes
import concourse.bass as bass
import concourse.mybir as mybir
from concourse.bass_utils import run_bass_kernel_spmd

F32 = mybir.dt.float32
BF16 = mybir.dt.bfloat16
AF = mybir.ActivationFunctionType
ALU = mybir.AluOpType

D = 1024
L = 8192
CTX = 256
NKT = (L + CTX) // 128
H = 8
EPS = 1e-6
SCALE = 96.0 ** -0.5
ENGS = ['pe', 'act', 'dve', 'pool', 'sp']


class Rec:
    def __init__(self, nc):
        self.nc = nc
        self.ops = []
        self.lastw = {}
        self.rd_eng = {}
        self.rd_dma = {}
        self.bank = {}
        self.bank_last = {}

    def op(self, eng, fn, reads=(), writes=(), dma=None):
        j = len(self.ops)
        deps = {}
        bset = set()
        for k in list(reads) + list(writes):
            if k in self.bank:
                bv = self.bank[k]
                bset.update(bv if isinstance(bv, tuple) else (bv,))
        for bnk in bset:
            la = self.bank_last.setdefault(bnk, {})
            for e2, i in la.items():
                if e2 != eng:
                    deps.setdefault(i, False)
            la[eng] = j
        for k in reads:
            i = self.lastw.get(k)
            if i is not None:
                deps[i] = True
        for k in writes:
            for i in self.rd_eng.get(k, {}).values():
                deps.setdefault(i, False)
            for i in self.rd_dma.get(k, ()):
                deps.setdefault(i, False)
            i = self.lastw.get(k)
            if i is not None:
                deps.setdefault(i, False)
        for k in reads:
            if dma is None:
                self.rd_eng.setdefault(k, {})[eng] = j
            else:
                self.rd_dma.setdefault(k, []).append(j)
        for k in writes:
            self.lastw[k] = j
            self.rd_eng[k] = {}
            self.rd_dma[k] = []
        keep = []
        for i, raw in deps.items():
            oi = self.ops[i]
            if oi['dma'] is None and dma is None and oi['eng'] == eng:
                if not raw or eng == 'pe':
                    continue
            keep.append(i)
        self.ops.append(dict(eng=eng, fn=fn, dma=dma, deps=keep, sig=False))
        return j

    def barrier(self):
        last = {}
        for idx, o in enumerate(self.ops):
            if o['fn'] is None:
                continue
            if o['dma'] is not None:
                last[('d', o['dma'])] = idx
            else:
                last[('e', o['eng'])] = idx
        deps = list(last.values())
        for e in ENGS:
            self.ops.append(dict(eng=e, fn=None, dma=None, deps=list(deps), sig=False))
        self.lastw.clear()
        self.rd_eng.clear()
        self.rd_dma.clear()
        self.bank_last.clear()

    def finalize(self):
        nc = self.nc
        ops = self.ops
        for o in ops:
            for i in o['deps']:
                ops[i]['sig'] = True
        esem = {e: nc.alloc_semaphore("s_" + e) for e in ENGS}
        dsem = {}
        ecnt = {e: 0 for e in ENGS}
        dcnt = {}
        for o in ops:
            if o['fn'] is None:
                continue
            if o['dma'] is not None:
                k = o['dma']
                if k not in dsem:
                    dsem[k] = nc.alloc_semaphore("d_%d" % len(dsem))
                    dcnt[k] = 0
                dcnt[k] += 16
                o['sem'] = dsem[k]
                o['semid'] = ('d', k)
                o['val'] = dcnt[k]
            elif o['sig']:
                ecnt[o['eng']] += 1
                o['sem'] = esem[o['eng']]
                o['semid'] = ('e', o['eng'])
                o['val'] = ecnt[o['eng']]
        streams = {e: [] for e in ENGS}
        for idx, o in enumerate(ops):
            streams[o['eng']].append(idx)

        def run(eng_name, engine):
            seen = {}
            for idx in streams[eng_name]:
                o = ops[idx]
                need = {}
                for i in o['deps']:
                    d = ops[i]
                    sid, v = d['semid'], d['val']
                    if seen.get(sid, 0) >= v:
                        continue
                    if need.get(sid, (None, 0))[1] < v:
                        need[sid] = (d['sem'], v)
                for sid, (s, v) in need.items():
                    seen[sid] = v
                    engine.wait_ge(s, v)
                if o['fn'] is None:
                    continue
                ins = o['fn'](engine)
                if o['dma'] is not None:
                    ins.then_inc(o['sem'], 16)
                elif o['sig']:
                    ins.then_inc(o['sem'], 1)

        with nc.Block() as block:
            @block.tensor
            def _(e):
                run('pe', e)

            @block.scalar
            def _(e):
                run('act', e)

            @block.vector
            def _(e):
                run('dve', e)

            @block.gpsimd
            def _(e):
                run('pool', e)

            @block.sync
            def _(e):
                run('sp', e)


class Arena:
    def __init__(self, nc, nbytes):
        self.t = nc.alloc_sbuf_tensor("arena", [128, nbytes // 2], BF16)
        self.cap = nbytes
        self.off = 0

    def mark(self):
        return self.off

    def release(self, m):
        self.off = m

    def alloc(self, shape, dtype, parts=128):
        esz = 4 if dtype == F32 else 2
        n = 1
        for s in shape:
            n *= s
        nb = n * esz
        off = (self.off + 63) // 64 * 64
        assert off + nb <= self.cap, ("arena overflow", off, nb, self.cap)
        self.off = off + nb
        ap = self.t[0:parts, off // 2:(off + nb) // 2]
        if dtype == F32:
            ap = ap.bitcast(F32)
        if len(shape) > 1:
            names = ["a%d" % i for i in range(len(shape))]
            pat = "p (" + " ".join(names) + ") -> p " + " ".join(names)
            ap = ap.rearrange(pat, **{nm: s for nm, s in zip(names[1:], shape[1:])})
        return ap


def f_dma(out, in_):
    return lambda e: e.dma_start(out=out, in_=in_)


def f_mm(out, lhsT, rhs, start, stop):
    return lambda e: e.matmul(out, lhsT=lhsT, rhs=rhs, start=start, stop=stop)


def f_tr(out, in_, ident):
    return lambda e: e.transpose(out=out, in_=in_, identity=ident)


def f_act(out, in_, func, **kw):
    return lambda e: e.activation(out=out, in_=in_, func=func, **kw)


def f_tt(out, in0, in1, op):
    return lambda e: e.tensor_tensor(out=out, in0=in0, in1=in1, op=op)


def f_ts(out, in0, s1, op0, s2=None, op1=None):
    if op1 is None:
        return lambda e: e.tensor_scalar(out=out, in0=in0, scalar1=s1, scalar2=None, op0=op0)
    return lambda e: e.tensor_scalar(out=out, in0=in0, scalar1=s1, scalar2=s2, op0=op0, op1=op1)


def f_stt(out, in0, scalar, in1, op0, op1):
    return lambda e: e.scalar_tensor_tensor(out=out, in0=in0, scalar=scalar, in1=in1, op0=op0, op1=op1)


def f_copy(out, in_):
    return lambda e: e.tensor_copy(out=out, in_=in_)


def f_recip(out, in_):
    return lambda e: e.reciprocal(out=out, in_=in_)


def f_memset(out, v):
    return lambda e: e.memset(out, v)


def cast_scaled(R, k, out, in_, scal, reads, writes):
    eng = ('dve', 'act', 'pool')[k % 3]
    if scal is None:
        if eng == 'act':
            R.op('act', f_act(out, in_, AF.Copy), reads, writes)
        else:
            R.op(eng, f_copy(out, in_), reads, writes)
    elif eng == 'act':
        R.op('act', f_act(out, in_, AF.Copy, scale=scal), reads, writes)
    elif eng == 'dve':
        R.op('dve', f_ts(out, in_, scal, ALU.mult), reads, writes)
    else:
        R.op('pool', f_ts(out, in_, scal, ALU.mult, 1.0, ALU.mult), reads, writes)


def build_program(stop=9, dbg=False):
    nc = bass.Bass("TRN2", target_bir_lowering=False)
    skind = "ExternalOutput" if dbg else "Internal"

    def din(name, shape, dt=F32):
        return nc.dram_tensor(name, shape, dt, kind="ExternalInput").ap()

    x = din("x", [L, D])
    ctx = din("ctx", [CTX, D])
    smallf = din("smallf", [128, 89])
    bigbc_d = din("bigbc", [128, 3072])
    w_ada = din("w_ada", [D, 6 * D])
    w_in = din("w_in", [D, 3232])
    w_uq = din("w_uq", [384, 768])
    w_ukv = din("w_ukv", [256, 1024])
    w_br_mla = din("w_br_mla", [512, 1024])
    pool_w = din("pool_w", [4, 128, 128])
    w_br_pool = din("w_br_pool", [512, 1024])
    w_out = din("w_out", [D, D])
    w_mlp1 = din("w_mlp1", [D, 4096])
    w_mlp2 = din("w_mlp2", [4096, D])
    ident_d = din("ident", [128, 128], BF16)
    cosF_d = din("cosF", [128, 64 * 32])
    sinS_d = din("sinS", [128, 64 * 32])
    band_d = din("band", [128, 20 * 128], BF16)
    out = nc.dram_tensor("out", [L, D], F32, kind="ExternalOutput").ap()

    QT = nc.dram_tensor("QT", [H, 96, L], BF16, kind=skind).ap()
    KT = nc.dram_tensor("KT", [H, 96, NKT * 128], BF16, kind=skind).ap()
    Vs = nc.dram_tensor("Vs", [2, 128, NKT, 4, 66], BF16, kind=skind).ap()
    OT = nc.dram_tensor("OT", [512, L], BF16, kind=skind).ap()
    U = nc.dram_tensor("U", [L, 512], BF16, kind=skind).ap()
    X1 = nc.dram_tensor("X1", [L, D], F32, kind=skind).ap()
    WB_wg = nc.dram_tensor("WB_wg", [128, 8 * 2048], BF16, kind="Internal").ap()
    WB_wbm = nc.dram_tensor("WB_wbm", [128, 4 * 1024], BF16, kind="Internal").ap()
    WB_wbp = nc.dram_tensor("WB_wbp", [128, 4 * 1024], BF16, kind="Internal").ap()
    WB_pw = nc.dram_tensor("WB_pw", [128, 4 * 128], BF16, kind="Internal").ap()
    WB_wo = nc.dram_tensor("WB_wo", [128, 8 * 1024], BF16, kind="Internal").ap()
    WB_w1 = nc.dram_tensor("WB_w1", [128, 8 * 4096], BF16, kind="Internal").ap()
    WB_w2 = nc.dram_tensor("WB_w2", [128, 32 * 1024], BF16, kind="Internal").ap()

    R = Rec(nc)

    def gt(name, shape, dt=F32):
        return nc.alloc_sbuf_tensor(name, shape, dt)

    sm = gt("sm", [128, 89])
    ident = gt("ident_s", [128, 128], BF16)
    ones_f = gt("ones_f", [128, 128])
    epsb = gt("epsb", [128, 1])
    sil = gt("sil", [128, 8, 2])
    vcol = gt("vcol", [128, 4, 8, 2])
    gtbc = gt("gtbc", [128, 2, 1024])
    A1 = gt("A1", [128, 8])
    cA1 = gt("cA1", [128, 8])
    A2 = gt("A2", [128, 8])
    rstd1 = gt("rstd1", [128, 64])
    stat = gt("stat", [128, 64])
    bias_g = gt("bias_g", [128, 16])
    b1p = gt("b1p", [128, 32])
    AR = Arena(nc, 196 * 1024)

    PSall = nc.alloc_psum_tensor("psall", [128, 8, 512], F32)

    def psb(i, dt=F32):
        ap = PSall[:, i, :]
        if dt == BF16:
            ap = ap.bitcast(BF16)
        return ap

    R.bank = {('pV', 0): 0, ('pV', 1): 1, ('pR', 0): 2, ('pR', 1): 3, 'pBias0': 4, 'pBias1': 5}
    R.op('sp', f_dma(sm[:], smallf), writes=['sm'], dma='sm')
    R.op('sp', f_dma(ident[:], ident_d), writes=['ident'], dma='ident')
    R.op('dve', f_memset(ones_f[:], 1.0), writes=['ones'])
    R.op('dve', f_memset(epsb[:], EPS), writes=['epsb'])
    R.op('act', f_act(sil[:].rearrange("p c v -> p (c v)"), sm[:, 0:16], AF.Silu), reads=['sm'], writes=['sil'])

    w1x = AR.alloc([8, 1184], BF16)
    w1c = AR.alloc([8, 288], BF16)
    wuq = AR.alloc([3, 768], BF16)
    wukv = AR.alloc([2, 1024], BF16)
    bias1x = AR.alloc([1184], F32)
    bias1c = AR.alloc([288], F32)
    cosF = AR.alloc([64, 32], F32)
    sinS = AR.alloc([64, 32], F32)
    m_p1 = AR.mark()
    stgh = {'b': [AR.alloc([8192], F32) for _ in range(2)]}
    bigbc = AR.alloc([2048], F32)
    R.op('sp', f_dma(bigbc, bigbc_d[:, 0:2048]), writes=['bigbc'], dma='bigbc')
    sil_rep = AR.alloc([8, 128], F32)
    sh1_rep = AR.alloc([8, 128], F32)
    csh1_rep = AR.alloc([8, 128], F32)

    R.op('sp', f_dma(cosF.rearrange("p a b -> p (a b)"), cosF_d), writes=['cosF'], dma='cosF')
    R.op('sp', f_dma(sinS.rearrange("p a b -> p (a b)"), sinS_d), writes=['sinS'], dma='sinS')

    for c in range(8):
        R.op('dve', f_ts(sil_rep[:, c, :], ones_f[:], sil[:, c, 0:1], ALU.mult),
             reads=['ones', 'sil'], writes=[('silrep', c)])

    stg_n = [0]

    def stage_load(src_ap, kc, ncols):
        i = stg_n[0] % 2
        stg_n[0] += 1
        view = stgh['b'][i][:, 0:kc * ncols].rearrange("p (c n) -> p c n", c=kc)
        R.op('sp', f_dma(view, src_ap), writes=[('stg', i)], dma=('stg', i))
        return view, ('stg', i)

    pV = [psb(0)[:, 0:16].rearrange("p (m v) -> p m v", v=2), psb(1)[:, 0:16].rearrange("p (m v) -> p m v", v=2)]
    pR = [psb(2), psb(3)]
    vmap = {0: 0, 1: 1, 3: 2, 4: 3}
    for v in range(6):
        view, skey = stage_load(w_ada[:, v * 1024:(v + 1) * 1024].rearrange("(c p) n -> p c n", p=128), 8, 1024)
        if v in vmap:
            vi = vmap[v]
            pk = ('pV', vi % 2)
            for m in range(8):
                for c in range(8):
                    R.op('pe', f_mm(pV[vi % 2][:, m, :], view[:, c, m * 128:(m + 1) * 128], sil[:, c, :], c == 0, c == 7),
                         reads=[skey, 'sil'], writes=[pk])
            R.op('dve', f_tt(vcol[:, vi], pV[vi % 2],
                             sm[:, 16 + v * 8:16 + (v + 1) * 8].unsqueeze(2).broadcast_to([128, 8, 2]), ALU.add),
                 reads=[pk, 'sm'], writes=[('vcol', vi)])
        else:
            gi = 0 if v == 2 else 1
            for half in range(2):
                pk = ('pR', half)
                for c in range(8):
                    R.op('pe', f_mm(pR[half], sil_rep[:, c, :], view[:, c, half * 512:(half + 1) * 512], c == 0, c == 7),
                         reads=[skey, ('silrep', c)], writes=[pk])
                R.op('dve', f_tt(gtbc[:, gi, half * 512:(half + 1) * 512], pR[half],
                                 bigbc[:, gi * 1024 + half * 512:gi * 1024 + (half + 1) * 512], ALU.add),
                     reads=[pk, 'bigbc'], writes=[('gtbc', gi)])

    R.op('dve', f_stt(A1[:], vcol[:, 1, :, 0], 1.0, sm[:, 64:72], ALU.add, ALU.mult), reads=[('vcol', 1), 'sm'], writes=['A1'])
    R.op('dve', f_stt(cA1[:], vcol[:, 1, :, 1], 1.0, sm[:, 64:72], ALU.add, ALU.mult), reads=[('vcol', 1), 'sm'], writes=['cA1'])
    R.op('dve', f_stt(A2[:], vcol[:, 3, :, 0], 1.0, sm[:, 72:80], ALU.add, ALU.mult), reads=[('vcol', 3), 'sm'], writes=['A2'])
    for c in range(8):
        R.op('dve', f_ts(sh1_rep[:, c, :], ones_f[:], vcol[:, 0, c, 0:1], ALU.mult), reads=['ones', ('vcol', 0)], writes=[('sh1rep', c)])
        R.op('dve', f_ts(csh1_rep[:, c, :], ones_f[:], vcol[:, 0, c, 1:2], ALU.mult), reads=['ones', ('vcol', 0)], writes=[('csh1rep', c)])

    kk = 0
    pBias = [psb(4), psb(5)]
    for pi, (n0, n1) in enumerate([(0, 384), (384, 672), (672, 1184)]):
        n = n1 - n0
        view, skey = stage_load(w_in[:, n0:n1].rearrange("(c p) n -> p c n", p=128), 8, n)
        for c in range(8):
            cast_scaled(R, kk, w1x[:, c, n0:n1], view[:, c, :], A1[:, c:c + 1], [skey, 'A1'], [('w1x', pi, c)])
            kk += 1
        for c in range(8):
            R.op('pe', f_mm(pBias[0][:, 0:n], sh1_rep[:, c, :], view[:, c, :], c == 0, c == 7),
                 reads=[skey, ('sh1rep', c)], writes=['pBias0'])
        R.op('dve', f_copy(bias1x[:, n0:n1], pBias[0][:, 0:n]), reads=['pBias0'], writes=[('bias1x', pi)])
        if pi == 1:
            for c in range(8):
                cast_scaled(R, kk, w1c[:, c, :], view[:, c, :], cA1[:, c:c + 1], [skey, 'cA1'], [('w1c', c)])
                kk += 1
            for c in range(8):
                R.op('pe', f_mm(pBias[1][:, 0:n], csh1_rep[:, c, :], view[:, c, :], c == 0, c == 7),
                     reads=[skey, ('csh1rep', c)], writes=['pBias1'])
            R.op('dve', f_copy(bias1c[:], pBias[1][:, 0:n]), reads=['pBias1'], writes=['bias1c'])
    view, skey = stage_load(w_uq.rearrange("(c p) n -> p c n", p=128), 3, 768)
    for c in range(3):
        cast_scaled(R, kk, wuq[:, c, :], view[:, c, :], sm[:, 80 + c:81 + c], [skey, 'sm'], [('wuq', c)])
        kk += 1
    view, skey = stage_load(w_ukv.rearrange("(c p) n -> p c n", p=128), 2, 1024)
    for c in range(2):
        cast_scaled(R, kk, wukv[:, c, :], view[:, c, :], sm[:, 83 + c:84 + c], [skey, 'sm'], [('wukv', c)])
        kk += 1

    if dbg:
        dbg0 = nc.dram_tensor("dbg0", [128, 4 * 16 + 2048 + 24 + 1184 + 288], F32, kind="ExternalOutput").ap()
        R.op('sp', f_dma(dbg0[:, 0:64], vcol[:].rearrange("p a b c -> p (a b c)")), reads=[('vcol', i) for i in range(4)], writes=['dbg0a'], dma='dbg0a')
        R.op('sp', f_dma(dbg0[:, 64:2112], gtbc[:].rearrange("p a b -> p (a b)")), reads=[('gtbc', 0), ('gtbc', 1)], writes=['dbg0b'], dma='dbg0b')
        R.op('sp', f_dma(dbg0[:, 2112:2120], A1[:]), reads=['A1'], writes=['dbg0c'], dma='dbg0c')
        R.op('sp', f_dma(dbg0[:, 2120:2128], cA1[:]), reads=['cA1'], writes=['dbg0d'], dma='dbg0d')
        R.op('sp', f_dma(dbg0[:, 2128:2136], A2[:]), reads=['A2'], writes=['dbg0e'], dma='dbg0e')
        R.op('sp', f_dma(dbg0[:, 2136:2136 + 1184], bias1x), reads=[('bias1x', i) for i in range(3)], writes=['dbg0f'], dma='dbg0f')
        R.op('sp', f_dma(dbg0[:, 2136 + 1184:2136 + 1184 + 288], bias1c), reads=['bias1c'], writes=['dbg0g'], dma='dbg0g')
        dbg1 = nc.dram_tensor("dbg1", [128, 8 * 1184], BF16, kind="ExternalOutput").ap()
        R.op('sp', f_dma(dbg1, w1x.rearrange("p a b -> p (a b)")), reads=[('w1x', pi, c) for pi in range(3) for c in range(8)], writes=['dbg1'], dma='dbg1')
    R.barrier()
    if stop == 0:
        R.finalize()
        return nc
    AR.release(m_p1)

    R.bank = {'pT': 0, 'psA': 1, 'psB': 2, 'psC': 3, 'pT2q': 4, 'pT2k': 4, 'psWk': 5, 'psWq': 6, 'pXTk': 7, 'pXTq': 7}
    NX, NB = 4, 4
    xs = [AR.alloc([1024], F32) for _ in range(NX)]
    junk = [AR.alloc([1024], BF16) for _ in range(3)]
    xn = [AR.alloc([1024], BF16) for _ in range(NB)]
    xT = [AR.alloc([8, 128], BF16) for _ in range(NB)]
    u_sb = [AR.alloc([512], BF16) for _ in range(NB)]
    cqn = [AR.alloc([384], BF16) for _ in range(NB)]
    cqT = [AR.alloc([3, 128], BF16) for _ in range(NB)]
    ckvn = [AR.alloc([256], BF16) for _ in range(NB)]
    ckvT = [AR.alloc([2, 128], BF16) for _ in range(NB)]
    krs_sb = [AR.alloc([32], F32) for _ in range(NB)]
    t1 = [AR.alloc([8, 32], F32) for _ in range(NB)]
    t2 = [AR.alloc([8, 32], F32) for _ in range(NB)]
    qf = [AR.alloc([8, 96], BF16) for _ in range(NB)]
    kf = [AR.alloc([8, 96], BF16) for _ in range(NB)]
    vaug = [AR.alloc([8, 66], BF16) for _ in range(NB)]
    qT_sb = [AR.alloc([8, 128], BF16) for _ in range(NB)]
    kT_sb = [AR.alloc([8, 128], BF16) for _ in range(NB)]
    t1k = [AR.alloc([32], F32) for _ in range(NB)]
    t2k = [AR.alloc([32], F32) for _ in range(NB)]
    kr = [AR.alloc([32], F32) for _ in range(NB)]
    ones2 = AR.alloc([128], BF16)
    bias2x = AR.alloc([1184], BF16)
    bias2c = AR.alloc([288], BF16)
    bhi32 = AR.alloc([1184], BF16)
    btmp = AR.alloc([1184], F32)
    R.op('dve', f_memset(ones2, 0.0), writes=['ones2'])
    R.op('dve', f_memset(ones2[0:1], 1.0), writes=['ones2'])
    R.op('dve', f_memset(ones2[32:33], 1.0), writes=['ones2'])
    for (b2, src_, n_, kn) in [(bias2x, bias1x, 1184, 'x'), (bias2c, bias1c, 288, 'c')]:
        R.op('dve', f_memset(b2, 0.0), writes=[('b2', kn)])
        R.op('dve', f_copy(b2[0:1], src_[0:1]), writes=[('b2', kn)])
        R.op('dve', f_copy(bhi32[32:33, 0:n_], src_[32:33]), writes=[('bhi32', kn)])
        R.op('dve', f_tt(btmp[32:33, 0:n_], src_[32:33], bhi32[32:33, 0:n_], ALU.subtract), reads=[('bhi32', kn)], writes=[('btmp', kn)])
        R.op('dve', f_copy(b2[32:33], btmp[32:33, 0:n_]), reads=[('btmp', kn)], writes=[('b2', kn)])
    for s in range(NB):
        R.op('dve', f_memset(vaug[s], 1.0), writes=[('vaug', s)])

    pT = psb(0, BF16).rearrange("p (c n) -> p c n", c=8)
    psA, psB, psC = psb(1), psb(2), psb(3)
    pT2 = psb(4, BF16).rearrange("p (c n) -> p c n", c=8)
    psWk = psb(5).rearrange("p (h d) -> p h d", d=128)
    psWq = psb(6)[:, 0:384].rearrange("p (h d) -> p h d", d=96)
    pXT = psb(7, BF16).rearrange("p (c n) -> p c n", c=8)
    QTv = QT.rearrange("h d t -> d h t")
    KTv = KT.rearrange("h d t -> d h t")

    def st(col, s):
        return stat[:, col * 4 + s:col * 4 + s + 1]

    class Stage:
        def __init__(self):
            self.l = []

        def op(self, eng, fn, reads=(), writes=(), dma=None):
            self.l.append((eng, fn, reads, writes, dma))

    def emit_merged(stages, group_pe=False):
        items = []
        for si, sg in enumerate(stages):
            n = len(sg.l)
            k = 0
            while k < n:
                k1 = k + 1
                if group_pe and sg.l[k][0] == 'pe':
                    while k1 < n and sg.l[k1][0] == 'pe':
                        k1 += 1
                items.append(((k + 0.5) / n, si, k, sg.l[k:k1]))
                k = k1
        items.sort(key=lambda z: (z[0], z[1]))
        for _, _, _, grp in items:
            for o in grp:
                R.op(o[0], o[1], reads=o[2], writes=o[3], dma=o[4])

    def tile_src(t):
        return ctx[t * 128:(t + 1) * 128, :] if t < 2 else x[(t - 2) * 128:(t - 1) * 128, :]

    def SL(G, t):
        G.op('sp', f_dma(xs[t % NX], tile_src(t)), writes=[('xs', t % NX)], dma=('xs', t % NX))

    def S1(G, t):
        is_ctx = t < 2
        xi = t - 2
        sx, s, s4_ = t % NX, t % NB, t % 4
        G.op('act', f_act(junk[0], xs[sx], AF.Square, accum_out=st(0, s4_)), reads=[('xs', sx)], writes=[('ss', s4_)])
        G.op('act', f_act(st(1, s4_), st(0, s4_), AF.Sqrt, scale=1.0 / D, bias=epsb[:]), reads=[('ss', s4_)], writes=[('sd', s4_)])
        if is_ctx:
            rst, rkey = st(2, s4_), ('rstc', s4_)
        else:
            rst, rkey = rstd1[:, xi:xi + 1], ('rstd1', xi)
        G.op('dve', f_recip(rst, st(1, s4_)), reads=[('sd', s4_)], writes=[rkey])
        G.op('dve', f_ts(xn[s], xs[sx], rst, ALU.mult), reads=[('xs', sx), rkey], writes=[('xn', s)])
        for c in range(8):
            G.op('pe', f_tr(pT[:, c, :], xn[s][:, c * 128:(c + 1) * 128], ident[:]), reads=[('xn', s)], writes=['pT'])
        G.op('act', f_copy_act(xT[s].rearrange("p c n -> p (c n)"), psb(0, BF16)), reads=['pT'], writes=[('xT', s)])

    def S1b(G, t):
        is_ctx = t < 2
        xi = t - 2
        sx, s, s4_ = t % NX, t % NB, t % 4
        if not is_ctx:
            groups = [(psA, 'psA', 0, 384), (psB, 'psB', 384, 672), (psC, 'psC', 672, 1184)]
            for (ps, key, n0, n1) in groups:
                for c in range(8):
                    G.op('pe', f_mm(ps[:, 0:n1 - n0], xT[s][:, c, :], w1x[:, c, n0:n1], c == 0, False), reads=[('xT', s)], writes=[key])
                G.op('pe', f_mm(ps[:, 0:n1 - n0], ones2, bias2x[:, n0:n1], False, True), writes=[key])
        else:
            for c in range(8):
                G.op('pe', f_mm(psB[:, 0:288], xT[s][:, c, :], w1c[:, c, :], c == 0, False), reads=[('xT', s)], writes=['psB'])
            G.op('pe', f_mm(psB[:, 0:288], ones2, bias2c, False, True), writes=['psB'])
        G.op('act', f_act(junk[0][:, 0:256], psB[:, 0:256], AF.Square, accum_out=st(3, s4_)), reads=['psB'], writes=[('sskv', s4_)])
        G.op('act', f_act(st(4, s4_), st(3, s4_), AF.Sqrt, scale=1.0 / 256, bias=epsb[:]), reads=[('sskv', s4_)], writes=[('sdkv', s4_)])
        G.op('dve', f_recip(st(5, s4_), st(4, s4_)), reads=[('sdkv', s4_)], writes=[('rkv', s4_)])
        G.op('dve', f_ts(ckvn[s], psB[:, 0:256], st(5, s4_), ALU.mult), reads=['psB', ('rkv', s4_)], writes=[('ckvn', s)])
        G.op('dve', f_copy(krs_sb[s], psB[:, 256:288]), reads=['psB'], writes=[('krs', s)])
        if not is_ctx:
            G.op('act', f_act(junk[0][:, 0:384], psA[:, 0:384], AF.Square, accum_out=st(6, s4_)), reads=['psA'], writes=[('ssq', s4_)])
            G.op('act', f_act(st(7, s4_), st(6, s4_), AF.Sqrt, scale=1.0 / 384, bias=epsb[:]), reads=[('ssq', s4_)], writes=[('sdq', s4_)])
            G.op('dve', f_recip(st(8, s4_), st(7, s4_)), reads=[('sdq', s4_)], writes=[('rq', s4_)])
            G.op('dve', f_ts(cqn[s], psA[:, 0:384], st(8, s4_), ALU.mult), reads=['psA', ('rq', s4_)], writes=[('cqn', s)])
            G.op('act', f_copy_act(u_sb[s], psC[:, 0:512]), reads=['psC'], writes=[('u', s)])
            G.op('pool', f_dma(U[xi * 128:(xi + 1) * 128, :], u_sb[s]), reads=[('u', s)], writes=[('U', xi)], dma=('ust', s))

    def S2(G, t):
        is_ctx = t < 2
        xi = t - 2
        s = t % NB
        for c in range(2):
            G.op('pe', f_tr(pT2[:, 3 + c, :], ckvn[s][:, c * 128:(c + 1) * 128], ident[:]), reads=[('ckvn', s)], writes=['pT2k'])
        G.op('dve', f_copy(ckvT[s], pT2[:, 3:5, :]), reads=['pT2k'], writes=[('ckvT', s)])
        if is_ctx:
            krsrc, krkey = krs_sb[s], ('krs', s)
        else:
            cos_t = cosF[:, xi, :]
            sin_t = sinS[:, xi, :]
            G.op('pool', f_tt(t1k[s], krs_sb[s], cos_t, ALU.mult), reads=[('krs', s)], writes=[('t1k', s)])
            kv4 = krs_sb[s].rearrange("p (a f i) -> p a f i", a=2, f=2)
            o4 = t2k[s].rearrange("p (a f i) -> p a f i", a=2, f=2)
            s4 = sin_t.rearrange("p (a f i) -> p a f i", a=2, f=2)
            for f in range(2):
                G.op('pool', f_tt(o4[:, :, f, :], kv4[:, :, 1 - f, :], s4[:, :, f, :], ALU.mult), reads=[('krs', s)], writes=[('t2k', s, f)])
            G.op('pool', f_tt(kr[s], t1k[s], t2k[s], ALU.add), reads=[('t1k', s), ('t2k', s, 0), ('t2k', s, 1)], writes=[('kr', s)])
            krsrc, krkey = kr[s], ('kr', s)
        G.op('pool', f_copy(kf[s][:, :, 64:96], krsrc.unsqueeze(1).broadcast_to([128, 8, 32])), reads=[krkey], writes=[('kfr', s)])
        for hb in range(2):
            hs = slice(hb * 4, (hb + 1) * 4)
            for c in range(2):
                G.op('pe', f_mm(psb(5), ckvT[s][:, c, :], wukv[:, c, hb * 512:(hb + 1) * 512], c == 0, c == 1), reads=[('ckvT', s)], writes=['psWk'])
            G.op('act', f_copy_act(kf[s][:, hs, 0:64], psWk[:, :, 0:64]), reads=['psWk'], writes=[('kfn', s, hb)])
            G.op('dve', f_copy(vaug[s][:, hs, 0:64], psWk[:, :, 64:128]), reads=['psWk'], writes=[('vaug', s, hb)])
        for hb in range(2):
            for h4 in range(4):
                h = hb * 4 + h4
                G.op('pe', f_tr(pXT[0:96, h4, :], kf[s][:, h, :], ident[:]), reads=[('kfr', s), ('kfn', s, hb)], writes=['pXTk'])
            G.op('dve' if hb == 0 else 'act', (f_copy if hb == 0 else f_copy_act)(kT_sb[s][0:96, hb * 4:(hb + 1) * 4, :], pXT[0:96, 0:4, :]),
                 reads=['pXTk'], writes=[('kT', s, hb)])
        G.op('sp', f_dma(KTv[:, :, t * 128:(t + 1) * 128], kT_sb[s][0:96]), reads=[('kT', s, 0), ('kT', s, 1)], writes=[('KT', t)], dma=('ktst', s))
        G.op('pool', f_dma(Vs[:, :, t].rearrange("g p hh e -> p g hh e"), vaug[s].rearrange("p (g hh) e -> p g hh e", g=2)),
             reads=[('vaug', s, 0), ('vaug', s, 1)], writes=[('Vs', t)], dma=('vst', s))

    def S3(G, t):
        xi = t - 2
        s = t % NB
        cos_t = cosF[:, xi, :]
        sin_t = sinS[:, xi, :]
        s4 = sin_t.rearrange("p (a f i) -> p a f i", a=2, f=2)
        for c in range(3):
            G.op('pe', f_tr(pT2[:, c, :], cqn[s][:, c * 128:(c + 1) * 128], ident[:]), reads=[('cqn', s)], writes=['pT2q'])
        G.op('dve', f_copy(cqT[s], pT2[:, 0:3, :]), reads=['pT2q'], writes=[('cqT', s)])
        for hb in range(2):
            hs = slice(hb * 4, (hb + 1) * 4)
            for c in range(3):
                G.op('pe', f_mm(psb(6)[:, 0:384], cqT[s][:, c, :], wuq[:, c, hb * 384:(hb + 1) * 384], c == 0, c == 2),
                     reads=[('cqT', s)], writes=['psWq'])
            qr = psWq[:, :, 64:96]
            G.op('dve', f_tt(t1[s][:, hs, :], qr, cos_t.unsqueeze(1).broadcast_to([128, 4, 32]), ALU.mult), reads=['psWq'], writes=[('t1', s, hb)])
            q5 = qr.rearrange("p h (a f i) -> p h a f i", a=2, f=2)
            o5 = t2[s][:, hs, :].rearrange("p h (a f i) -> p h a f i", a=2, f=2)
            for f in range(2):
                G.op('dve', f_tt(o5[:, :, :, f, :], q5[:, :, :, 1 - f, :], s4[:, :, f, :].unsqueeze(1).broadcast_to([128, 4, 2, 8]), ALU.mult),
                     reads=['psWq'], writes=[('t2', s, hb, f)])
            G.op('act', f_copy_act(qf[s][:, hs, 0:64], psWq[:, :, 0:64]), reads=['psWq'], writes=[('qfn', s, hb)])
        G.op('pool', f_tt(qf[s][:, :, 64:96], t1[s], t2[s], ALU.add),
             reads=[('t1', s, 0), ('t1', s, 1)] + [('t2', s, hb, f) for hb in range(2) for f in range(2)], writes=[('qfr', s)])
        for hb in range(2):
            for h4 in range(4):
                h = hb * 4 + h4
                G.op('pe', f_tr(pXT[0:96, 4 + h4, :], qf[s][:, h, :], ident[:]), reads=[('qfr', s), ('qfn', s, hb)], writes=['pXTq'])
            G.op('dve', f_copy(qT_sb[s][0:96, hb * 4:(hb + 1) * 4, :], pXT[0:96, 4:8, :]), reads=['pXTq'], writes=[('qT', s, hb)])
        G.op('sp', f_dma(QTv[:, :, xi * 128:(xi + 1) * 128], qT_sb[s][0:96]), reads=[('qT', s, 0), ('qT', s, 1)], writes=[('QT', xi)], dma=('qtst', s))

    _nt = NKT
    G0 = Stage()
    SL(G0, 0)
    if _nt > 1:
        SL(G0, 1)
    emit_merged([G0])
    for i in range(_nt + 3):
        stages = []
        if i + 2 < _nt:
            g = Stage(); SL(g, i + 2); stages.append(g)
        if i < _nt:
            g = Stage(); S1(g, i); stages.append(g)
        if 0 <= i - 1 < _nt:
            g = Stage(); S1b(g, i - 1); stages.append(g)
        if 0 <= i - 2 < _nt:
            g = Stage(); S2(g, i - 2); stages.append(g)
        if 2 <= i - 3 < _nt:
            g = Stage(); S3(g, i - 3); stages.append(g)
        emit_merged(stages)

    R.barrier()
    if stop == 1:
        R.finalize()
        return nc
    AR.release(0)

    R.bank = {('pS', 0): 0, ('pS', 1): 1, ('pS', 2): 2, ('pS', 3): 3, ('pO', 0): 4, ('pO', 1): 5, 'pB': 6, 'pBg': 7, 'pB1': 7}
    kt_sb = AR.alloc([4, NKT * 128], BF16)
    v_sb = AR.alloc([NKT, 4, 66], BF16)
    qt_sb = [AR.alloc([4, 512], BF16) for _ in range(2)]
    NS, NP, LOOK = 4, 6, 3
    PT = [AR.alloc([512], BF16) for _ in range(NP)]
    rden = [AR.alloc([512], F32) for _ in range(2)]
    bcs = [AR.alloc([512], F32) for _ in range(2)]
    OTn = [AR.alloc([512], BF16) for _ in range(2)]
    ones_sel = AR.alloc([128], BF16)
    rhi = [AR.alloc([512], BF16) for _ in range(2)]
    rlo = [AR.alloc([512], BF16) for _ in range(2)]
    rtmp = AR.alloc([512], F32)
    R.op('dve', f_memset(ones_sel, 0.0), writes=['ones_sel'])
    R.op('dve', f_memset(ones_sel[64:65], 1.0), writes=['ones_sel'])
    for o_ in range(2):
        R.op('dve', f_memset(rhi[o_], 0.0), writes=[('rhi', o_)])
        R.op('dve', f_memset(rlo[o_], 0.0), writes=[('rlo', o_)])
    pstg = [AR.alloc([4096], F32) for _ in range(2)]
    pcast = [AR.alloc([4096], BF16) for _ in range(2)]
    pS = [psb(i) for i in range(NS)]
    pO = [psb(4), psb(5)]
    pB = psb(6)
    pBg = psb(7)[:, 0:32].rearrange("p (m v) -> p m v", v=2)
    pB1 = psb(7)[:, 32:96].rearrange("p (m v) -> p m v", v=2)

    PREP = Stage()
    pcnt = [0]

    def prep_piece(src_ap, kc, ncols, dst_ap, row_scale=None, col_scale=None, bias=None):
        i = pcnt[0] % 2
        pcnt[0] += 1
        view = pstg[i][:, 0:kc * ncols].rearrange("p (c n) -> p c n", c=kc)
        cview = pcast[i][:, 0:kc * ncols].rearrange("p (c n) -> p c n", c=kc)
        PREP.op('sp', f_dma(view, src_ap), writes=[('pstg', i)], dma=('pstg', i))
        for c in range(kc):
            eng = 'dve' if c % 2 == 0 else 'pool'
            if col_scale is not None:
                fn = f_tt(cview[:, c, :], view[:, c, :], col_scale, ALU.mult)
            elif row_scale is not None:
                fn = f_ts(cview[:, c, :], view[:, c, :], row_scale(c), ALU.mult, 1.0, ALU.mult)
            else:
                fn = f_copy(cview[:, c, :], view[:, c, :])
            PREP.op(eng, fn, reads=[('pstg', i)], writes=[('pcast', i, c)])
        if bias is not None:
            ptile, m0, rhs_of, key = bias
            for m in range(ncols // 128):
                for c in range(kc):
                    PREP.op('pe', f_mm(ptile[:, m0 + m, :], view[:, c, m * 128:(m + 1) * 128], rhs_of(c), c == 0, c == kc - 1),
                            reads=[('pstg', i)], writes=[key])
        PREP.op('pool', f_dma(dst_ap, cview), reads=[('pcast', i, c) for c in range(kc)], writes=[('WB', pcnt[0])], dma=('pcst', i))

    WBg3 = WB_wg.rearrange("p (c n) -> p c n", c=8)
    for pi in range(4):
        n0 = 1184 + pi * 512
        prep_piece(w_in[:, n0:n0 + 512].rearrange("(c p) n -> p c n", p=128), 8, 512, WBg3[:, :, pi * 512:(pi + 1) * 512],
                   row_scale=lambda c: A1[:, c:c + 1], bias=(pBg, pi * 4, lambda c: vcol[:, 0, c, :], 'pBg'))
    PREP.op('dve', f_copy(bias_g[:], pBg[:, :, 0]), reads=['pBg'], writes=['bias_g'])
    prep_piece(w_br_mla.rearrange("(c p) n -> p c n", p=128), 4, 1024, WB_wbm.rearrange("p (c n) -> p c n", c=4))
    prep_piece(w_br_pool.rearrange("(c p) n -> p c n", p=128), 4, 1024, WB_wbp.rearrange("p (c n) -> p c n", c=4),
               row_scale=lambda c: sm[:, 85 + c:86 + c])
    prep_piece(pool_w.rearrange("g c d -> c g d"), 4, 128, WB_pw.rearrange("p (c n) -> p c n", c=4))
    WBo3 = WB_wo.rearrange("p (c n) -> p c n", c=8)
    for pi in range(2):
        prep_piece(w_out[:, pi * 512:(pi + 1) * 512].rearrange("(c p) n -> p c n", p=128), 8, 512, WBo3[:, :, pi * 512:(pi + 1) * 512],
                   col_scale=gtbc[:, 0, pi * 512:(pi + 1) * 512])
    WB13 = WB_w1.rearrange("p (c n) -> p c n", c=8)
    for pi in range(8):
        prep_piece(w_mlp1[:, pi * 512:(pi + 1) * 512].rearrange("(c p) n -> p c n", p=128), 8, 512, WB13[:, :, pi * 512:(pi + 1) * 512],
                   row_scale=lambda c: A2[:, c:c + 1], bias=(pB1, pi * 4, lambda c: vcol[:, 2, c, :], 'pB1'))
    PREP.op('dve', f_copy(b1p[:], pB1[:, :, 0]), reads=['pB1'], writes=['b1p'])
    WB23 = WB_w2.rearrange("p (c n) -> p c n", c=32)
    for pi in range(8):
        prep_piece(w_mlp2[pi * 512:(pi + 1) * 512, :].rearrange("(c p) n -> p c n", p=128), 4, 1024, WB23[:, pi * 4:(pi + 1) * 4, :],
                   col_scale=gtbc[:, 1, :])
    prepq = list(PREP.l)
    PSTEP = 6

    def emit_prep(k=1):
        for _ in range(k):
            if prepq:
                o = prepq.pop(0)
                R.op(o[0], o[1], reads=o[2], writes=o[3], dma=o[4])

    for g in range(2):
        for hh in range(4):
            R.op('sp', f_dma(kt_sb[0:96, hh, :], KT[g * 4 + hh]), writes=[('kt', hh)], dma=('kt', hh))
        R.op('sp', f_dma(v_sb.rearrange("p a b c -> p (a b c)"), Vs[g].rearrange("p a b c -> p (a b c)")), writes=['v'], dma='v')
        its = [(qb, hh, kt) for qb in range(16) for hh in range(4) for kt in range(NKT)]
        n = len(its)
        pend = []

        def emit_S(i):
            qb, hh, kt = its[i]
            if hh == 0 and kt == 0:
                R.op('sp', f_dma(qt_sb[qb % 2][0:96], QTv[:, g * 4:(g + 1) * 4, qb * 512:(qb + 1) * 512]),
                     writes=[('qt', qb % 2)], dma=('qt', qb % 2))
            R.op('pe', f_mm(pS[i % NS], kt_sb[0:96, hh, kt * 128:(kt + 1) * 128], qt_sb[qb % 2][0:96, hh, :], True, True),
                 reads=[('kt', hh), ('qt', qb % 2)], writes=[('pS', i % NS)])
            R.op('act', f_act(PT[i % NP], pS[i % NS], AF.Exp, scale=SCALE), reads=[('pS', i % NS)], writes=[('PT', i % NP)])

        def emit_PV(i):
            qb, hh, kt = its[i]
            hidx = qb * 4 + hh
            o = hidx % 2
            R.op('pe', f_mm(pO[o][0:65, :], v_sb[:, kt, hh, 0:65], PT[i % NP], kt == 0, kt == NKT - 1),
                 reads=['v', ('PT', i % NP)], writes=[('pO', o)])
            if kt == NKT - 1:
                R.op('dve', f_recip(rden[o][64:65, :], pO[o][64:65, :]), reads=[('pO', o)], writes=[('rden', o)])
                R.op('dve', f_copy(rhi[o][64:65, :], rden[o][64:65, :]), reads=[('rden', o)], writes=[('rhi', o)])
                R.op('dve', f_tt(rtmp[64:65, :], rden[o][64:65, :], rhi[o][64:65, :], ALU.subtract), reads=[('rden', o), ('rhi', o)], writes=['rtmp'])
                R.op('dve', f_copy(rlo[o][64:65, :], rtmp[64:65, :]), reads=['rtmp'], writes=[('rlo', o)])
                pend.append((i + 6, hidx, qb, hh))

        def emit_epi(hidx, qb, hh):
            o = hidx % 2
            R.op('pe', f_mm(pB, ones_sel, rhi[o], True, False), reads=[('rhi', o), 'ones_sel'], writes=['pB'])
            R.op('pe', f_mm(pB, ones_sel, rlo[o], False, True), reads=[('rlo', o), 'ones_sel'], writes=['pB'])
            R.op('dve', f_copy(bcs[o][0:64], pB[0:64, :]), reads=['pB'], writes=[('bcs', o)])
            R.op('dve', f_tt(OTn[o][0:64], pO[o][0:64, :], bcs[o][0:64], ALU.mult), reads=[('pO', o), ('bcs', o)], writes=[('OTn', o)])
            r0 = (g * 4 + hh) * 64
            R.op('pool', f_dma(OT[r0:r0 + 64, qb * 512:(qb + 1) * 512], OTn[o][0:64]), reads=[('OTn', o)], writes=[('OT', g, hidx)], dma=('otst', o))

        for i in range(n + LOOK):
            if i < n:
                emit_S(i)
            if i >= LOOK:
                emit_PV(i - LOOK)
            while pend and (pend[0][0] <= i - LOOK or i == n + LOOK - 1):
                _, hidx, qb, hh = pend.pop(0)
                emit_epi(hidx, qb, hh)
            if i % PSTEP == 0:
                emit_prep()
    emit_prep(len(prepq))

    R.barrier()
    if stop == 2:
        R.finalize()
        return nc
    AR.release(0)

    R.bank = {'pT': 0, ('pG', 0): 1, ('pG', 1): 2, ('pD', 0): 3, ('pD', 1): 5, ('pY', 0): 4, ('pY', 1): 6, ('pM', 0): 5, ('pM', 1): 1, ('pP', 0): 6, ('pP', 1): 2, ('pW', 0): 7, ('pW', 1): 3}
    wg = AR.alloc([8, 2048], BF16)
    wbm = AR.alloc([4, 1024], BF16)
    wbp = AR.alloc([4, 1024], BF16)
    poolw = AR.alloc([4, 128], BF16)
    wo = AR.alloc([8, 1024], BF16)
    band = AR.alloc([20, 128], BF16)
    R.op('sp', f_dma(band.rearrange("p a b -> p (a b)"), band_d), writes=['band'], dma='band')
    for nm, dst, src in [('wg', wg, WB_wg), ('wbm', wbm, WB_wbm), ('wbp', wbp, WB_wbp), ('poolw', poolw, WB_pw), ('wo', wo, WB_wo)]:
        R.op('sp', f_dma(dst.rearrange("p a b -> p (a b)"), src), writes=[nm], dma=nm)
    R.barrier()

    xs3 = [AR.alloc([4, 1024], F32) for _ in range(2)]
    xn3 = [AR.alloc([1024], BF16) for _ in range(2)]
    xT3 = [AR.alloc([8, 512], BF16) for _ in range(2)]
    sig = AR.alloc([16, 512], BF16)
    us = [AR.alloc([6, 512], BF16) for _ in range(2)]
    dT = [AR.alloc([512], BF16) for _ in range(2)]
    yT = AR.alloc([4, 512], BF16)
    ot_sb = [AR.alloc([4, 512], BF16) for _ in range(2)]
    tA = [AR.alloc([512], F32) for _ in range(2)]
    tB = [AR.alloc([512], F32) for _ in range(2)]
    mT = AR.alloc([8, 512], BF16)

    pT = psb(0, BF16).rearrange("p (c n) -> p c n", c=8)
    pG = [psb(1), psb(2)]
    pDs = [psb(3), psb(5)]
    pYs = [psb(4), psb(6)]
    pMs = [psb(5), psb(1)]
    pPs = [psb(6), psb(2)]
    pWs = [psb(7), psb(3)]

    def F3(G, b):
        s = b % 2
        G.op('sp', f_dma(xs3[s], x[b * 512:(b + 1) * 512, :].rearrange("(j p) f -> p j f", p=128)),
             writes=[('xs3', s, j) for j in range(4)], dma=('xs3', s))
        G.op('sp', f_dma(ot_sb[s], OT[:, b * 512:(b + 1) * 512].rearrange("(c p) t -> p c t", p=128)), writes=[('ot', s)], dma=('ot', s))
        lo = max(4 * b - 1, 0)
        hi = min(4 * b + 5, 64)
        d0 = lo - (4 * b - 1)
        G.op('sp', f_dma(us[s][:, d0:d0 + hi - lo, :], U[lo * 128:hi * 128, :].rearrange("(j p) f -> p j f", p=128)),
             writes=[('us', s)], dma=('us', s))
        for j in range(4):
            T = 4 * b + j
            G.op('act', f_act(xn3[j % 2], xs3[s][:, j, :], AF.Copy, scale=rstd1[:, T:T + 1]), reads=[('xs3', s, j)], writes=[('xn3', j % 2)])
            for c in range(8):
                G.op('pe', f_tr(pT[:, c, :], xn3[j % 2][:, c * 128:(c + 1) * 128], ident[:]), reads=[('xn3', j % 2)], writes=['pT'])
            G.op('dve', f_copy(xT3[s][:, :, j * 128:(j + 1) * 128], pT), reads=['pT'], writes=[('xT3', s, j)])

    def G3(G, b):
        s = b % 2
        xT3keys = [('xT3', s, j) for j in range(4)]
        G_out = G
        GA, GB = Stage(), Stage()
        G = GA
        for m in range(16):
            for c in range(8):
                G.op('pe', f_mm(pG[m % 2], wg[:, c, m * 128:(m + 1) * 128], xT3[s][:, c, :], c == 0, c == 7), reads=xT3keys, writes=[('pG', m % 2)])
            G.op('act', f_act(sig[:, m, :], pG[m % 2], AF.Sigmoid, bias=bias_g[:, m:m + 1]), reads=[('pG', m % 2)], writes=[('sig', m)])
        G = GB
        for g in range(4):
            for j in range(4):
                T = 4 * b + j
                parts = []
                if T > 0:
                    parts.append((j, g * 5 + 0))
                parts.append((j + 1, g * 5 + (3 if T == 0 else (4 if T == 63 else 1))))
                if T < 63:
                    parts.append((j + 2, g * 5 + 2))
                for k, (slot, bi) in enumerate(parts):
                    G.op('pe', f_mm(pDs[g % 2][:, j * 128:(j + 1) * 128], us[s][:, slot, g * 128:(g + 1) * 128], band[:, bi, :], k == 0, k == len(parts) - 1),
                         reads=[('us', s), 'band'], writes=[('pD', g % 2)])
            G.op('dve', f_copy(dT[g % 2], pDs[g % 2]), reads=[('pD', g % 2)], writes=[('dT', g % 2)])
            G.op('pe', f_mm(pYs[g % 2], poolw[:, g, :], dT[g % 2], True, True), reads=[('dT', g % 2)], writes=[('pY', g % 2)])
            G.op('act', f_copy_act(yT[:, g, :], pYs[g % 2]), reads=[('pY', g % 2)], writes=[('yT', g)])
        G = G_out
        items = []
        for si, sg in enumerate([GA, GB]):
            for k, o in enumerate(sg.l):
                items.append(((k + 0.5) / len(sg.l), si, k, o))
        items.sort(key=lambda z: (z[0], z[1]))
        for _, _, _, o in items:
            G.l.append(o)
        yTkeys = [('yT', g) for g in range(4)]
        for m in range(8):
            k2 = m % 2
            for c in range(4):
                G.op('pe', f_mm(pMs[k2], wbm[:, c, m * 128:(m + 1) * 128], ot_sb[s][:, c, :], c == 0, c == 3), reads=[('ot', s)], writes=[('pM', k2)])
            for g in range(4):
                G.op('pe', f_mm(pPs[k2], wbp[:, g, m * 128:(m + 1) * 128], yT[:, g, :], g == 0, g == 3), reads=yTkeys, writes=[('pP', k2)])
            G.op('dve', f_tt(tA[k2], pMs[k2], sig[:, m, :], ALU.mult), reads=[('pM', k2), ('sig', m)], writes=[('tA', k2)])
            G.op('dve', f_tt(tB[k2], pPs[k2], sig[:, 8 + m, :], ALU.mult), reads=[('pP', k2), ('sig', 8 + m)], writes=[('tB', k2)])
            G.op('pool', f_tt(mT[:, m, :], tA[k2], tB[k2], ALU.add), reads=[('tA', k2), ('tB', k2)], writes=[('mT', m)])
        mTkeys = [('mT', m) for m in range(8)]
        for j in range(4):
            for half in range(2):
                k2 = half
                for c in range(8):
                    G.op('pe', f_mm(pWs[k2], mT[:, c, j * 128:(j + 1) * 128], wo[:, c, half * 512:(half + 1) * 512], c == 0, c == 7), reads=mTkeys, writes=[('pW', k2)])
                G.op('dve', f_tt(xs3[s][:, j, half * 512:(half + 1) * 512], pWs[k2], xs3[s][:, j, half * 512:(half + 1) * 512], ALU.add),
                     reads=[('pW', k2), ('xs3', s, j)], writes=[('xs3', s, j)])
        G.op('pool', f_dma(X1[b * 512:(b + 1) * 512, :].rearrange("(j p) f -> p j f", p=128), xs3[s]),
             reads=[('xs3', s, j) for j in range(4)], writes=[('X1', b)], dma=('x1st', s))

    g0 = Stage(); F3(g0, 0); emit_merged([g0])
    for b in range(16):
        stages = [Stage()]
        G3(stages[0], b)
        if b + 1 < 16:
            g1 = Stage(); F3(g1, b + 1); stages.append(g1)
        emit_merged(stages)

    R.barrier()
    if stop == 3:
        R.finalize()
        return nc
    AR.release(0)

    R.bank = {'pB1': 0, 'pT': 0, ('pH', 0): 1, ('pH', 1): 2, ('pH', 2): 3, ('pY2', 0): 4, ('pY2', 1): 5}
    w1 = AR.alloc([8, 4096], BF16)
    w2 = AR.alloc([32, 1024], BF16)
    fg = AR.alloc([1024], F32)
    b1 = b1p
    R.op('sp', f_dma(fg, bigbc_d[:, 2048:3072]), writes=['fg'], dma='fg')
    R.op('sp', f_dma(w1.rearrange("p a b -> p (a b)"), WB_w1), writes=['w1'], dma='w1')
    R.op('sp', f_dma(w2.rearrange("p a b -> p (a b)"), WB_w2), writes=['w2'], dma='w2')
    R.barrier()

    x1s = [AR.alloc([2, 1024], F32) for _ in range(2)]
    xn4 = [AR.alloc([1024], BF16) for _ in range(2)]
    h2T = [AR.alloc([8, 256], BF16) for _ in range(2)]
    rr = [AR.alloc([256], F32) for _ in range(2)]
    hidT = AR.alloc([32, 256], BF16)
    x2 = [AR.alloc([1024], F32) for _ in range(2)]
    outs = x2
    junk4 = AR.alloc([1024], BF16)

    pT = psb(0, BF16).rearrange("p (c n) -> p c n", c=8)
    pH = [psb(1), psb(2), psb(3)]
    pY2 = [psb(4), psb(5)]
    outkeys = []
    def F4(G, b):
        s = b % 2
        G.op('sp', f_dma(x1s[s], X1[b * 256:(b + 1) * 256, :].rearrange("(j p) f -> p j f", p=128)),
             writes=[('x1s', s, 0), ('x1s', s, 1)], dma=('x1s', s))
        for j in range(2):
            G.op('act', f_act(junk4, x1s[s][:, j, :], AF.Square, accum_out=st(9, j)), reads=[('x1s', s, j)], writes=[('ss2', j)])
            G.op('act', f_act(st(10, j), st(9, j), AF.Sqrt, scale=1.0 / D, bias=epsb[:]), reads=[('ss2', j)], writes=[('sd2', j)])
            G.op('dve', f_recip(st(11, j), st(10, j)), reads=[('sd2', j)], writes=[('r2', j)])
            G.op('dve', f_ts(xn4[j], x1s[s][:, j, :], st(11, j), ALU.mult), reads=[('x1s', s, j), ('r2', j)], writes=[('xn4', j)])
            for c in range(8):
                G.op('pe', f_tr(pT[:, c, :], xn4[j][:, c * 128:(c + 1) * 128], ident[:]), reads=[('xn4', j)], writes=['pT'])
            G.op('act', f_copy_act(h2T[s][:, :, j * 128:(j + 1) * 128], pT), reads=['pT'], writes=[('h2T', s, j)])

    def M1(G, b):
        s = b % 2
        for m in range(32):
            for c in range(8):
                G.op('pe', f_mm(pH[m % 3][:, 0:256], w1[:, c, m * 128:(m + 1) * 128], h2T[s][:, c, :], c == 0, c == 7),
                     reads=[('h2T', s, 0), ('h2T', s, 1)], writes=[('pH', m % 3)])
            G.op('act', f_act(rr[m % 2], pH[m % 3][:, 0:256], AF.Relu, bias=b1[:, m:m + 1]), reads=[('pH', m % 3)], writes=[('rr', m % 2)])
            G.op('dve' if m % 2 == 0 else 'pool', f_tt(hidT[:, m, :], rr[m % 2], rr[m % 2], ALU.mult), reads=[('rr', m % 2)], writes=[('hidT', m)])

    def M2(G, b):
        s = b % 2
        hkeys = [('hidT', m) for m in range(32)]
        for j in range(2):
            for half in range(2):
                pk = (j * 2 + half) % 2
                for m in range(32):
                    G.op('pe', f_mm(pY2[pk], hidT[:, m, j * 128:(j + 1) * 128], w2[:, m, half * 512:(half + 1) * 512], m == 0, m == 31),
                         reads=hkeys, writes=[('pY2', pk)])
                G.op('dve', f_tt(x2[j][:, half * 512:(half + 1) * 512], pY2[pk], x1s[s][:, j, half * 512:(half + 1) * 512], ALU.add),
                     reads=[('pY2', pk), ('x1s', s, j)], writes=[('x2', j, half)])
            G.op('act', f_act(junk4, x2[j], AF.Square, accum_out=st(12, j)), reads=[('x2', j, 0), ('x2', j, 1)], writes=[('ss3', j)])
            G.op('act', f_act(st(13, j), st(12, j), AF.Sqrt, scale=1.0 / D, bias=epsb[:]), reads=[('ss3', j)], writes=[('sd3', j)])
            G.op('dve', f_recip(st(14, j), st(13, j)), reads=[('sd3', j)], writes=[('r3', j)])
            G.op('dve', f_stt(outs[j], x2[j], st(14, j), fg, ALU.mult, ALU.mult), reads=[('x2', j, 0), ('x2', j, 1), ('r3', j), 'fg'],
                 writes=[('x2', j, 0), ('x2', j, 1)])
            row = (b * 2 + j) * 128
            G.op('pool', f_dma(out[row:row + 128, :], outs[j]), reads=[('x2', j, 0), ('x2', j, 1)], writes=[('out', b, j)], dma=('outst', j))
            outkeys.append(('out', b, j))

    g0 = Stage(); F4(g0, 0); emit_merged([g0])
    for b in range(32):
        stages = [Stage()]
        M1(stages[0], b)
        if b + 1 < 32:
            g1 = Stage(); F4(g1, b + 1); stages.append(g1)
        emit_merged(stages)
        g2 = Stage(); M2(g2, b); emit_merged([g2])
    R.op('sp', None, reads=outkeys)
    R.finalize()
    return nc


def f_copy_act(out, in_):
    return lambda e: e.activation(out=out, in_=in_, func=AF.Copy)


def _host_consts():
    bf = ml_dtypes.bfloat16
    ident = np.eye(128, dtype=np.float32).astype(bf)
    t = np.arange(L)
    row = (t // 64).astype(np.float32)
    col = (t % 64).astype(np.float32)
    inv_freq = (np.float32(10000.0) ** (-np.arange(0, 16, 2, dtype=np.float32) / np.float32(16))).astype(np.float32)
    ar = (row[:, None] * inv_freq).astype(np.float32)
    ac = (col[:, None] * inv_freq).astype(np.float32)
    cr, sr, cc, sc = np.cos(ar), np.sin(ar), np.cos(ac), np.sin(ac)
    cosF = np.concatenate([cr, cr, cc, cc], axis=1).astype(np.float32)
    sinS = np.concatenate([-sr, sr, -sc, sc], axis=1).astype(np.float32)
    cosF = np.ascontiguousarray(cosF.reshape(64, 128, 32).transpose(1, 0, 2).reshape(128, 64 * 32))
    sinS = np.ascontiguousarray(sinS.reshape(64, 128, 32).transpose(1, 0, 2).reshape(128, 64 * 32))
    band = np.zeros((128, 20, 128), np.float32)
    for g, w in enumerate((2, 4, 8, 16)):
        hw = w // 2
        for kind in range(5):
            T = {0: 5, 1: 5, 2: 5, 3: 0, 4: 63}[kind]
            dT = {0: -1, 1: 0, 2: 1, 3: 0, 4: 0}[kind]
            m = np.zeros((128, 128), np.float32)
            for tl in range(128):
                tg = T * 128 + tl
                lo = max(tg - hw, 0)
                hi = min(tg + hw, L)
                cnt = hi - lo
                for tp in range(lo, hi):
                    p = tp - (T + dT) * 128
                    if 0 <= p < 128:
                        m[p, tl] += 1.0 / cnt
                p = tg - (T + dT) * 128
                if 0 <= p < 128:
                    m[p, tl] -= 1.0
            band[:, g * 5 + kind, :] = m
    band = np.ascontiguousarray(band.reshape(128, 20 * 128)).astype(bf)
    return ident, cosF, sinS, band


_CACHE = {}


def prep_inputs(x, c, ctx, c_ctx, w_ada, b_ada, norm1_g, w_in, q_norm_g, kv_norm_g, w_uq, w_ukv,
                w_br_mla, pool_w, pool_scale, w_br_pool, w_out, norm2_g, w_mlp1, w_mlp2, final_g):
    f = lambda a: np.ascontiguousarray(np.asarray(a, dtype=np.float32))
    x, c, ctx, c_ctx = f(x), f(c), f(ctx), f(c_ctx)
    w_ada, b_ada, w_in, w_uq, w_ukv = f(w_ada)[0], f(b_ada)[0], f(w_in)[0], f(w_uq)[0], f(w_ukv)[0]
    w_br_mla, pool_w, w_br_pool, w_out = f(w_br_mla)[0], f(pool_w)[0], f(w_br_pool)[0], f(w_out)[0]
    w_mlp1, w_mlp2 = f(w_mlp1)[0], f(w_mlp2)[0]
    norm1_g, norm2_g, q_norm_g, kv_norm_g, pool_scale, final_g = (f(norm1_g)[0], f(norm2_g)[0], f(q_norm_g)[0],
                                                                   f(kv_norm_g)[0], f(pool_scale)[0], f(final_g))
    if 'consts' not in _CACHE:
        _CACHE['consts'] = _host_consts()
    ident, cosF, sinS, band = _CACHE['consts']
    col = lambda v: v.reshape(-1, 128).T
    bigbc = np.ascontiguousarray(np.broadcast_to(
        np.concatenate([b_ada[2 * D:3 * D], b_ada[5 * D:6 * D], final_g])[None, :], (128, 3 * D)))
    shared = dict(w_ada=w_ada, w_in=w_in, w_uq=w_uq, w_ukv=w_ukv, w_br_mla=w_br_mla, pool_w=pool_w,
                  w_br_pool=w_br_pool, w_out=w_out, w_mlp1=w_mlp1, w_mlp2=w_mlp2, ident=ident,
                  cosF=cosF, sinS=sinS, band=band, bigbc=bigbc)
    in_maps = []
    for b in range(x.shape[0]):
        ccol = np.stack([col(c[b]), col(c_ctx)], axis=2).reshape(128, 16)
        smallf = np.ascontiguousarray(np.concatenate(
            [ccol, col(b_ada), col(norm1_g), col(norm2_g), col(q_norm_g), col(kv_norm_g), col(pool_scale)], axis=1).astype(np.float32))
        assert smallf.shape == (128, 89)
        m = dict(shared)
        m.update(x=x[b], ctx=ctx[b], smallf=smallf)
        in_maps.append(m)
    return in_maps


def kernel(**inputs):
    in_maps = prep_inputs(**inputs)
    if 'nc' not in _CACHE:
        _CACHE['nc'] = build_program()
    nc = _CACHE['nc']
    res = run_bass_kernel_spmd(nc, in_maps, core_ids=list(range(8)))
    return np.stack([np.asarray(r["out"], dtype=np.float32) for r in res.results], axis=0)
```
